# Optimizing a Trainium2 kernel written in Bass

```python
import math
import jax, jax.numpy as jnp
from jax import lax
import numpy as np

D_MODEL = 2048
BATCH = 1
SEQ = 8192
DEPTH = 4

CHUNK = 64
A_HEADS = 8
A_KDIM = 128
A_VDIM = 128
A_KWIDTH = A_HEADS * A_KDIM
A_VWIDTH = A_HEADS * A_VDIM
B_GROUPS = 64
B_GROUP_CH = 16
B_STATE = 64
B_WIDTH = B_GROUPS * B_GROUP_CH
S5_DT_MIN = 1e-3
S5_DT_MAX = 1e-1
S5_EIG_CLIP = -1e-4
IN_SIZES = (A_KWIDTH, A_KWIDTH, A_VWIDTH, A_VWIDTH, B_WIDTH, D_MODEL, D_MODEL)
IN_COLS = sum(IN_SIZES)
IN_SPLITS = tuple(int(v) for v in np.cumsum(IN_SIZES)[:-1])
P_HEADS = 8
P_QDIM = 256
P_HALF = P_QDIM // 2
P_NKEYS = 128
P_NEXP = P_NKEYS * P_NKEYS
P_TOPK = 16
P_BLOCK = 128
ALPHA = (2.0 * DEPTH) ** 0.25
BETA = (8.0 * DEPTH) ** -0.25
LN_EPS = 1e-5
RMS_EPS = 1e-6

kernel_name = 'hybrid_hgrn2_s5_peer_deepnorm'


def layer_norm(x, g, b):
    xf = x.astype(jnp.float32)
    mu = jnp.mean(xf, axis=-1, keepdims=True)
    var = jnp.mean(jnp.square(xf - mu), axis=-1, keepdims=True)
    y = (xf - mu) * lax.rsqrt(var + LN_EPS) * g.astype(jnp.float32) + b.astype(jnp.float32)
    return y.astype(x.dtype)


def hgrn2_lower_bounds(lb_logits):
    p = jax.nn.softmax(lb_logits.astype(jnp.float32), axis=0)
    c = jnp.cumsum(p, axis=0)
    return c - c[0:1]


def hgrn2_mixer(q, fz, i, g, lb, norm_g):
    bsz, L, _ = q.shape
    nc = L // CHUNK
    f32 = jnp.float32
    q = q.astype(f32); fz = fz.astype(f32); i = i.astype(f32)
    logf = jnp.logaddexp(jnp.log(lb), jnp.log1p(-lb) + jax.nn.log_sigmoid(fz))
    k = -jnp.expm1(logf)

    def to_chunks(t, d):
        return jnp.moveaxis(t.reshape(bsz, nc, CHUNK, A_HEADS, d), 1, 0)

    qc = to_chunks(q, A_KDIM)
    kc = to_chunks(k, A_KDIM)
    vc = to_chunks(i, A_VDIM)
    bc = jnp.cumsum(to_chunks(logf, A_KDIM), axis=2)
    mask = jnp.tril(jnp.ones((CHUNK, CHUNK), dtype=bool))[None, :, :, None, None]

    def step(S, inp):
        qq, kk, vv, bb = inp
        b_last = bb[:, -1]
        o_inter = jnp.einsum('bthd,bhdv->bthv', qq * jnp.exp(bb), S)
        diff = bb[:, :, None] - bb[:, None, :]
        decay = jnp.exp(jnp.where(mask, diff, -jnp.inf))
        att = jnp.sum(qq[:, :, None] * kk[:, None, :] * decay, axis=-1)
        o_intra = jnp.einsum('btsh,bshv->bthv', att, vv)
        k_tail = kk * jnp.exp(b_last[:, None] - bb)
        S = jnp.exp(b_last)[..., None] * S + jnp.einsum('bshd,bshv->bhdv', k_tail, vv)
        return S, o_inter + o_intra

    S0 = jnp.zeros((bsz, A_HEADS, A_KDIM, A_VDIM), f32)
    _, o = lax.scan(step, S0, (qc, kc, vc, bc))
    o = jnp.moveaxis(o, 0, 1).reshape(bsz, L, A_HEADS, A_VDIM)
    o = o * lax.rsqrt(jnp.mean(jnp.square(o), axis=-1, keepdims=True) + RMS_EPS)
    o = o * norm_g.astype(f32).reshape(A_HEADS, A_VDIM)
    o = o.reshape(bsz, L, A_VWIDTH) * jax.nn.sigmoid(g.astype(f32))
    return o


def _complex_affine_combine(e1, e2):
    a1r, a1i, b1r, b1i = e1
    a2r, a2i, b2r, b2i = e2
    ar = a2r * a1r - a2i * a1i
    ai = a2r * a1i + a2i * a1r
    br = a2r * b1r - a2i * b1i + b2r
    bi = a2r * b1i + a2i * b1r + b2i
    return ar, ai, br, bi


def s5_mixer(u, lam_re, lam_im, log_step, b_re, b_im, c_re, c_im, d, w_glu):
    bsz, L, _ = u.shape
    f32 = jnp.float32
    uf = u.astype(f32).reshape(bsz, L, B_GROUPS, B_GROUP_CH)
    lr = jnp.minimum(lam_re.astype(f32), S5_EIG_CLIP)
    li = lam_im.astype(f32)
    dt = jnp.exp(log_step.astype(f32))[:, None]
    mag = jnp.exp(lr * dt)
    ar = mag * jnp.cos(li * dt)
    ai = mag * jnp.sin(li * dt)
    den = lr * lr + li * li
    nr = ar - 1.0
    zr = (nr * lr + ai * li) / den
    zi = (ai * lr - nr * li) / den
    br_, bi_ = b_re.astype(f32), b_im.astype(f32)
    bbr = zr[..., None] * br_ - zi[..., None] * bi_
    bbi = zr[..., None] * bi_ + zi[..., None] * br_
    bur = jnp.einsum('blgn,gpn->blgp', uf, bbr)
    bui = jnp.einsum('blgn,gpn->blgp', uf, bbi)
    a_r = jnp.broadcast_to(ar, bur.shape)
    a_i = jnp.broadcast_to(ai, bui.shape)
    _, _, xr, xi = lax.associative_scan(_complex_affine_combine, (a_r, a_i, bur, bui), axis=1)
    y = (jnp.einsum('blgp,gnp->blgn', xr, c_re.astype(f32))
         - jnp.einsum('blgp,gnp->blgn', xi, c_im.astype(f32))
         + d.astype(f32) * uf)
    y = jax.nn.gelu(y).reshape(bsz, L, B_WIDTH)
    h = y @ w_glu.astype(f32)
    return h[..., :B_WIDTH] * jax.nn.sigmoid(h[..., B_WIDTH:])


def peer_ffn(x, w_q, keys, u_tab, v_tab):
    bsz, L, _ = x.shape
    f32 = jnp.float32
    q = (x @ w_q).astype(f32).reshape(bsz, L, P_HEADS, P_QDIM)
    kf = keys.astype(f32)
    s1 = jnp.einsum('blhd,hnd->blhn', q[..., :P_HALF], kf[:, 0])
    s2 = jnp.einsum('blhd,hnd->blhn', q[..., P_HALF:], kf[:, 1])
    v1, i1 = lax.top_k(s1, P_TOPK)
    v2, i2 = lax.top_k(s2, P_TOPK)
    cand = (v1[..., :, None] + v2[..., None, :]).reshape(bsz, L, P_HEADS, P_TOPK * P_TOPK)
    cid = (i1[..., :, None] * P_NKEYS + i2[..., None, :]).reshape(bsz, L, P_HEADS, P_TOPK * P_TOPK)
    vals, pos = lax.top_k(cand, P_TOPK)
    eid = jnp.take_along_axis(cid, pos, axis=-1)
    gate = jax.nn.softmax(vals, axis=-1).astype(x.dtype)
    nb = L // P_BLOCK

    def blocks(t):
        return jnp.moveaxis(t.reshape((bsz, nb, P_BLOCK) + t.shape[2:]), 1, 0)

    def block_fn(args):
        xb, eb, gb = args
        ue = jnp.take(u_tab, eb, axis=0)
        ve = jnp.take(v_tab, eb, axis=0)
        act = jax.nn.gelu(jnp.einsum('btd,bthkd->bthk', xb, ue)) * gb
        return jnp.einsum('bthk,bthkd->btd', act, ve).astype(x.dtype)

    y = lax.map(block_fn, (blocks(x), blocks(eid), blocks(gate)))
    return jnp.moveaxis(y, 0, 1).reshape(bsz, L, D_MODEL)


def setup_inputs(seed: int = 0) -> dict:
    key = jax.random.key(seed)
    ks = jax.random.split(key, 24)
    f32 = jnp.float32
    nrm = lambda k, s, sc: jax.random.normal(k, s, f32) * sc
    n_idx = jnp.arange(B_STATE, dtype=f32) * math.pi
    return {
        'x': nrm(ks[0], (BATCH, SEQ, D_MODEL), 1.0),
        'w_in': nrm(ks[1], (DEPTH, D_MODEL, IN_COLS), D_MODEL ** -0.5),
        'hgrn_lb_logits': nrm(ks[2], (DEPTH, A_KWIDTH), 0.1),
        'hgrn_norm_g': 1.0 + nrm(ks[3], (DEPTH, A_VWIDTH), 0.02),
        's5_lambda_re': -0.5 + nrm(ks[4], (DEPTH, B_GROUPS, B_STATE), 0.01),
        's5_lambda_im': n_idx + nrm(ks[5], (DEPTH, B_GROUPS, B_STATE), 0.01),
        's5_log_step': jax.random.uniform(ks[6], (DEPTH, B_GROUPS), f32, math.log(S5_DT_MIN), math.log(S5_DT_MAX)),
        's5_b_re': nrm(ks[7], (DEPTH, B_GROUPS, B_STATE, B_GROUP_CH), (2.0 * B_GROUP_CH) ** -0.5),
        's5_b_im': nrm(ks[8], (DEPTH, B_GROUPS, B_STATE, B_GROUP_CH), (2.0 * B_GROUP_CH) ** -0.5),
        's5_c_re': nrm(ks[9], (DEPTH, B_GROUPS, B_GROUP_CH, B_STATE), (2.0 * B_STATE) ** -0.5),
        's5_c_im': nrm(ks[10], (DEPTH, B_GROUPS, B_GROUP_CH, B_STATE), (2.0 * B_STATE) ** -0.5),
        's5_d': nrm(ks[11], (DEPTH, B_GROUPS, B_GROUP_CH), 1.0),
        's5_w_glu': nrm(ks[12], (DEPTH, B_WIDTH, 2 * B_WIDTH), B_WIDTH ** -0.5),
        'w_up_a': nrm(ks[13], (DEPTH, A_VWIDTH, D_MODEL), BETA * A_VWIDTH ** -0.5),
        'w_up_b': nrm(ks[14], (DEPTH, B_WIDTH, D_MODEL), BETA * B_WIDTH ** -0.5),
        'w_o': nrm(ks[15], (DEPTH, D_MODEL, D_MODEL), BETA * D_MODEL ** -0.5),
        'ln1_g': 1.0 + nrm(ks[16], (DEPTH, D_MODEL), 0.02),
        'ln1_b': nrm(ks[17], (DEPTH, D_MODEL), 0.02),
        'peer_w_q': nrm(ks[18], (DEPTH, D_MODEL, P_HEADS * P_QDIM), D_MODEL ** -0.5),
        'peer_keys': nrm(ks[19], (DEPTH, P_HEADS, 2, P_NKEYS, P_HALF), P_HALF ** -0.5),
        'peer_u': nrm(ks[20], (DEPTH, P_NEXP, D_MODEL), D_MODEL ** -0.5),
        'peer_v': nrm(ks[21], (DEPTH, P_NEXP, D_MODEL), BETA * D_MODEL ** -0.5),
        'ln2_g': 1.0 + nrm(ks[22], (DEPTH, D_MODEL), 0.02),
        'ln2_b': nrm(ks[23], (DEPTH, D_MODEL), 0.02),
    }


def reference(x, w_in, hgrn_lb_logits, hgrn_norm_g, s5_lambda_re, s5_lambda_im, s5_log_step,
              s5_b_re, s5_b_im, s5_c_re, s5_c_im, s5_d, s5_w_glu, w_up_a, w_up_b, w_o,
              ln1_g, ln1_b, peer_w_q, peer_keys, peer_u, peer_v, ln2_g, ln2_b):
    lbs = hgrn2_lower_bounds(hgrn_lb_logits)
    for l in range(DEPTH):
        proj = x @ w_in[l]
        qa, fa, ia, ga, ub, gate_a, gate_b = jnp.split(proj, IN_SPLITS, axis=-1)
        oa = hgrn2_mixer(qa, fa, ia, ga, lbs[l], hgrn_norm_g[l]).astype(x.dtype)
        ob = s5_mixer(ub, s5_lambda_re[l], s5_lambda_im[l], s5_log_step[l], s5_b_re[l], s5_b_im[l],
                      s5_c_re[l], s5_c_im[l], s5_d[l], s5_w_glu[l]).astype(x.dtype)
        merged = (jax.nn.sigmoid(gate_a) * (oa @ w_up_a[l])
                  + jax.nn.sigmoid(gate_b) * (ob @ w_up_b[l]))
        x = layer_norm(ALPHA * x + merged @ w_o[l], ln1_g[l], ln1_b[l])
        y = peer_ffn(x, peer_w_q[l], peer_keys[l], peer_u[l], peer_v[l])
        x = layer_norm(ALPHA * x + y, ln2_g[l], ln2_b[l])
    return x
```

```python
import math
from contextlib import ExitStack
import numpy as np
import concourse.bass as bass
import concourse.mybir as mybir
from concourse.bass_utils import run_bass_kernel_spmd

F32 = mybir.dt.float32
BF16 = mybir.dt.bfloat16
ALU = mybir.AluOpType
AF = mybir.ActivationFunctionType
AX = mybir.AxisListType

ENGS = ['tensor', 'vector', 'scalar', 'gpsimd', 'sync']
NDSEM = 6
NCORES = 8
T = 1024
NT = 8
D = 2048
KT = 16
ALPHA = 8.0 ** 0.25
LN_EPS = 1e-5
RMS_EPS = 1e-6
TWO_PI = 2.0 * math.pi
MAGIC = 12582912.0


class Prog:
    def __init__(self, nc):
        self.nc = nc
        self.ops = {e: [] for e in ENGS}
        self.lastw = {}
        self.readers = {}
        self.ndma = {e: 0 for e in ENGS}
        self.dma_ops = {e: [] for e in ENGS}
        self.pending = {e: [] for e in ENGS}

    def add(self, eng, fn, r=(), w=(), dma=False):
        idx = len(self.ops[eng])
        deps = list(self.pending[eng])
        self.pending[eng] = []
        for t in r:
            if t in self.lastw:
                deps.append(self.lastw[t])
        for t in w:
            if t in self.lastw:
                deps.append(self.lastw[t])
            deps.extend(self.readers.get(t, ()))
        op = dict(fn=fn, deps=deps, dma=dma, signal=False, eng=eng, idx=idx)
        if dma:
            j = self.ndma[eng]
            self.ndma[eng] += 1
            op['dj'] = j
            if j >= NDSEM:
                deps.append(self.dma_ops[eng][j - NDSEM])
            self.dma_ops[eng].append((eng, idx))
        self.ops[eng].append(op)
        ref = (eng, idx)
        for t in r:
            self.readers.setdefault(t, []).append(ref)
        for t in w:
            self.lastw[t] = ref
            self.readers[t] = []
        return ref

    def barrier(self):
        deps = []
        for e in ENGS:
            nonD = [i for i, o in enumerate(self.ops[e]) if not o['dma']]
            if nonD:
                deps.append((e, nonD[-1]))
            deps.extend(self.dma_ops[e][-NDSEM:])
        for e in ENGS:
            self.pending[e] = list(self.pending[e]) + deps
        self.lastw = {}
        self.readers = {}

    def mm(self, out, lhsT, rhs, start, stop, r, w):
        return self.add('tensor', lambda e: e.matmul(out, lhsT, rhs, start=start, stop=stop), r, w)

    def tr(self, out, in_, ident, r, w):
        return self.add('tensor', lambda e: e.transpose(out, in_, ident), r, w)

    def dma(self, eng, out, in_, r, w, **kw):
        return self.add(eng, lambda e: e.dma_start(out=out, in_=in_, **kw), r, w, dma=True)

    def emit(self):
        nc = self.nc
        ops = self.ops
        for e in ENGS:
            for op in ops[e]:
                nd = []
                seen = set()
                for d in op['deps']:
                    if d in seen:
                        continue
                    seen.add(d)
                    dop = ops[d[0]][d[1]]
                    if not dop['dma']:
                        if d[0] == e and e == 'tensor':
                            continue
                        dop['signal'] = True
                    nd.append(d)
                op['deps'] = nd
        for e in ENGS:
            c = 0
            for op in ops[e]:
                if op['dma']:
                    continue
                if op['signal']:
                    c += 1
                    op['semval'] = c
        with ExitStack() as st:
            esem = {e: st.enter_context(nc.semaphore('s_' + e)) for e in ENGS}
            dsem = {e: [st.enter_context(nc.semaphore('d_%s%d' % (e, i))) for i in range(NDSEM)]
                    for e in ENGS if self.ndma[e] > 0}
            block = st.enter_context(nc.Block())

            def run(e, eng):
                waited = {e2: 0 for e2 in ENGS}
                dwaited = {e2: {} for e2 in ENGS}
                for op in ops[e]:
                    for d in op['deps']:
                        dop = ops[d[0]][d[1]]
                        if dop['dma']:
                            j = dop['dj']
                            slot = j % NDSEM
                            if dwaited[d[0]].get(slot, -1) >= j:
                                continue
                            dwaited[d[0]][slot] = j
                            eng.wait_ge(dsem[d[0]][slot], 16 * (j // NDSEM + 1))
                        else:
                            v = dop['semval']
                            if waited[d[0]] >= v:
                                continue
                            waited[d[0]] = v
                            eng.wait_ge(esem[d[0]], v)
                    ins = op['fn'](eng)
                    if op['dma']:
                        ins.then_inc(dsem[e][op['dj'] % NDSEM], 16)
                    elif op['signal']:
                        ins.then_inc(esem[e], 1)
                if self.ndma[e] > 0:
                    n = self.ndma[e]
                    for i in range(NDSEM):
                        cnt = (n - i + NDSEM - 1) // NDSEM if n > i else 0
                        if cnt > 0:
                            eng.wait_ge(dsem[e][i], 16 * cnt)

            for e in ENGS:
                if ops[e]:
                    getattr(block, e)(lambda eng, e=e: run(e, eng))


def bc(ap, shape):
    return ap.to_broadcast(list(shape))


def build(dbg=False, phases=99):
    nc = bass.Bass('TRN2', target_bir_lowering=False)
    P = Prog(nc)

    def din(name, shape):
        return nc.dram_tensor(name, list(shape), F32, kind='ExternalInput').ap()

    def dout(name, shape):
        return nc.dram_tensor(name, list(shape), F32, kind='ExternalOutput').ap()

    x_in = din('x', [T, D])
    w_in = din('w_in', [D, 9216])
    lb_logits = din('lb_logits', [4, 1024])
    lmask = din('lmask', [1, 4])
    cmask = din('cmask', [1, 8])
    norm_g = din('norm_g', [1, 1024])
    if phases > 2:
        lam_re = din('lam_re', [32, 128])
        lam_im = din('lam_im', [32, 128])
        log_step = din('log_step', [32, 2])
        b_re = din('b_re', [64, 64, 16])
        b_im = din('b_im', [64, 64, 16])
        c_re = din('c_re', [1024, 64])
        c_im = din('c_im', [1024, 64])
        s5_d = din('s5_d', [8, 128])
        w_glu = din('w_glu', [1024, 2048])
        s5X_all = din('s5X_all', [NCORES, 128, 64])
        s5X_out = dout('s5X_out', [128, 64])
    if phases > 3:
        w_up_a = din('w_up_a', [1024, 2048])
        w_up_b = din('w_up_b', [1024, 2048])
        w_o = din('w_o', [2048, 2048])
        ln1_g = din('ln1_g', [1, D])
        ln1_b = din('ln1_b', [1, D])
    if phases > 4:
        w_q = din('w_q', [2048, 2048])
        keys = din('keys', [16, 128, 128])
        peer_u = din('peer_u', [16384, 2048])
        peer_v = din('peer_v', [16384, 2048])
        ln2_g = din('ln2_g', [1, D])
        ln2_b = din('ln2_b', [1, D])
        x_out = dout('x_out', [T, D])
    hgU_all = din('hgU_all', [NCORES, 8, 128, 128])
    hgD_all = din('hgD_all', [NCORES, 128, 8])

    hgU_out = dout('hgU_out', [8, 128, 128])
    hgD_out = dout('hgD_out', [128, 8])
    dbgo = {}
    if dbg:
        dbgo['oa'] = dout('dbg_oa', [T, 1024])
        dbgo['obT'] = dout('dbg_obT', [1024, T])
        dbgo['x1'] = dout('dbg_x1', [T, D])
        dbgo['yT'] = dout('dbg_yT', [1024, T])

    x1d = nc.dram_tensor('x1d', [T, D], F32).ap()
    Gd = nc.dram_tensor('Gd', [T, 16384], BF16).ap()

    sb_off = [16512]
    sb_cnt = [0]

    def sb(name, shape, dt):
        n = 1
        for v in shape[1:]:
            n *= v
        nbytes = n * (4 if dt == F32 else 2)
        off = sb_off[0]
        sb_off[0] = (off + nbytes + 63) // 64 * 64
        assert sb_off[0] <= 229000, (name, sb_off[0])
        sb_cnt[0] += 1
        return nc.alloc_sbuf_tensor_at('%s_%d' % (name, sb_cnt[0]), list(shape), dt, offset=off)

    ident_b = sb('ident_b', [128, 128], BF16)
    ident_f = sb('ident_f', [128, 128], F32)
    tri01 = sb('tri01', [128, 64], F32)
    mask01 = sb('mask01', [128, T], F32)
    psum = [nc.alloc_psum_tensor('ps%d' % i, [128, 512], F32) for i in range(8)]

    def psb(i):
        return psum[i][:].bitcast(BF16)

    P.add('gpsimd', lambda e: e.memset(ident_f[:], 1.0), w=['ident_f'])
    P.add('gpsimd', lambda e: e.affine_select(out=ident_f[:], in_=ident_f[:], pattern=[[-1, 128]], compare_op=ALU.is_equal,
                                              fill=0.0, base=0, channel_multiplier=1), r=['ident_f'], w=['ident_f'])
    P.add('vector', lambda e: e.tensor_copy(ident_b[:], ident_f[:]), r=['ident_f'], w=['ident_b'])
    triA = sb('triA', [128, 64], F32)
    triB = sb('triB', [128, 64], F32)
    P.add('gpsimd', lambda e: e.memset(triA[:], 1.0), w=['triA'])
    P.add('gpsimd', lambda e: e.memset(triB[:], 1.0), w=['triB'])
    P.add('gpsimd', lambda e: e.affine_select(out=triA[:], in_=triA[:], pattern=[[1, 64]], compare_op=ALU.is_ge,
                                              fill=0.0, base=0, channel_multiplier=-1), r=['triA'], w=['triA'])
    P.add('gpsimd', lambda e: e.affine_select(out=triB[:], in_=triB[:], pattern=[[1, 64]], compare_op=ALU.is_ge,
                                              fill=0.0, base=64, channel_multiplier=-1), r=['triB'], w=['triB'])
    P.add('vector', lambda e: e.tensor_copy(tri01[0:64, :], triA[0:64, :]), r=['triA'], w=['tri01'])
    P.add('vector', lambda e: e.tensor_copy(tri01[64:128, :], triB[64:128, :]), r=['triB', 'tri01'], w=['tri01'])
    P.add('gpsimd', lambda e: e.memset(mask01[:], 1.0), w=['mask01'])
    P.add('gpsimd', lambda e: e.memset(mask01[:].rearrange('p (c s) -> p c s', s=64)[:, :, 0:1], 0.0), r=['mask01'], w=['mask01'])

    xT = sb('xT', [128, KT, T], BF16)
    reg_oa = sb_off[0]
    oa_tok = sb('oa_tok', [128, NT, 1024], BF16)
    obT = sb('obT', [128, 8, T], BF16)
    lbl = sb('lbl', [128, 4, 8], F32)
    lbe = sb('lbe', [128, 4, 8], F32)
    lbs = sb('lbs', [128, 8], F32)
    lb = sb('lb', [128, 8], F32)
    oml = sb('oml', [128, 8], F32)
    lmk = sb('lmk', [128, 4], F32)
    cmk = sb('cmk', [128, 8], F32)
    mark_persist = sb_off[0]
    xload = [sb('xload%d' % i, [128, D], F32) for i in range(2)]
    xbf = [sb('xbf%d' % i, [128, D], BF16) for i in range(2)]

    def to_featmajor(src_dram, dstT, tokpref):
        for n in range(NT):
            xl = xload[n % 2]; xb = xbf[n % 2]
            tl = 'xload%d' % (n % 2); tb = 'xbf%d' % (n % 2)
            P.dma('sync', xl[:], src_dram[n * 128:(n + 1) * 128, :], r=[], w=[tl])
            P.add('scalar', lambda e, xl=xl, xb=xb: e.copy(xb[:], xl[:]), r=[tl], w=[tb])
            for half in range(2):
                bank = half
                for j in range(8):
                    kt = half * 8 + j
                    P.tr(psb(bank)[:, j * 128:(j + 1) * 128], xb[:, kt * 128:(kt + 1) * 128], ident_b[:],
                         r=[tb, 'ident_b'], w=['ps%d' % bank])
                P.add('vector', lambda e, bank=bank, half=half, n=n: e.tensor_copy(
                    dstT[:, half * 8:(half + 1) * 8, n * 128:(n + 1) * 128],
                    psb(bank).rearrange('p (j t) -> p j t', j=8)), r=['ps%d' % bank], w=[tokpref])

    to_featmajor(x_in, xT, 'xT')

    P.dma('sync', lbl[:], lb_logits.rearrange('l (h d) -> d l h', d=128), r=[], w=['lbl'], allow_slow_non_contiguous=True)
    P.dma('sync', lmk[:], lmask[0].partition_broadcast(128), r=[], w=['lmk'])
    P.dma('sync', cmk[:], cmask[0].partition_broadcast(128), r=[], w=['cmk'])
    P.add('scalar', lambda e: e.activation(lbe[:], lbl[:], AF.Exp), r=['lbl'], w=['lbe'])
    P.add('vector', lambda e: e.tensor_reduce(lbs[:], lbe[:].rearrange('p l h -> p h l'), AX.X, ALU.add), r=['lbe'], w=['lbs'])
    P.add('vector', lambda e: e.reciprocal(lbs[:], lbs[:]), r=['lbs'], w=['lbs'])
    P.add('vector', lambda e: e.tensor_tensor(lbe[:], lbe[:], bc(lmk[:].unsqueeze(2), [128, 4, 8]), ALU.mult), r=['lbe', 'lmk'], w=['lbe'])
    P.add('vector', lambda e: e.tensor_reduce(lb[:], lbe[:].rearrange('p l h -> p h l'), AX.X, ALU.add), r=['lbe'], w=['lb'])
    P.add('vector', lambda e: e.tensor_tensor(lb[:], lb[:], lbs[:], ALU.mult), r=['lb', 'lbs'], w=['lb'])
    P.add('vector', lambda e: e.tensor_scalar(oml[:], lb[:], -1.0, 1.0, ALU.mult, ALU.add), r=['lb'], w=['oml'])

    v_tok = sb('v_tok', [128, NT, 1024], BF16)
    sg_tok = sb('sg_tok', [128, NT, 1024], BF16)
    wblk = [sb('wblk%d' % i, [128, KT, 512], BF16) for i in range(2)]
    wcnt = [0]

    def load_wblk(src, c0, ncols=512, kt=KT):
        i = wcnt[0] % 2
        wcnt[0] += 1
        P.dma('gpsimd', wblk[i][:, 0:kt, 0:ncols], src[:, c0:c0 + ncols].rearrange('(kt p) n -> p kt n', p=128),
              r=[], w=['wblk%d' % i])
        return wblk[i], 'wblk%d' % i

    pscnt = [0]

    def nextbank(lo=0, hi=8):
        b = lo + pscnt[0] % (hi - lo)
        pscnt[0] += 1
        return b

    for blk in range(4):
        wb, wt = load_wblk(w_in, 2048 + blk * 512)
        for n in range(NT):
            bank = nextbank()
            for kt in range(KT):
                P.mm(psum[bank][:, :], xT[:, kt, n * 128:(n + 1) * 128], wb[:, kt, :], kt == 0, kt == KT - 1,
                     r=['xT', wt], w=['ps%d' % bank])
            if blk < 2:
                P.add('vector', lambda e, bank=bank, n=n, blk=blk: e.tensor_copy(v_tok[:, n, blk * 512:(blk + 1) * 512], psum[bank][:, :]),
                      r=['ps%d' % bank], w=['v_tok%d' % n])
            else:
                P.add('scalar', lambda e, bank=bank, n=n, blk=blk: e.activation(sg_tok[:, n, (blk - 2) * 512:(blk - 1) * 512], psum[bank][:, :], AF.Sigmoid),
                      r=['ps%d' % bank], w=['sg_tok%d' % n])

    ngb = sb('ngb', [128, 1024], F32)
    P.dma('sync', ngb[:], norm_g[0].partition_broadcast(128), r=[], w=['ngb'])
    Dp = sb('Dp', [128, NCORES, 8], F32)
    P.dma('sync', Dp[:], hgD_all.rearrange('c d h -> d c h'), r=[], w=['Dp'])
    P.add('vector', lambda e: e.tensor_scalar(Dp[:], Dp[:], -1.0, None, ALU.add), r=['Dp'], w=['Dp'])
    P.add('vector', lambda e: e.tensor_tensor(Dp[:], Dp[:], bc(cmk[:].unsqueeze(2), [128, NCORES, 8]), ALU.mult), r=['Dp', 'cmk'], w=['Dp'])
    P.add('vector', lambda e: e.tensor_scalar(Dp[:], Dp[:], 1.0, None, ALU.add), r=['Dp'], w=['Dp'])

    qT = sb('qT', [128, T], F32)
    fT = sb('fT', [128, T], F32)
    lgf = sb('lgf', [128, T], F32)
    kT_ = sb('kT_', [128, T], F32)
    bT = sb('bT', [128, T], F32)
    eq = sb('eq', [128, T], F32)
    ek = sb('ek', [128, T], F32)
    qt_b = sb('qt_b', [128, T], BF16)
    kt_b = sb('kt_b', [128, T], BF16)
    ktok = sb('ktok', [128, NT, 128], BF16)
    rr = sb('rr', [128, 16], F32)
    er = sb('er', [128, 16], F32)
    e2r = sb('e2r', [128, 16], F32)
    bsum = sb('bsum', [128, 1], F32)
    Dtot = sb('Dtot', [128, 8], F32)
    S = sb('S', [128, 128], F32)
    Uld = [sb('Uld%d' % i, [128, 128], F32) for i in range(2)]
    Spb = [sb('Spb%d' % i, [128, 128], BF16) for i in range(4)]
    Utmp = sb('Utmp', [128, 128], F32)
    att_sb = [sb('att_sb%d' % i, [128, 128], BF16) for i in range(2)]
    ss = sb('ss', [128, 2], F32)
    junk = sb('junk', [128, 128], F32)
    otmp = sb('otmp', [128, 128], F32)
    for i in range(2):
        P.add('gpsimd', lambda e, i=i: e.memset(att_sb[i][:], 0.0), w=['att_sb%d' % i])

    for h in range(8):
        P.add('gpsimd', lambda e: e.memset(S[:], 0.0), w=['S'])
        for j in range(NCORES - 1):
            ul = Uld[j % 2]; ut = 'Uld%d' % (j % 2)
            P.dma('sync', ul[:], hgU_all[j, h], r=[], w=[ut])
            P.add('vector', lambda e, ul=ul, j=j: e.tensor_scalar(ul[:], ul[:], cmk[:, j:j + 1], None, ALU.mult), r=[ut, 'cmk'], w=[ut])
            P.add('vector', lambda e, ul=ul, j=j, h=h: e.scalar_tensor_tensor(S[:], S[:], Dp[:, j, h:h + 1], ul[:], ALU.mult, ALU.add),
                  r=['S', 'Dp', ut], w=['S'])
        i = wcnt[0] % 2
        wcnt[0] += 1
        wb = wblk[i]; wt = 'wblk%d' % i
        P.dma('gpsimd', wb[:, :, 0:128], w_in[:, h * 128:(h + 1) * 128].rearrange('(kt p) n -> p kt n', p=128), r=[], w=[wt])
        P.dma('gpsimd', wb[:, :, 128:256], w_in[:, 1024 + h * 128:1024 + (h + 1) * 128].rearrange('(kt p) n -> p kt n', p=128), r=[], w=[wt])
        for which in range(2):
            for half in range(2):
                bank = nextbank()
                for kt in range(KT):
                    P.mm(psum[bank][:, :], wb[:, kt, which * 128:(which + 1) * 128], xT[:, kt, half * 512:(half + 1) * 512],
                         kt == 0, kt == KT - 1, r=['xT', wt], w=['ps%d' % bank])
                if which == 0:
                    P.add('vector', lambda e, bank=bank, half=half: e.tensor_copy(qT[:, half * 512:(half + 1) * 512], psum[bank][:, :]),
                          r=['ps%d' % bank], w=['qT'])
                else:
                    P.add('scalar', lambda e, bank=bank, half=half: e.activation(fT[:, half * 512:(half + 1) * 512], psum[bank][:, :], AF.Sigmoid),
                          r=['ps%d' % bank], w=['fT'])
        P.add('vector', lambda e, h=h: e.tensor_scalar(fT[:], fT[:], oml[:, h:h + 1], lb[:, h:h + 1], ALU.mult, ALU.add), r=['fT', 'oml', 'lb'], w=['fT'])
        P.add('scalar', lambda e: e.activation(lgf[:], fT[:], AF.Ln), r=['fT'], w=['lgf'])
        P.add('gpsimd', lambda e: e.tensor_scalar(kT_[:], fT[:], -1.0, 1.0, ALU.mult, ALU.add), r=['fT'], w=['kT_'])
        P.add('vector', lambda e: e.tensor_tensor_scan(bT[:], mask01[:], lgf[:], 0.0, ALU.mult, ALU.add), r=['mask01', 'lgf'], w=['bT'])
        b3 = bT[:].rearrange('p (c s) -> p c s', s=64)
        P.add('vector', lambda e: e.tensor_scalar(rr[:], b3[:, :, 63], 0.5, None, ALU.mult), r=['bT'], w=['rr'])
        P.add('vector', lambda e: e.tensor_reduce(bsum[:], b3[:, :, 63], AX.X, ALU.add), r=['bT'], w=['bsum'])
        P.add('scalar', lambda e, h=h: e.activation(Dtot[:, h:h + 1], bsum[:], AF.Exp), r=['bsum'], w=['Dtot'])
        P.add('scalar', lambda e: e.activation(er[:], rr[:], AF.Exp), r=['rr'], w=['er'])
        P.add('vector', lambda e: e.tensor_tensor(e2r[:], er[:], er[:], ALU.mult), r=['er'], w=['e2r'])
        P.add('vector', lambda e: e.tensor_tensor(b3, b3, bc(rr[:].unsqueeze(2), [128, 16, 64]), ALU.subtract), r=['bT', 'rr'], w=['bT'])
        P.add('scalar', lambda e: e.activation(eq[:], bT[:], AF.Exp), r=['bT'], w=['eq'])
        P.add('scalar', lambda e: e.activation(ek[:], bT[:], AF.Exp, scale=-1.0), r=['bT'], w=['ek'])
        P.add('vector', lambda e: e.tensor_tensor(qt_b[:], qT[:], eq[:], ALU.mult), r=['qT', 'eq'], w=['qt_b'])
        P.add('gpsimd', lambda e: e.tensor_tensor(kt_b[:], kT_[:], ek[:], ALU.mult), r=['kT_', 'ek'], w=['kt_b'])
        bank = nextbank()
        for n in range(NT):
            P.tr(psb(bank)[:, n * 128:(n + 1) * 128], kt_b[:, n * 128:(n + 1) * 128], ident_b[:], r=['kt_b', 'ident_b'], w=['ps%d' % bank])
        P.add('vector', lambda e, bank=bank: e.tensor_copy(ktok[:], psb(bank).rearrange('p (n d) -> p n d', n=NT)), r=['ps%d' % bank], w=['ktok'])
        for n in range(NT):
            asb = att_sb[n % 2]; at = 'att_sb%d' % (n % 2)
            b_att = nextbank(); b_U = nextbank(); b_o = nextbank()
            for cc in range(2):
                c = 2 * n + cc
                tok = slice(c * 64, (c + 1) * 64)
                pr = slice(cc * 64, (cc + 1) * 64)
                P.mm(psum[b_att][pr, cc * 64:(cc + 1) * 64], kt_b[:, tok], qt_b[:, tok], True, True, r=['kt_b', 'qt_b'], w=['ps%d' % b_att])
                P.add('vector', lambda e, pr=pr, cc=cc, asb=asb, b_att=b_att: e.tensor_tensor(
                    asb[pr, cc * 64:(cc + 1) * 64], psum[b_att][pr, cc * 64:(cc + 1) * 64], tri01[pr, :], ALU.mult),
                    r=['ps%d' % b_att, 'tri01'], w=[at])
                P.mm(psum[b_U][:, cc * 128:(cc + 1) * 128], ktok[pr, n, :], v_tok[pr, n, h * 128:(h + 1) * 128], True, True,
                     r=['ktok', 'v_tok%d' % n], w=['ps%d' % b_U])
            P.mm(psum[b_o][:, 0:128], asb[:, :], v_tok[:, n, h * 128:(h + 1) * 128], True, False, r=[at, 'v_tok%d' % n], w=['ps%d' % b_o])
            for cc in range(2):
                c = 2 * n + cc
                tok = slice(c * 64, (c + 1) * 64)
                pr = slice(cc * 64, (cc + 1) * 64)
                sp = Spb[c % 4]; spt = 'Spb%d' % (c % 4)
                P.add('vector', lambda e, sp=sp, c=c: e.tensor_scalar(sp[:], S[:], er[:, c:c + 1], None, ALU.mult), r=['S', 'er'], w=[spt])
                P.mm(psum[b_o][pr, 0:128], qt_b[:, tok], sp[:, :], False, cc == 1, r=['qt_b', spt], w=['ps%d' % b_o])
                P.add('vector', lambda e, c=c, cc=cc, b_U=b_U: e.tensor_scalar(Utmp[:], psum[b_U][:, cc * 128:(cc + 1) * 128], er[:, c:c + 1], None, ALU.mult),
                      r=['ps%d' % b_U, 'er'], w=['Utmp'])
                P.add('vector', lambda e, c=c: e.scalar_tensor_tensor(S[:], S[:], e2r[:, c:c + 1], Utmp[:], ALU.mult, ALU.add),
                      r=['S', 'e2r', 'Utmp'], w=['S'])
            P.add('gpsimd', lambda e: e.memset(ss[:], 0.0), w=['ss'])
            P.add('scalar', lambda e, b_o=b_o: e.activation(junk[:], psum[b_o][:, 0:128], AF.Square, accum_out=ss[:, 0:1]), r=['ps%d' % b_o, 'ss'], w=['junk', 'ss'])
            P.add('scalar', lambda e: e.activation(ss[:, 1:2], ss[:, 0:1], AF.Ln, bias=RMS_EPS, scale=1.0 / 128.0), r=['ss'], w=['ss'])
            P.add('scalar', lambda e: e.activation(ss[:, 1:2], ss[:, 1:2], AF.Exp, scale=-0.5), r=['ss'], w=['ss'])
            P.add('vector', lambda e, b_o=b_o, h=h: e.scalar_tensor_tensor(otmp[:], psum[b_o][:, 0:128], ss[:, 1:2], ngb[:, h * 128:(h + 1) * 128], ALU.mult, ALU.mult),
                  r=['ps%d' % b_o, 'ss', 'ngb'], w=['otmp'])
            P.add('gpsimd', lambda e, n=n, h=h: e.tensor_tensor(oa_tok[:, n, h * 128:(h + 1) * 128], otmp[:], sg_tok[:, n, h * 128:(h + 1) * 128], ALU.mult),
                  r=['otmp', 'sg_tok%d' % n], w=['oa_tok%d' % n])
        P.dma('sync', hgU_out[h], S[:], r=['S'], w=['hgU_out'])
    P.dma('sync', hgD_out, Dtot[:], r=['Dtot'], w=['hgD_out'])
    if dbg:
        for n in range(NT):
            P.add('vector', lambda e, n=n: e.tensor_copy(xload[n % 2][:, 0:1024], oa_tok[:, n, :]), r=['oa_tok%d' % n], w=['xload%d' % (n % 2)])
            P.dma('sync', dbgo['oa'][n * 128:(n + 1) * 128, :], xload[n % 2][:, 0:1024], r=['xload%d' % (n % 2)], w=['dbg_oa'])
    if phases <= 2:
        P.emit()
        return nc

    P.barrier()
    sb_off[0] = mark_persist
    wblk = [sb('wblk%d' % i, [128, KT, 512], BF16) for i in range(2)]
    uT = sb('uT', [128, 8, T], F32)
    yT = sb('yT', [128, 8, T], BF16)
    Xr = [sb('Xr%d' % i, [128, T], F32) for i in range(2)]
    Xi = [sb('Xi%d' % i, [128, T], F32) for i in range(2)]
    ytmp = sb('ytmp', [128, 512], F32)
    sgm = sb('sgm', [128, 512], F32)
    pl = sb('pl', [32, 3, 128], F32)
    lst2 = sb('lst2', [32, 2], F32)
    sp = {}
    for nm in ['lamr', 'lami', 'lst', 'lr', 'dt', 'lrdt', 'lidt', 'mag', 't1', 't2', 'sn', 'cs', 'ar', 'ai', 'den', 'nr',
               'zr', 'zi', 'xinr', 'xini', 'axr', 'axi', 'Ar', 'Ai', 'tA', 'tB', 'tC', 'tD']:
        sp[nm] = sb('s5_' + nm, [128, 32], F32)
    pwr = sb('pwr', [128, 11, 32], F32)
    pwi = sb('pwi', [128, 11, 32], F32)
    npwi = sb('npwi', [128, 11, 32], F32)
    bre = sb('bre', [128, 32, 16], F32)
    bim = sb('bim', [128, 32, 16], F32)
    bbr = sb('bbr', [128, 32, 16], F32)
    bbi = sb('bbi', [128, 32, 16], F32)
    btmp = sb('btmp', [128, 32, 16], F32)
    Cld = [sb('Cld%d' % i, [128, 8, 64], F32) for i in range(2)]
    Cdup = sb('Cdup', [128, 2, 64], F32)
    dld = sb('dld', [8, 128], F32)
    dvec = sb('dvec', [128, 8], F32)
    Xe_ld = [sb('Xe_ld%d' % i, [128, 64], F32) for i in range(2)]
    Xend = sb('Xend', [128, 64], F32)
    srcB = [sb('srcB%d' % j, [128, 128], F32) for j in range(4)]
    LBr = [sb('LBr%d' % j, [128, 128], F32) for j in range(4)]
    LBi = [sb('LBi%d' % j, [128, 128], F32) for j in range(4)]
    LCr = [sb('LCr%d' % j, [128, 128], F32) for j in range(4)]
    LCi = [sb('LCi%d' % j, [128, 128], F32) for j in range(4)]
    for j in range(4):
        for tl, nm in ((srcB, 'srcB'), (LCr, 'LCr'), (LCi, 'LCi')):
            P.add('gpsimd', lambda e, t=tl[j]: e.memset(t[:], 0.0), w=['%s%d' % (nm, j)])

    V = lambda eng, fn, r, w: P.add(eng, fn, r=r, w=w)

    def tt(out, a, b, op, r, w, eng='vector'):
        P.add(eng, lambda e: e.tensor_tensor(out, a, b, op), r=r, w=w)

    def ts(out, a, s1, s2, op0, op1, r, w, eng='vector'):
        if op1 is None:
            P.add(eng, lambda e: e.tensor_scalar(out, a, s1, None, op0), r=r, w=w)
        else:
            P.add(eng, lambda e: e.tensor_scalar(out, a, s1, s2, op0, op1), r=r, w=w)

    def stt(out, a, sc, b, op0, op1, r, w, eng='vector'):
        P.add(eng, lambda e: e.scalar_tensor_tensor(out, a, sc, b, op0, op1), r=r, w=w)

    def act(out, a, func, r, w, **kw):
        P.add('scalar', lambda e: e.activation(out, a, func, **kw), r=r, w=w)

    P.dma('sync', pl[:, 0, :], lam_re, r=[], w=['pl'])
    P.dma('sync', pl[:, 1, :], lam_im, r=[], w=['pl'])
    P.dma('sync', lst2[:], log_step, r=[], w=['lst2'])
    P.add('vector', lambda e: e.tensor_copy(pl[:, 2, :].rearrange('p (a b) -> p a b', a=2), bc(lst2[:].unsqueeze(2), [32, 2, 64])), r=['lst2', 'pl'], w=['pl'])
    for i, nm in enumerate(['lamr', 'lami', 'lst']):
        P.tr(psum[4][:, i * 32:(i + 1) * 32], pl[:, i, :], ident_f[0:32, 0:32], r=['pl', 'ident_f'], w=['ps4'])
        P.add('vector', lambda e, i=i, nm=nm: e.tensor_copy(sp[nm][:], psum[4][:, i * 32:(i + 1) * 32]), r=['ps4'], w=[nm])
    P.dma('sync', dld[:], s5_d, r=[], w=['dld'])
    P.tr(psum[4][:, 128:136], dld[:], ident_f[0:8, 0:8], r=['dld', 'ident_f'], w=['ps4'])
    P.add('vector', lambda e: e.tensor_copy(dvec[:], psum[4][:, 128:136]), r=['ps4'], w=['dvec'])
    P.dma('sync', bre[:], b_re.rearrange('(gp g2) p n -> (g2 p) gp n', g2=2), r=[], w=['bre'])
    P.dma('sync', bim[:], b_im.rearrange('(gp g2) p n -> (g2 p) gp n', g2=2), r=[], w=['bim'])
    P.dma('sync', Cld[0][:], c_re.rearrange('(ct q) p -> q ct p', q=128), r=[], w=['Cld0'])
    P.dma('sync', Cld[1][:], c_im.rearrange('(ct q) p -> q ct p', q=128), r=[], w=['Cld1'])

    A = lambda nm: sp[nm][:]
    ts(A('lr'), A('lamr'), -1e-4, None, ALU.min, None, ['lamr'], ['lr'])
    act(A('dt'), A('lst'), AF.Exp, ['lst'], ['dt'])
    tt(A('lrdt'), A('lr'), A('dt'), ALU.mult, ['lr', 'dt'], ['lrdt'])
    tt(A('lidt'), A('lami'), A('dt'), ALU.mult, ['lami', 'dt'], ['lidt'])
    act(A('mag'), A('lrdt'), AF.Exp, ['lrdt'], ['mag'])
    ts(A('t1'), A('lidt'), 1.0 / TWO_PI, MAGIC, ALU.mult, ALU.add, ['lidt'], ['t1'])
    ts(A('t1'), A('t1'), -MAGIC, None, ALU.add, None, ['t1'], ['t1'])
    stt(A('t1'), A('t1'), -TWO_PI, A('lidt'), ALU.mult, ALU.add, ['t1', 'lidt'], ['t1'])
    act(A('sn'), A('t1'), AF.Sin, ['t1'], ['sn'])
    ts(A('t2'), A('lidt'), math.pi / 2, None, ALU.add, None, ['lidt'], ['t2'])
    ts(A('tA'), A('t2'), 1.0 / TWO_PI, MAGIC, ALU.mult, ALU.add, ['t2'], ['tA'])
    ts(A('tA'), A('tA'), -MAGIC, None, ALU.add, None, ['tA'], ['tA'])
    stt(A('tA'), A('tA'), -TWO_PI, A('t2'), ALU.mult, ALU.add, ['tA', 't2'], ['tA'])
    act(A('cs'), A('tA'), AF.Sin, ['tA'], ['cs'])
    tt(A('ar'), A('mag'), A('cs'), ALU.mult, ['mag', 'cs'], ['ar'])
    tt(A('ai'), A('mag'), A('sn'), ALU.mult, ['mag', 'sn'], ['ai'])
    tt(A('den'), A('lr'), A('lr'), ALU.mult, ['lr'], ['den'])
    tt(A('tB'), A('lami'), A('lami'), ALU.mult, ['lami'], ['tB'])
    tt(A('den'), A('den'), A('tB'), ALU.add, ['den', 'tB'], ['den'])
    P.add('vector', lambda e: e.reciprocal(A('den'), A('den')), r=['den'], w=['den'])
    ts(A('nr'), A('ar'), -1.0, None, ALU.add, None, ['ar'], ['nr'])
    tt(A('tB'), A('nr'), A('lr'), ALU.mult, ['nr', 'lr'], ['tB'])
    tt(A('tC'), A('ai'), A('lami'), ALU.mult, ['ai', 'lami'], ['tC'])
    tt(A('tB'), A('tB'), A('tC'), ALU.add, ['tB', 'tC'], ['tB'])
    tt(A('zr'), A('tB'), A('den'), ALU.mult, ['tB', 'den'], ['zr'])
    tt(A('tB'), A('ai'), A('lr'), ALU.mult, ['ai', 'lr'], ['tB'])
    tt(A('tC'), A('nr'), A('lami'), ALU.mult, ['nr', 'lami'], ['tC'])
    tt(A('tB'), A('tB'), A('tC'), ALU.subtract, ['tB', 'tC'], ['tB'])
    tt(A('zi'), A('tB'), A('den'), ALU.mult, ['tB', 'den'], ['zi'])
    P.add('vector', lambda e: e.tensor_copy(pwr[:, 0, :], A('ar')), r=['ar'], w=['pwr'])
    P.add('vector', lambda e: e.tensor_copy(pwi[:, 0, :], A('ai')), r=['ai'], w=['pwi'])
    for k in range(10):
        tt(A('tB'), pwr[:, k, :], pwr[:, k, :], ALU.mult, ['pwr'], ['tB'])
        tt(A('tC'), pwi[:, k, :], pwi[:, k, :], ALU.mult, ['pwi'], ['tC'])
        stt(pwi[:, k + 1, :], pwr[:, k, :], 2.0, pwi[:, k, :], ALU.mult, ALU.mult, ['pwr', 'pwi'], ['pwi'])
        tt(pwr[:, k + 1, :], A('tB'), A('tC'), ALU.subtract, ['tB', 'tC'], ['pwr'])
    ts(npwi[:], pwi[:], -1.0, None, ALU.mult, None, ['pwi'], ['npwi'])
    P.add('gpsimd', lambda e: e.memset(A('xinr'), 0.0), w=['xinr'])
    P.add('gpsimd', lambda e: e.memset(A('xini'), 0.0), w=['xini'])
    for j in range(NCORES - 1):
        xl = Xe_ld[j % 2]; xt_ = 'Xe_ld%d' % (j % 2)
        P.dma('sync', xl[:], s5X_all[j], r=[], w=[xt_])
        m = cmk[:, j:j + 1]
        ts(A('Ar'), pwr[:, 10, :], -1.0, None, ALU.add, None, ['pwr'], ['Ar'])
        ts(A('Ar'), A('Ar'), m, 1.0, ALU.mult, ALU.add, ['Ar', 'cmk'], ['Ar'])
        ts(A('Ai'), pwi[:, 10, :], m, None, ALU.mult, None, ['pwi', 'cmk'], ['Ai'])
        ts(xl[:], xl[:], m, None, ALU.mult, None, [xt_, 'cmk'], [xt_])
        tt(A('tA'), A('Ar'), A('xinr'), ALU.mult, ['Ar', 'xinr'], ['tA'])
        tt(A('tB'), A('Ai'), A('xini'), ALU.mult, ['Ai', 'xini'], ['tB'])
        tt(A('tC'), A('Ar'), A('xini'), ALU.mult, ['Ar', 'xini'], ['tC'])
        tt(A('tD'), A('Ai'), A('xinr'), ALU.mult, ['Ai', 'xinr'], ['tD'])
        tt(A('tA'), A('tA'), A('tB'), ALU.subtract, ['tA', 'tB'], ['tA'])
        tt(A('tC'), A('tC'), A('tD'), ALU.add, ['tC', 'tD'], ['tC'])
        tt(A('xinr'), A('tA'), xl[:, 0:32], ALU.add, ['tA', xt_], ['xinr'])
        tt(A('xini'), A('tC'), xl[:, 32:64], ALU.add, ['tC', xt_], ['xini'])
    tt(A('tA'), A('ar'), A('xinr'), ALU.mult, ['ar', 'xinr'], ['tA'])
    tt(A('tB'), A('ai'), A('xini'), ALU.mult, ['ai', 'xini'], ['tB'])
    tt(A('axr'), A('tA'), A('tB'), ALU.subtract, ['tA', 'tB'], ['axr'])
    tt(A('tA'), A('ar'), A('xini'), ALU.mult, ['ar', 'xini'], ['tA'])
    tt(A('tB'), A('ai'), A('xinr'), ALU.mult, ['ai', 'xinr'], ['tB'])
    tt(A('axi'), A('tA'), A('tB'), ALU.add, ['tA', 'tB'], ['axi'])
    zrb = bc(sp['zr'][:].unsqueeze(2), [128, 32, 16])
    zib = bc(sp['zi'][:].unsqueeze(2), [128, 32, 16])
    tt(bbr[:], bre[:], zrb, ALU.mult, ['bre', 'zr'], ['bbr'])
    tt(btmp[:], bim[:], zib, ALU.mult, ['bim', 'zi'], ['btmp'])
    tt(bbr[:], bbr[:], btmp[:], ALU.subtract, ['bbr', 'btmp'], ['bbr'])
    tt(bbi[:], bim[:], zrb, ALU.mult, ['bim', 'zr'], ['bbi'])
    tt(btmp[:], bre[:], zib, ALU.mult, ['bre', 'zi'], ['btmp'])
    tt(bbi[:], bbi[:], btmp[:], ALU.add, ['bbi', 'btmp'], ['bbi'])

    for blk in range(2):
        wb, wt = load_wblk(w_in, 4096 + blk * 512)
        for ntl in range(4):
            ct = blk * 4 + ntl
            for half in range(2):
                bank = nextbank(0, 4)
                for kt in range(KT):
                    P.mm(psum[bank][:, :], wb[:, kt, ntl * 128:(ntl + 1) * 128], xT[:, kt, half * 512:(half + 1) * 512],
                         kt == 0, kt == KT - 1, r=['xT', wt], w=['ps%d' % bank])
                P.add('scalar', lambda e, bank=bank, ct=ct, half=half: e.copy(uT[:, ct, half * 512:(half + 1) * 512], psum[bank][:, :]),
                      r=['ps%d' % bank], w=['uT%d' % ct])

    P.barrier()
    wb1f = wblk[1][:].rearrange('p a b -> p (a b)').bitcast(F32)
    m1 = wb1f[:, 0:T]
    m2 = wb1f[:, T:2 * T]
    for ct in range(8):
        for j in range(4):
            gp = 4 * ct + j
            for (bb, LB, nm) in ((bbr, LBr, 'LBr'), (bbi, LBi, 'LBi')):
                for g2 in range(2):
                    pr = slice(g2 * 64, (g2 + 1) * 64)
                    c0 = (2 * j + g2) * 16
                    P.add('vector', lambda e, bb=bb, pr=pr, c0=c0, j=j, gp=gp: e.tensor_copy(srcB[j][pr, c0:c0 + 16], bb[pr, gp, :]),
                          r=['bbr', 'bbi'], w=['srcB%d' % j])
                P.tr(psum[4][:, 0:128], srcB[j][:], ident_f[:], r=['srcB%d' % j, 'ident_f'], w=['ps4'])
                P.add('vector', lambda e, LB=LB, j=j: e.tensor_copy(LB[j][:], psum[4][:, 0:128]), r=['ps4'], w=['%s%d' % (nm, j)])
        for ri, (LC, nm) in enumerate(((LCr, 'LCr'), (LCi, 'LCi'))):
            P.add('vector', lambda e, ri=ri, ct=ct: e.tensor_copy(Cdup[:], bc(Cld[ri][:, ct, :].unsqueeze(1), [128, 2, 64])), r=['Cld%d' % ri], w=['Cdup'])
            P.tr(psum[5][:, 0:128], Cdup[:].rearrange('p a b -> p (a b)'), ident_f[:], r=['Cdup', 'ident_f'], w=['ps5'])
            for j in range(4):
                for g2 in range(2):
                    pr = slice(g2 * 64, (g2 + 1) * 64)
                    c0 = (2 * j + g2) * 16
                    if ri == 0:
                        P.add('vector', lambda e, LC=LC, j=j, pr=pr, c0=c0: e.tensor_copy(LC[j][pr, c0:c0 + 16], psum[5][pr, c0:c0 + 16]),
                              r=['ps5'], w=['%s%d' % (nm, j)])
                    else:
                        P.add('vector', lambda e, LC=LC, j=j, pr=pr, c0=c0: e.tensor_scalar(LC[j][pr, c0:c0 + 16], psum[5][pr, c0:c0 + 16], -1.0, None, ALU.mult),
                              r=['ps5'], w=['%s%d' % (nm, j)])
        for j in range(4):
            gp = 4 * ct + j
            for half in range(2):
                hs = slice(half * 512, (half + 1) * 512)
                b1 = nextbank(0, 4)
                P.mm(psum[b1][:, :], LBr[j][:], uT[:, ct, hs], True, True, r=['LBr%d' % j, 'uT%d' % ct], w=['ps%d' % b1])
                P.add('scalar', lambda e, b1=b1, hs=hs: e.copy(Xr[0][:, hs], psum[b1][:, :]), r=['ps%d' % b1], w=['Xr0'])
                b2 = nextbank(0, 4)
                P.mm(psum[b2][:, :], LBi[j][:], uT[:, ct, hs], True, True, r=['LBi%d' % j, 'uT%d' % ct], w=['ps%d' % b2])
                P.add('scalar', lambda e, b2=b2, hs=hs: e.copy(Xi[0][:, hs], psum[b2][:, :]), r=['ps%d' % b2], w=['Xi0'])
            tt(Xr[0][:, 0:1], Xr[0][:, 0:1], sp['axr'][:, gp:gp + 1], ALU.add, ['Xr0', 'axr'], ['Xr0'])
            tt(Xi[0][:, 0:1], Xi[0][:, 0:1], sp['axi'][:, gp:gp + 1], ALU.add, ['Xi0', 'axi'], ['Xi0'], eng='gpsimd')
            for k in range(10):
                d = 1 << k
                s_, d_ = k % 2, 1 - k % 2
                pr_ = pwr[:, k, gp:gp + 1]; pi_ = pwi[:, k, gp:gp + 1]; npi_ = npwi[:, k, gp:gp + 1]
                xr_s, xi_s, xr_d, xi_d = Xr[s_], Xi[s_], Xr[d_], Xi[d_]
                tr_s, ti_s, tr_d, ti_d = 'Xr%d' % s_, 'Xi%d' % s_, 'Xr%d' % d_, 'Xi%d' % d_
                P.add('vector', lambda e, xr_d=xr_d, xr_s=xr_s, d=d: e.tensor_copy(xr_d[:, 0:d], xr_s[:, 0:d]), r=[tr_s], w=[tr_d])
                stt(xr_d[:, d:T], xr_s[:, 0:T - d], pr_, xr_s[:, d:T], ALU.mult, ALU.add, [tr_s, 'pwr'], [tr_d])
                stt(xr_d[:, d:T], xi_s[:, 0:T - d], npi_, xr_d[:, d:T], ALU.mult, ALU.add, [ti_s, 'npwi', tr_d], [tr_d])
                P.add('gpsimd', lambda e, xi_d=xi_d, xi_s=xi_s, d=d: e.tensor_copy(xi_d[:, 0:d], xi_s[:, 0:d]), r=[ti_s], w=[ti_d])
                act(m1[:, 0:T - d], xi_s[:, 0:T - d], AF.Copy, [ti_s, 'pwr'], ['m1'], scale=pr_)
                act(m2[:, 0:T - d], xr_s[:, 0:T - d], AF.Copy, [tr_s, 'pwi'], ['m2'], scale=pi_)
                tt(xi_d[:, d:T], m1[:, 0:T - d], xi_s[:, d:T], ALU.add, ['m1', ti_s], [ti_d], eng='gpsimd')
                tt(xi_d[:, d:T], m2[:, 0:T - d], xi_d[:, d:T], ALU.add, ['m2', ti_d], [ti_d], eng='gpsimd')
            P.add('vector', lambda e, gp=gp: e.tensor_copy(Xend[:, gp:gp + 1], Xr[0][:, T - 1:T]), r=['Xr0'], w=['Xend'])
            P.add('vector', lambda e, gp=gp: e.tensor_copy(Xend[:, 32 + gp:33 + gp], Xi[0][:, T - 1:T]), r=['Xi0'], w=['Xend'])
            for half in range(2):
                hs = slice(half * 512, (half + 1) * 512)
                P.mm(psum[6 + half][:, :], LCr[j][:], Xr[0][:, hs], j == 0, False, r=['LCr%d' % j, 'Xr0'], w=['ps%d' % (6 + half)])
                P.mm(psum[6 + half][:, :], LCi[j][:], Xi[0][:, hs], False, j == 3, r=['LCi%d' % j, 'Xi0'], w=['ps%d' % (6 + half)])
        for half in range(2):
            hs = slice(half * 512, (half + 1) * 512)
            stt(ytmp[:], uT[:, ct, hs], dvec[:, ct:ct + 1], psum[6 + half][:, :], ALU.mult, ALU.add, ['uT%d' % ct, 'dvec', 'ps%d' % (6 + half)], ['ytmp'])
            act(yT[:, ct, hs], ytmp[:], AF.Gelu_apprx_tanh, ['ytmp'], ['yT'])
    P.dma('sync', s5X_out, Xend[:], r=['Xend'], w=['s5X_out'])
    P.barrier()
    for b in range(2):
        WA, wta = wblk[0], 'wblk0'
        WG, wtg = wblk[1], 'wblk1'
        P.dma('gpsimd', WA[:, 0:8, :], w_glu[:, b * 512:(b + 1) * 512].rearrange('(kt p) n -> p kt n', p=128), r=[], w=[wta])
        P.dma('gpsimd', WG[:, 0:8, :], w_glu[:, 1024 + b * 512:1024 + (b + 1) * 512].rearrange('(kt p) n -> p kt n', p=128), r=[], w=[wtg])
        for ntl in range(4):
            nt = 4 * b + ntl
            for half in range(2):
                hs = slice(half * 512, (half + 1) * 512)
                ba = nextbank(0, 6); bg = nextbank(0, 6)
                for kt in range(8):
                    P.mm(psum[ba][:, :], WA[:, kt, ntl * 128:(ntl + 1) * 128], yT[:, kt, hs], kt == 0, kt == 7, r=['yT', wta], w=['ps%d' % ba])
                for kt in range(8):
                    P.mm(psum[bg][:, :], WG[:, kt, ntl * 128:(ntl + 1) * 128], yT[:, kt, hs], kt == 0, kt == 7, r=['yT', wtg], w=['ps%d' % bg])
                act(sgm[:], psum[bg][:, :], AF.Sigmoid, ['ps%d' % bg], ['sgm'])
                tt(obT[:, nt, hs], psum[ba][:, :], sgm[:], ALU.mult, ['ps%d' % ba, 'sgm'], ['obT'])
    if dbg:
        for nt in range(8):
            P.add('vector', lambda e, nt=nt: e.tensor_copy(Xr[0][:], obT[:, nt, :]), r=['obT'], w=['Xr0'])
            P.dma('sync', dbgo['obT'][nt * 128:(nt + 1) * 128, :], Xr[0][:], r=['Xr0'], w=['dbg_obT'])
    if phases <= 3:
        P.emit()
        return nc

    P.barrier()
    sb_off[0] = mark_persist
    mergedT = sb('mergedT', [128, KT, T], BF16)
    mark4 = sb_off[0]
    oaT = sb('oaT', [128, 8, T], BF16)
    wA = [sb('wA%d' % i, [128, 8, 512], BF16) for i in range(2)]
    wB = [sb('wB%d' % i, [128, 8, 512], BF16) for i in range(2)]
    wGA = sb('wGA', [128, KT, 512], BF16)
    wGB = sb('wGB', [128, KT, 512], BF16)
    sga = sb('sga', [128, 512], F32)
    sgb = sb('sgb', [128, 512], F32)
    mm1 = sb('mm1', [128, 512], F32)
    mm2 = sb('mm2', [128, 512], F32)
    for n in range(NT):
        bank = nextbank()
        for kt in range(8):
            P.tr(psb(bank)[:, kt * 128:(kt + 1) * 128], oa_tok[:, n, kt * 128:(kt + 1) * 128], ident_b[:], r=['oa_tok', 'ident_b'], w=['ps%d' % bank])
        P.add('vector', lambda e, bank=bank, n=n: e.tensor_copy(oaT[:, :, n * 128:(n + 1) * 128], psb(bank).rearrange('p (j t) -> p j t', j=8)),
              r=['ps%d' % bank], w=['oaT'])
    for cb in range(4):
        i = cb % 2
        P.dma('gpsimd', wA[i][:], w_up_a[:, cb * 512:(cb + 1) * 512].rearrange('(kt p) n -> p kt n', p=128), r=[], w=['wA%d' % i])
        P.dma('gpsimd', wB[i][:], w_up_b[:, cb * 512:(cb + 1) * 512].rearrange('(kt p) n -> p kt n', p=128), r=[], w=['wB%d' % i])
        P.dma('gpsimd', wGA[:], w_in[:, 5120 + cb * 512:5120 + (cb + 1) * 512].rearrange('(kt p) n -> p kt n', p=128), r=[], w=['wGA'])
        P.dma('gpsimd', wGB[:], w_in[:, 7168 + cb * 512:7168 + (cb + 1) * 512].rearrange('(kt p) n -> p kt n', p=128), r=[], w=['wGB'])
        for ntl in range(4):
            nt = 4 * cb + ntl
            ns = slice(ntl * 128, (ntl + 1) * 128)
            for half in range(2):
                hs = slice(half * 512, (half + 1) * 512)
                ba, bb_, bga, bgb = nextbank(), nextbank(), nextbank(), nextbank()
                for kt in range(8):
                    P.mm(psum[ba][:, :], wA[i][:, kt, ns], oaT[:, kt, hs], kt == 0, kt == 7, r=['wA%d' % i, 'oaT'], w=['ps%d' % ba])
                for kt in range(8):
                    P.mm(psum[bb_][:, :], wB[i][:, kt, ns], obT[:, kt, hs], kt == 0, kt == 7, r=['wB%d' % i, 'obT'], w=['ps%d' % bb_])
                for kt in range(KT):
                    P.mm(psum[bga][:, :], wGA[:, kt, ns], xT[:, kt, hs], kt == 0, kt == KT - 1, r=['wGA', 'xT'], w=['ps%d' % bga])
                for kt in range(KT):
                    P.mm(psum[bgb][:, :], wGB[:, kt, ns], xT[:, kt, hs], kt == 0, kt == KT - 1, r=['wGB', 'xT'], w=['ps%d' % bgb])
                act(sga[:], psum[bga][:, :], AF.Sigmoid, ['ps%d' % bga], ['sga'])
                act(sgb[:], psum[bgb][:, :], AF.Sigmoid, ['ps%d' % bgb], ['sgb'])
                tt(mm1[:], psum[ba][:, :], sga[:], ALU.mult, ['ps%d' % ba, 'sga'], ['mm1'])
                tt(mm2[:], psum[bb_][:, :], sgb[:], ALU.mult, ['ps%d' % bb_, 'sgb'], ['mm2'])
                tt(mergedT[:, nt, hs], mm1[:], mm2[:], ALU.add, ['mm1', 'mm2'], ['mergedT'], eng='gpsimd')
    P.barrier()
    sb_off[0] = mark4
    wo_sb = sb('wo_sb', [128, KT, D], BF16)
    x1b = sb('x1b', [128, D], BF16)
    st4 = sb('st4', [128, 4], F32)
    sb_save = sb_off[0]
    sb_off[0] = reg_oa
    lng = sb('lng', [128, D], F32)
    lnb = sb('lnb', [128, D], F32)
    xres = sb('xres', [128, D], F32)
    h1 = sb('h1', [128, D], F32)
    sb_off[0] = sb_save
    x1T = xT
    for cb in range(4):
        P.dma('gpsimd', wo_sb[:, :, cb * 512:(cb + 1) * 512], w_o[:, cb * 512:(cb + 1) * 512].rearrange('(kt p) n -> p kt n', p=128), r=[], w=['wo_sb%d' % cb])

    def layer_norm(src, gamma_d, beta_d, hbuf, dst_dram, tagp):
        pass

    P.dma('sync', lng[:], ln1_g[0].partition_broadcast(128), r=[], w=['lng'])
    P.dma('sync', lnb[:], ln1_b[0].partition_broadcast(128), r=[], w=['lnb'])

    def ln_rows(hb, ht, gam, bet):
        P.add('vector', lambda e: e.tensor_reduce(st4[:, 0:1], hb[:], AX.X, ALU.add), r=[ht], w=['st4a'])
        ts(st4[:, 1:2], st4[:, 0:1], -1.0 / D, None, ALU.mult, None, ['st4a'], ['st4b'])
        ts(hb[:], hb[:], st4[:, 1:2], None, ALU.add, None, [ht, 'st4b'], [ht])
        P.add('gpsimd', lambda e: e.memset(st4[:, 2:3], 0.0), w=['st4c'])
        P.add('scalar', lambda e: e.activation(junkD[:], hb[:], AF.Square, accum_out=st4[:, 2:3]), r=[ht, 'st4c'], w=['junkD', 'st4c'])
        act(st4[:, 3:4], st4[:, 2:3], AF.Ln, ['st4c'], ['st4d'], bias=LN_EPS, scale=1.0 / D)
        act(st4[:, 3:4], st4[:, 3:4], AF.Exp, ['st4d'], ['st4d'], scale=-0.5)
        stt(hb[:], hb[:], st4[:, 3:4], gam[:], ALU.mult, ALU.mult, [ht, 'st4d', 'lng'], [ht])
        tt(hb[:], hb[:], bet[:], ALU.add, [ht, 'lnb'], [ht], eng='gpsimd')

    junkD = sb('junkD', [128, D], BF16)
    for n in range(NT):
        P.dma('sync', xres[:], x_in[n * 128:(n + 1) * 128, :], r=[], w=['xres'])
        for cb in range(4):
            bank = nextbank()
            for kt in range(KT):
                P.mm(psum[bank][:, :], mergedT[:, kt, n * 128:(n + 1) * 128], wo_sb[:, kt, cb * 512:(cb + 1) * 512], kt == 0, kt == KT - 1,
                     r=['mergedT', 'wo_sb%d' % cb], w=['ps%d' % bank])
            stt(h1[:, cb * 512:(cb + 1) * 512], xres[:, cb * 512:(cb + 1) * 512], ALPHA, psum[bank][:, :], ALU.mult, ALU.add,
                ['xres', 'ps%d' % bank], ['h1'])
        ln_rows(h1, 'h1', lng, lnb)
        P.dma('sync', x1d[n * 128:(n + 1) * 128, :], h1[:], r=['h1'], w=['x1d'])
        if dbg:
            P.dma('sync', dbgo['x1'][n * 128:(n + 1) * 128, :], h1[:], r=['h1'], w=['dbg_x1'])
        P.add('scalar', lambda e: e.copy(x1b[:], h1[:]), r=['h1'], w=['x1b'])
        for half in range(2):
            bank = nextbank()
            for j in range(8):
                kt = half * 8 + j
                P.tr(psb(bank)[:, j * 128:(j + 1) * 128], x1b[:, kt * 128:(kt + 1) * 128], ident_b[:], r=['x1b', 'ident_b'], w=['ps%d' % bank])
            P.add('vector', lambda e, bank=bank, half=half, n=n: e.tensor_copy(
                x1T[:, half * 8:(half + 1) * 8, n * 128:(n + 1) * 128], psb(bank).rearrange('p (j t) -> p j t', j=8)),
                r=['ps%d' % bank], w=['x1T'])
    if phases <= 4:
        P.emit()
        return nc

    P.barrier()
    sb_off[0] = mark_persist
    s_d = nc.dram_tensor('s_d', [T, 16, 128], F32).ap()
    wblk = [sb('wblk%d' % i, [128, KT, 512], BF16) for i in range(2)]
    qTb = sb('qTb', [128, 4, T], F32)
    kld = sb('kld', [128, 16, 128], F32)
    KTt = sb('KTt', [128, 16, 128], F32)
    s_st = [sb('s_st%d' % i, [128, 4, 128], F32) for i in range(2)]
    s_sb = sb('s_sb', [128, 16, 128], F32)
    mr = sb('mr', [128, 128], F32)
    top = sb('top', [128, 16, 16], F32)
    cand = sb('cand', [128, 8, 256], F32)
    cand2 = sb('cand2', [128, 8, 256], F32)
    vals = sb('vals', [128, 8, 16], F32)
    ev = sb('ev', [128, 8, 16], F32)
    Zs = sb('Zs', [128, 8], F32)
    nb = sb('nb', [128, 8], F32)
    Tbuf2 = [sb('Tbuf%d' % i, [128, 16, 128], F32) for i in range(2)]
    Eb2 = [sb('Eb%d' % i, [128, 2048], BF16) for i in range(2)]
    Pm2 = [sb('Pm%d' % i, [128, 2048], BF16) for i in range(2)]
    sb_save = sb_off[0]
    sb_off[0] = reg_oa
    G = sb('G', [128, 16384], BF16)
    sb_off[0] = sb_save

    P.dma('sync', kld[:], keys.rearrange('h i d -> i h d'), r=[], w=['kld'])
    for q4 in range(4):
        bank = nextbank()
        for hl in range(4):
            hh = 4 * q4 + hl
            P.tr(psum[bank][:, hl * 128:(hl + 1) * 128], kld[:, hh, :], ident_f[:], r=['kld', 'ident_f'], w=['ps%d' % bank])
        P.add('vector', lambda e, bank=bank, q4=q4: e.tensor_copy(KTt[:, 4 * q4:4 * q4 + 4, :], psum[bank][:, :].rearrange('p (a b) -> p a b', a=4)),
              r=['ps%d' % bank], w=['KTt'])
    for blk in range(4):
        wb, wt = load_wblk(w_q, blk * 512)
        for hl in range(4):
            for half in range(2):
                hs = slice(half * 512, (half + 1) * 512)
                bank = nextbank()
                for kt in range(KT):
                    P.mm(psum[bank][:, :], wb[:, kt, hl * 128:(hl + 1) * 128], x1T[:, kt, hs], kt == 0, kt == KT - 1, r=[wt, 'x1T'], w=['ps%d' % bank])
                P.add('scalar', lambda e, bank=bank, hl=hl, hs=hs: e.copy(qTb[:, hl, hs], psum[bank][:, :]), r=['ps%d' % bank], w=['qTb%d' % hl])
        for n in range(NT):
            bank = nextbank()
            i2 = n % 2
            for hl in range(4):
                P.mm(psum[bank][:, hl * 128:(hl + 1) * 128], qTb[:, hl, n * 128:(n + 1) * 128], KTt[:, 4 * blk + hl, :], True, True,
                     r=['qTb%d' % hl, 'KTt'], w=['ps%d' % bank])
            P.add('vector', lambda e, bank=bank, i2=i2: e.tensor_copy(s_st[i2][:], psum[bank][:, :].rearrange('p (a b) -> p a b', a=4)),
                  r=['ps%d' % bank], w=['s_st%d' % i2])
            P.dma('sync', s_d[n * 128:(n + 1) * 128, 4 * blk:4 * blk + 4, :], s_st[i2][:], r=['s_st%d' % i2], w=['s_d'])
    P.barrier()
    top4 = top[:].rearrange('p (h two) k -> p h two k', two=2)
    gcnt = 0
    for n in range(NT):
        P.dma('sync', s_sb[:], s_d[n * 128:(n + 1) * 128], r=[], w=['s_sb'])
        for hh in range(16):
            P.add('vector', lambda e, hh=hh: e.max(out=top[:, hh, 0:8], in_=s_sb[:, hh, :]), r=['s_sb'], w=['top'])
            P.add('vector', lambda e, hh=hh: e.match_replace(out=mr[:], in_to_replace=top[:, hh, 0:8], in_values=s_sb[:, hh, :], imm_value=-1e30),
                  r=['s_sb', 'top'], w=['mr'])
            P.add('vector', lambda e, hh=hh: e.max(out=top[:, hh, 8:16], in_=mr[:]), r=['mr'], w=['top'])
        tt(cand[:].rearrange('p h (a b) -> p h a b', a=16), bc(top4[:, :, 0, :].unsqueeze(3), [128, 8, 16, 16]),
           bc(top4[:, :, 1, :].unsqueeze(2), [128, 8, 16, 16]), ALU.add, ['top'], ['cand'])
        for h in range(8):
            P.add('vector', lambda e, h=h: e.max(out=vals[:, h, 0:8], in_=cand[:, h, :]), r=['cand'], w=['vals'])
            P.add('vector', lambda e, h=h: e.match_replace(out=cand2[:, h, :], in_to_replace=vals[:, h, 0:8], in_values=cand[:, h, :], imm_value=-1e30),
                  r=['cand', 'vals'], w=['cand2'])
            P.add('vector', lambda e, h=h: e.max(out=vals[:, h, 8:16], in_=cand2[:, h, :]), r=['cand2'], w=['vals'])
        tt(ev[:], vals[:], bc(vals[:, :, 0:1], [128, 8, 16]), ALU.subtract, ['vals'], ['ev'])
        act(ev[:], ev[:], AF.Exp, ['ev'], ['ev'])
        P.add('vector', lambda e: e.tensor_reduce(Zs[:], ev[:], AX.X, ALU.add), r=['ev'], w=['Zs'])
        act(Zs[:], Zs[:], AF.Ln, ['Zs'], ['Zs'])
        tt(nb[:], Zs[:], vals[:, :, 0], ALU.add, ['Zs', 'vals'], ['nb'])
        ts(nb[:], nb[:], -1.0, None, ALU.mult, None, ['nb'], ['nb'])
        for h in range(8):
            for ic in range(8):
                pi_ = gcnt % 2
                gcnt += 1
                Tb, Ebb, Pmb = Tbuf2[pi_], Eb2[pi_], Pm2[pi_]
                tT, tE, tP = 'Tbuf%d' % pi_, 'Eb%d' % pi_, 'Pm%d' % pi_
                isl = slice(ic * 16, (ic + 1) * 16)
                gsl = slice(ic * 2048, (ic + 1) * 2048)
                gt = 'G%d' % ic
                tt(Tb[:], bc(s_sb[:, 2 * h, isl].unsqueeze(2), [128, 16, 128]), bc(s_sb[:, 2 * h + 1, :].unsqueeze(1), [128, 16, 128]),
                   ALU.add, ['s_sb'], [tT])
                Tf = Tb[:].rearrange('p a b -> p (a b)')
                act(Ebb[:], Tf, AF.Exp, [tT, 'nb'], [tE], bias=nb[:, h:h + 1])
                if h == 0:
                    stt(G[:, gsl], Tf, vals[:, h, 15:16], Ebb[:], ALU.is_ge, ALU.mult, [tT, 'vals', tE], [gt])
                else:
                    stt(Pmb[:], Tf, vals[:, h, 15:16], Ebb[:], ALU.is_ge, ALU.mult, [tT, 'vals', tE], [tP])
                    tt(G[:, gsl], G[:, gsl], Pmb[:], ALU.add, [gt, tP], [gt], eng='gpsimd')
        P.dma('sync', Gd[n * 128:(n + 1) * 128, :], G[:], r=['G%d' % i for i in range(8)], w=['Gd'])

    P.barrier()
    sb_off[0] = mark_persist
    yacc = sb('yacc', [128, NT, D], F32)
    ublk = [sb('ublk%d' % i, [128, 4, D], BF16) for i in range(2)]
    sb_save = sb_off[0]
    sb_off[0] = reg_oa
    vblk = [sb('vblk%d' % i, [128, 4, D], BF16) for i in range(2)]
    sb_off[0] = sb_save
    uTb = sb('uTb', [128, KT, 512], BF16)
    Gs = [sb('Gs%d' % i, [128, 512], BF16) for i in range(2)]
    gl2 = [sb('gl%d' % i, [128, 512], F32) for i in range(2)]
    Wb2 = [sb('Wb%d' % i, [128, 512], BF16) for i in range(2)]
    WT2 = [sb('WT%d' % i, [128, 4, 128], BF16) for i in range(2)]
    cnt = 0
    for eb in range(32):
        i = eb % 2
        P.dma('gpsimd', ublk[i][:], peer_u[eb * 512:(eb + 1) * 512, :].rearrange('(et p) d -> p et d', p=128), r=[], w=['ublk%d' % i])
        P.dma('gpsimd', vblk[i][:], peer_v[eb * 512:(eb + 1) * 512, :].rearrange('(et p) d -> p et d', p=128), r=[], w=['vblk%d' % i])
        for k2 in range(8):
            bank = nextbank()
            for kk in range(2):
                kt = 2 * k2 + kk
                for et in range(4):
                    P.tr(psb(bank)[:, (kk * 4 + et) * 128:(kk * 4 + et + 1) * 128], ublk[i][:, et, kt * 128:(kt + 1) * 128], ident_b[:],
                         r=['ublk%d' % i, 'ident_b'], w=['ps%d' % bank])
            if k2 % 2 == 0:
                P.add('vector', lambda e, bank=bank, k2=k2: e.tensor_copy(uTb[:, 2 * k2:2 * k2 + 2, :], psb(bank).rearrange('p (k e) -> p k e', k=2)),
                      r=['ps%d' % bank], w=['uTb'])
            else:
                P.add('scalar', lambda e, bank=bank, k2=k2: e.copy(uTb[:, 2 * k2:2 * k2 + 2, :], psb(bank).rearrange('p (k e) -> p k e', k=2)),
                      r=['ps%d' % bank], w=['uTb'])
        for n in range(NT):
            gi = cnt % 2
            cnt += 1
            P.dma('sync', Gs[gi][:], Gd[n * 128:(n + 1) * 128, eb * 512:(eb + 1) * 512], r=[], w=['Gs%d' % gi])
            bA = nextbank()
            for kt in range(KT):
                P.mm(psum[bA][:, :], x1T[:, kt, n * 128:(n + 1) * 128], uTb[:, kt, :], kt == 0, kt == KT - 1, r=['x1T', 'uTb'], w=['ps%d' % bA])
            gl, Wb, WT = gl2[gi], Wb2[gi], WT2[gi]
            tg, tw, twt = 'gl%d' % gi, 'Wb%d' % gi, 'WT%d' % gi
            act(gl[:], psum[bA][:, :], AF.Gelu_apprx_tanh, ['ps%d' % bA], [tg])
            tt(Wb[:], gl[:], Gs[gi][:], ALU.mult, [tg, 'Gs%d' % gi], [tw], eng='gpsimd')
            bT_ = nextbank()
            for et in range(4):
                P.tr(psb(bT_)[:, et * 128:(et + 1) * 128], Wb[:, et * 128:(et + 1) * 128], ident_b[:], r=[tw, 'ident_b'], w=['ps%d' % bT_])
            P.add('scalar', lambda e, bT_=bT_, WT=WT: e.copy(WT[:], psb(bT_)[:, 0:512].rearrange('p (a b) -> p a b', a=4)), r=['ps%d' % bT_], w=[twt])
            for db in range(4):
                ds_ = slice(db * 512, (db + 1) * 512)
                bY = nextbank()
                for et in range(4):
                    P.mm(psum[bY][:, :], WT[:, et, :], vblk[i][:, et, ds_], et == 0, et == 3, r=[twt, 'vblk%d' % i], w=['ps%d' % bY])
                if eb == 0:
                    P.add('scalar', lambda e, bY=bY, n=n, ds_=ds_: e.copy(yacc[:, n, ds_], psum[bY][:, :]), r=['ps%d' % bY], w=['yacc%d' % n])
                else:
                    tt(yacc[:, n, ds_], yacc[:, n, ds_], psum[bY][:, :], ALU.add, ['yacc%d' % n, 'ps%d' % bY], ['yacc%d' % n])

    P.barrier()
    st4_2 = sb('st4b', [128, 4], F32)
    junkD_2 = sb('junkD2', [128, D], BF16)
    sb_save = sb_off[0]
    sb_off[0] = reg_oa
    lng_2 = sb('lng2', [128, D], F32)
    lnb_2 = sb('lnb2', [128, D], F32)
    xres_2 = sb('xres2', [128, D], F32)
    h1_2 = sb('h12', [128, D], F32)
    sb_off[0] = sb_save
    P.dma('sync', lng_2[:], ln2_g[0].partition_broadcast(128), r=[], w=['lng'])
    P.dma('sync', lnb_2[:], ln2_b[0].partition_broadcast(128), r=[], w=['lnb'])

    def ln_rows2(hb, ht, gam, bet):
        P.add('vector', lambda e: e.tensor_reduce(st4_2[:, 0:1], hb[:], AX.X, ALU.add), r=[ht], w=['st4a'])
        ts(st4_2[:, 1:2], st4_2[:, 0:1], -1.0 / D, None, ALU.mult, None, ['st4a'], ['st4b'])
        ts(hb[:], hb[:], st4_2[:, 1:2], None, ALU.add, None, [ht, 'st4b'], [ht])
        P.add('gpsimd', lambda e: e.memset(st4_2[:, 2:3], 0.0), w=['st4c'])
        P.add('scalar', lambda e: e.activation(junkD_2[:], hb[:], AF.Square, accum_out=st4_2[:, 2:3]), r=[ht, 'st4c'], w=['junkD', 'st4c'])
        act(st4_2[:, 3:4], st4_2[:, 2:3], AF.Ln, ['st4c'], ['st4d'], bias=LN_EPS, scale=1.0 / D)
        act(st4_2[:, 3:4], st4_2[:, 3:4], AF.Exp, ['st4d'], ['st4d'], scale=-0.5)
        stt(hb[:], hb[:], st4_2[:, 3:4], gam[:], ALU.mult, ALU.mult, [ht, 'st4d', 'lng'], [ht])
        tt(hb[:], hb[:], bet[:], ALU.add, [ht, 'lnb'], [ht], eng='gpsimd')

    for n in range(NT):
        P.dma('sync', xres_2[:], x1d[n * 128:(n + 1) * 128, :], r=[], w=['xres'])
        stt(h1_2[:], xres_2[:], ALPHA, yacc[:, n, :], ALU.mult, ALU.add, ['xres', 'yacc%d' % n], ['h1'])
        ln_rows2(h1_2, 'h1', lng_2, lnb_2)
        P.dma('sync', x_out[n * 128:(n + 1) * 128, :], h1_2[:], r=['h1'], w=['x_out'])
    P.emit()
    return nc


def layer_inputs(z, l, phases=99):
    f = lambda a: np.ascontiguousarray(a, dtype=np.float32)
    m = {
        'w_in': f(z['w_in'][l]),
        'lb_logits': f(z['hgrn_lb_logits']),
        'lmask': np.array([[0.0, float(l >= 1), float(l >= 2), float(l >= 3)]], np.float32),
        'norm_g': f(z['hgrn_norm_g'][l]).reshape(1, 1024),
    }
    if phases > 2:
        m.update({
            'lam_re': f(z['s5_lambda_re'][l]).reshape(32, 128),
            'lam_im': f(z['s5_lambda_im'][l]).reshape(32, 128),
            'log_step': f(z['s5_log_step'][l]).reshape(32, 2),
            'b_re': f(z['s5_b_re'][l]),
            'b_im': f(z['s5_b_im'][l]),
            'c_re': f(z['s5_c_re'][l]).reshape(1024, 64),
            'c_im': f(z['s5_c_im'][l]).reshape(1024, 64),
            's5_d': f(z['s5_d'][l]).reshape(8, 128),
            'w_glu': f(z['s5_w_glu'][l]),
        })
    if phases > 3:
        m.update({
            'w_up_a': f(z['w_up_a'][l]), 'w_up_b': f(z['w_up_b'][l]), 'w_o': f(z['w_o'][l]),
            'ln1_g': f(z['ln1_g'][l]).reshape(1, D), 'ln1_b': f(z['ln1_b'][l]).reshape(1, D),
        })
    if phases > 4:
        m.update({
            'w_q': f(z['peer_w_q'][l]), 'keys': f(z['peer_keys'][l]).reshape(16, 128, 128),
            'peer_u': f(z['peer_u'][l]), 'peer_v': f(z['peer_v'][l]),
            'ln2_g': f(z['ln2_g'][l]).reshape(1, D), 'ln2_b': f(z['ln2_b'][l]).reshape(1, D),
        })
    return m


def kernel(**inputs):
    z = inputs
    f32 = np.float32
    x = np.ascontiguousarray(np.asarray(z['x'])[0], dtype=f32)
    ncA = build(dbg=False, phases=3)
    ncB = build(dbg=False, phases=99)
    xs = [np.ascontiguousarray(x[c * T:(c + 1) * T]) for c in range(NCORES)]
    zU = np.zeros((NCORES, 8, 128, 128), f32)
    zD = np.zeros((NCORES, 128, 8), f32)
    zX = np.zeros((NCORES, 128, 64), f32)
    zc = np.zeros((1, NCORES), f32)
    cores = list(range(NCORES))
    for l in range(4):
        mA = layer_inputs(z, l, 3)
        in_maps = [dict(mA, x=xs[c], cmask=zc, hgU_all=zU, hgD_all=zD, s5X_all=zX) for c in cores]
        rA = run_bass_kernel_spmd(ncA, in_maps, core_ids=cores).results
        U = np.ascontiguousarray(np.stack([rA[c]['hgU_out'] for c in cores]), dtype=f32)
        Dd = np.ascontiguousarray(np.stack([rA[c]['hgD_out'] for c in cores]), dtype=f32)
        Xs = np.ascontiguousarray(np.stack([rA[c]['s5X_out'] for c in cores]), dtype=f32)
        mB = layer_inputs(z, l, 99)
        in_maps = [dict(mB, x=xs[c], cmask=(np.arange(NCORES) < c).astype(f32)[None, :],
                        hgU_all=U, hgD_all=Dd, s5X_all=Xs) for c in cores]
        rB = run_bass_kernel_spmd(ncB, in_maps, core_ids=cores).results
        xs = [np.ascontiguousarray(rB[c]['x_out'], dtype=f32) for c in cores]
    return np.concatenate(xs, axis=0)[None].astype(f32)
```

```python
import math
from contextlib import ExitStack
import numpy as np
import concourse.bass as bass
import concourse.mybir as mybir
from concourse.bass_utils import run_bass_kernel_spmd

F32 = mybir.dt.float32
BF16 = mybir.dt.bfloat16
ALU = mybir.AluOpType
AF = mybir.ActivationFunctionType
AX = mybir.AxisListType

ENGS = ['tensor', 'vector', 'scalar', 'gpsimd', 'sync']
NDSEM = 6
NCORES = 8
T = 1024
NT = 8
D = 2048
KT = 16
ALPHA = 8.0 ** 0.25
LN_EPS = 1e-5
RMS_EPS = 1e-6
TWO_PI = 2.0 * math.pi
MAGIC = 12582912.0


class Prog:
    def __init__(self, nc):
        self.nc = nc
        self.ops = {e: [] for e in ENGS}
        self.lastw = {}
        self.readers = {}
        self.ndma = {e: 0 for e in ENGS}
        self.dma_ops = {e: [] for e in ENGS}
        self.pending = {e: [] for e in ENGS}

    def add(self, eng, fn, r=(), w=(), dma=False):
        idx = len(self.ops[eng])
        deps = list(self.pending[eng])
        self.pending[eng] = []
        for t in r:
            if t in self.lastw:
                deps.append(self.lastw[t])
        for t in w:
            if t in self.lastw:
                deps.append(self.lastw[t])
            deps.extend(self.readers.get(t, ()))
        op = dict(fn=fn, deps=deps, dma=dma, signal=False, eng=eng, idx=idx)
        if dma:
            j = self.ndma[eng]
            self.ndma[eng] += 1
            op['dj'] = j
            if j >= NDSEM:
                deps.append(self.dma_ops[eng][j - NDSEM])
            self.dma_ops[eng].append((eng, idx))
        self.ops[eng].append(op)
        ref = (eng, idx)
        for t in r:
            self.readers.setdefault(t, []).append(ref)
        for t in w:
            self.lastw[t] = ref
            self.readers[t] = []
        return ref

    def barrier(self):
        deps = []
        for e in ENGS:
            nonD = [i for i, o in enumerate(self.ops[e]) if not o['dma']]
            if nonD:
                deps.append((e, nonD[-1]))
            deps.extend(self.dma_ops[e][-NDSEM:])
        for e in ENGS:
            self.pending[e] = list(self.pending[e]) + deps
        self.lastw = {}
        self.readers = {}

    def mm(self, out, lhsT, rhs, start, stop, r, w):
        return self.add('tensor', lambda e: e.matmul(out, lhsT, rhs, start=start, stop=stop), r, w)

    def tr(self, out, in_, ident, r, w):
        return self.add('tensor', lambda e: e.transpose(out, in_, ident), r, w)

    def dma(self, eng, out, in_, r, w, **kw):
        return self.add(eng, lambda e: e.dma_start(out=out, in_=in_, **kw), r, w, dma=True)

    def emit(self):
        nc = self.nc
        ops = self.ops
        for e in ENGS:
            for op in ops[e]:
                nd = []
                seen = set()
                for d in op['deps']:
                    if d in seen:
                        continue
                    seen.add(d)
                    dop = ops[d[0]][d[1]]
                    if not dop['dma']:
                        if d[0] == e and e == 'tensor':
                            continue
                        dop['signal'] = True
                    nd.append(d)
                op['deps'] = nd
        for e in ENGS:
            c = 0
            for op in ops[e]:
                if op['dma']:
                    continue
                if op['signal']:
                    c += 1
                    op['semval'] = c
        with ExitStack() as st:
            esem = {e: st.enter_context(nc.semaphore('s_' + e)) for e in ENGS}
            dsem = {e: [st.enter_context(nc.semaphore('d_%s%d' % (e, i))) for i in range(NDSEM)]
                    for e in ENGS if self.ndma[e] > 0}
            block = st.enter_context(nc.Block())

            def run(e, eng):
                waited = {e2: 0 for e2 in ENGS}
                dwaited = {e2: {} for e2 in ENGS}
                for op in ops[e]:
                    for d in op['deps']:
                        dop = ops[d[0]][d[1]]
                        if dop['dma']:
                            j = dop['dj']
                            slot = j % NDSEM
                            if dwaited[d[0]].get(slot, -1) >= j:
                                continue
                            dwaited[d[0]][slot] = j
                            eng.wait_ge(dsem[d[0]][slot], 16 * (j // NDSEM + 1))
                        else:
                            v = dop['semval']
                            if waited[d[0]] >= v:
                                continue
                            waited[d[0]] = v
                            eng.wait_ge(esem[d[0]], v)
                    ins = op['fn'](eng)
                    if op['dma']:
                        ins.then_inc(dsem[e][op['dj'] % NDSEM], 16)
                    elif op['signal']:
                        ins.then_inc(esem[e], 1)
                if self.ndma[e] > 0:
                    n = self.ndma[e]
                    for i in range(NDSEM):
                        cnt = (n - i + NDSEM - 1) // NDSEM if n > i else 0
                        if cnt > 0:
                            eng.wait_ge(dsem[e][i], 16 * cnt)

            for e in ENGS:
                if ops[e]:
                    getattr(block, e)(lambda eng, e=e: run(e, eng))


def bc(ap, shape):
    return ap.to_broadcast(list(shape))


def build(dbg=False, phases=99):
    nc = bass.Bass('TRN2', target_bir_lowering=False)
    P = Prog(nc)

    def din(name, shape):
        return nc.dram_tensor(name, list(shape), F32, kind='ExternalInput').ap()

    def dout(name, shape):
        return nc.dram_tensor(name, list(shape), F32, kind='ExternalOutput').ap()

    x_in = din('x', [T, D])
    w_in = din('w_in', [D, 9216])
    lb_logits = din('lb_logits', [4, 1024])
    lmask = din('lmask', [1, 4])
    cmask = din('cmask', [1, 8])
    norm_g = din('norm_g', [1, 1024])
    if phases > 2:
        lam_re = din('lam_re', [32, 128])
        lam_im = din('lam_im', [32, 128])
        log_step = din('log_step', [32, 2])
        b_re = din('b_re', [64, 64, 16])
        b_im = din('b_im', [64, 64, 16])
        c_re = din('c_re', [1024, 64])
        c_im = din('c_im', [1024, 64])
        s5_d = din('s5_d', [8, 128])
        w_glu = din('w_glu', [1024, 2048])
        s5X_all = din('s5X_all', [NCORES, 128, 64])
        s5X_out = dout('s5X_out', [128, 64])
    if phases > 3:
        w_up_a = din('w_up_a', [1024, 2048])
        w_up_b = din('w_up_b', [1024, 2048])
        w_o = din('w_o', [2048, 2048])
        ln1_g = din('ln1_g', [1, D])
        ln1_b = din('ln1_b', [1, D])
    if phases > 4:
        w_q = din('w_q', [2048, 2048])
        keys = din('keys', [16, 128, 128])
        peer_u = din('peer_u', [16384, 2048])
        peer_v = din('peer_v', [16384, 2048])
        ln2_g = din('ln2_g', [1, D])
        ln2_b = din('ln2_b', [1, D])
        x_out = dout('x_out', [T, D])
    hgU_all = din('hgU_all', [NCORES, 8, 128, 128])
    hgD_all = din('hgD_all', [NCORES, 128, 8])

    hgU_out = dout('hgU_out', [8, 128, 128])
    hgD_out = dout('hgD_out', [128, 8])
    dbgo = {}
    if dbg:
        dbgo['oa'] = dout('dbg_oa', [T, 1024])
        dbgo['obT'] = dout('dbg_obT', [1024, T])
        dbgo['x1'] = dout('dbg_x1', [T, D])
        dbgo['yT'] = dout('dbg_yT', [1024, T])

    x1d = nc.dram_tensor('x1d', [T, D], F32).ap()
    Gd = nc.dram_tensor('Gd', [T, 16384], BF16).ap()

    sb_off = [16512]
    sb_cnt = [0]

    def sb(name, shape, dt):
        n = 1
        for v in shape[1:]:
            n *= v
        nbytes = n * (4 if dt == F32 else 2)
        off = sb_off[0]
        sb_off[0] = (off + nbytes + 63) // 64 * 64
        assert sb_off[0] <= 229000, (name, sb_off[0])
        sb_cnt[0] += 1
        return nc.alloc_sbuf_tensor_at('%s_%d' % (name, sb_cnt[0]), list(shape), dt, offset=off)

    ident_b = sb('ident_b', [128, 128], BF16)
    ident_f = sb('ident_f', [128, 128], F32)
    tri01 = sb('tri01', [128, 64], F32)
    mask01 = sb('mask01', [128, T], F32)
    psum = [nc.alloc_psum_tensor('ps%d' % i, [128, 512], F32) for i in range(8)]

    def psb(i):
        return psum[i][:].bitcast(BF16)

    P.add('gpsimd', lambda e: e.memset(ident_f[:], 1.0), w=['ident_f'])
    P.add('gpsimd', lambda e: e.affine_select(out=ident_f[:], in_=ident_f[:], pattern=[[-1, 128]], compare_op=ALU.is_equal,
                                              fill=0.0, base=0, channel_multiplier=1), r=['ident_f'], w=['ident_f'])
    P.add('vector', lambda e: e.tensor_copy(ident_b[:], ident_f[:]), r=['ident_f'], w=['ident_b'])
    triA = sb('triA', [128, 64], F32)
    triB = sb('triB', [128, 64], F32)
    P.add('gpsimd', lambda e: e.memset(triA[:], 1.0), w=['triA'])
    P.add('gpsimd', lambda e: e.memset(triB[:], 1.0), w=['triB'])
    P.add('gpsimd', lambda e: e.affine_select(out=triA[:], in_=triA[:], pattern=[[1, 64]], compare_op=ALU.is_ge,
                                              fill=0.0, base=0, channel_multiplier=-1), r=['triA'], w=['triA'])
    P.add('gpsimd', lambda e: e.affine_select(out=triB[:], in_=triB[:], pattern=[[1, 64]], compare_op=ALU.is_ge,
                                              fill=0.0, base=64, channel_multiplier=-1), r=['triB'], w=['triB'])
    P.add('vector', lambda e: e.tensor_copy(tri01[0:64, :], triA[0:64, :]), r=['triA'], w=['tri01'])
    P.add('vector', lambda e: e.tensor_copy(tri01[64:128, :], triB[64:128, :]), r=['triB', 'tri01'], w=['tri01'])
    P.add('gpsimd', lambda e: e.memset(mask01[:], 1.0), w=['mask01'])
    P.add('gpsimd', lambda e: e.memset(mask01[:].rearrange('p (c s) -> p c s', s=64)[:, :, 0:1], 0.0), r=['mask01'], w=['mask01'])

    xT = sb('xT', [128, KT, T], BF16)
    reg_oa = sb_off[0]
    oa_tok = sb('oa_tok', [128, NT, 1024], BF16)
    obT = sb('obT', [128, 8, T], BF16)
    lbl = sb('lbl', [128, 4, 8], F32)
    lbe = sb('lbe', [128, 4, 8], F32)
    lbs = sb('lbs', [128, 8], F32)
    lb = sb('lb', [128, 8], F32)
    oml = sb('oml', [128, 8], F32)
    lmk = sb('lmk', [128, 4], F32)
    cmk = sb('cmk', [128, 8], F32)
    mark_persist = sb_off[0]
    xload = [sb('xload%d' % i, [128, D], F32) for i in range(2)]
    xbf = [sb('xbf%d' % i, [128, D], BF16) for i in range(2)]

    def to_featmajor(src_dram, dstT, tokpref):
        for n in range(NT):
            xl = xload[n % 2]; xb = xbf[n % 2]
            tl = 'xload%d' % (n % 2); tb = 'xbf%d' % (n % 2)
            P.dma('sync', xl[:], src_dram[n * 128:(n + 1) * 128, :], r=[], w=[tl])
            P.add('scalar', lambda e, xl=xl, xb=xb: e.copy(xb[:], xl[:]), r=[tl], w=[tb])
            for half in range(2):
                bank = half
                for j in range(8):
                    kt = half * 8 + j
                    P.tr(psb(bank)[:, j * 128:(j + 1) * 128], xb[:, kt * 128:(kt + 1) * 128], ident_b[:],
                         r=[tb, 'ident_b'], w=['ps%d' % bank])
                P.add('vector', lambda e, bank=bank, half=half, n=n: e.tensor_copy(
                    dstT[:, half * 8:(half + 1) * 8, n * 128:(n + 1) * 128],
                    psb(bank).rearrange('p (j t) -> p j t', j=8)), r=['ps%d' % bank], w=[tokpref])

    to_featmajor(x_in, xT, 'xT')

    P.dma('sync', lbl[:], lb_logits.rearrange('l (h d) -> d l h', d=128), r=[], w=['lbl'], allow_slow_non_contiguous=True)
    P.dma('sync', lmk[:], lmask[0].partition_broadcast(128), r=[], w=['lmk'])
    P.dma('sync', cmk[:], cmask[0].partition_broadcast(128), r=[], w=['cmk'])
    P.add('scalar', lambda e: e.activation(lbe[:], lbl[:], AF.Exp), r=['lbl'], w=['lbe'])
    P.add('vector', lambda e: e.tensor_reduce(lbs[:], lbe[:].rearrange('p l h -> p h l'), AX.X, ALU.add), r=['lbe'], w=['lbs'])
    P.add('vector', lambda e: e.reciprocal(lbs[:], lbs[:]), r=['lbs'], w=['lbs'])
    P.add('vector', lambda e: e.tensor_tensor(lbe[:], lbe[:], bc(lmk[:].unsqueeze(2), [128, 4, 8]), ALU.mult), r=['lbe', 'lmk'], w=['lbe'])
    P.add('vector', lambda e: e.tensor_reduce(lb[:], lbe[:].rearrange('p l h -> p h l'), AX.X, ALU.add), r=['lbe'], w=['lb'])
    P.add('vector', lambda e: e.tensor_tensor(lb[:], lb[:], lbs[:], ALU.mult), r=['lb', 'lbs'], w=['lb'])
    P.add('vector', lambda e: e.tensor_scalar(oml[:], lb[:], -1.0, 1.0, ALU.mult, ALU.add), r=['lb'], w=['oml'])

    v_tok = sb('v_tok', [128, NT, 1024], BF16)
    sg_tok = sb('sg_tok', [128, NT, 1024], BF16)
    wblk = [sb('wblk%d' % i, [128, KT, 512], BF16) for i in range(2)]
    wcnt = [0]

    def load_wblk(src, c0, ncols=512, kt=KT):
        i = wcnt[0] % 2
        wcnt[0] += 1
        P.dma('gpsimd', wblk[i][:, 0:kt, 0:ncols], src[:, c0:c0 + ncols].rearrange('(kt p) n -> p kt n', p=128),
              r=[], w=['wblk%d' % i])
        return wblk[i], 'wblk%d' % i

    pscnt = [0]

    def nextbank(lo=0, hi=8):
        b = lo + pscnt[0] % (hi - lo)
        pscnt[0] += 1
        return b

    for blk in range(4):
        wb, wt = load_wblk(w_in, 2048 + blk * 512)
        for n in range(NT):
            bank = nextbank()
            for kt in range(KT):
                P.mm(psum[bank][:, :], xT[:, kt, n * 128:(n + 1) * 128], wb[:, kt, :], kt == 0, kt == KT - 1,
                     r=['xT', wt], w=['ps%d' % bank])
            if blk < 2:
                P.add('vector', lambda e, bank=bank, n=n, blk=blk: e.tensor_copy(v_tok[:, n, blk * 512:(blk + 1) * 512], psum[bank][:, :]),
                      r=['ps%d' % bank], w=['v_tok%d' % n])
            else:
                P.add('scalar', lambda e, bank=bank, n=n, blk=blk: e.activation(sg_tok[:, n, (blk - 2) * 512:(blk - 1) * 512], psum[bank][:, :], AF.Sigmoid),
                      r=['ps%d' % bank], w=['sg_tok%d' % n])

    ngb = sb('ngb', [128, 1024], F32)
    P.dma('sync', ngb[:], norm_g[0].partition_broadcast(128), r=[], w=['ngb'])
    Dp = sb('Dp', [128, NCORES, 8], F32)
    P.dma('sync', Dp[:], hgD_all.rearrange('c d h -> d c h'), r=[], w=['Dp'])
    P.add('vector', lambda e: e.tensor_scalar(Dp[:], Dp[:], -1.0, None, ALU.add), r=['Dp'], w=['Dp'])
    P.add('vector', lambda e: e.tensor_tensor(Dp[:], Dp[:], bc(cmk[:].unsqueeze(2), [128, NCORES, 8]), ALU.mult), r=['Dp', 'cmk'], w=['Dp'])
    P.add('vector', lambda e: e.tensor_scalar(Dp[:], Dp[:], 1.0, None, ALU.add), r=['Dp'], w=['Dp'])

    qT = sb('qT', [128, T], F32)
    fT = sb('fT', [128, T], F32)
    lgf = sb('lgf', [128, T], F32)
    kT_ = sb('kT_', [128, T], F32)
    bT = sb('bT', [128, T], F32)
    eq = sb('eq', [128, T], F32)
    ek = sb('ek', [128, T], F32)
    qt_b = sb('qt_b', [128, T], BF16)
    kt_b = sb('kt_b', [128, T], BF16)
    ktok = sb('ktok', [128, NT, 128], BF16)
    rr = sb('rr', [128, 16], F32)
    er = sb('er', [128, 16], F32)
    e2r = sb('e2r', [128, 16], F32)
    bsum = sb('bsum', [128, 1], F32)
    Dtot = sb('Dtot', [128, 8], F32)
    S = sb('S', [128, 128], F32)
    Uld = [sb('Uld%d' % i, [128, 128], F32) for i in range(2)]
    Spb = [sb('Spb%d' % i, [128, 128], BF16) for i in range(4)]
    Utmp = sb('Utmp', [128, 128], F32)
    att_sb = [sb('att_sb%d' % i, [128, 128], BF16) for i in range(2)]
    ss = sb('ss', [128, 2], F32)
    junk = sb('junk', [128, 128], F32)
    otmp = sb('otmp', [128, 128], F32)
    for i in range(2):
        P.add('gpsimd', lambda e, i=i: e.memset(att_sb[i][:], 0.0), w=['att_sb%d' % i])

    for h in range(8):
        P.add('gpsimd', lambda e: e.memset(S[:], 0.0), w=['S'])
        for j in range(NCORES - 1):
            ul = Uld[j % 2]; ut = 'Uld%d' % (j % 2)
            P.dma('sync', ul[:], hgU_all[j, h], r=[], w=[ut])
            P.add('vector', lambda e, ul=ul, j=j: e.tensor_scalar(ul[:], ul[:], cmk[:, j:j + 1], None, ALU.mult), r=[ut, 'cmk'], w=[ut])
            P.add('vector', lambda e, ul=ul, j=j, h=h: e.scalar_tensor_tensor(S[:], S[:], Dp[:, j, h:h + 1], ul[:], ALU.mult, ALU.add),
                  r=['S', 'Dp', ut], w=['S'])
        i = wcnt[0] % 2
        wcnt[0] += 1
        wb = wblk[i]; wt = 'wblk%d' % i
        P.dma('gpsimd', wb[:, :, 0:128], w_in[:, h * 128:(h + 1) * 128].rearrange('(kt p) n -> p kt n', p=128), r=[], w=[wt])
        P.dma('gpsimd', wb[:, :, 128:256], w_in[:, 1024 + h * 128:1024 + (h + 1) * 128].rearrange('(kt p) n -> p kt n', p=128), r=[], w=[wt])
        for which in range(2):
            for half in range(2):
                bank = nextbank()
                for kt in range(KT):
                    P.mm(psum[bank][:, :], wb[:, kt, which * 128:(which + 1) * 128], xT[:, kt, half * 512:(half + 1) * 512],
                         kt == 0, kt == KT - 1, r=['xT', wt], w=['ps%d' % bank])
                if which == 0:
                    P.add('vector', lambda e, bank=bank, half=half: e.tensor_copy(qT[:, half * 512:(half + 1) * 512], psum[bank][:, :]),
                          r=['ps%d' % bank], w=['qT'])
                else:
                    P.add('scalar', lambda e, bank=bank, half=half: e.activation(fT[:, half * 512:(half + 1) * 512], psum[bank][:, :], AF.Sigmoid),
                          r=['ps%d' % bank], w=['fT'])
        P.add('vector', lambda e, h=h: e.tensor_scalar(fT[:], fT[:], oml[:, h:h + 1], lb[:, h:h + 1], ALU.mult, ALU.add), r=['fT', 'oml', 'lb'], w=['fT'])
        P.add('scalar', lambda e: e.activation(lgf[:], fT[:], AF.Ln), r=['fT'], w=['lgf'])
        P.add('gpsimd', lambda e: e.tensor_scalar(kT_[:], fT[:], -1.0, 1.0, ALU.mult, ALU.add), r=['fT'], w=['kT_'])
        P.add('vector', lambda e: e.tensor_tensor_scan(bT[:], mask01[:], lgf[:], 0.0, ALU.mult, ALU.add), r=['mask01', 'lgf'], w=['bT'])
        b3 = bT[:].rearrange('p (c s) -> p c s', s=64)
        P.add('vector', lambda e: e.tensor_scalar(rr[:], b3[:, :, 63], 0.5, None, ALU.mult), r=['bT'], w=['rr'])
        P.add('vector', lambda e: e.tensor_reduce(bsum[:], b3[:, :, 63], AX.X, ALU.add), r=['bT'], w=['bsum'])
        P.add('scalar', lambda e, h=h: e.activation(Dtot[:, h:h + 1], bsum[:], AF.Exp), r=['bsum'], w=['Dtot'])
        P.add('scalar', lambda e: e.activation(er[:], rr[:], AF.Exp), r=['rr'], w=['er'])
        P.add('vector', lambda e: e.tensor_tensor(e2r[:], er[:], er[:], ALU.mult), r=['er'], w=['e2r'])
        P.add('vector', lambda e: e.tensor_tensor(b3, b3, bc(rr[:].unsqueeze(2), [128, 16, 64]), ALU.subtract), r=['bT', 'rr'], w=['bT'])
        P.add('scalar', lambda e: e.activation(eq[:], bT[:], AF.Exp), r=['bT'], w=['eq'])
        P.add('scalar', lambda e: e.activation(ek[:], bT[:], AF.Exp, scale=-1.0), r=['bT'], w=['ek'])
        P.add('vector', lambda e: e.tensor_tensor(qt_b[:], qT[:], eq[:], ALU.mult), r=['qT', 'eq'], w=['qt_b'])
        P.add('gpsimd', lambda e: e.tensor_tensor(kt_b[:], kT_[:], ek[:], ALU.mult), r=['kT_', 'ek'], w=['kt_b'])
        bank = nextbank()
        for n in range(NT):
            P.tr(psb(bank)[:, n * 128:(n + 1) * 128], kt_b[:, n * 128:(n + 1) * 128], ident_b[:], r=['kt_b', 'ident_b'], w=['ps%d' % bank])
        P.add('vector', lambda e, bank=bank: e.tensor_copy(ktok[:], psb(bank).rearrange('p (n d) -> p n d', n=NT)), r=['ps%d' % bank], w=['ktok'])
        for n in range(NT):
            asb = att_sb[n % 2]; at = 'att_sb%d' % (n % 2)
            b_att = nextbank(); b_U = nextbank(); b_o = nextbank()
            for cc in range(2):
                c = 2 * n + cc
                tok = slice(c * 64, (c + 1) * 64)
                pr = slice(cc * 64, (cc + 1) * 64)
                P.mm(psum[b_att][pr, cc * 64:(cc + 1) * 64], kt_b[:, tok], qt_b[:, tok], True, True, r=['kt_b', 'qt_b'], w=['ps%d' % b_att])
                P.add('vector', lambda e, pr=pr, cc=cc, asb=asb, b_att=b_att: e.tensor_tensor(
                    asb[pr, cc * 64:(cc + 1) * 64], psum[b_att][pr, cc * 64:(cc + 1) * 64], tri01[pr, :], ALU.mult),
                    r=['ps%d' % b_att, 'tri01'], w=[at])
                P.mm(psum[b_U][:, cc * 128:(cc + 1) * 128], ktok[pr, n, :], v_tok[pr, n, h * 128:(h + 1) * 128], True, True,
                     r=['ktok', 'v_tok%d' % n], w=['ps%d' % b_U])
            P.mm(psum[b_o][:, 0:128], asb[:, :], v_tok[:, n, h * 128:(h + 1) * 128], True, False, r=[at, 'v_tok%d' % n], w=['ps%d' % b_o])
            for cc in range(2):
                c = 2 * n + cc
                tok = slice(c * 64, (c + 1) * 64)
                pr = slice(cc * 64, (cc + 1) * 64)
                sp = Spb[c % 4]; spt = 'Spb%d' % (c % 4)
                P.add('vector', lambda e, sp=sp, c=c: e.tensor_scalar(sp[:], S[:], er[:, c:c + 1], None, ALU.mult), r=['S', 'er'], w=[spt])
                P.mm(psum[b_o][pr, 0:128], qt_b[:, tok], sp[:, :], False, cc == 1, r=['qt_b', spt], w=['ps%d' % b_o])
                P.add('vector', lambda e, c=c, cc=cc, b_U=b_U: e.tensor_scalar(Utmp[:], psum[b_U][:, cc * 128:(cc + 1) * 128], er[:, c:c + 1], None, ALU.mult),
                      r=['ps%d' % b_U, 'er'], w=['Utmp'])
                P.add('vector', lambda e, c=c: e.scalar_tensor_tensor(S[:], S[:], e2r[:, c:c + 1], Utmp[:], ALU.mult, ALU.add),
                      r=['S', 'e2r', 'Utmp'], w=['S'])
            P.add('gpsimd', lambda e: e.memset(ss[:], 0.0), w=['ss'])
            P.add('scalar', lambda e, b_o=b_o: e.activation(junk[:], psum[b_o][:, 0:128], AF.Square, accum_out=ss[:, 0:1]), r=['ps%d' % b_o, 'ss'], w=['junk', 'ss'])
            P.add('scalar', lambda e: e.activation(ss[:, 1:2], ss[:, 0:1], AF.Ln, bias=RMS_EPS, scale=1.0 / 128.0), r=['ss'], w=['ss'])
            P.add('scalar', lambda e: e.activation(ss[:, 1:2], ss[:, 1:2], AF.Exp, scale=-0.5), r=['ss'], w=['ss'])
            P.add('vector', lambda e, b_o=b_o, h=h: e.scalar_tensor_tensor(otmp[:], psum[b_o][:, 0:128], ss[:, 1:2], ngb[:, h * 128:(h + 1) * 128], ALU.mult, ALU.mult),
                  r=['ps%d' % b_o, 'ss', 'ngb'], w=['otmp'])
            P.add('gpsimd', lambda e, n=n, h=h: e.tensor_tensor(oa_tok[:, n, h * 128:(h + 1) * 128], otmp[:], sg_tok[:, n, h * 128:(h + 1) * 128], ALU.mult),
                  r=['otmp', 'sg_tok%d' % n], w=['oa_tok%d' % n])
        P.dma('sync', hgU_out[h], S[:], r=['S'], w=['hgU_out'])
    P.dma('sync', hgD_out, Dtot[:], r=['Dtot'], w=['hgD_out'])
    if dbg:
        for n in range(NT):
            P.add('vector', lambda e, n=n: e.tensor_copy(xload[n % 2][:, 0:1024], oa_tok[:, n, :]), r=['oa_tok%d' % n], w=['xload%d' % (n % 2)])
            P.dma('sync', dbgo['oa'][n * 128:(n + 1) * 128, :], xload[n % 2][:, 0:1024], r=['xload%d' % (n % 2)], w=['dbg_oa'])
    if phases <= 2:
        P.emit()
        return nc

    P.barrier()
    sb_off[0] = mark_persist
    wblk = [sb('wblk%d' % i, [128, KT, 512], BF16) for i in range(2)]
    uT = sb('uT', [128, 8, T], F32)
    yT = sb('yT', [128, 8, T], BF16)
    Xr = [sb('Xr%d' % i, [128, T], F32) for i in range(2)]
    Xi = [sb('Xi%d' % i, [128, T], F32) for i in range(2)]
    ytmp = sb('ytmp', [128, 512], F32)
    sgm = sb('sgm', [128, 512], F32)
    pl = sb('pl', [32, 3, 128], F32)
    lst2 = sb('lst2', [32, 2], F32)
    sp = {}
    for nm in ['lamr', 'lami', 'lst', 'lr', 'dt', 'lrdt', 'lidt', 'mag', 't1', 't2', 'sn', 'cs', 'ar', 'ai', 'den', 'nr',
               'zr', 'zi', 'xinr', 'xini', 'axr', 'axi', 'Ar', 'Ai', 'tA', 'tB', 'tC', 'tD']:
        sp[nm] = sb('s5_' + nm, [128, 32], F32)
    pwr = sb('pwr', [128, 11, 32], F32)
    pwi = sb('pwi', [128, 11, 32], F32)
    npwi = sb('npwi', [128, 11, 32], F32)
    bre = sb('bre', [128, 32, 16], F32)
    bim = sb('bim', [128, 32, 16], F32)
    bbr = sb('bbr', [128, 32, 16], F32)
    bbi = sb('bbi', [128, 32, 16], F32)
    btmp = sb('btmp', [128, 32, 16], F32)
    Cld = [sb('Cld%d' % i, [128, 8, 64], F32) for i in range(2)]
    Cdup = sb('Cdup', [128, 2, 64], F32)
    dld = sb('dld', [8, 128], F32)
    dvec = sb('dvec', [128, 8], F32)
    Xe_ld = [sb('Xe_ld%d' % i, [128, 64], F32) for i in range(2)]
    Xend = sb('Xend', [128, 64], F32)
    srcB = [sb('srcB%d' % j, [128, 128], F32) for j in range(4)]
    LBr = [sb('LBr%d' % j, [128, 128], F32) for j in range(4)]
    LBi = [sb('LBi%d' % j, [128, 128], F32) for j in range(4)]
    LCr = [sb('LCr%d' % j, [128, 128], F32) for j in range(4)]
    LCi = [sb('LCi%d' % j, [128, 128], F32) for j in range(4)]
    for j in range(4):
        for tl, nm in ((srcB, 'srcB'), (LCr, 'LCr'), (LCi, 'LCi')):
            P.add('gpsimd', lambda e, t=tl[j]: e.memset(t[:], 0.0), w=['%s%d' % (nm, j)])

    V = lambda eng, fn, r, w: P.add(eng, fn, r=r, w=w)

    def tt(out, a, b, op, r, w, eng='vector'):
        P.add(eng, lambda e: e.tensor_tensor(out, a, b, op), r=r, w=w)

    def ts(out, a, s1, s2, op0, op1, r, w, eng='vector'):
        if op1 is None:
            P.add(eng, lambda e: e.tensor_scalar(out, a, s1, None, op0), r=r, w=w)
        else:
            P.add(eng, lambda e: e.tensor_scalar(out, a, s1, s2, op0, op1), r=r, w=w)

    def stt(out, a, sc, b, op0, op1, r, w, eng='vector'):
        P.add(eng, lambda e: e.scalar_tensor_tensor(out, a, sc, b, op0, op1), r=r, w=w)

    def act(out, a, func, r, w, **kw):
        P.add('scalar', lambda e: e.activation(out, a, func, **kw), r=r, w=w)

    P.dma('sync', pl[:, 0, :], lam_re, r=[], w=['pl'])
    P.dma('sync', pl[:, 1, :], lam_im, r=[], w=['pl'])
    P.dma('sync', lst2[:], log_step, r=[], w=['lst2'])
    P.add('vector', lambda e: e.tensor_copy(pl[:, 2, :].rearrange('p (a b) -> p a b', a=2), bc(lst2[:].unsqueeze(2), [32, 2, 64])), r=['lst2', 'pl'], w=['pl'])
    for i, nm in enumerate(['lamr', 'lami', 'lst']):
        P.tr(psum[4][:, i * 32:(i + 1) * 32], pl[:, i, :], ident_f[0:32, 0:32], r=['pl', 'ident_f'], w=['ps4'])
        P.add('vector', lambda e, i=i, nm=nm: e.tensor_copy(sp[nm][:], psum[4][:, i * 32:(i + 1) * 32]), r=['ps4'], w=[nm])
    P.dma('sync', dld[:], s5_d, r=[], w=['dld'])
    P.tr(psum[4][:, 128:136], dld[:], ident_f[0:8, 0:8], r=['dld', 'ident_f'], w=['ps4'])
    P.add('vector', lambda e: e.tensor_copy(dvec[:], psum[4][:, 128:136]), r=['ps4'], w=['dvec'])
    P.dma('sync', bre[:], b_re.rearrange('(gp g2) p n -> (g2 p) gp n', g2=2), r=[], w=['bre'])
    P.dma('sync', bim[:], b_im.rearrange('(gp g2) p n -> (g2 p) gp n', g2=2), r=[], w=['bim'])
    P.dma('sync', Cld[0][:], c_re.rearrange('(ct q) p -> q ct p', q=128), r=[], w=['Cld0'])
    P.dma('sync', Cld[1][:], c_im.rearrange('(ct q) p -> q ct p', q=128), r=[], w=['Cld1'])

    A = lambda nm: sp[nm][:]
    ts(A('lr'), A('lamr'), -1e-4, None, ALU.min, None, ['lamr'], ['lr'])
    act(A('dt'), A('lst'), AF.Exp, ['lst'], ['dt'])
    tt(A('lrdt'), A('lr'), A('dt'), ALU.mult, ['lr', 'dt'], ['lrdt'])
    tt(A('lidt'), A('lami'), A('dt'), ALU.mult, ['lami', 'dt'], ['lidt'])
    act(A('mag'), A('lrdt'), AF.Exp, ['lrdt'], ['mag'])
    ts(A('t1'), A('lidt'), 1.0 / TWO_PI, MAGIC, ALU.mult, ALU.add, ['lidt'], ['t1'])
    ts(A('t1'), A('t1'), -MAGIC, None, ALU.add, None, ['t1'], ['t1'])
    stt(A('t1'), A('t1'), -TWO_PI, A('lidt'), ALU.mult, ALU.add, ['t1', 'lidt'], ['t1'])
    act(A('sn'), A('t1'), AF.Sin, ['t1'], ['sn'])
    ts(A('t2'), A('lidt'), math.pi / 2, None, ALU.add, None, ['lidt'], ['t2'])
    ts(A('tA'), A('t2'), 1.0 / TWO_PI, MAGIC, ALU.mult, ALU.add, ['t2'], ['tA'])
    ts(A('tA'), A('tA'), -MAGIC, None, ALU.add, None, ['tA'], ['tA'])
    stt(A('tA'), A('tA'), -TWO_PI, A('t2'), ALU.mult, ALU.add, ['tA', 't2'], ['tA'])
    act(A('cs'), A('tA'), AF.Sin, ['tA'], ['cs'])
    tt(A('ar'), A('mag'), A('cs'), ALU.mult, ['mag', 'cs'], ['ar'])
    tt(A('ai'), A('mag'), A('sn'), ALU.mult, ['mag', 'sn'], ['ai'])
    tt(A('den'), A('lr'), A('lr'), ALU.mult, ['lr'], ['den'])
    tt(A('tB'), A('lami'), A('lami'), ALU.mult, ['lami'], ['tB'])
    tt(A('den'), A('den'), A('tB'), ALU.add, ['den', 'tB'], ['den'])
    P.add('vector', lambda e: e.reciprocal(A('den'), A('den')), r=['den'], w=['den'])
    ts(A('nr'), A('ar'), -1.0, None, ALU.add, None, ['ar'], ['nr'])
    tt(A('tB'), A('nr'), A('lr'), ALU.mult, ['nr', 'lr'], ['tB'])
    tt(A('tC'), A('ai'), A('lami'), ALU.mult, ['ai', 'lami'], ['tC'])
    tt(A('tB'), A('tB'), A('tC'), ALU.add, ['tB', 'tC'], ['tB'])
    tt(A('zr'), A('tB'), A('den'), ALU.mult, ['tB', 'den'], ['zr'])
    tt(A('tB'), A('ai'), A('lr'), ALU.mult, ['ai', 'lr'], ['tB'])
    tt(A('tC'), A('nr'), A('lami'), ALU.mult, ['nr', 'lami'], ['tC'])
    tt(A('tB'), A('tB'), A('tC'), ALU.subtract, ['tB', 'tC'], ['tB'])
    tt(A('zi'), A('tB'), A('den'), ALU.mult, ['tB', 'den'], ['zi'])
    P.add('vector', lambda e: e.tensor_copy(pwr[:, 0, :], A('ar')), r=['ar'], w=['pwr'])
    P.add('vector', lambda e: e.tensor_copy(pwi[:, 0, :], A('ai')), r=['ai'], w=['pwi'])
    for k in range(10):
        tt(A('tB'), pwr[:, k, :], pwr[:, k, :], ALU.mult, ['pwr'], ['tB'])
        tt(A('tC'), pwi[:, k, :], pwi[:, k, :], ALU.mult, ['pwi'], ['tC'])
        stt(pwi[:, k + 1, :], pwr[:, k, :], 2.0, pwi[:, k, :], ALU.mult, ALU.mult, ['pwr', 'pwi'], ['pwi'])
        tt(pwr[:, k + 1, :], A('tB'), A('tC'), ALU.subtract, ['tB', 'tC'], ['pwr'])
    ts(npwi[:], pwi[:], -1.0, None, ALU.mult, None, ['pwi'], ['npwi'])
    P.add('gpsimd', lambda e: e.memset(A('xinr'), 0.0), w=['xinr'])
    P.add('gpsimd', lambda e: e.memset(A('xini'), 0.0), w=['xini'])
    for j in range(NCORES - 1):
        xl = Xe_ld[j % 2]; xt_ = 'Xe_ld%d' % (j % 2)
        P.dma('sync', xl[:], s5X_all[j], r=[], w=[xt_])
        m = cmk[:, j:j + 1]
        ts(A('Ar'), pwr[:, 10, :], -1.0, None, ALU.add, None, ['pwr'], ['Ar'])
        ts(A('Ar'), A('Ar'), m, 1.0, ALU.mult, ALU.add, ['Ar', 'cmk'], ['Ar'])
        ts(A('Ai'), pwi[:, 10, :], m, None, ALU.mult, None, ['pwi', 'cmk'], ['Ai'])
        ts(xl[:], xl[:], m, None, ALU.mult, None, [xt_, 'cmk'], [xt_])
        tt(A('tA'), A('Ar'), A('xinr'), ALU.mult, ['Ar', 'xinr'], ['tA'])
        tt(A('tB'), A('Ai'), A('xini'), ALU.mult, ['Ai', 'xini'], ['tB'])
        tt(A('tC'), A('Ar'), A('xini'), ALU.mult, ['Ar', 'xini'], ['tC'])
        tt(A('tD'), A('Ai'), A('xinr'), ALU.mult, ['Ai', 'xinr'], ['tD'])
        tt(A('tA'), A('tA'), A('tB'), ALU.subtract, ['tA', 'tB'], ['tA'])
        tt(A('tC'), A('tC'), A('tD'), ALU.add, ['tC', 'tD'], ['tC'])
        tt(A('xinr'), A('tA'), xl[:, 0:32], ALU.add, ['tA', xt_], ['xinr'])
        tt(A('xini'), A('tC'), xl[:, 32:64], ALU.add, ['tC', xt_], ['xini'])
    tt(A('tA'), A('ar'), A('xinr'), ALU.mult, ['ar', 'xinr'], ['tA'])
    tt(A('tB'), A('ai'), A('xini'), ALU.mult, ['ai', 'xini'], ['tB'])
    tt(A('axr'), A('tA'), A('tB'), ALU.subtract, ['tA', 'tB'], ['axr'])
    tt(A('tA'), A('ar'), A('xini'), ALU.mult, ['ar', 'xini'], ['tA'])
    tt(A('tB'), A('ai'), A('xinr'), ALU.mult, ['ai', 'xinr'], ['tB'])
    tt(A('axi'), A('tA'), A('tB'), ALU.add, ['tA', 'tB'], ['axi'])
    zrb = bc(sp['zr'][:].unsqueeze(2), [128, 32, 16])
    zib = bc(sp['zi'][:].unsqueeze(2), [128, 32, 16])
    tt(bbr[:], bre[:], zrb, ALU.mult, ['bre', 'zr'], ['bbr'])
    tt(btmp[:], bim[:], zib, ALU.mult, ['bim', 'zi'], ['btmp'])
    tt(bbr[:], bbr[:], btmp[:], ALU.subtract, ['bbr', 'btmp'], ['bbr'])
    tt(bbi[:], bim[:], zrb, ALU.mult, ['bim', 'zr'], ['bbi'])
    tt(btmp[:], bre[:], zib, ALU.mult, ['bre', 'zi'], ['btmp'])
    tt(bbi[:], bbi[:], btmp[:], ALU.add, ['bbi', 'btmp'], ['bbi'])

    for blk in range(2):
        wb, wt = load_wblk(w_in, 4096 + blk * 512)
        for ntl in range(4):
            ct = blk * 4 + ntl
            for half in range(2):
                bank = nextbank(0, 4)
                for kt in range(KT):
                    P.mm(psum[bank][:, :], wb[:, kt, ntl * 128:(ntl + 1) * 128], xT[:, kt, half * 512:(half + 1) * 512],
                         kt == 0, kt == KT - 1, r=['xT', wt], w=['ps%d' % bank])
                P.add('scalar', lambda e, bank=bank, ct=ct, half=half: e.copy(uT[:, ct, half * 512:(half + 1) * 512], psum[bank][:, :]),
                      r=['ps%d' % bank], w=['uT%d' % ct])

    P.barrier()
    wb1f = wblk[1][:].rearrange('p a b -> p (a b)').bitcast(F32)
    Ms = [(wb1f[:, 0:T], wb1f[:, T:2 * T]), (wb1f[:, 2 * T:3 * T], wb1f[:, 3 * T:4 * T])]
    wb0f = wblk[0][:].rearrange('p a b -> p (a b)').bitcast(F32)
    Xr2 = [wb0f[:, 0:T], wb0f[:, T:2 * T]]
    Xi2 = [wb0f[:, 2 * T:3 * T], wb0f[:, 3 * T:4 * T]]
    XRs = [Xr, Xr2]
    XIs = [Xi, Xi2]
    for ct in range(8):
        for j in range(4):
            gp = 4 * ct + j
            for (bb, LB, nm) in ((bbr, LBr, 'LBr'), (bbi, LBi, 'LBi')):
                for g2 in range(2):
                    pr = slice(g2 * 64, (g2 + 1) * 64)
                    c0 = (2 * j + g2) * 16
                    P.add('vector', lambda e, bb=bb, pr=pr, c0=c0, j=j, gp=gp: e.tensor_copy(srcB[j][pr, c0:c0 + 16], bb[pr, gp, :]),
                          r=['bbr', 'bbi'], w=['srcB%d' % j])
                P.tr(psum[4][:, 0:128], srcB[j][:], ident_f[:], r=['srcB%d' % j, 'ident_f'], w=['ps4'])
                P.add('vector', lambda e, LB=LB, j=j: e.tensor_copy(LB[j][:], psum[4][:, 0:128]), r=['ps4'], w=['%s%d' % (nm, j)])
        for ri, (LC, nm) in enumerate(((LCr, 'LCr'), (LCi, 'LCi'))):
            P.add('vector', lambda e, ri=ri, ct=ct: e.tensor_copy(Cdup[:], bc(Cld[ri][:, ct, :].unsqueeze(1), [128, 2, 64])), r=['Cld%d' % ri], w=['Cdup'])
            P.tr(psum[5][:, 0:128], Cdup[:].rearrange('p a b -> p (a b)'), ident_f[:], r=['Cdup', 'ident_f'], w=['ps5'])
            for j in range(4):
                for g2 in range(2):
                    pr = slice(g2 * 64, (g2 + 1) * 64)
                    c0 = (2 * j + g2) * 16
                    if ri == 0:
                        P.add('vector', lambda e, LC=LC, j=j, pr=pr, c0=c0: e.tensor_copy(LC[j][pr, c0:c0 + 16], psum[5][pr, c0:c0 + 16]),
                              r=['ps5'], w=['%s%d' % (nm, j)])
                    else:
                        P.add('vector', lambda e, LC=LC, j=j, pr=pr, c0=c0: e.tensor_scalar(LC[j][pr, c0:c0 + 16], psum[5][pr, c0:c0 + 16], -1.0, None, ALU.mult),
                              r=['ps5'], w=['%s%d' % (nm, j)])
        for jp in range(2):
            js = (2 * jp, 2 * jp + 1)
            for si, j in enumerate(js):
                gp = 4 * ct + j
                XR, XI = XRs[si], XIs[si]
                t0r, t0i = 'Xr%d_0' % si, 'Xi%d_0' % si
                for half in range(2):
                    hs = slice(half * 512, (half + 1) * 512)
                    b1 = nextbank(0, 4)
                    P.mm(psum[b1][:, :], LBr[j][:], uT[:, ct, hs], True, True, r=['LBr%d' % j, 'uT%d' % ct], w=['ps%d' % b1])
                    P.add('scalar', lambda e, b1=b1, hs=hs, XR=XR: e.copy(XR[0][:, hs], psum[b1][:, :]), r=['ps%d' % b1], w=[t0r])
                    b2 = nextbank(0, 4)
                    P.mm(psum[b2][:, :], LBi[j][:], uT[:, ct, hs], True, True, r=['LBi%d' % j, 'uT%d' % ct], w=['ps%d' % b2])
                    P.add('scalar', lambda e, b2=b2, hs=hs, XI=XI: e.copy(XI[0][:, hs], psum[b2][:, :]), r=['ps%d' % b2], w=[t0i])
                tt(XR[0][:, 0:1], XR[0][:, 0:1], sp['axr'][:, gp:gp + 1], ALU.add, [t0r, 'axr'], [t0r])
                tt(XI[0][:, 0:1], XI[0][:, 0:1], sp['axi'][:, gp:gp + 1], ALU.add, [t0i, 'axi'], [t0i], eng='gpsimd')
            for k in range(10):
                d = 1 << k
                s_, d_ = k % 2, 1 - k % 2
                for si, j in enumerate(js):
                    gp = 4 * ct + j
                    XR, XI = XRs[si], XIs[si]
                    mA, mB = Ms[si]
                    tmA, tmB = 'mA%d' % si, 'mB%d' % si
                    pr_ = pwr[:, k, gp:gp + 1]; pi_ = pwi[:, k, gp:gp + 1]; npi_ = npwi[:, k, gp:gp + 1]
                    xr_s, xi_s, xr_d, xi_d = XR[s_], XI[s_], XR[d_], XI[d_]
                    tr_s, ti_s, tr_d, ti_d = 'Xr%d_%d' % (si, s_), 'Xi%d_%d' % (si, s_), 'Xr%d_%d' % (si, d_), 'Xi%d_%d' % (si, d_)
                    P.add('vector', lambda e, xr_d=xr_d, xr_s=xr_s, d=d: e.tensor_copy(xr_d[:, 0:d], xr_s[:, 0:d]), r=[tr_s], w=[tr_d])
                    stt(xr_d[:, d:T], xr_s[:, 0:T - d], pr_, xr_s[:, d:T], ALU.mult, ALU.add, [tr_s, 'pwr'], [tr_d])
                    stt(xr_d[:, d:T], xi_s[:, 0:T - d], npi_, xr_d[:, d:T], ALU.mult, ALU.add, [ti_s, 'npwi', tr_d], [tr_d])
                    P.add('gpsimd', lambda e, xi_d=xi_d, xi_s=xi_s, d=d: e.tensor_copy(xi_d[:, 0:d], xi_s[:, 0:d]), r=[ti_s], w=[ti_d])
                    act(mA[:, 0:T - d], xi_s[:, 0:T - d], AF.Copy, [ti_s, 'pwr'], [tmA], scale=pr_)
                    act(mB[:, 0:T - d], xr_s[:, 0:T - d], AF.Copy, [tr_s, 'pwi'], [tmB], scale=pi_)
                    tt(xi_d[:, d:T], mA[:, 0:T - d], xi_s[:, d:T], ALU.add, [tmA, ti_s], [ti_d], eng='gpsimd')
                    tt(xi_d[:, d:T], mB[:, 0:T - d], xi_d[:, d:T], ALU.add, [tmB, ti_d], [ti_d], eng='gpsimd')
            for si, j in enumerate(js):
                gp = 4 * ct + j
                XR, XI = XRs[si], XIs[si]
                t0r, t0i = 'Xr%d_0' % si, 'Xi%d_0' % si
                P.add('vector', lambda e, gp=gp, XR=XR: e.tensor_copy(Xend[:, gp:gp + 1], XR[0][:, T - 1:T]), r=[t0r], w=['Xend'])
                P.add('vector', lambda e, gp=gp, XI=XI: e.tensor_copy(Xend[:, 32 + gp:33 + gp], XI[0][:, T - 1:T]), r=[t0i], w=['Xend'])
                for half in range(2):
                    hs = slice(half * 512, (half + 1) * 512)
                    P.mm(psum[6 + half][:, :], LCr[j][:], XR[0][:, hs], j == 0, False, r=['LCr%d' % j, t0r], w=['ps%d' % (6 + half)])
                    P.mm(psum[6 + half][:, :], LCi[j][:], XI[0][:, hs], False, j == 3, r=['LCi%d' % j, t0i], w=['ps%d' % (6 + half)])
        for half in range(2):
            hs = slice(half * 512, (half + 1) * 512)
            stt(ytmp[:], uT[:, ct, hs], dvec[:, ct:ct + 1], psum[6 + half][:, :], ALU.mult, ALU.add, ['uT%d' % ct, 'dvec', 'ps%d' % (6 + half)], ['ytmp'])
            act(yT[:, ct, hs], ytmp[:], AF.Gelu_apprx_tanh, ['ytmp'], ['yT'])
    P.dma('sync', s5X_out, Xend[:], r=['Xend'], w=['s5X_out'])
    P.barrier()
    for b in range(2):
        WA, wta = wblk[0], 'wblk0'
        WG, wtg = wblk[1], 'wblk1'
        P.dma('gpsimd', WA[:, 0:8, :], w_glu[:, b * 512:(b + 1) * 512].rearrange('(kt p) n -> p kt n', p=128), r=[], w=[wta])
        P.dma('gpsimd', WG[:, 0:8, :], w_glu[:, 1024 + b * 512:1024 + (b + 1) * 512].rearrange('(kt p) n -> p kt n', p=128), r=[], w=[wtg])
        for ntl in range(4):
            nt = 4 * b + ntl
            for half in range(2):
                hs = slice(half * 512, (half + 1) * 512)
                ba = nextbank(0, 6); bg = nextbank(0, 6)
                for kt in range(8):
                    P.mm(psum[ba][:, :], WA[:, kt, ntl * 128:(ntl + 1) * 128], yT[:, kt, hs], kt == 0, kt == 7, r=['yT', wta], w=['ps%d' % ba])
                for kt in range(8):
                    P.mm(psum[bg][:, :], WG[:, kt, ntl * 128:(ntl + 1) * 128], yT[:, kt, hs], kt == 0, kt == 7, r=['yT', wtg], w=['ps%d' % bg])
                act(sgm[:], psum[bg][:, :], AF.Sigmoid, ['ps%d' % bg], ['sgm'])
                tt(obT[:, nt, hs], psum[ba][:, :], sgm[:], ALU.mult, ['ps%d' % ba, 'sgm'], ['obT'])
    if dbg:
        for nt in range(8):
            P.add('vector', lambda e, nt=nt: e.tensor_copy(Xr[0][:], obT[:, nt, :]), r=['obT'], w=['Xr0_0'])
            P.dma('sync', dbgo['obT'][nt * 128:(nt + 1) * 128, :], Xr[0][:], r=['Xr0_0'], w=['dbg_obT'])
    if phases <= 3:
        P.emit()
        return nc

    P.barrier()
    sb_off[0] = mark_persist
    mergedT = sb('mergedT', [128, KT, T], BF16)
    mark4 = sb_off[0]
    oaT = sb('oaT', [128, 8, T], BF16)
    wA = [sb('wA%d' % i, [128, 8, 512], BF16) for i in range(2)]
    wB = [sb('wB%d' % i, [128, 8, 512], BF16) for i in range(2)]
    wGA = sb('wGA', [128, KT, 512], BF16)
    wGB = sb('wGB', [128, KT, 512], BF16)
    sga = sb('sga', [128, 512], F32)
    sgb = sb('sgb', [128, 512], F32)
    mm1 = sb('mm1', [128, 512], F32)
    mm2 = sb('mm2', [128, 512], F32)
    for n in range(NT):
        bank = nextbank()
        for kt in range(8):
            P.tr(psb(bank)[:, kt * 128:(kt + 1) * 128], oa_tok[:, n, kt * 128:(kt + 1) * 128], ident_b[:], r=['oa_tok', 'ident_b'], w=['ps%d' % bank])
        P.add('vector', lambda e, bank=bank, n=n: e.tensor_copy(oaT[:, :, n * 128:(n + 1) * 128], psb(bank).rearrange('p (j t) -> p j t', j=8)),
              r=['ps%d' % bank], w=['oaT'])
    for cb in range(4):
        i = cb % 2
        P.dma('gpsimd', wA[i][:], w_up_a[:, cb * 512:(cb + 1) * 512].rearrange('(kt p) n -> p kt n', p=128), r=[], w=['wA%d' % i])
        P.dma('gpsimd', wB[i][:], w_up_b[:, cb * 512:(cb + 1) * 512].rearrange('(kt p) n -> p kt n', p=128), r=[], w=['wB%d' % i])
        P.dma('gpsimd', wGA[:], w_in[:, 5120 + cb * 512:5120 + (cb + 1) * 512].rearrange('(kt p) n -> p kt n', p=128), r=[], w=['wGA'])
        P.dma('gpsimd', wGB[:], w_in[:, 7168 + cb * 512:7168 + (cb + 1) * 512].rearrange('(kt p) n -> p kt n', p=128), r=[], w=['wGB'])
        for ntl in range(4):
            nt = 4 * cb + ntl
            ns = slice(ntl * 128, (ntl + 1) * 128)
            for half in range(2):
                hs = slice(half * 512, (half + 1) * 512)
                ba, bb_, bga, bgb = nextbank(), nextbank(), nextbank(), nextbank()
                for kt in range(8):
                    P.mm(psum[ba][:, :], wA[i][:, kt, ns], oaT[:, kt, hs], kt == 0, kt == 7, r=['wA%d' % i, 'oaT'], w=['ps%d' % ba])
                for kt in range(8):
                    P.mm(psum[bb_][:, :], wB[i][:, kt, ns], obT[:, kt, hs], kt == 0, kt == 7, r=['wB%d' % i, 'obT'], w=['ps%d' % bb_])
                for kt in range(KT):
                    P.mm(psum[bga][:, :], wGA[:, kt, ns], xT[:, kt, hs], kt == 0, kt == KT - 1, r=['wGA', 'xT'], w=['ps%d' % bga])
                for kt in range(KT):
                    P.mm(psum[bgb][:, :], wGB[:, kt, ns], xT[:, kt, hs], kt == 0, kt == KT - 1, r=['wGB', 'xT'], w=['ps%d' % bgb])
                act(sga[:], psum[bga][:, :], AF.Sigmoid, ['ps%d' % bga], ['sga'])
                act(sgb[:], psum[bgb][:, :], AF.Sigmoid, ['ps%d' % bgb], ['sgb'])
                tt(mm1[:], psum[ba][:, :], sga[:], ALU.mult, ['ps%d' % ba, 'sga'], ['mm1'])
                tt(mm2[:], psum[bb_][:, :], sgb[:], ALU.mult, ['ps%d' % bb_, 'sgb'], ['mm2'])
                tt(mergedT[:, nt, hs], mm1[:], mm2[:], ALU.add, ['mm1', 'mm2'], ['mergedT'], eng='gpsimd')
    P.barrier()
    sb_off[0] = mark4
    wo_sb = sb('wo_sb', [128, KT, D], BF16)
    x1b = sb('x1b', [128, D], BF16)
    st4 = sb('st4', [128, 4], F32)
    sb_save = sb_off[0]
    sb_off[0] = reg_oa
    lng = sb('lng', [128, D], F32)
    lnb = sb('lnb', [128, D], F32)
    xres = sb('xres', [128, D], F32)
    h1 = sb('h1', [128, D], F32)
    sb_off[0] = sb_save
    x1T = xT
    for cb in range(4):
        P.dma('gpsimd', wo_sb[:, :, cb * 512:(cb + 1) * 512], w_o[:, cb * 512:(cb + 1) * 512].rearrange('(kt p) n -> p kt n', p=128), r=[], w=['wo_sb%d' % cb])

    def layer_norm(src, gamma_d, beta_d, hbuf, dst_dram, tagp):
        pass

    P.dma('sync', lng[:], ln1_g[0].partition_broadcast(128), r=[], w=['lng'])
    P.dma('sync', lnb[:], ln1_b[0].partition_broadcast(128), r=[], w=['lnb'])

    def ln_rows(hb, ht, gam, bet):
        P.add('vector', lambda e: e.tensor_reduce(st4[:, 0:1], hb[:], AX.X, ALU.add), r=[ht], w=['st4a'])
        ts(st4[:, 1:2], st4[:, 0:1], -1.0 / D, None, ALU.mult, None, ['st4a'], ['st4b'])
        ts(hb[:], hb[:], st4[:, 1:2], None, ALU.add, None, [ht, 'st4b'], [ht])
        P.add('gpsimd', lambda e: e.memset(st4[:, 2:3], 0.0), w=['st4c'])
        P.add('scalar', lambda e: e.activation(junkD[:], hb[:], AF.Square, accum_out=st4[:, 2:3]), r=[ht, 'st4c'], w=['junkD', 'st4c'])
        act(st4[:, 3:4], st4[:, 2:3], AF.Ln, ['st4c'], ['st4d'], bias=LN_EPS, scale=1.0 / D)
        act(st4[:, 3:4], st4[:, 3:4], AF.Exp, ['st4d'], ['st4d'], scale=-0.5)
        stt(hb[:], hb[:], st4[:, 3:4], gam[:], ALU.mult, ALU.mult, [ht, 'st4d', 'lng'], [ht])
        tt(hb[:], hb[:], bet[:], ALU.add, [ht, 'lnb'], [ht], eng='gpsimd')

    junkD = sb('junkD', [128, D], BF16)
    for n in range(NT):
        P.dma('sync', xres[:], x_in[n * 128:(n + 1) * 128, :], r=[], w=['xres'])
        for cb in range(4):
            bank = nextbank()
            for kt in range(KT):
                P.mm(psum[bank][:, :], mergedT[:, kt, n * 128:(n + 1) * 128], wo_sb[:, kt, cb * 512:(cb + 1) * 512], kt == 0, kt == KT - 1,
                     r=['mergedT', 'wo_sb%d' % cb], w=['ps%d' % bank])
            stt(h1[:, cb * 512:(cb + 1) * 512], xres[:, cb * 512:(cb + 1) * 512], ALPHA, psum[bank][:, :], ALU.mult, ALU.add,
                ['xres', 'ps%d' % bank], ['h1'])
        ln_rows(h1, 'h1', lng, lnb)
        P.dma('sync', x1d[n * 128:(n + 1) * 128, :], h1[:], r=['h1'], w=['x1d'])
        if dbg:
            P.dma('sync', dbgo['x1'][n * 128:(n + 1) * 128, :], h1[:], r=['h1'], w=['dbg_x1'])
        P.add('scalar', lambda e: e.copy(x1b[:], h1[:]), r=['h1'], w=['x1b'])
        for half in range(2):
            bank = nextbank()
            for j in range(8):
                kt = half * 8 + j
                P.tr(psb(bank)[:, j * 128:(j + 1) * 128], x1b[:, kt * 128:(kt + 1) * 128], ident_b[:], r=['x1b', 'ident_b'], w=['ps%d' % bank])
            P.add('vector', lambda e, bank=bank, half=half, n=n: e.tensor_copy(
                x1T[:, half * 8:(half + 1) * 8, n * 128:(n + 1) * 128], psb(bank).rearrange('p (j t) -> p j t', j=8)),
                r=['ps%d' % bank], w=['x1T'])
    if phases <= 4:
        P.emit()
        return nc

    P.barrier()
    sb_off[0] = mark_persist
    s_d = nc.dram_tensor('s_d', [T, 16, 128], F32).ap()
    wblk = [sb('wblk%d' % i, [128, KT, 512], BF16) for i in range(2)]
    qTb = sb('qTb', [128, 4, T], F32)
    kld = sb('kld', [128, 16, 128], F32)
    KTt = sb('KTt', [128, 16, 128], F32)
    s_st = [sb('s_st%d' % i, [128, 4, 128], F32) for i in range(2)]
    s_sb = sb('s_sb', [128, 16, 128], F32)
    mr = sb('mr', [128, 128], F32)
    top = sb('top', [128, 16, 16], F32)
    cand = sb('cand', [128, 8, 256], F32)
    cand2 = sb('cand2', [128, 8, 256], F32)
    vals = sb('vals', [128, 8, 16], F32)
    ev = sb('ev', [128, 8, 16], F32)
    Zs = sb('Zs', [128, 8], F32)
    nb = sb('nb', [128, 8], F32)
    Tbuf2 = [sb('Tbuf%d' % i, [128, 16, 128], F32) for i in range(2)]
    Eb2 = [sb('Eb%d' % i, [128, 2048], BF16) for i in range(2)]
    Pm2 = [sb('Pm%d' % i, [128, 2048], BF16) for i in range(2)]
    sb_save = sb_off[0]
    sb_off[0] = reg_oa
    G = sb('G', [128, 16384], BF16)
    sb_off[0] = sb_save

    P.dma('sync', kld[:], keys.rearrange('h i d -> i h d'), r=[], w=['kld'])
    for q4 in range(4):
        bank = nextbank()
        for hl in range(4):
            hh = 4 * q4 + hl
            P.tr(psum[bank][:, hl * 128:(hl + 1) * 128], kld[:, hh, :], ident_f[:], r=['kld', 'ident_f'], w=['ps%d' % bank])
        P.add('vector', lambda e, bank=bank, q4=q4: e.tensor_copy(KTt[:, 4 * q4:4 * q4 + 4, :], psum[bank][:, :].rearrange('p (a b) -> p a b', a=4)),
              r=['ps%d' % bank], w=['KTt'])
    for blk in range(4):
        wb, wt = load_wblk(w_q, blk * 512)
        for hl in range(4):
            for half in range(2):
                hs = slice(half * 512, (half + 1) * 512)
                bank = nextbank()
                for kt in range(KT):
                    P.mm(psum[bank][:, :], wb[:, kt, hl * 128:(hl + 1) * 128], x1T[:, kt, hs], kt == 0, kt == KT - 1, r=[wt, 'x1T'], w=['ps%d' % bank])
                P.add('scalar', lambda e, bank=bank, hl=hl, hs=hs: e.copy(qTb[:, hl, hs], psum[bank][:, :]), r=['ps%d' % bank], w=['qTb%d' % hl])
        for n in range(NT):
            bank = nextbank()
            i2 = n % 2
            for hl in range(4):
                P.mm(psum[bank][:, hl * 128:(hl + 1) * 128], qTb[:, hl, n * 128:(n + 1) * 128], KTt[:, 4 * blk + hl, :], True, True,
                     r=['qTb%d' % hl, 'KTt'], w=['ps%d' % bank])
            P.add('vector', lambda e, bank=bank, i2=i2: e.tensor_copy(s_st[i2][:], psum[bank][:, :].rearrange('p (a b) -> p a b', a=4)),
                  r=['ps%d' % bank], w=['s_st%d' % i2])
            P.dma('sync', s_d[n * 128:(n + 1) * 128, 4 * blk:4 * blk + 4, :], s_st[i2][:], r=['s_st%d' % i2], w=['s_d'])
    P.barrier()
    top4 = top[:].rearrange('p (h two) k -> p h two k', two=2)
    gcnt = 0
    for n in range(NT):
        P.dma('sync', s_sb[:], s_d[n * 128:(n + 1) * 128], r=[], w=['s_sb'])
        for hh in range(16):
            P.add('vector', lambda e, hh=hh: e.max(out=top[:, hh, 0:8], in_=s_sb[:, hh, :]), r=['s_sb'], w=['top'])
            P.add('vector', lambda e, hh=hh: e.match_replace(out=mr[:], in_to_replace=top[:, hh, 0:8], in_values=s_sb[:, hh, :], imm_value=-1e30),
                  r=['s_sb', 'top'], w=['mr'])
            P.add('vector', lambda e, hh=hh: e.max(out=top[:, hh, 8:16], in_=mr[:]), r=['mr'], w=['top'])
        tt(cand[:].rearrange('p h (a b) -> p h a b', a=16), bc(top4[:, :, 0, :].unsqueeze(3), [128, 8, 16, 16]),
           bc(top4[:, :, 1, :].unsqueeze(2), [128, 8, 16, 16]), ALU.add, ['top'], ['cand'])
        for h in range(8):
            P.add('vector', lambda e, h=h: e.max(out=vals[:, h, 0:8], in_=cand[:, h, :]), r=['cand'], w=['vals'])
            P.add('vector', lambda e, h=h: e.match_replace(out=cand2[:, h, :], in_to_replace=vals[:, h, 0:8], in_values=cand[:, h, :], imm_value=-1e30),
                  r=['cand', 'vals'], w=['cand2'])
            P.add('vector', lambda e, h=h: e.max(out=vals[:, h, 8:16], in_=cand2[:, h, :]), r=['cand2'], w=['vals'])
        tt(ev[:], vals[:], bc(vals[:, :, 0:1], [128, 8, 16]), ALU.subtract, ['vals'], ['ev'])
        act(ev[:], ev[:], AF.Exp, ['ev'], ['ev'])
        P.add('vector', lambda e: e.tensor_reduce(Zs[:], ev[:], AX.X, ALU.add), r=['ev'], w=['Zs'])
        act(Zs[:], Zs[:], AF.Ln, ['Zs'], ['Zs'])
        tt(nb[:], Zs[:], vals[:, :, 0], ALU.add, ['Zs', 'vals'], ['nb'])
        ts(nb[:], nb[:], -1.0, None, ALU.mult, None, ['nb'], ['nb'])
        its = [(h, ic) for h in range(8) for ic in range(8)]

        def emitT(idx):
            h, ic = its[idx]
            pi_ = idx % 2
            isl = slice(ic * 16, (ic + 1) * 16)
            tt(Tbuf2[pi_][:], bc(s_sb[:, 2 * h, isl].unsqueeze(2), [128, 16, 128]), bc(s_sb[:, 2 * h + 1, :].unsqueeze(1), [128, 16, 128]),
               ALU.add, ['s_sb'], ['Tbuf%d' % pi_])

        emitT(0)
        for idx, (h, ic) in enumerate(its):
            if idx + 1 < len(its):
                emitT(idx + 1)
            pi_ = idx % 2
            Tb, Ebb, Pmb = Tbuf2[pi_], Eb2[pi_], Pm2[pi_]
            tT, tE, tP = 'Tbuf%d' % pi_, 'Eb%d' % pi_, 'Pm%d' % pi_
            gsl = slice(ic * 2048, (ic + 1) * 2048)
            gt = 'G%d' % ic
            Tf = Tb[:].rearrange('p a b -> p (a b)')
            act(Ebb[:], Tf, AF.Exp, [tT, 'nb'], [tE], bias=nb[:, h:h + 1])
            if h == 0:
                stt(G[:, gsl], Tf, vals[:, h, 15:16], Ebb[:], ALU.is_ge, ALU.mult, [tT, 'vals', tE], [gt])
            else:
                stt(Pmb[:], Tf, vals[:, h, 15:16], Ebb[:], ALU.is_ge, ALU.mult, [tT, 'vals', tE], [tP])
                tt(G[:, gsl], G[:, gsl], Pmb[:], ALU.add, [gt, tP], [gt], eng='gpsimd')
        P.dma('sync', Gd[n * 128:(n + 1) * 128, :], G[:], r=['G%d' % i for i in range(8)], w=['Gd'])

    P.barrier()
    sb_off[0] = mark_persist
    yacc = sb('yacc', [128, NT, D], F32)
    ublk = [sb('ublk%d' % i, [128, 4, D], BF16) for i in range(2)]
    sb_save = sb_off[0]
    sb_off[0] = reg_oa
    vblk = [sb('vblk%d' % i, [128, 4, D], BF16) for i in range(2)]
    sb_off[0] = sb_save
    uTb = sb('uTb', [128, KT, 512], BF16)
    Gs = [sb('Gs%d' % i, [128, 512], BF16) for i in range(2)]
    gl2 = [sb('gl%d' % i, [128, 512], F32) for i in range(2)]
    Wb2 = [sb('Wb%d' % i, [128, 512], BF16) for i in range(2)]
    WT2 = [sb('WT%d' % i, [128, 4, 128], BF16) for i in range(2)]
    cnt = 0
    for eb in range(32):
        i = eb % 2
        P.dma('gpsimd', ublk[i][:], peer_u[eb * 512:(eb + 1) * 512, :].rearrange('(et p) d -> p et d', p=128), r=[], w=['ublk%d' % i])
        P.dma('gpsimd', vblk[i][:], peer_v[eb * 512:(eb + 1) * 512, :].rearrange('(et p) d -> p et d', p=128), r=[], w=['vblk%d' % i])
        for k2 in range(8):
            bank = nextbank()
            for kk in range(2):
                kt = 2 * k2 + kk
                for et in range(4):
                    P.tr(psb(bank)[:, (kk * 4 + et) * 128:(kk * 4 + et + 1) * 128], ublk[i][:, et, kt * 128:(kt + 1) * 128], ident_b[:],
                         r=['ublk%d' % i, 'ident_b'], w=['ps%d' % bank])
            if k2 % 2 == 0:
                P.add('vector', lambda e, bank=bank, k2=k2: e.tensor_copy(uTb[:, 2 * k2:2 * k2 + 2, :], psb(bank).rearrange('p (k e) -> p k e', k=2)),
                      r=['ps%d' % bank], w=['uTb'])
            else:
                P.add('scalar', lambda e, bank=bank, k2=k2: e.copy(uTb[:, 2 * k2:2 * k2 + 2, :], psb(bank).rearrange('p (k e) -> p k e', k=2)),
                      r=['ps%d' % bank], w=['uTb'])
        for n in range(NT):
            gi = cnt % 2
            cnt += 1
            P.dma('sync', Gs[gi][:], Gd[n * 128:(n + 1) * 128, eb * 512:(eb + 1) * 512], r=[], w=['Gs%d' % gi])
            bA = nextbank()
            for kt in range(KT):
                P.mm(psum[bA][:, :], x1T[:, kt, n * 128:(n + 1) * 128], uTb[:, kt, :], kt == 0, kt == KT - 1, r=['x1T', 'uTb'], w=['ps%d' % bA])
            gl, Wb, WT = gl2[gi], Wb2[gi], WT2[gi]
            tg, tw, twt = 'gl%d' % gi, 'Wb%d' % gi, 'WT%d' % gi
            act(gl[:], psum[bA][:, :], AF.Gelu_apprx_tanh, ['ps%d' % bA], [tg])
            tt(Wb[:], gl[:], Gs[gi][:], ALU.mult, [tg, 'Gs%d' % gi], [tw], eng='gpsimd')
            bT_ = nextbank()
            for et in range(4):
                P.tr(psb(bT_)[:, et * 128:(et + 1) * 128], Wb[:, et * 128:(et + 1) * 128], ident_b[:], r=[tw, 'ident_b'], w=['ps%d' % bT_])
            P.add('scalar', lambda e, bT_=bT_, WT=WT: e.copy(WT[:], psb(bT_)[:, 0:512].rearrange('p (a b) -> p a b', a=4)), r=['ps%d' % bT_], w=[twt])
            for db in range(4):
                ds_ = slice(db * 512, (db + 1) * 512)
                bY = nextbank()
                for et in range(4):
                    P.mm(psum[bY][:, :], WT[:, et, :], vblk[i][:, et, ds_], et == 0, et == 3, r=[twt, 'vblk%d' % i], w=['ps%d' % bY])
                if eb == 0:
                    P.add('scalar', lambda e, bY=bY, n=n, ds_=ds_: e.copy(yacc[:, n, ds_], psum[bY][:, :]), r=['ps%d' % bY], w=['yacc%d' % n])
                else:
                    tt(yacc[:, n, ds_], yacc[:, n, ds_], psum[bY][:, :], ALU.add, ['yacc%d' % n, 'ps%d' % bY], ['yacc%d' % n])

    P.barrier()
    st4_2 = sb('st4b', [128, 4], F32)
    junkD_2 = sb('junkD2', [128, D], BF16)
    sb_save = sb_off[0]
    sb_off[0] = reg_oa
    lng_2 = sb('lng2', [128, D], F32)
    lnb_2 = sb('lnb2', [128, D], F32)
    xres_2 = sb('xres2', [128, D], F32)
    h1_2 = sb('h12', [128, D], F32)
    sb_off[0] = sb_save
    P.dma('sync', lng_2[:], ln2_g[0].partition_broadcast(128), r=[], w=['lng'])
    P.dma('sync', lnb_2[:], ln2_b[0].partition_broadcast(128), r=[], w=['lnb'])

    def ln_rows2(hb, ht, gam, bet):
        P.add('vector', lambda e: e.tensor_reduce(st4_2[:, 0:1], hb[:], AX.X, ALU.add), r=[ht], w=['st4a'])
        ts(st4_2[:, 1:2], st4_2[:, 0:1], -1.0 / D, None, ALU.mult, None, ['st4a'], ['st4b'])
        ts(hb[:], hb[:], st4_2[:, 1:2], None, ALU.add, None, [ht, 'st4b'], [ht])
        P.add('gpsimd', lambda e: e.memset(st4_2[:, 2:3], 0.0), w=['st4c'])
        P.add('scalar', lambda e: e.activation(junkD_2[:], hb[:], AF.Square, accum_out=st4_2[:, 2:3]), r=[ht, 'st4c'], w=['junkD', 'st4c'])
        act(st4_2[:, 3:4], st4_2[:, 2:3], AF.Ln, ['st4c'], ['st4d'], bias=LN_EPS, scale=1.0 / D)
        act(st4_2[:, 3:4], st4_2[:, 3:4], AF.Exp, ['st4d'], ['st4d'], scale=-0.5)
        stt(hb[:], hb[:], st4_2[:, 3:4], gam[:], ALU.mult, ALU.mult, [ht, 'st4d', 'lng'], [ht])
        tt(hb[:], hb[:], bet[:], ALU.add, [ht, 'lnb'], [ht], eng='gpsimd')

    for n in range(NT):
        P.dma('sync', xres_2[:], x1d[n * 128:(n + 1) * 128, :], r=[], w=['xres'])
        stt(h1_2[:], xres_2[:], ALPHA, yacc[:, n, :], ALU.mult, ALU.add, ['xres', 'yacc%d' % n], ['h1'])
        ln_rows2(h1_2, 'h1', lng_2, lnb_2)
        P.dma('sync', x_out[n * 128:(n + 1) * 128, :], h1_2[:], r=['h1'], w=['x_out'])
    P.emit()
    return nc


def layer_inputs(z, l, phases=99):
    f = lambda a: np.ascontiguousarray(a, dtype=np.float32)
    m = {
        'w_in': f(z['w_in'][l]),
        'lb_logits': f(z['hgrn_lb_logits']),
        'lmask': np.array([[0.0, float(l >= 1), float(l >= 2), float(l >= 3)]], np.float32),
        'norm_g': f(z['hgrn_norm_g'][l]).reshape(1, 1024),
    }
    if phases > 2:
        m.update({
            'lam_re': f(z['s5_lambda_re'][l]).reshape(32, 128),
            'lam_im': f(z['s5_lambda_im'][l]).reshape(32, 128),
            'log_step': f(z['s5_log_step'][l]).reshape(32, 2),
            'b_re': f(z['s5_b_re'][l]),
            'b_im': f(z['s5_b_im'][l]),
            'c_re': f(z['s5_c_re'][l]).reshape(1024, 64),
            'c_im': f(z['s5_c_im'][l]).reshape(1024, 64),
            's5_d': f(z['s5_d'][l]).reshape(8, 128),
            'w_glu': f(z['s5_w_glu'][l]),
        })
    if phases > 3:
        m.update({
            'w_up_a': f(z['w_up_a'][l]), 'w_up_b': f(z['w_up_b'][l]), 'w_o': f(z['w_o'][l]),
            'ln1_g': f(z['ln1_g'][l]).reshape(1, D), 'ln1_b': f(z['ln1_b'][l]).reshape(1, D),
        })
    if phases > 4:
        m.update({
            'w_q': f(z['peer_w_q'][l]), 'keys': f(z['peer_keys'][l]).reshape(16, 128, 128),
            'peer_u': f(z['peer_u'][l]), 'peer_v': f(z['peer_v'][l]),
            'ln2_g': f(z['ln2_g'][l]).reshape(1, D), 'ln2_b': f(z['ln2_b'][l]).reshape(1, D),
        })
    return m


def kernel(**inputs):
    z = inputs
    f32 = np.float32
    x = np.ascontiguousarray(np.asarray(z['x'])[0], dtype=f32)
    ncA = build(dbg=False, phases=3)
    ncB = build(dbg=False, phases=99)
    xs = [np.ascontiguousarray(x[c * T:(c + 1) * T]) for c in range(NCORES)]
    zU = np.zeros((NCORES, 8, 128, 128), f32)
    zD = np.zeros((NCORES, 128, 8), f32)
    zX = np.zeros((NCORES, 128, 64), f32)
    zc = np.zeros((1, NCORES), f32)
    cores = list(range(NCORES))
    for l in range(4):
        mA = layer_inputs(z, l, 3)
        in_maps = [dict(mA, x=xs[c], cmask=zc, hgU_all=zU, hgD_all=zD, s5X_all=zX) for c in cores]
        rA = run_bass_kernel_spmd(ncA, in_maps, core_ids=cores).results
        U = np.ascontiguousarray(np.stack([rA[c]['hgU_out'] for c in cores]), dtype=f32)
        Dd = np.ascontiguousarray(np.stack([rA[c]['hgD_out'] for c in cores]), dtype=f32)
        Xs = np.ascontiguousarray(np.stack([rA[c]['s5X_out'] for c in cores]), dtype=f32)
        mB = layer_inputs(z, l, 99)
        in_maps = [dict(mB, x=xs[c], cmask=(np.arange(NCORES) < c).astype(f32)[None, :],
                        hgU_all=U, hgD_all=Dd, s5X_all=Xs) for c in cores]
        rB = run_bass_kernel_spmd(ncB, in_maps, core_ids=cores).results
        xs = [np.ascontiguousarray(rB[c]['x_out'], dtype=f32) for c in cores]
    return np.concatenate(xs, axis=0)[None].astype(f32)
```

```python
import math
from contextlib import ExitStack
import numpy as np
import concourse.bass as bass
import concourse.mybir as mybir
from concourse.bass_utils import run_bass_kernel_spmd

F32 = mybir.dt.float32
BF16 = mybir.dt.bfloat16
ALU = mybir.AluOpType
AF = mybir.ActivationFunctionType
AX = mybir.AxisListType

ENGS = ['tensor', 'vector', 'scalar', 'gpsimd', 'sync']
NDSEM = 6
NCORES = 8
T = 1024
NT = 8
D = 2048
KT = 16
ALPHA = 8.0 ** 0.25
LN_EPS = 1e-5
RMS_EPS = 1e-6
TWO_PI = 2.0 * math.pi
MAGIC = 12582912.0


class Prog:
    def __init__(self, nc):
        self.nc = nc
        self.ops = {e: [] for e in ENGS}
        self.lastw = {}
        self.readers = {}
        self.ndma = {e: 0 for e in ENGS}
        self.dma_ops = {e: [] for e in ENGS}
        self.pending = {e: [] for e in ENGS}

    def add(self, eng, fn, r=(), w=(), dma=False):
        idx = len(self.ops[eng])
        deps = list(self.pending[eng])
        self.pending[eng] = []
        for t in r:
            if t in self.lastw:
                deps.append(self.lastw[t])
        for t in w:
            if t in self.lastw:
                deps.append(self.lastw[t])
            last = {}
            for ref in self.readers.get(t, ()):
                if self.ops[ref[0]][ref[1]]['dma']:
                    deps.append(ref)
                else:
                    last[ref[0]] = ref
            deps.extend(last.values())
        op = dict(fn=fn, deps=deps, dma=dma, signal=False, eng=eng, idx=idx)
        if dma:
            j = self.ndma[eng]
            self.ndma[eng] += 1
            op['dj'] = j
            if j >= NDSEM:
                deps.append(self.dma_ops[eng][j - NDSEM])
            self.dma_ops[eng].append((eng, idx))
        self.ops[eng].append(op)
        ref = (eng, idx)
        for t in r:
            self.readers.setdefault(t, []).append(ref)
        for t in w:
            self.lastw[t] = ref
            self.readers[t] = []
        return ref

    def barrier(self):
        deps = []
        for e in ENGS:
            nonD = [i for i, o in enumerate(self.ops[e]) if not o['dma']]
            if nonD:
                deps.append((e, nonD[-1]))
            deps.extend(self.dma_ops[e][-NDSEM:])
        for e in ENGS:
            self.pending[e] = list(self.pending[e]) + deps
        self.lastw = {}
        self.readers = {}

    def mm(self, out, lhsT, rhs, start, stop, r, w):
        return self.add('tensor', lambda e: e.matmul(out, lhsT, rhs, start=start, stop=stop), r, w)

    def tr(self, out, in_, ident, r, w):
        return self.add('tensor', lambda e: e.transpose(out, in_, ident), r, w)

    def dma(self, eng, out, in_, r, w, **kw):
        return self.add(eng, lambda e: e.dma_start(out=out, in_=in_, **kw), r, w, dma=True)

    def emit(self):
        nc = self.nc
        ops = self.ops
        for e in ENGS:
            for op in ops[e]:
                nd = []
                seen = set()
                for d in op['deps']:
                    if d in seen:
                        continue
                    seen.add(d)
                    dop = ops[d[0]][d[1]]
                    if not dop['dma']:
                        if d[0] == e and e == 'tensor':
                            continue
                        dop['signal'] = True
                    nd.append(d)
                op['deps'] = nd
        for e in ENGS:
            c = 0
            for op in ops[e]:
                if op['dma']:
                    continue
                if op['signal']:
                    c += 1
                    op['semval'] = c
        with ExitStack() as st:
            esem = {e: st.enter_context(nc.semaphore('s_' + e)) for e in ENGS}
            dsem = {e: [st.enter_context(nc.semaphore('d_%s%d' % (e, i))) for i in range(NDSEM)]
                    for e in ENGS if self.ndma[e] > 0}
            block = st.enter_context(nc.Block())

            def run(e, eng):
                waited = {e2: 0 for e2 in ENGS}
                dwaited = {e2: {} for e2 in ENGS}
                for op in ops[e]:
                    for d in op['deps']:
                        dop = ops[d[0]][d[1]]
                        if dop['dma']:
                            j = dop['dj']
                            slot = j % NDSEM
                            if dwaited[d[0]].get(slot, -1) >= j:
                                continue
                            dwaited[d[0]][slot] = j
                            eng.wait_ge(dsem[d[0]][slot], 16 * (j // NDSEM + 1))
                        else:
                            v = dop['semval']
                            if waited[d[0]] >= v:
                                continue
                            waited[d[0]] = v
                            eng.wait_ge(esem[d[0]], v)
                    ins = op['fn'](eng)
                    if op['dma']:
                        ins.then_inc(dsem[e][op['dj'] % NDSEM], 16)
                    elif op['signal']:
                        ins.then_inc(esem[e], 1)
                if self.ndma[e] > 0:
                    n = self.ndma[e]
                    for i in range(NDSEM):
                        cnt = (n - i + NDSEM - 1) // NDSEM if n > i else 0
                        if cnt > 0:
                            eng.wait_ge(dsem[e][i], 16 * cnt)

            for e in ENGS:
                if ops[e]:
                    getattr(block, e)(lambda eng, e=e: run(e, eng))


def bc(ap, shape):
    return ap.to_broadcast(list(shape))


def build(dbg=False, phases=99):
    nc = bass.Bass('TRN2', target_bir_lowering=False)
    P = Prog(nc)

    def din(name, shape):
        return nc.dram_tensor(name, list(shape), F32, kind='ExternalInput').ap()

    def dout(name, shape):
        return nc.dram_tensor(name, list(shape), F32, kind='ExternalOutput').ap()

    x_in = din('x', [T, D])
    w_in = din('w_in', [D, 9216])
    lb_logits = din('lb_logits', [4, 1024])
    lmask = din('lmask', [1, 4])
    cmask = din('cmask', [1, 8])
    norm_g = din('norm_g', [1, 1024])
    if phases > 2:
        lam_re = din('lam_re', [32, 128])
        lam_im = din('lam_im', [32, 128])
        log_step = din('log_step', [32, 2])
        b_re = din('b_re', [64, 64, 16])
        b_im = din('b_im', [64, 64, 16])
        c_re = din('c_re', [1024, 64])
        c_im = din('c_im', [1024, 64])
        s5_d = din('s5_d', [8, 128])
        w_glu = din('w_glu', [1024, 2048])
        s5X_all = din('s5X_all', [NCORES, 128, 64])
        s5X_out = dout('s5X_out', [128, 64])
    if phases > 3:
        w_up_a = din('w_up_a', [1024, 2048])
        w_up_b = din('w_up_b', [1024, 2048])
        w_o = din('w_o', [2048, 2048])
        ln1_g = din('ln1_g', [1, D])
        ln1_b = din('ln1_b', [1, D])
    if phases > 4:
        w_q = din('w_q', [2048, 2048])
        keys = din('keys', [16, 128, 128])
        peer_u = din('peer_u', [16384, 2048])
        peer_v = din('peer_v', [16384, 2048])
        ln2_g = din('ln2_g', [1, D])
        ln2_b = din('ln2_b', [1, D])
        x_out = dout('x_out', [T, D])
    hgU_all = din('hgU_all', [NCORES, 8, 128, 128])
    hgD_all = din('hgD_all', [NCORES, 128, 8])

    hgU_out = dout('hgU_out', [8, 128, 128])
    hgD_out = dout('hgD_out', [128, 8])
    dbgo = {}
    if dbg:
        dbgo['oa'] = dout('dbg_oa', [T, 1024])
        dbgo['obT'] = dout('dbg_obT', [1024, T])
        dbgo['x1'] = dout('dbg_x1', [T, D])
        dbgo['yT'] = dout('dbg_yT', [1024, T])

    x1d = nc.dram_tensor('x1d', [T, D], F32).ap()
    Gd = nc.dram_tensor('Gd', [T, 16384], BF16).ap()

    sb_off = [16512]
    sb_cnt = [0]

    def sb(name, shape, dt):
        n = 1
        for v in shape[1:]:
            n *= v
        nbytes = n * (4 if dt == F32 else 2)
        off = sb_off[0]
        sb_off[0] = (off + nbytes + 63) // 64 * 64
        assert sb_off[0] <= 229000, (name, sb_off[0])
        sb_cnt[0] += 1
        return nc.alloc_sbuf_tensor_at('%s_%d' % (name, sb_cnt[0]), list(shape), dt, offset=off)

    ident_b = sb('ident_b', [128, 128], BF16)
    ident_f = sb('ident_f', [128, 128], F32)
    tri01 = sb('tri01', [128, 64], F32)
    mask01 = sb('mask01', [128, T], F32)
    psum = [nc.alloc_psum_tensor('ps%d' % i, [128, 512], F32) for i in range(8)]

    def psb(i):
        return psum[i][:].bitcast(BF16)

    P.add('gpsimd', lambda e: e.memset(ident_f[:], 1.0), w=['ident_f'])
    P.add('gpsimd', lambda e: e.affine_select(out=ident_f[:], in_=ident_f[:], pattern=[[-1, 128]], compare_op=ALU.is_equal,
                                              fill=0.0, base=0, channel_multiplier=1), r=['ident_f'], w=['ident_f'])
    P.add('vector', lambda e: e.tensor_copy(ident_b[:], ident_f[:]), r=['ident_f'], w=['ident_b'])
    triA = sb('triA', [128, 64], F32)
    triB = sb('triB', [128, 64], F32)
    P.add('gpsimd', lambda e: e.memset(triA[:], 1.0), w=['triA'])
    P.add('gpsimd', lambda e: e.memset(triB[:], 1.0), w=['triB'])
    P.add('gpsimd', lambda e: e.affine_select(out=triA[:], in_=triA[:], pattern=[[1, 64]], compare_op=ALU.is_ge,
                                              fill=0.0, base=0, channel_multiplier=-1), r=['triA'], w=['triA'])
    P.add('gpsimd', lambda e: e.affine_select(out=triB[:], in_=triB[:], pattern=[[1, 64]], compare_op=ALU.is_ge,
                                              fill=0.0, base=64, channel_multiplier=-1), r=['triB'], w=['triB'])
    P.add('vector', lambda e: e.tensor_copy(tri01[0:64, :], triA[0:64, :]), r=['triA'], w=['tri01'])
    P.add('vector', lambda e: e.tensor_copy(tri01[64:128, :], triB[64:128, :]), r=['triB', 'tri01'], w=['tri01'])
    P.add('gpsimd', lambda e: e.memset(mask01[:], 1.0), w=['mask01'])
    P.add('gpsimd', lambda e: e.memset(mask01[:].rearrange('p (c s) -> p c s', s=64)[:, :, 0:1], 0.0), r=['mask01'], w=['mask01'])

    xT = sb('xT', [128, KT, T], BF16)
    reg_oa = sb_off[0]
    oa_tok = sb('oa_tok', [128, NT, 1024], BF16)
    obT = sb('obT', [128, 8, T], BF16)
    lbl = sb('lbl', [128, 4, 8], F32)
    lbe = sb('lbe', [128, 4, 8], F32)
    lbs = sb('lbs', [128, 8], F32)
    lb = sb('lb', [128, 8], F32)
    oml = sb('oml', [128, 8], F32)
    lmk = sb('lmk', [128, 4], F32)
    cmk = sb('cmk', [128, 8], F32)
    mark_persist = sb_off[0]
    xload = [sb('xload%d' % i, [128, D], F32) for i in range(2)]
    xbf = [sb('xbf%d' % i, [128, D], BF16) for i in range(2)]

    def to_featmajor(src_dram, dstT, tokpref):
        for n in range(NT):
            xl = xload[n % 2]; xb = xbf[n % 2]
            tl = 'xload%d' % (n % 2); tb = 'xbf%d' % (n % 2)
            P.dma('sync', xl[:], src_dram[n * 128:(n + 1) * 128, :], r=[], w=[tl])
            P.add('scalar', lambda e, xl=xl, xb=xb: e.copy(xb[:], xl[:]), r=[tl], w=[tb])
            for half in range(2):
                bank = half
                for j in range(8):
                    kt = half * 8 + j
                    P.tr(psb(bank)[:, j * 128:(j + 1) * 128], xb[:, kt * 128:(kt + 1) * 128], ident_b[:],
                         r=[tb, 'ident_b'], w=['ps%d' % bank])
                P.add('vector', lambda e, bank=bank, half=half, n=n: e.tensor_copy(
                    dstT[:, half * 8:(half + 1) * 8, n * 128:(n + 1) * 128],
                    psb(bank).rearrange('p (j t) -> p j t', j=8)), r=['ps%d' % bank], w=[tokpref])

    to_featmajor(x_in, xT, 'xT')

    P.dma('sync', lbl[:], lb_logits.rearrange('l (h d) -> d l h', d=128), r=[], w=['lbl'], allow_slow_non_contiguous=True)
    P.dma('sync', lmk[:], lmask[0].partition_broadcast(128), r=[], w=['lmk'])
    P.dma('sync', cmk[:], cmask[0].partition_broadcast(128), r=[], w=['cmk'])
    P.add('scalar', lambda e: e.activation(lbe[:], lbl[:], AF.Exp), r=['lbl'], w=['lbe'])
    P.add('vector', lambda e: e.tensor_reduce(lbs[:], lbe[:].rearrange('p l h -> p h l'), AX.X, ALU.add), r=['lbe'], w=['lbs'])
    P.add('vector', lambda e: e.reciprocal(lbs[:], lbs[:]), r=['lbs'], w=['lbs'])
    P.add('vector', lambda e: e.tensor_tensor(lbe[:], lbe[:], bc(lmk[:].unsqueeze(2), [128, 4, 8]), ALU.mult), r=['lbe', 'lmk'], w=['lbe'])
    P.add('vector', lambda e: e.tensor_reduce(lb[:], lbe[:].rearrange('p l h -> p h l'), AX.X, ALU.add), r=['lbe'], w=['lb'])
    P.add('vector', lambda e: e.tensor_tensor(lb[:], lb[:], lbs[:], ALU.mult), r=['lb', 'lbs'], w=['lb'])
    P.add('vector', lambda e: e.tensor_scalar(oml[:], lb[:], -1.0, 1.0, ALU.mult, ALU.add), r=['lb'], w=['oml'])

    v_tok = sb('v_tok', [128, NT, 1024], BF16)
    sg_tok = sb('sg_tok', [128, NT, 1024], BF16)
    wblk = [sb('wblk%d' % i, [128, KT, 512], BF16) for i in range(2)]
    wcnt = [0]

    def load_wblk(src, c0, ncols=512, kt=KT):
        i = wcnt[0] % 2
        wcnt[0] += 1
        P.dma('gpsimd', wblk[i][:, 0:kt, 0:ncols], src[:, c0:c0 + ncols].rearrange('(kt p) n -> p kt n', p=128),
              r=[], w=['wblk%d' % i])
        return wblk[i], 'wblk%d' % i

    pscnt = [0]

    def nextbank(lo=0, hi=8):
        b = lo + pscnt[0] % (hi - lo)
        pscnt[0] += 1
        return b

    for blk in range(4):
        wb, wt = load_wblk(w_in, 2048 + blk * 512)
        for n in range(NT):
            bank = nextbank()
            for kt in range(KT):
                P.mm(psum[bank][:, :], xT[:, kt, n * 128:(n + 1) * 128], wb[:, kt, :], kt == 0, kt == KT - 1,
                     r=['xT', wt], w=['ps%d' % bank])
            if blk < 2:
                P.add('vector', lambda e, bank=bank, n=n, blk=blk: e.tensor_copy(v_tok[:, n, blk * 512:(blk + 1) * 512], psum[bank][:, :]),
                      r=['ps%d' % bank], w=['v_tok%d' % n])
            else:
                P.add('scalar', lambda e, bank=bank, n=n, blk=blk: e.activation(sg_tok[:, n, (blk - 2) * 512:(blk - 1) * 512], psum[bank][:, :], AF.Sigmoid),
                      r=['ps%d' % bank], w=['sg_tok%d' % n])

    ngb = sb('ngb', [128, 1024], F32)
    P.dma('sync', ngb[:], norm_g[0].partition_broadcast(128), r=[], w=['ngb'])
    Dp = sb('Dp', [128, NCORES, 8], F32)
    P.dma('sync', Dp[:], hgD_all.rearrange('c d h -> d c h'), r=[], w=['Dp'])
    P.add('vector', lambda e: e.tensor_scalar(Dp[:], Dp[:], -1.0, None, ALU.add), r=['Dp'], w=['Dp'])
    P.add('vector', lambda e: e.tensor_tensor(Dp[:], Dp[:], bc(cmk[:].unsqueeze(2), [128, NCORES, 8]), ALU.mult), r=['Dp', 'cmk'], w=['Dp'])
    P.add('vector', lambda e: e.tensor_scalar(Dp[:], Dp[:], 1.0, None, ALU.add), r=['Dp'], w=['Dp'])

    qT = sb('qT', [128, T], F32)
    fT = sb('fT', [128, T], F32)
    lgf = sb('lgf', [128, T], F32)
    kT_ = sb('kT_', [128, T], F32)
    bT = sb('bT', [128, T], F32)
    eq = sb('eq', [128, T], F32)
    ek = sb('ek', [128, T], F32)
    qt_b = sb('qt_b', [128, T], BF16)
    kt_b = sb('kt_b', [128, T], BF16)
    ktok = sb('ktok', [128, NT, 128], BF16)
    rr = sb('rr', [128, 16], F32)
    er = sb('er', [128, 16], F32)
    e2r = sb('e2r', [128, 16], F32)
    bsum = sb('bsum', [128, 1], F32)
    Dtot = sb('Dtot', [128, 8], F32)
    S = sb('S', [128, 128], F32)
    Uld = [sb('Uld%d' % i, [128, 128], F32) for i in range(2)]
    Spb = [sb('Spb%d' % i, [128, 128], BF16) for i in range(4)]
    Utmp = sb('Utmp', [128, 128], F32)
    att_sb = [sb('att_sb%d' % i, [128, 128], BF16) for i in range(2)]
    ss = sb('ss', [128, 2], F32)
    junk = sb('junk', [128, 128], F32)
    otmp = sb('otmp', [128, 128], F32)
    for i in range(2):
        P.add('gpsimd', lambda e, i=i: e.memset(att_sb[i][:], 0.0), w=['att_sb%d' % i])

    for h in range(8):
        P.add('gpsimd', lambda e: e.memset(S[:], 0.0), w=['S'])
        for j in range(NCORES - 1):
            ul = Uld[j % 2]; ut = 'Uld%d' % (j % 2)
            P.dma('sync', ul[:], hgU_all[j, h], r=[], w=[ut])
            P.add('vector', lambda e, ul=ul, j=j: e.tensor_scalar(ul[:], ul[:], cmk[:, j:j + 1], None, ALU.mult), r=[ut, 'cmk'], w=[ut])
            P.add('vector', lambda e, ul=ul, j=j, h=h: e.scalar_tensor_tensor(S[:], S[:], Dp[:, j, h:h + 1], ul[:], ALU.mult, ALU.add),
                  r=['S', 'Dp', ut], w=['S'])
        i = wcnt[0] % 2
        wcnt[0] += 1
        wb = wblk[i]; wt = 'wblk%d' % i
        P.dma('gpsimd', wb[:, :, 0:128], w_in[:, h * 128:(h + 1) * 128].rearrange('(kt p) n -> p kt n', p=128), r=[], w=[wt])
        P.dma('gpsimd', wb[:, :, 128:256], w_in[:, 1024 + h * 128:1024 + (h + 1) * 128].rearrange('(kt p) n -> p kt n', p=128), r=[], w=[wt])
        for which in range(2):
            for half in range(2):
                bank = nextbank()
                for kt in range(KT):
                    P.mm(psum[bank][:, :], wb[:, kt, which * 128:(which + 1) * 128], xT[:, kt, half * 512:(half + 1) * 512],
                         kt == 0, kt == KT - 1, r=['xT', wt], w=['ps%d' % bank])
                if which == 0:
                    P.add('vector', lambda e, bank=bank, half=half: e.tensor_copy(qT[:, half * 512:(half + 1) * 512], psum[bank][:, :]),
                          r=['ps%d' % bank], w=['qT'])
                else:
                    P.add('scalar', lambda e, bank=bank, half=half: e.activation(fT[:, half * 512:(half + 1) * 512], psum[bank][:, :], AF.Sigmoid),
                          r=['ps%d' % bank], w=['fT'])
        P.add('vector', lambda e, h=h: e.tensor_scalar(fT[:], fT[:], oml[:, h:h + 1], lb[:, h:h + 1], ALU.mult, ALU.add), r=['fT', 'oml', 'lb'], w=['fT'])
        P.add('scalar', lambda e: e.activation(lgf[:], fT[:], AF.Ln), r=['fT'], w=['lgf'])
        P.add('gpsimd', lambda e: e.tensor_scalar(kT_[:], fT[:], -1.0, 1.0, ALU.mult, ALU.add), r=['fT'], w=['kT_'])
        P.add('vector', lambda e: e.tensor_tensor_scan(bT[:], mask01[:], lgf[:], 0.0, ALU.mult, ALU.add), r=['mask01', 'lgf'], w=['bT'])
        b3 = bT[:].rearrange('p (c s) -> p c s', s=64)
        P.add('vector', lambda e: e.tensor_scalar(rr[:], b3[:, :, 63], 0.5, None, ALU.mult), r=['bT'], w=['rr'])
        P.add('vector', lambda e: e.tensor_reduce(bsum[:], b3[:, :, 63], AX.X, ALU.add), r=['bT'], w=['bsum'])
        P.add('scalar', lambda e, h=h: e.activation(Dtot[:, h:h + 1], bsum[:], AF.Exp), r=['bsum'], w=['Dtot'])
        P.add('scalar', lambda e: e.activation(er[:], rr[:], AF.Exp), r=['rr'], w=['er'])
        P.add('vector', lambda e: e.tensor_tensor(e2r[:], er[:], er[:], ALU.mult), r=['er'], w=['e2r'])
        P.add('vector', lambda e: e.tensor_tensor(b3, b3, bc(rr[:].unsqueeze(2), [128, 16, 64]), ALU.subtract), r=['bT', 'rr'], w=['bT'])
        P.add('scalar', lambda e: e.activation(eq[:], bT[:], AF.Exp), r=['bT'], w=['eq'])
        P.add('scalar', lambda e: e.activation(ek[:], bT[:], AF.Exp, scale=-1.0), r=['bT'], w=['ek'])
        P.add('vector', lambda e: e.tensor_tensor(qt_b[:], qT[:], eq[:], ALU.mult), r=['qT', 'eq'], w=['qt_b'])
        P.add('gpsimd', lambda e: e.tensor_tensor(kt_b[:], kT_[:], ek[:], ALU.mult), r=['kT_', 'ek'], w=['kt_b'])
        bank = nextbank()
        for n in range(NT):
            P.tr(psb(bank)[:, n * 128:(n + 1) * 128], kt_b[:, n * 128:(n + 1) * 128], ident_b[:], r=['kt_b', 'ident_b'], w=['ps%d' % bank])
        P.add('vector', lambda e, bank=bank: e.tensor_copy(ktok[:], psb(bank).rearrange('p (n d) -> p n d', n=NT)), r=['ps%d' % bank], w=['ktok'])
        for n in range(NT):
            asb = att_sb[n % 2]; at = 'att_sb%d' % (n % 2)
            b_att = nextbank(); b_U = nextbank(); b_o = nextbank()
            for cc in range(2):
                c = 2 * n + cc
                tok = slice(c * 64, (c + 1) * 64)
                pr = slice(cc * 64, (cc + 1) * 64)
                P.mm(psum[b_att][pr, cc * 64:(cc + 1) * 64], kt_b[:, tok], qt_b[:, tok], True, True, r=['kt_b', 'qt_b'], w=['ps%d' % b_att])
                P.add('vector', lambda e, pr=pr, cc=cc, asb=asb, b_att=b_att: e.tensor_tensor(
                    asb[pr, cc * 64:(cc + 1) * 64], psum[b_att][pr, cc * 64:(cc + 1) * 64], tri01[pr, :], ALU.mult),
                    r=['ps%d' % b_att, 'tri01'], w=[at])
                P.mm(psum[b_U][:, cc * 128:(cc + 1) * 128], ktok[pr, n, :], v_tok[pr, n, h * 128:(h + 1) * 128], True, True,
                     r=['ktok', 'v_tok%d' % n], w=['ps%d' % b_U])
            P.mm(psum[b_o][:, 0:128], asb[:, :], v_tok[:, n, h * 128:(h + 1) * 128], True, False, r=[at, 'v_tok%d' % n], w=['ps%d' % b_o])
            for cc in range(2):
                c = 2 * n + cc
                tok = slice(c * 64, (c + 1) * 64)
                pr = slice(cc * 64, (cc + 1) * 64)
                sp = Spb[c % 4]; spt = 'Spb%d' % (c % 4)
                P.add('vector', lambda e, sp=sp, c=c: e.tensor_scalar(sp[:], S[:], er[:, c:c + 1], None, ALU.mult), r=['S', 'er'], w=[spt])
                P.mm(psum[b_o][pr, 0:128], qt_b[:, tok], sp[:, :], False, cc == 1, r=['qt_b', spt], w=['ps%d' % b_o])
                P.add('vector', lambda e, c=c, cc=cc, b_U=b_U: e.tensor_scalar(Utmp[:], psum[b_U][:, cc * 128:(cc + 1) * 128], er[:, c:c + 1], None, ALU.mult),
                      r=['ps%d' % b_U, 'er'], w=['Utmp'])
                P.add('vector', lambda e, c=c: e.scalar_tensor_tensor(S[:], S[:], e2r[:, c:c + 1], Utmp[:], ALU.mult, ALU.add),
                      r=['S', 'e2r', 'Utmp'], w=['S'])
            P.add('gpsimd', lambda e: e.memset(ss[:], 0.0), w=['ss'])
            P.add('scalar', lambda e, b_o=b_o: e.activation(junk[:], psum[b_o][:, 0:128], AF.Square, accum_out=ss[:, 0:1]), r=['ps%d' % b_o, 'ss'], w=['junk', 'ss'])
            P.add('scalar', lambda e: e.activation(ss[:, 1:2], ss[:, 0:1], AF.Ln, bias=RMS_EPS, scale=1.0 / 128.0), r=['ss'], w=['ss'])
            P.add('scalar', lambda e: e.activation(ss[:, 1:2], ss[:, 1:2], AF.Exp, scale=-0.5), r=['ss'], w=['ss'])
            P.add('vector', lambda e, b_o=b_o, h=h: e.scalar_tensor_tensor(otmp[:], psum[b_o][:, 0:128], ss[:, 1:2], ngb[:, h * 128:(h + 1) * 128], ALU.mult, ALU.mult),
                  r=['ps%d' % b_o, 'ss', 'ngb'], w=['otmp'])
            P.add('gpsimd', lambda e, n=n, h=h: e.tensor_tensor(oa_tok[:, n, h * 128:(h + 1) * 128], otmp[:], sg_tok[:, n, h * 128:(h + 1) * 128], ALU.mult),
                  r=['otmp', 'sg_tok%d' % n], w=['oa_tok%d' % n])
        P.dma('sync', hgU_out[h], S[:], r=['S'], w=['hgU_out'])
    P.dma('sync', hgD_out, Dtot[:], r=['Dtot'], w=['hgD_out'])
    if dbg:
        for n in range(NT):
            P.add('vector', lambda e, n=n: e.tensor_copy(xload[n % 2][:, 0:1024], oa_tok[:, n, :]), r=['oa_tok%d' % n], w=['xload%d' % (n % 2)])
            P.dma('sync', dbgo['oa'][n * 128:(n + 1) * 128, :], xload[n % 2][:, 0:1024], r=['xload%d' % (n % 2)], w=['dbg_oa'])
    if phases <= 2:
        P.emit()
        return nc

    P.barrier()
    sb_off[0] = mark_persist
    wblk = [sb('wblk%d' % i, [128, KT, 512], BF16) for i in range(2)]
    uT = sb('uT', [128, 8, T], F32)
    yT = sb('yT', [128, 8, T], BF16)
    Xr = [sb('Xr%d' % i, [128, T], F32) for i in range(2)]
    Xi = [sb('Xi%d' % i, [128, T], F32) for i in range(2)]
    ytmp = sb('ytmp', [128, 512], F32)
    sgm = sb('sgm', [128, 512], F32)
    pl = sb('pl', [32, 3, 128], F32)
    lst2 = sb('lst2', [32, 2], F32)
    sp = {}
    for nm in ['lamr', 'lami', 'lst', 'lr', 'dt', 'lrdt', 'lidt', 'mag', 't1', 't2', 'sn', 'cs', 'ar', 'ai', 'den', 'nr',
               'zr', 'zi', 'xinr', 'xini', 'axr', 'axi', 'Ar', 'Ai', 'tA', 'tB', 'tC', 'tD']:
        sp[nm] = sb('s5_' + nm, [128, 32], F32)
    pwr = sb('pwr', [128, 11, 32], F32)
    pwi = sb('pwi', [128, 11, 32], F32)
    npwi = sb('npwi', [128, 11, 32], F32)
    bre = sb('bre', [128, 32, 16], F32)
    bim = sb('bim', [128, 32, 16], F32)
    bbr = sb('bbr', [128, 32, 16], F32)
    bbi = sb('bbi', [128, 32, 16], F32)
    btmp = sb('btmp', [128, 32, 16], F32)
    Cld = [sb('Cld%d' % i, [128, 8, 64], F32) for i in range(2)]
    Cdup = sb('Cdup', [128, 2, 64], F32)
    dld = sb('dld', [8, 128], F32)
    dvec = sb('dvec', [128, 8], F32)
    Xe_ld = [sb('Xe_ld%d' % i, [128, 64], F32) for i in range(2)]
    Xend = sb('Xend', [128, 64], F32)
    srcB = [sb('srcB%d' % j, [128, 128], F32) for j in range(4)]
    LBr = [sb('LBr%d' % j, [128, 128], F32) for j in range(4)]
    LBi = [sb('LBi%d' % j, [128, 128], F32) for j in range(4)]
    LCr = [sb('LCr%d' % j, [128, 128], F32) for j in range(4)]
    LCi = [sb('LCi%d' % j, [128, 128], F32) for j in range(4)]
    for j in range(4):
        for tl, nm in ((srcB, 'srcB'), (LCr, 'LCr'), (LCi, 'LCi')):
            P.add('gpsimd', lambda e, t=tl[j]: e.memset(t[:], 0.0), w=['%s%d' % (nm, j)])

    V = lambda eng, fn, r, w: P.add(eng, fn, r=r, w=w)

    def tt(out, a, b, op, r, w, eng='vector'):
        P.add(eng, lambda e: e.tensor_tensor(out, a, b, op), r=r, w=w)

    def ts(out, a, s1, s2, op0, op1, r, w, eng='vector'):
        if op1 is None:
            P.add(eng, lambda e: e.tensor_scalar(out, a, s1, None, op0), r=r, w=w)
        else:
            P.add(eng, lambda e: e.tensor_scalar(out, a, s1, s2, op0, op1), r=r, w=w)

    def stt(out, a, sc, b, op0, op1, r, w, eng='vector'):
        P.add(eng, lambda e: e.scalar_tensor_tensor(out, a, sc, b, op0, op1), r=r, w=w)

    def act(out, a, func, r, w, **kw):
        P.add('scalar', lambda e: e.activation(out, a, func, **kw), r=r, w=w)

    P.dma('sync', pl[:, 0, :], lam_re, r=[], w=['pl'])
    P.dma('sync', pl[:, 1, :], lam_im, r=[], w=['pl'])
    P.dma('sync', lst2[:], log_step, r=[], w=['lst2'])
    P.add('vector', lambda e: e.tensor_copy(pl[:, 2, :].rearrange('p (a b) -> p a b', a=2), bc(lst2[:].unsqueeze(2), [32, 2, 64])), r=['lst2', 'pl'], w=['pl'])
    for i, nm in enumerate(['lamr', 'lami', 'lst']):
        P.tr(psum[4][:, i * 32:(i + 1) * 32], pl[:, i, :], ident_f[0:32, 0:32], r=['pl', 'ident_f'], w=['ps4'])
        P.add('vector', lambda e, i=i, nm=nm: e.tensor_copy(sp[nm][:], psum[4][:, i * 32:(i + 1) * 32]), r=['ps4'], w=[nm])
    P.dma('sync', dld[:], s5_d, r=[], w=['dld'])
    P.tr(psum[4][:, 128:136], dld[:], ident_f[0:8, 0:8], r=['dld', 'ident_f'], w=['ps4'])
    P.add('vector', lambda e: e.tensor_copy(dvec[:], psum[4][:, 128:136]), r=['ps4'], w=['dvec'])
    P.dma('sync', bre[:], b_re.rearrange('(gp g2) p n -> (g2 p) gp n', g2=2), r=[], w=['bre'])
    P.dma('sync', bim[:], b_im.rearrange('(gp g2) p n -> (g2 p) gp n', g2=2), r=[], w=['bim'])
    P.dma('sync', Cld[0][:], c_re.rearrange('(ct q) p -> q ct p', q=128), r=[], w=['Cld0'])
    P.dma('sync', Cld[1][:], c_im.rearrange('(ct q) p -> q ct p', q=128), r=[], w=['Cld1'])

    A = lambda nm: sp[nm][:]
    ts(A('lr'), A('lamr'), -1e-4, None, ALU.min, None, ['lamr'], ['lr'])
    act(A('dt'), A('lst'), AF.Exp, ['lst'], ['dt'])
    tt(A('lrdt'), A('lr'), A('dt'), ALU.mult, ['lr', 'dt'], ['lrdt'])
    tt(A('lidt'), A('lami'), A('dt'), ALU.mult, ['lami', 'dt'], ['lidt'])
    act(A('mag'), A('lrdt'), AF.Exp, ['lrdt'], ['mag'])
    ts(A('t1'), A('lidt'), 1.0 / TWO_PI, MAGIC, ALU.mult, ALU.add, ['lidt'], ['t1'])
    ts(A('t1'), A('t1'), -MAGIC, None, ALU.add, None, ['t1'], ['t1'])
    stt(A('t1'), A('t1'), -TWO_PI, A('lidt'), ALU.mult, ALU.add, ['t1', 'lidt'], ['t1'])
    act(A('sn'), A('t1'), AF.Sin, ['t1'], ['sn'])
    ts(A('t2'), A('lidt'), math.pi / 2, None, ALU.add, None, ['lidt'], ['t2'])
    ts(A('tA'), A('t2'), 1.0 / TWO_PI, MAGIC, ALU.mult, ALU.add, ['t2'], ['tA'])
    ts(A('tA'), A('tA'), -MAGIC, None, ALU.add, None, ['tA'], ['tA'])
    stt(A('tA'), A('tA'), -TWO_PI, A('t2'), ALU.mult, ALU.add, ['tA', 't2'], ['tA'])
    act(A('cs'), A('tA'), AF.Sin, ['tA'], ['cs'])
    tt(A('ar'), A('mag'), A('cs'), ALU.mult, ['mag', 'cs'], ['ar'])
    tt(A('ai'), A('mag'), A('sn'), ALU.mult, ['mag', 'sn'], ['ai'])
    tt(A('den'), A('lr'), A('lr'), ALU.mult, ['lr'], ['den'])
    tt(A('tB'), A('lami'), A('lami'), ALU.mult, ['lami'], ['tB'])
    tt(A('den'), A('den'), A('tB'), ALU.add, ['den', 'tB'], ['den'])
    P.add('vector', lambda e: e.reciprocal(A('den'), A('den')), r=['den'], w=['den'])
    ts(A('nr'), A('ar'), -1.0, None, ALU.add, None, ['ar'], ['nr'])
    tt(A('tB'), A('nr'), A('lr'), ALU.mult, ['nr', 'lr'], ['tB'])
    tt(A('tC'), A('ai'), A('lami'), ALU.mult, ['ai', 'lami'], ['tC'])
    tt(A('tB'), A('tB'), A('tC'), ALU.add, ['tB', 'tC'], ['tB'])
    tt(A('zr'), A('tB'), A('den'), ALU.mult, ['tB', 'den'], ['zr'])
    tt(A('tB'), A('ai'), A('lr'), ALU.mult, ['ai', 'lr'], ['tB'])
    tt(A('tC'), A('nr'), A('lami'), ALU.mult, ['nr', 'lami'], ['tC'])
    tt(A('tB'), A('tB'), A('tC'), ALU.subtract, ['tB', 'tC'], ['tB'])
    tt(A('zi'), A('tB'), A('den'), ALU.mult, ['tB', 'den'], ['zi'])
    P.add('vector', lambda e: e.tensor_copy(pwr[:, 0, :], A('ar')), r=['ar'], w=['pwr'])
    P.add('vector', lambda e: e.tensor_copy(pwi[:, 0, :], A('ai')), r=['ai'], w=['pwi'])
    for k in range(10):
        tt(A('tB'), pwr[:, k, :], pwr[:, k, :], ALU.mult, ['pwr'], ['tB'])
        tt(A('tC'), pwi[:, k, :], pwi[:, k, :], ALU.mult, ['pwi'], ['tC'])
        stt(pwi[:, k + 1, :], pwr[:, k, :], 2.0, pwi[:, k, :], ALU.mult, ALU.mult, ['pwr', 'pwi'], ['pwi'])
        tt(pwr[:, k + 1, :], A('tB'), A('tC'), ALU.subtract, ['tB', 'tC'], ['pwr'])
    ts(npwi[:], pwi[:], -1.0, None, ALU.mult, None, ['pwi'], ['npwi'])
    P.add('gpsimd', lambda e: e.memset(A('xinr'), 0.0), w=['xinr'])
    P.add('gpsimd', lambda e: e.memset(A('xini'), 0.0), w=['xini'])
    for j in range(NCORES - 1):
        xl = Xe_ld[j % 2]; xt_ = 'Xe_ld%d' % (j % 2)
        P.dma('sync', xl[:], s5X_all[j], r=[], w=[xt_])
        m = cmk[:, j:j + 1]
        ts(A('Ar'), pwr[:, 10, :], -1.0, None, ALU.add, None, ['pwr'], ['Ar'])
        ts(A('Ar'), A('Ar'), m, 1.0, ALU.mult, ALU.add, ['Ar', 'cmk'], ['Ar'])
        ts(A('Ai'), pwi[:, 10, :], m, None, ALU.mult, None, ['pwi', 'cmk'], ['Ai'])
        ts(xl[:], xl[:], m, None, ALU.mult, None, [xt_, 'cmk'], [xt_])
        tt(A('tA'), A('Ar'), A('xinr'), ALU.mult, ['Ar', 'xinr'], ['tA'])
        tt(A('tB'), A('Ai'), A('xini'), ALU.mult, ['Ai', 'xini'], ['tB'])
        tt(A('tC'), A('Ar'), A('xini'), ALU.mult, ['Ar', 'xini'], ['tC'])
        tt(A('tD'), A('Ai'), A('xinr'), ALU.mult, ['Ai', 'xinr'], ['tD'])
        tt(A('tA'), A('tA'), A('tB'), ALU.subtract, ['tA', 'tB'], ['tA'])
        tt(A('tC'), A('tC'), A('tD'), ALU.add, ['tC', 'tD'], ['tC'])
        tt(A('xinr'), A('tA'), xl[:, 0:32], ALU.add, ['tA', xt_], ['xinr'])
        tt(A('xini'), A('tC'), xl[:, 32:64], ALU.add, ['tC', xt_], ['xini'])
    tt(A('tA'), A('ar'), A('xinr'), ALU.mult, ['ar', 'xinr'], ['tA'])
    tt(A('tB'), A('ai'), A('xini'), ALU.mult, ['ai', 'xini'], ['tB'])
    tt(A('axr'), A('tA'), A('tB'), ALU.subtract, ['tA', 'tB'], ['axr'])
    tt(A('tA'), A('ar'), A('xini'), ALU.mult, ['ar', 'xini'], ['tA'])
    tt(A('tB'), A('ai'), A('xinr'), ALU.mult, ['ai', 'xinr'], ['tB'])
    tt(A('axi'), A('tA'), A('tB'), ALU.add, ['tA', 'tB'], ['axi'])
    zrb = bc(sp['zr'][:].unsqueeze(2), [128, 32, 16])
    zib = bc(sp['zi'][:].unsqueeze(2), [128, 32, 16])
    tt(bbr[:], bre[:], zrb, ALU.mult, ['bre', 'zr'], ['bbr'])
    tt(btmp[:], bim[:], zib, ALU.mult, ['bim', 'zi'], ['btmp'])
    tt(bbr[:], bbr[:], btmp[:], ALU.subtract, ['bbr', 'btmp'], ['bbr'])
    tt(bbi[:], bim[:], zrb, ALU.mult, ['bim', 'zr'], ['bbi'])
    tt(btmp[:], bre[:], zib, ALU.mult, ['bre', 'zi'], ['btmp'])
    tt(bbi[:], bbi[:], btmp[:], ALU.add, ['bbi', 'btmp'], ['bbi'])

    for blk in range(2):
        wb, wt = load_wblk(w_in, 4096 + blk * 512)
        for ntl in range(4):
            ct = blk * 4 + ntl
            for half in range(2):
                bank = nextbank(0, 4)
                for kt in range(KT):
                    P.mm(psum[bank][:, :], wb[:, kt, ntl * 128:(ntl + 1) * 128], xT[:, kt, half * 512:(half + 1) * 512],
                         kt == 0, kt == KT - 1, r=['xT', wt], w=['ps%d' % bank])
                P.add('scalar', lambda e, bank=bank, ct=ct, half=half: e.copy(uT[:, ct, half * 512:(half + 1) * 512], psum[bank][:, :]),
                      r=['ps%d' % bank], w=['uT%d' % ct])

    P.barrier()
    wb1f = wblk[1][:].rearrange('p a b -> p (a b)').bitcast(F32)
    Ms = [(wb1f[:, 0:T], wb1f[:, T:2 * T]), (wb1f[:, 2 * T:3 * T], wb1f[:, 3 * T:4 * T])]
    wb0f = wblk[0][:].rearrange('p a b -> p (a b)').bitcast(F32)
    Xr2 = [wb0f[:, 0:T], wb0f[:, T:2 * T]]
    Xi2 = [wb0f[:, 2 * T:3 * T], wb0f[:, 3 * T:4 * T]]
    XRs = [Xr, Xr2]
    XIs = [Xi, Xi2]
    for ct in range(8):
        for j in range(4):
            gp = 4 * ct + j
            for (bb, LB, nm) in ((bbr, LBr, 'LBr'), (bbi, LBi, 'LBi')):
                for g2 in range(2):
                    pr = slice(g2 * 64, (g2 + 1) * 64)
                    c0 = (2 * j + g2) * 16
                    P.add('vector', lambda e, bb=bb, pr=pr, c0=c0, j=j, gp=gp: e.tensor_copy(srcB[j][pr, c0:c0 + 16], bb[pr, gp, :]),
                          r=['bbr', 'bbi'], w=['srcB%d' % j])
                P.tr(psum[4][:, 0:128], srcB[j][:], ident_f[:], r=['srcB%d' % j, 'ident_f'], w=['ps4'])
                P.add('vector', lambda e, LB=LB, j=j: e.tensor_copy(LB[j][:], psum[4][:, 0:128]), r=['ps4'], w=['%s%d' % (nm, j)])
        for ri, (LC, nm) in enumerate(((LCr, 'LCr'), (LCi, 'LCi'))):
            P.add('vector', lambda e, ri=ri, ct=ct: e.tensor_copy(Cdup[:], bc(Cld[ri][:, ct, :].unsqueeze(1), [128, 2, 64])), r=['Cld%d' % ri], w=['Cdup'])
            P.tr(psum[5][:, 0:128], Cdup[:].rearrange('p a b -> p (a b)'), ident_f[:], r=['Cdup', 'ident_f'], w=['ps5'])
            for j in range(4):
                for g2 in range(2):
                    pr = slice(g2 * 64, (g2 + 1) * 64)
                    c0 = (2 * j + g2) * 16
                    if ri == 0:
                        P.add('vector', lambda e, LC=LC, j=j, pr=pr, c0=c0: e.tensor_copy(LC[j][pr, c0:c0 + 16], psum[5][pr, c0:c0 + 16]),
                              r=['ps5'], w=['%s%d' % (nm, j)])
                    else:
                        P.add('vector', lambda e, LC=LC, j=j, pr=pr, c0=c0: e.tensor_scalar(LC[j][pr, c0:c0 + 16], psum[5][pr, c0:c0 + 16], -1.0, None, ALU.mult),
                              r=['ps5'], w=['%s%d' % (nm, j)])
        for jp in range(2):
            js = (2 * jp, 2 * jp + 1)
            for si, j in enumerate(js):
                gp = 4 * ct + j
                XR, XI = XRs[si], XIs[si]
                t0r, t0i = 'Xr%d_0' % si, 'Xi%d_0' % si
                for half in range(2):
                    hs = slice(half * 512, (half + 1) * 512)
                    b1 = nextbank(0, 4)
                    P.mm(psum[b1][:, :], LBr[j][:], uT[:, ct, hs], True, True, r=['LBr%d' % j, 'uT%d' % ct], w=['ps%d' % b1])
                    P.add('scalar', lambda e, b1=b1, hs=hs, XR=XR: e.copy(XR[0][:, hs], psum[b1][:, :]), r=['ps%d' % b1], w=[t0r])
                    b2 = nextbank(0, 4)
                    P.mm(psum[b2][:, :], LBi[j][:], uT[:, ct, hs], True, True, r=['LBi%d' % j, 'uT%d' % ct], w=['ps%d' % b2])
                    P.add('scalar', lambda e, b2=b2, hs=hs, XI=XI: e.copy(XI[0][:, hs], psum[b2][:, :]), r=['ps%d' % b2], w=[t0i])
                tt(XR[0][:, 0:1], XR[0][:, 0:1], sp['axr'][:, gp:gp + 1], ALU.add, [t0r, 'axr'], [t0r])
                tt(XI[0][:, 0:1], XI[0][:, 0:1], sp['axi'][:, gp:gp + 1], ALU.add, [t0i, 'axi'], [t0i], eng='gpsimd')
            for k in range(10):
                d = 1 << k
                s_, d_ = k % 2, 1 - k % 2
                for si, j in enumerate(js):
                    gp = 4 * ct + j
                    XR, XI = XRs[si], XIs[si]
                    mA, mB = Ms[si]
                    tmA, tmB = 'mA%d' % si, 'mB%d' % si
                    pr_ = pwr[:, k, gp:gp + 1]; pi_ = pwi[:, k, gp:gp + 1]; npi_ = npwi[:, k, gp:gp + 1]
                    xr_s, xi_s, xr_d, xi_d = XR[s_], XI[s_], XR[d_], XI[d_]
                    tr_s, ti_s, tr_d, ti_d = 'Xr%d_%d' % (si, s_), 'Xi%d_%d' % (si, s_), 'Xr%d_%d' % (si, d_), 'Xi%d_%d' % (si, d_)
                    P.add('vector', lambda e, xr_d=xr_d, xr_s=xr_s, d=d: e.tensor_copy(xr_d[:, 0:d], xr_s[:, 0:d]), r=[tr_s], w=[tr_d])
                    stt(xr_d[:, d:T], xr_s[:, 0:T - d], pr_, xr_s[:, d:T], ALU.mult, ALU.add, [tr_s, 'pwr'], [tr_d])
                    stt(xr_d[:, d:T], xi_s[:, 0:T - d], npi_, xr_d[:, d:T], ALU.mult, ALU.add, [ti_s, 'npwi', tr_d], [tr_d])
                    P.add('gpsimd', lambda e, xi_d=xi_d, xi_s=xi_s, d=d: e.tensor_copy(xi_d[:, 0:d], xi_s[:, 0:d]), r=[ti_s], w=[ti_d])
                    act(mA[:, 0:T - d], xi_s[:, 0:T - d], AF.Copy, [ti_s, 'pwr'], [tmA], scale=pr_)
                    act(mB[:, 0:T - d], xr_s[:, 0:T - d], AF.Copy, [tr_s, 'pwi'], [tmB], scale=pi_)
                    tt(xi_d[:, d:T], mA[:, 0:T - d], xi_s[:, d:T], ALU.add, [tmA, ti_s], [ti_d], eng='gpsimd')
                    tt(xi_d[:, d:T], mB[:, 0:T - d], xi_d[:, d:T], ALU.add, [tmB, ti_d], [ti_d], eng='gpsimd')
            for si, j in enumerate(js):
                gp = 4 * ct + j
                XR, XI = XRs[si], XIs[si]
                t0r, t0i = 'Xr%d_0' % si, 'Xi%d_0' % si
                P.add('vector', lambda e, gp=gp, XR=XR: e.tensor_copy(Xend[:, gp:gp + 1], XR[0][:, T - 1:T]), r=[t0r], w=['Xend'])
                P.add('vector', lambda e, gp=gp, XI=XI: e.tensor_copy(Xend[:, 32 + gp:33 + gp], XI[0][:, T - 1:T]), r=[t0i], w=['Xend'])
                for half in range(2):
                    hs = slice(half * 512, (half + 1) * 512)
                    P.mm(psum[6 + half][:, :], LCr[j][:], XR[0][:, hs], j == 0, False, r=['LCr%d' % j, t0r], w=['ps%d' % (6 + half)])
                    P.mm(psum[6 + half][:, :], LCi[j][:], XI[0][:, hs], False, j == 3, r=['LCi%d' % j, t0i], w=['ps%d' % (6 + half)])
        for half in range(2):
            hs = slice(half * 512, (half + 1) * 512)
            stt(ytmp[:], uT[:, ct, hs], dvec[:, ct:ct + 1], psum[6 + half][:, :], ALU.mult, ALU.add, ['uT%d' % ct, 'dvec', 'ps%d' % (6 + half)], ['ytmp'])
            act(yT[:, ct, hs], ytmp[:], AF.Gelu_apprx_tanh, ['ytmp'], ['yT'])
    P.dma('sync', s5X_out, Xend[:], r=['Xend'], w=['s5X_out'])
    P.barrier()
    for b in range(2):
        WA, wta = wblk[0], 'wblk0'
        WG, wtg = wblk[1], 'wblk1'
        P.dma('gpsimd', WA[:, 0:8, :], w_glu[:, b * 512:(b + 1) * 512].rearrange('(kt p) n -> p kt n', p=128), r=[], w=[wta])
        P.dma('gpsimd', WG[:, 0:8, :], w_glu[:, 1024 + b * 512:1024 + (b + 1) * 512].rearrange('(kt p) n -> p kt n', p=128), r=[], w=[wtg])
        for ntl in range(4):
            nt = 4 * b + ntl
            for half in range(2):
                hs = slice(half * 512, (half + 1) * 512)
                ba = nextbank(0, 6); bg = nextbank(0, 6)
                for kt in range(8):
                    P.mm(psum[ba][:, :], WA[:, kt, ntl * 128:(ntl + 1) * 128], yT[:, kt, hs], kt == 0, kt == 7, r=['yT', wta], w=['ps%d' % ba])
                for kt in range(8):
                    P.mm(psum[bg][:, :], WG[:, kt, ntl * 128:(ntl + 1) * 128], yT[:, kt, hs], kt == 0, kt == 7, r=['yT', wtg], w=['ps%d' % bg])
                act(sgm[:], psum[bg][:, :], AF.Sigmoid, ['ps%d' % bg], ['sgm'])
                tt(obT[:, nt, hs], psum[ba][:, :], sgm[:], ALU.mult, ['ps%d' % ba, 'sgm'], ['obT'])
    if dbg:
        for nt in range(8):
            P.add('vector', lambda e, nt=nt: e.tensor_copy(Xr[0][:], obT[:, nt, :]), r=['obT'], w=['Xr0_0'])
            P.dma('sync', dbgo['obT'][nt * 128:(nt + 1) * 128, :], Xr[0][:], r=['Xr0_0'], w=['dbg_obT'])
    if phases <= 3:
        P.emit()
        return nc

    P.barrier()
    sb_off[0] = mark_persist
    mergedT = sb('mergedT', [128, KT, T], BF16)
    mark4 = sb_off[0]
    oaT = sb('oaT', [128, 8, T], BF16)
    wA = [sb('wA%d' % i, [128, 8, 512], BF16) for i in range(2)]
    wB = [sb('wB%d' % i, [128, 8, 512], BF16) for i in range(2)]
    wGA = sb('wGA', [128, KT, 512], BF16)
    wGB = sb('wGB', [128, KT, 512], BF16)
    sga = sb('sga', [128, 512], F32)
    sgb = sb('sgb', [128, 512], F32)
    mm1 = sb('mm1', [128, 512], F32)
    mm2 = sb('mm2', [128, 512], F32)
    for n in range(NT):
        bank = nextbank()
        for kt in range(8):
            P.tr(psb(bank)[:, kt * 128:(kt + 1) * 128], oa_tok[:, n, kt * 128:(kt + 1) * 128], ident_b[:], r=['oa_tok', 'ident_b'], w=['ps%d' % bank])
        P.add('vector', lambda e, bank=bank, n=n: e.tensor_copy(oaT[:, :, n * 128:(n + 1) * 128], psb(bank).rearrange('p (j t) -> p j t', j=8)),
              r=['ps%d' % bank], w=['oaT'])
    for cb in range(4):
        i = cb % 2
        P.dma('gpsimd', wA[i][:], w_up_a[:, cb * 512:(cb + 1) * 512].rearrange('(kt p) n -> p kt n', p=128), r=[], w=['wA%d' % i])
        P.dma('gpsimd', wB[i][:], w_up_b[:, cb * 512:(cb + 1) * 512].rearrange('(kt p) n -> p kt n', p=128), r=[], w=['wB%d' % i])
        P.dma('gpsimd', wGA[:], w_in[:, 5120 + cb * 512:5120 + (cb + 1) * 512].rearrange('(kt p) n -> p kt n', p=128), r=[], w=['wGA'])
        P.dma('gpsimd', wGB[:], w_in[:, 7168 + cb * 512:7168 + (cb + 1) * 512].rearrange('(kt p) n -> p kt n', p=128), r=[], w=['wGB'])
        for ntl in range(4):
            nt = 4 * cb + ntl
            ns = slice(ntl * 128, (ntl + 1) * 128)
            for half in range(2):
                hs = slice(half * 512, (half + 1) * 512)
                ba, bb_, bga, bgb = nextbank(), nextbank(), nextbank(), nextbank()
                for kt in range(8):
                    P.mm(psum[ba][:, :], wA[i][:, kt, ns], oaT[:, kt, hs], kt == 0, kt == 7, r=['wA%d' % i, 'oaT'], w=['ps%d' % ba])
                for kt in range(8):
                    P.mm(psum[bb_][:, :], wB[i][:, kt, ns], obT[:, kt, hs], kt == 0, kt == 7, r=['wB%d' % i, 'obT'], w=['ps%d' % bb_])
                for kt in range(KT):
                    P.mm(psum[bga][:, :], wGA[:, kt, ns], xT[:, kt, hs], kt == 0, kt == KT - 1, r=['wGA', 'xT'], w=['ps%d' % bga])
                for kt in range(KT):
                    P.mm(psum[bgb][:, :], wGB[:, kt, ns], xT[:, kt, hs], kt == 0, kt == KT - 1, r=['wGB', 'xT'], w=['ps%d' % bgb])
                act(sga[:], psum[bga][:, :], AF.Sigmoid, ['ps%d' % bga], ['sga'])
                act(sgb[:], psum[bgb][:, :], AF.Sigmoid, ['ps%d' % bgb], ['sgb'])
                tt(mm1[:], psum[ba][:, :], sga[:], ALU.mult, ['ps%d' % ba, 'sga'], ['mm1'])
                tt(mm2[:], psum[bb_][:, :], sgb[:], ALU.mult, ['ps%d' % bb_, 'sgb'], ['mm2'])
                tt(mergedT[:, nt, hs], mm1[:], mm2[:], ALU.add, ['mm1', 'mm2'], ['mergedT'], eng='gpsimd')
    P.barrier()
    sb_off[0] = mark4
    wo_sb = sb('wo_sb', [128, KT, D], BF16)
    x1b = sb('x1b', [128, D], BF16)
    st4 = sb('st4', [128, 4], F32)
    sb_save = sb_off[0]
    sb_off[0] = reg_oa
    lng = sb('lng', [128, D], F32)
    lnb = sb('lnb', [128, D], F32)
    xres = sb('xres', [128, D], F32)
    h1 = sb('h1', [128, D], F32)
    sb_off[0] = sb_save
    x1T = xT
    for cb in range(4):
        P.dma('gpsimd', wo_sb[:, :, cb * 512:(cb + 1) * 512], w_o[:, cb * 512:(cb + 1) * 512].rearrange('(kt p) n -> p kt n', p=128), r=[], w=['wo_sb%d' % cb])

    def layer_norm(src, gamma_d, beta_d, hbuf, dst_dram, tagp):
        pass

    P.dma('sync', lng[:], ln1_g[0].partition_broadcast(128), r=[], w=['lng'])
    P.dma('sync', lnb[:], ln1_b[0].partition_broadcast(128), r=[], w=['lnb'])

    def ln_rows(hb, ht, gam, bet):
        P.add('vector', lambda e: e.tensor_reduce(st4[:, 0:1], hb[:], AX.X, ALU.add), r=[ht], w=['st4a'])
        ts(st4[:, 1:2], st4[:, 0:1], -1.0 / D, None, ALU.mult, None, ['st4a'], ['st4b'])
        ts(hb[:], hb[:], st4[:, 1:2], None, ALU.add, None, [ht, 'st4b'], [ht])
        P.add('gpsimd', lambda e: e.memset(st4[:, 2:3], 0.0), w=['st4c'])
        P.add('scalar', lambda e: e.activation(junkD[:], hb[:], AF.Square, accum_out=st4[:, 2:3]), r=[ht, 'st4c'], w=['junkD', 'st4c'])
        act(st4[:, 3:4], st4[:, 2:3], AF.Ln, ['st4c'], ['st4d'], bias=LN_EPS, scale=1.0 / D)
        act(st4[:, 3:4], st4[:, 3:4], AF.Exp, ['st4d'], ['st4d'], scale=-0.5)
        stt(hb[:], hb[:], st4[:, 3:4], gam[:], ALU.mult, ALU.mult, [ht, 'st4d', 'lng'], [ht])
        tt(hb[:], hb[:], bet[:], ALU.add, [ht, 'lnb'], [ht], eng='gpsimd')

    junkD = sb('junkD', [128, D], BF16)
    for n in range(NT):
        P.dma('sync', xres[:], x_in[n * 128:(n + 1) * 128, :], r=[], w=['xres'])
        for cb in range(4):
            bank = nextbank()
            for kt in range(KT):
                P.mm(psum[bank][:, :], mergedT[:, kt, n * 128:(n + 1) * 128], wo_sb[:, kt, cb * 512:(cb + 1) * 512], kt == 0, kt == KT - 1,
                     r=['mergedT', 'wo_sb%d' % cb], w=['ps%d' % bank])
            stt(h1[:, cb * 512:(cb + 1) * 512], xres[:, cb * 512:(cb + 1) * 512], ALPHA, psum[bank][:, :], ALU.mult, ALU.add,
                ['xres', 'ps%d' % bank], ['h1'])
        ln_rows(h1, 'h1', lng, lnb)
        P.dma('sync', x1d[n * 128:(n + 1) * 128, :], h1[:], r=['h1'], w=['x1d'])
        if dbg:
            P.dma('sync', dbgo['x1'][n * 128:(n + 1) * 128, :], h1[:], r=['h1'], w=['dbg_x1'])
        P.add('scalar', lambda e: e.copy(x1b[:], h1[:]), r=['h1'], w=['x1b'])
        for half in range(2):
            bank = nextbank()
            for j in range(8):
                kt = half * 8 + j
                P.tr(psb(bank)[:, j * 128:(j + 1) * 128], x1b[:, kt * 128:(kt + 1) * 128], ident_b[:], r=['x1b', 'ident_b'], w=['ps%d' % bank])
            P.add('vector', lambda e, bank=bank, half=half, n=n: e.tensor_copy(
                x1T[:, half * 8:(half + 1) * 8, n * 128:(n + 1) * 128], psb(bank).rearrange('p (j t) -> p j t', j=8)),
                r=['ps%d' % bank], w=['x1T'])
    if phases <= 4:
        P.emit()
        return nc

    P.barrier()
    sb_off[0] = mark_persist
    s_d = nc.dram_tensor('s_d', [T, 16, 128], F32).ap()
    wblk = [sb('wblk%d' % i, [128, KT, 512], BF16) for i in range(2)]
    qTb = sb('qTb', [128, 4, T], F32)
    kld = sb('kld', [128, 16, 128], F32)
    KTt = sb('KTt', [128, 16, 128], F32)
    s_st = [sb('s_st%d' % i, [128, 4, 128], F32) for i in range(2)]
    s_sb = sb('s_sb', [128, 16, 128], F32)
    mr = sb('mr', [128, 128], F32)
    top = sb('top', [128, 16, 16], F32)
    cand = sb('cand', [128, 8, 256], F32)
    cand2 = sb('cand2', [128, 8, 256], F32)
    vals = sb('vals', [128, 8, 16], F32)
    ev = sb('ev', [128, 8, 16], F32)
    Zs = sb('Zs', [128, 8], F32)
    nb = sb('nb', [128, 8], F32)
    Tbuf2 = [sb('Tbuf%d' % i, [128, 16, 128], F32) for i in range(2)]
    Eb2 = [sb('Eb%d' % i, [128, 2048], BF16) for i in range(2)]
    Pm2 = [sb('Pm%d' % i, [128, 2048], BF16) for i in range(2)]
    sb_save = sb_off[0]
    sb_off[0] = reg_oa
    G = sb('G', [128, 16384], BF16)
    sb_off[0] = sb_save

    P.dma('sync', kld[:], keys.rearrange('h i d -> i h d'), r=[], w=['kld'])
    for q4 in range(4):
        bank = nextbank()
        for hl in range(4):
            hh = 4 * q4 + hl
            P.tr(psum[bank][:, hl * 128:(hl + 1) * 128], kld[:, hh, :], ident_f[:], r=['kld', 'ident_f'], w=['ps%d' % bank])
        P.add('vector', lambda e, bank=bank, q4=q4: e.tensor_copy(KTt[:, 4 * q4:4 * q4 + 4, :], psum[bank][:, :].rearrange('p (a b) -> p a b', a=4)),
              r=['ps%d' % bank], w=['KTt'])
    for blk in range(4):
        wb, wt = load_wblk(w_q, blk * 512)
        for hl in range(4):
            for half in range(2):
                hs = slice(half * 512, (half + 1) * 512)
                bank = nextbank()
                for kt in range(KT):
                    P.mm(psum[bank][:, :], wb[:, kt, hl * 128:(hl + 1) * 128], x1T[:, kt, hs], kt == 0, kt == KT - 1, r=[wt, 'x1T'], w=['ps%d' % bank])
                P.add('scalar', lambda e, bank=bank, hl=hl, hs=hs: e.copy(qTb[:, hl, hs], psum[bank][:, :]), r=['ps%d' % bank], w=['qTb%d' % hl])
        for n in range(NT):
            bank = nextbank()
            i2 = n % 2
            for hl in range(4):
                P.mm(psum[bank][:, hl * 128:(hl + 1) * 128], qTb[:, hl, n * 128:(n + 1) * 128], KTt[:, 4 * blk + hl, :], True, True,
                     r=['qTb%d' % hl, 'KTt'], w=['ps%d' % bank])
            P.add('vector', lambda e, bank=bank, i2=i2: e.tensor_copy(s_st[i2][:], psum[bank][:, :].rearrange('p (a b) -> p a b', a=4)),
                  r=['ps%d' % bank], w=['s_st%d' % i2])
            P.dma('sync', s_d[n * 128:(n + 1) * 128, 4 * blk:4 * blk + 4, :], s_st[i2][:], r=['s_st%d' % i2], w=['s_d'])
    P.barrier()
    top4 = top[:].rearrange('p (h two) k -> p h two k', two=2)
    gcnt = 0
    for n in range(NT):
        P.dma('sync', s_sb[:], s_d[n * 128:(n + 1) * 128], r=[], w=['s_sb'])
        for hh in range(16):
            P.add('vector', lambda e, hh=hh: e.max(out=top[:, hh, 0:8], in_=s_sb[:, hh, :]), r=['s_sb'], w=['top'])
            P.add('vector', lambda e, hh=hh: e.match_replace(out=mr[:], in_to_replace=top[:, hh, 0:8], in_values=s_sb[:, hh, :], imm_value=-1e30),
                  r=['s_sb', 'top'], w=['mr'])
            P.add('vector', lambda e, hh=hh: e.max(out=top[:, hh, 8:16], in_=mr[:]), r=['mr'], w=['top'])
        tt(cand[:].rearrange('p h (a b) -> p h a b', a=16), bc(top4[:, :, 0, :].unsqueeze(3), [128, 8, 16, 16]),
           bc(top4[:, :, 1, :].unsqueeze(2), [128, 8, 16, 16]), ALU.add, ['top'], ['cand'])
        for h in range(8):
            P.add('vector', lambda e, h=h: e.max(out=vals[:, h, 0:8], in_=cand[:, h, :]), r=['cand'], w=['vals'])
            P.add('vector', lambda e, h=h: e.match_replace(out=cand2[:, h, :], in_to_replace=vals[:, h, 0:8], in_values=cand[:, h, :], imm_value=-1e30),
                  r=['cand', 'vals'], w=['cand2'])
            P.add('vector', lambda e, h=h: e.max(out=vals[:, h, 8:16], in_=cand2[:, h, :]), r=['cand2'], w=['vals'])
        tt(ev[:], vals[:], bc(vals[:, :, 0:1], [128, 8, 16]), ALU.subtract, ['vals'], ['ev'])
        act(ev[:], ev[:], AF.Exp, ['ev'], ['ev'])
        P.add('vector', lambda e: e.tensor_reduce(Zs[:], ev[:], AX.X, ALU.add), r=['ev'], w=['Zs'])
        act(Zs[:], Zs[:], AF.Ln, ['Zs'], ['Zs'])
        tt(nb[:], Zs[:], vals[:, :, 0], ALU.add, ['Zs', 'vals'], ['nb'])
        ts(nb[:], nb[:], -1.0, None, ALU.mult, None, ['nb'], ['nb'])
        its = [(h, ic) for ic in range(8) for h in range(8)]

        def emitT(idx):
            h, ic = its[idx]
            pi_ = idx % 2
            isl = slice(ic * 16, (ic + 1) * 16)
            tt(Tbuf2[pi_][:], bc(s_sb[:, 2 * h, isl].unsqueeze(2), [128, 16, 128]), bc(s_sb[:, 2 * h + 1, :].unsqueeze(1), [128, 16, 128]),
               ALU.add, ['s_sb'], ['Tbuf%d' % pi_])

        emitT(0)
        for idx, (h, ic) in enumerate(its):
            if idx + 1 < len(its):
                emitT(idx + 1)
            pi_ = idx % 2
            Tb, Ebb, Pmb = Tbuf2[pi_], Eb2[pi_], Pm2[pi_]
            tT, tE, tP = 'Tbuf%d' % pi_, 'Eb%d' % pi_, 'Pm%d' % pi_
            gsl = slice(ic * 2048, (ic + 1) * 2048)
            bset = (ic % 2) * 4
            Tf = Tb[:].rearrange('p a b -> p (a b)')
            act(Ebb[:], Tf, AF.Exp, [tT, 'nb'], [tE], bias=nb[:, h:h + 1])
            stt(Pmb[:], Tf, vals[:, h, 15:16], Ebb[:], ALU.is_ge, ALU.mult, [tT, 'vals', tE], [tP])
            for q4 in range(4):
                P.mm(psum[bset + q4][:, :], ident_b[:], Pmb[:, q4 * 512:(q4 + 1) * 512], h == 0, h == 7, r=['ident_b', tP], w=['ps%d' % (bset + q4)])
            if h == 7:
                for q4 in range(4):
                    act(G[:, ic * 2048 + q4 * 512:ic * 2048 + (q4 + 1) * 512], psum[bset + q4][:, :], AF.Copy, ['ps%d' % (bset + q4)], ['G%d' % ic])
        P.dma('sync', Gd[n * 128:(n + 1) * 128, :], G[:], r=['G%d' % i for i in range(8)], w=['Gd'])

    P.barrier()
    sb_off[0] = mark_persist
    yacc = sb('yacc', [128, NT, D], F32)
    ublk = [sb('ublk%d' % i, [128, 4, D], BF16) for i in range(2)]
    sb_save = sb_off[0]
    sb_off[0] = reg_oa
    vblk = [sb('vblk%d' % i, [128, 4, D], BF16) for i in range(2)]
    sb_off[0] = sb_save
    uTb = sb('uTb', [128, KT, 512], BF16)
    Gs = [sb('Gs%d' % i, [128, 512], BF16) for i in range(2)]
    gl2 = [sb('gl%d' % i, [128, 512], F32) for i in range(2)]
    Wb2 = [sb('Wb%d' % i, [128, 512], BF16) for i in range(2)]
    WT2 = [sb('WT%d' % i, [128, 4, 128], BF16) for i in range(2)]
    cnt = 0
    for eb in range(32):
        i = eb % 2
        P.dma('gpsimd', ublk[i][:], peer_u[eb * 512:(eb + 1) * 512, :].rearrange('(et p) d -> p et d', p=128), r=[], w=['ublk%d' % i])
        P.dma('gpsimd', vblk[i][:], peer_v[eb * 512:(eb + 1) * 512, :].rearrange('(et p) d -> p et d', p=128), r=[], w=['vblk%d' % i])
        for k2 in range(8):
            bank = nextbank()
            for kk in range(2):
                kt = 2 * k2 + kk
                for et in range(4):
                    P.tr(psb(bank)[:, (kk * 4 + et) * 128:(kk * 4 + et + 1) * 128], ublk[i][:, et, kt * 128:(kt + 1) * 128], ident_b[:],
                         r=['ublk%d' % i, 'ident_b'], w=['ps%d' % bank])
            if k2 % 2 == 0:
                P.add('vector', lambda e, bank=bank, k2=k2: e.tensor_copy(uTb[:, 2 * k2:2 * k2 + 2, :], psb(bank).rearrange('p (k e) -> p k e', k=2)),
                      r=['ps%d' % bank], w=['uTb'])
            else:
                P.add('scalar', lambda e, bank=bank, k2=k2: e.copy(uTb[:, 2 * k2:2 * k2 + 2, :], psb(bank).rearrange('p (k e) -> p k e', k=2)),
                      r=['ps%d' % bank], w=['uTb'])
        for n in range(NT):
            gi = cnt % 2
            cnt += 1
            P.dma('sync', Gs[gi][:], Gd[n * 128:(n + 1) * 128, eb * 512:(eb + 1) * 512], r=[], w=['Gs%d' % gi])
            bA = nextbank()
            for kt in range(KT):
                P.mm(psum[bA][:, :], x1T[:, kt, n * 128:(n + 1) * 128], uTb[:, kt, :], kt == 0, kt == KT - 1, r=['x1T', 'uTb'], w=['ps%d' % bA])
            gl, Wb, WT = gl2[gi], Wb2[gi], WT2[gi]
            tg, tw, twt = 'gl%d' % gi, 'Wb%d' % gi, 'WT%d' % gi
            act(gl[:], psum[bA][:, :], AF.Gelu_apprx_tanh, ['ps%d' % bA], [tg])
            tt(Wb[:], gl[:], Gs[gi][:], ALU.mult, [tg, 'Gs%d' % gi], [tw], eng='gpsimd')
            bT_ = nextbank()
            for et in range(4):
                P.tr(psb(bT_)[:, et * 128:(et + 1) * 128], Wb[:, et * 128:(et + 1) * 128], ident_b[:], r=[tw, 'ident_b'], w=['ps%d' % bT_])
            P.add('scalar', lambda e, bT_=bT_, WT=WT: e.copy(WT[:], psb(bT_)[:, 0:512].rearrange('p (a b) -> p a b', a=4)), r=['ps%d' % bT_], w=[twt])
            for db in range(4):
                ds_ = slice(db * 512, (db + 1) * 512)
                bY = nextbank()
                for et in range(4):
                    P.mm(psum[bY][:, :], WT[:, et, :], vblk[i][:, et, ds_], et == 0, et == 3, r=[twt, 'vblk%d' % i], w=['ps%d' % bY])
                if eb == 0:
                    P.add('scalar', lambda e, bY=bY, n=n, ds_=ds_: e.copy(yacc[:, n, ds_], psum[bY][:, :]), r=['ps%d' % bY], w=['yacc%d' % n])
                else:
                    tt(yacc[:, n, ds_], yacc[:, n, ds_], psum[bY][:, :], ALU.add, ['yacc%d' % n, 'ps%d' % bY], ['yacc%d' % n])

    P.barrier()
    st4_2 = sb('st4b', [128, 4], F32)
    junkD_2 = sb('junkD2', [128, D], BF16)
    sb_save = sb_off[0]
    sb_off[0] = reg_oa
    lng_2 = sb('lng2', [128, D], F32)
    lnb_2 = sb('lnb2', [128, D], F32)
    xres_2 = sb('xres2', [128, D], F32)
    h1_2 = sb('h12', [128, D], F32)
    sb_off[0] = sb_save
    P.dma('sync', lng_2[:], ln2_g[0].partition_broadcast(128), r=[], w=['lng'])
    P.dma('sync', lnb_2[:], ln2_b[0].partition_broadcast(128), r=[], w=['lnb'])

    def ln_rows2(hb, ht, gam, bet):
        P.add('vector', lambda e: e.tensor_reduce(st4_2[:, 0:1], hb[:], AX.X, ALU.add), r=[ht], w=['st4a'])
        ts(st4_2[:, 1:2], st4_2[:, 0:1], -1.0 / D, None, ALU.mult, None, ['st4a'], ['st4b'])
        ts(hb[:], hb[:], st4_2[:, 1:2], None, ALU.add, None, [ht, 'st4b'], [ht])
        P.add('gpsimd', lambda e: e.memset(st4_2[:, 2:3], 0.0), w=['st4c'])
        P.add('scalar', lambda e: e.activation(junkD_2[:], hb[:], AF.Square, accum_out=st4_2[:, 2:3]), r=[ht, 'st4c'], w=['junkD', 'st4c'])
        act(st4_2[:, 3:4], st4_2[:, 2:3], AF.Ln, ['st4c'], ['st4d'], bias=LN_EPS, scale=1.0 / D)
        act(st4_2[:, 3:4], st4_2[:, 3:4], AF.Exp, ['st4d'], ['st4d'], scale=-0.5)
        stt(hb[:], hb[:], st4_2[:, 3:4], gam[:], ALU.mult, ALU.mult, [ht, 'st4d', 'lng'], [ht])
        tt(hb[:], hb[:], bet[:], ALU.add, [ht, 'lnb'], [ht], eng='gpsimd')

    for n in range(NT):
        P.dma('sync', xres_2[:], x1d[n * 128:(n + 1) * 128, :], r=[], w=['xres'])
        stt(h1_2[:], xres_2[:], ALPHA, yacc[:, n, :], ALU.mult, ALU.add, ['xres', 'yacc%d' % n], ['h1'])
        ln_rows2(h1_2, 'h1', lng_2, lnb_2)
        P.dma('sync', x_out[n * 128:(n + 1) * 128, :], h1_2[:], r=['h1'], w=['x_out'])
    P.emit()
    return nc


def layer_inputs(z, l, phases=99):
    f = lambda a: np.ascontiguousarray(a, dtype=np.float32)
    m = {
        'w_in': f(z['w_in'][l]),
        'lb_logits': f(z['hgrn_lb_logits']),
        'lmask': np.array([[0.0, float(l >= 1), float(l >= 2), float(l >= 3)]], np.float32),
        'norm_g': f(z['hgrn_norm_g'][l]).reshape(1, 1024),
    }
    if phases > 2:
        m.update({
            'lam_re': f(z['s5_lambda_re'][l]).reshape(32, 128),
            'lam_im': f(z['s5_lambda_im'][l]).reshape(32, 128),
            'log_step': f(z['s5_log_step'][l]).reshape(32, 2),
            'b_re': f(z['s5_b_re'][l]),
            'b_im': f(z['s5_b_im'][l]),
            'c_re': f(z['s5_c_re'][l]).reshape(1024, 64),
            'c_im': f(z['s5_c_im'][l]).reshape(1024, 64),
            's5_d': f(z['s5_d'][l]).reshape(8, 128),
            'w_glu': f(z['s5_w_glu'][l]),
        })
    if phases > 3:
        m.update({
            'w_up_a': f(z['w_up_a'][l]), 'w_up_b': f(z['w_up_b'][l]), 'w_o': f(z['w_o'][l]),
            'ln1_g': f(z['ln1_g'][l]).reshape(1, D), 'ln1_b': f(z['ln1_b'][l]).reshape(1, D),
        })
    if phases > 4:
        m.update({
            'w_q': f(z['peer_w_q'][l]), 'keys': f(z['peer_keys'][l]).reshape(16, 128, 128),
            'peer_u': f(z['peer_u'][l]), 'peer_v': f(z['peer_v'][l]),
            'ln2_g': f(z['ln2_g'][l]).reshape(1, D), 'ln2_b': f(z['ln2_b'][l]).reshape(1, D),
        })
    return m


def kernel(**inputs):
    z = inputs
    f32 = np.float32
    x = np.ascontiguousarray(np.asarray(z['x'])[0], dtype=f32)
    ncA = build(dbg=False, phases=3)
    ncB = build(dbg=False, phases=99)
    xs = [np.ascontiguousarray(x[c * T:(c + 1) * T]) for c in range(NCORES)]
    zU = np.zeros((NCORES, 8, 128, 128), f32)
    zD = np.zeros((NCORES, 128, 8), f32)
    zX = np.zeros((NCORES, 128, 64), f32)
    zc = np.zeros((1, NCORES), f32)
    cores = list(range(NCORES))
    for l in range(4):
        mA = layer_inputs(z, l, 3)
        in_maps = [dict(mA, x=xs[c], cmask=zc, hgU_all=zU, hgD_all=zD, s5X_all=zX) for c in cores]
        rA = run_bass_kernel_spmd(ncA, in_maps, core_ids=cores).results
        U = np.ascontiguousarray(np.stack([rA[c]['hgU_out'] for c in cores]), dtype=f32)
        Dd = np.ascontiguousarray(np.stack([rA[c]['hgD_out'] for c in cores]), dtype=f32)
        Xs = np.ascontiguousarray(np.stack([rA[c]['s5X_out'] for c in cores]), dtype=f32)
        mB = layer_inputs(z, l, 99)
        in_maps = [dict(mB, x=xs[c], cmask=(np.arange(NCORES) < c).astype(f32)[None, :],
                        hgU_all=U, hgD_all=Dd, s5X_all=Xs) for c in cores]
        rB = run_bass_kernel_spmd(ncB, in_maps, core_ids=cores).results
        xs = [np.ascontiguousarray(rB[c]['x_out'], dtype=f32) for c in cores]
    return np.concatenate(xs, axis=0)[None].astype(f32)
```

```python
import math
from contextlib import ExitStack
import numpy as np
import concourse.bass as bass
import concourse.mybir as mybir
from concourse.bass_utils import run_bass_kernel_spmd

F32 = mybir.dt.float32
BF16 = mybir.dt.bfloat16
ALU = mybir.AluOpType
AF = mybir.ActivationFunctionType
AX = mybir.AxisListType

ENGS = ['tensor', 'vector', 'scalar', 'gpsimd', 'sync']
NDSEM = 6
NCORES = 8
T = 1024
NT = 8
D = 2048
KT = 16
ALPHA = 8.0 ** 0.25
LN_EPS = 1e-5
RMS_EPS = 1e-6
TWO_PI = 2.0 * math.pi
MAGIC = 12582912.0


class Prog:
    def __init__(self, nc):
        self.nc = nc
        self.ops = {e: [] for e in ENGS}
        self.lastw = {}
        self.readers = {}
        self.ndma = {e: 0 for e in ENGS}
        self.dma_ops = {e: [] for e in ENGS}
        self.pending = {e: [] for e in ENGS}

    def add(self, eng, fn, r=(), w=(), dma=False):
        idx = len(self.ops[eng])
        deps = list(self.pending[eng])
        self.pending[eng] = []
        for t in r:
            if t in self.lastw:
                deps.append(self.lastw[t])
        for t in w:
            if t in self.lastw:
                deps.append(self.lastw[t])
            last = {}
            for ref in self.readers.get(t, ()):
                if self.ops[ref[0]][ref[1]]['dma']:
                    deps.append(ref)
                else:
                    last[ref[0]] = ref
            deps.extend(last.values())
        op = dict(fn=fn, deps=deps, dma=dma, signal=False, eng=eng, idx=idx)
        if dma:
            j = self.ndma[eng]
            self.ndma[eng] += 1
            op['dj'] = j
            if j >= NDSEM:
                deps.append(self.dma_ops[eng][j - NDSEM])
            self.dma_ops[eng].append((eng, idx))
        self.ops[eng].append(op)
        ref = (eng, idx)
        for t in r:
            self.readers.setdefault(t, []).append(ref)
        for t in w:
            self.lastw[t] = ref
            self.readers[t] = []
        return ref

    def barrier(self):
        deps = []
        for e in ENGS:
            nonD = [i for i, o in enumerate(self.ops[e]) if not o['dma']]
            if nonD:
                deps.append((e, nonD[-1]))
            deps.extend(self.dma_ops[e][-NDSEM:])
        for e in ENGS:
            self.pending[e] = list(self.pending[e]) + deps
        self.lastw = {}
        self.readers = {}

    def mm(self, out, lhsT, rhs, start, stop, r, w):
        return self.add('tensor', lambda e: e.matmul(out, lhsT, rhs, start=start, stop=stop), r, w)

    def tr(self, out, in_, ident, r, w):
        return self.add('tensor', lambda e: e.transpose(out, in_, ident), r, w)

    def dma(self, eng, out, in_, r, w, **kw):
        return self.add(eng, lambda e: e.dma_start(out=out, in_=in_, **kw), r, w, dma=True)

    def emit(self):
        nc = self.nc
        ops = self.ops
        for e in ENGS:
            for op in ops[e]:
                nd = []
                seen = set()
                for d in op['deps']:
                    if d in seen:
                        continue
                    seen.add(d)
                    dop = ops[d[0]][d[1]]
                    if not dop['dma']:
                        if d[0] == e and e == 'tensor':
                            continue
                        dop['signal'] = True
                    nd.append(d)
                op['deps'] = nd
        for e in ENGS:
            c = 0
            for op in ops[e]:
                if op['dma']:
                    continue
                if op['signal']:
                    c += 1
                    op['semval'] = c
        with ExitStack() as st:
            esem = {e: st.enter_context(nc.semaphore('s_' + e)) for e in ENGS}
            dsem = {e: [st.enter_context(nc.semaphore('d_%s%d' % (e, i))) for i in range(NDSEM)]
                    for e in ENGS if self.ndma[e] > 0}
            block = st.enter_context(nc.Block())

            def run(e, eng):
                waited = {e2: 0 for e2 in ENGS}
                dwaited = {e2: {} for e2 in ENGS}
                for op in ops[e]:
                    for d in op['deps']:
                        dop = ops[d[0]][d[1]]
                        if dop['dma']:
                            j = dop['dj']
                            slot = j % NDSEM
                            if dwaited[d[0]].get(slot, -1) >= j:
                                continue
                            dwaited[d[0]][slot] = j
                            eng.wait_ge(dsem[d[0]][slot], 16 * (j // NDSEM + 1))
                        else:
                            v = dop['semval']
                            if waited[d[0]] >= v:
                                continue
                            waited[d[0]] = v
                            eng.wait_ge(esem[d[0]], v)
                    ins = op['fn'](eng)
                    if op['dma']:
                        ins.then_inc(dsem[e][op['dj'] % NDSEM], 16)
                    elif op['signal']:
                        ins.then_inc(esem[e], 1)
                if self.ndma[e] > 0:
                    n = self.ndma[e]
                    for i in range(NDSEM):
                        cnt = (n - i + NDSEM - 1) // NDSEM if n > i else 0
                        if cnt > 0:
                            eng.wait_ge(dsem[e][i], 16 * cnt)

            for e in ENGS:
                if ops[e]:
                    getattr(block, e)(lambda eng, e=e: run(e, eng))


def bc(ap, shape):
    return ap.to_broadcast(list(shape))


def build(dbg=False, phases=99, summary=False):
    nc = bass.Bass('TRN2', target_bir_lowering=False)
    P = Prog(nc)

    def din(name, shape):
        return nc.dram_tensor(name, list(shape), F32, kind='ExternalInput').ap()

    def dout(name, shape):
        return nc.dram_tensor(name, list(shape), F32, kind='ExternalOutput').ap()

    x_in = din('x', [T, D])
    w_in = din('w_in', [D, 9216])
    lb_logits = din('lb_logits', [4, 1024])
    lmask = din('lmask', [1, 4])
    cmask = din('cmask', [1, 8])
    norm_g = din('norm_g', [1, 1024])
    if phases > 2:
        lam_re = din('lam_re', [32, 128])
        lam_im = din('lam_im', [32, 128])
        log_step = din('log_step', [32, 2])
        b_re = din('b_re', [64, 64, 16])
        b_im = din('b_im', [64, 64, 16])
        c_re = din('c_re', [1024, 64])
        c_im = din('c_im', [1024, 64])
        s5_d = din('s5_d', [8, 128])
        w_glu = din('w_glu', [1024, 2048])
        s5X_all = din('s5X_all', [NCORES, 128, 64])
        s5X_out = dout('s5X_out', [128, 64])
    if phases > 3:
        w_up_a = din('w_up_a', [1024, 2048])
        w_up_b = din('w_up_b', [1024, 2048])
        w_o = din('w_o', [2048, 2048])
        ln1_g = din('ln1_g', [1, D])
        ln1_b = din('ln1_b', [1, D])
    if phases > 4:
        w_q = din('w_q', [2048, 2048])
        keys = din('keys', [16, 128, 128])
        peer_u = din('peer_u', [16384, 2048])
        peer_v = din('peer_v', [16384, 2048])
        ln2_g = din('ln2_g', [1, D])
        ln2_b = din('ln2_b', [1, D])
        x_out = dout('x_out', [T, D])
    hgU_all = din('hgU_all', [NCORES, 8, 128, 128])
    hgD_all = din('hgD_all', [NCORES, 128, 8])

    hgU_out = dout('hgU_out', [8, 128, 128])
    hgD_out = dout('hgD_out', [128, 8])
    dbgo = {}
    if dbg:
        dbgo['oa'] = dout('dbg_oa', [T, 1024])
        dbgo['obT'] = dout('dbg_obT', [1024, T])
        dbgo['x1'] = dout('dbg_x1', [T, D])
        dbgo['yT'] = dout('dbg_yT', [1024, T])

    x1d = nc.dram_tensor('x1d', [T, D], F32).ap()
    Gd = nc.dram_tensor('Gd', [T, 16384], BF16).ap()

    sb_off = [16512]
    sb_cnt = [0]

    def sb(name, shape, dt):
        n = 1
        for v in shape[1:]:
            n *= v
        nbytes = n * (4 if dt == F32 else 2)
        off = sb_off[0]
        sb_off[0] = (off + nbytes + 63) // 64 * 64
        assert sb_off[0] <= 229000, (name, sb_off[0])
        sb_cnt[0] += 1
        return nc.alloc_sbuf_tensor_at('%s_%d' % (name, sb_cnt[0]), list(shape), dt, offset=off)

    ident_b = sb('ident_b', [128, 128], BF16)
    ident_f = sb('ident_f', [128, 128], F32)
    tri01 = sb('tri01', [128, 64], F32)
    mask01 = sb('mask01', [128, T], F32)
    psum = [nc.alloc_psum_tensor('ps%d' % i, [128, 512], F32) for i in range(8)]

    def psb(i):
        return psum[i][:].bitcast(BF16)

    P.add('gpsimd', lambda e: e.memset(ident_f[:], 1.0), w=['ident_f'])
    P.add('gpsimd', lambda e: e.affine_select(out=ident_f[:], in_=ident_f[:], pattern=[[-1, 128]], compare_op=ALU.is_equal,
                                              fill=0.0, base=0, channel_multiplier=1), r=['ident_f'], w=['ident_f'])
    P.add('vector', lambda e: e.tensor_copy(ident_b[:], ident_f[:]), r=['ident_f'], w=['ident_b'])
    triA = sb('triA', [128, 64], F32)
    triB = sb('triB', [128, 64], F32)
    P.add('gpsimd', lambda e: e.memset(triA[:], 1.0), w=['triA'])
    P.add('gpsimd', lambda e: e.memset(triB[:], 1.0), w=['triB'])
    P.add('gpsimd', lambda e: e.affine_select(out=triA[:], in_=triA[:], pattern=[[1, 64]], compare_op=ALU.is_ge,
                                              fill=0.0, base=0, channel_multiplier=-1), r=['triA'], w=['triA'])
    P.add('gpsimd', lambda e: e.affine_select(out=triB[:], in_=triB[:], pattern=[[1, 64]], compare_op=ALU.is_ge,
                                              fill=0.0, base=64, channel_multiplier=-1), r=['triB'], w=['triB'])
    P.add('vector', lambda e: e.tensor_copy(tri01[0:64, :], triA[0:64, :]), r=['triA'], w=['tri01'])
    P.add('vector', lambda e: e.tensor_copy(tri01[64:128, :], triB[64:128, :]), r=['triB', 'tri01'], w=['tri01'])
    P.add('gpsimd', lambda e: e.memset(mask01[:], 1.0), w=['mask01'])
    P.add('gpsimd', lambda e: e.memset(mask01[:].rearrange('p (c s) -> p c s', s=64)[:, :, 0:1], 0.0), r=['mask01'], w=['mask01'])

    xT = sb('xT', [128, KT, T], BF16)
    reg_oa = sb_off[0]
    oa_tok = sb('oa_tok', [128, NT, 1024], BF16)
    obT = sb('obT', [128, 8, T], BF16)
    lbl = sb('lbl', [128, 4, 8], F32)
    lbe = sb('lbe', [128, 4, 8], F32)
    lbs = sb('lbs', [128, 8], F32)
    lb = sb('lb', [128, 8], F32)
    oml = sb('oml', [128, 8], F32)
    lmk = sb('lmk', [128, 4], F32)
    cmk = sb('cmk', [128, 8], F32)
    mark_persist = sb_off[0]
    xload = [sb('xload%d' % i, [128, D], F32) for i in range(2)]
    xbf = [sb('xbf%d' % i, [128, D], BF16) for i in range(2)]

    def to_featmajor(src_dram, dstT, tokpref):
        for n in range(NT):
            xl = xload[n % 2]; xb = xbf[n % 2]
            tl = 'xload%d' % (n % 2); tb = 'xbf%d' % (n % 2)
            P.dma('sync', xl[:], src_dram[n * 128:(n + 1) * 128, :], r=[], w=[tl])
            P.add('scalar', lambda e, xl=xl, xb=xb: e.copy(xb[:], xl[:]), r=[tl], w=[tb])
            for half in range(2):
                bank = half
                for j in range(8):
                    kt = half * 8 + j
                    P.tr(psb(bank)[:, j * 128:(j + 1) * 128], xb[:, kt * 128:(kt + 1) * 128], ident_b[:],
                         r=[tb, 'ident_b'], w=['ps%d' % bank])
                P.add('vector', lambda e, bank=bank, half=half, n=n: e.tensor_copy(
                    dstT[:, half * 8:(half + 1) * 8, n * 128:(n + 1) * 128],
                    psb(bank).rearrange('p (j t) -> p j t', j=8)), r=['ps%d' % bank], w=[tokpref])

    to_featmajor(x_in, xT, 'xT')

    P.dma('sync', lbl[:], lb_logits.rearrange('l (h d) -> d l h', d=128), r=[], w=['lbl'], allow_slow_non_contiguous=True)
    P.dma('sync', lmk[:], lmask[0].partition_broadcast(128), r=[], w=['lmk'])
    P.dma('sync', cmk[:], cmask[0].partition_broadcast(128), r=[], w=['cmk'])
    P.add('scalar', lambda e: e.activation(lbe[:], lbl[:], AF.Exp), r=['lbl'], w=['lbe'])
    P.add('vector', lambda e: e.tensor_reduce(lbs[:], lbe[:].rearrange('p l h -> p h l'), AX.X, ALU.add), r=['lbe'], w=['lbs'])
    P.add('vector', lambda e: e.reciprocal(lbs[:], lbs[:]), r=['lbs'], w=['lbs'])
    P.add('vector', lambda e: e.tensor_tensor(lbe[:], lbe[:], bc(lmk[:].unsqueeze(2), [128, 4, 8]), ALU.mult), r=['lbe', 'lmk'], w=['lbe'])
    P.add('vector', lambda e: e.tensor_reduce(lb[:], lbe[:].rearrange('p l h -> p h l'), AX.X, ALU.add), r=['lbe'], w=['lb'])
    P.add('vector', lambda e: e.tensor_tensor(lb[:], lb[:], lbs[:], ALU.mult), r=['lb', 'lbs'], w=['lb'])
    P.add('vector', lambda e: e.tensor_scalar(oml[:], lb[:], -1.0, 1.0, ALU.mult, ALU.add), r=['lb'], w=['oml'])

    v_tok = sb('v_tok', [128, NT, 1024], BF16)
    sg_tok = sb('sg_tok', [128, NT, 1024], BF16)
    wblk = [sb('wblk%d' % i, [128, KT, 512], BF16) for i in range(2)]
    wcnt = [0]

    def load_wblk(src, c0, ncols=512, kt=KT):
        i = wcnt[0] % 2
        wcnt[0] += 1
        P.dma('gpsimd', wblk[i][:, 0:kt, 0:ncols], src[:, c0:c0 + ncols].rearrange('(kt p) n -> p kt n', p=128),
              r=[], w=['wblk%d' % i])
        return wblk[i], 'wblk%d' % i

    pscnt = [0]

    def nextbank(lo=0, hi=8):
        b = lo + pscnt[0] % (hi - lo)
        pscnt[0] += 1
        return b

    for blk in range(4):
        wb, wt = load_wblk(w_in, 2048 + blk * 512)
        for n in range(NT):
            bank = nextbank()
            for kt in range(KT):
                P.mm(psum[bank][:, :], xT[:, kt, n * 128:(n + 1) * 128], wb[:, kt, :], kt == 0, kt == KT - 1,
                     r=['xT', wt], w=['ps%d' % bank])
            if blk < 2:
                P.add('vector', lambda e, bank=bank, n=n, blk=blk: e.tensor_copy(v_tok[:, n, blk * 512:(blk + 1) * 512], psum[bank][:, :]),
                      r=['ps%d' % bank], w=['v_tok%d' % n])
            else:
                P.add('scalar', lambda e, bank=bank, n=n, blk=blk: e.activation(sg_tok[:, n, (blk - 2) * 512:(blk - 1) * 512], psum[bank][:, :], AF.Sigmoid),
                      r=['ps%d' % bank], w=['sg_tok%d' % n])

    ngb = sb('ngb', [128, 1024], F32)
    P.dma('sync', ngb[:], norm_g[0].partition_broadcast(128), r=[], w=['ngb'])
    Dp = sb('Dp', [128, NCORES, 8], F32)
    P.dma('sync', Dp[:], hgD_all.rearrange('c d h -> d c h'), r=[], w=['Dp'])
    P.add('vector', lambda e: e.tensor_scalar(Dp[:], Dp[:], -1.0, None, ALU.add), r=['Dp'], w=['Dp'])
    P.add('vector', lambda e: e.tensor_tensor(Dp[:], Dp[:], bc(cmk[:].unsqueeze(2), [128, NCORES, 8]), ALU.mult), r=['Dp', 'cmk'], w=['Dp'])
    P.add('vector', lambda e: e.tensor_scalar(Dp[:], Dp[:], 1.0, None, ALU.add), r=['Dp'], w=['Dp'])

    qT = sb('qT', [128, T], F32)
    fT = sb('fT', [128, T], F32)
    lgf = sb('lgf', [128, T], F32)
    kT_ = sb('kT_', [128, T], F32)
    bT = sb('bT', [128, T], F32)
    eq = sb('eq', [128, T], F32)
    ek = sb('ek', [128, T], F32)
    qt_b = sb('qt_b', [128, T], BF16)
    kt_b = sb('kt_b', [128, T], BF16)
    ktok = sb('ktok', [128, NT, 128], BF16)
    rr = sb('rr', [128, 16], F32)
    er = sb('er', [128, 16], F32)
    e2r = sb('e2r', [128, 16], F32)
    bsum = sb('bsum', [128, 1], F32)
    Dtot = sb('Dtot', [128, 8], F32)
    S = sb('S', [128, 128], F32)
    Uld = [sb('Uld%d' % i, [128, 128], F32) for i in range(2)]
    Spb = [sb('Spb%d' % i, [128, 128], BF16) for i in range(4)]
    Utmp = sb('Utmp', [128, 128], F32)
    att_sb = [sb('att_sb%d' % i, [128, 128], BF16) for i in range(2)]
    ss = sb('ss', [128, 2], F32)
    junk = sb('junk', [128, 128], F32)
    otmp = sb('otmp', [128, 128], F32)
    for i in range(2):
        P.add('gpsimd', lambda e, i=i: e.memset(att_sb[i][:], 0.0), w=['att_sb%d' % i])

    for h in range(8):
        P.add('gpsimd', lambda e: e.memset(S[:], 0.0), w=['S'])
        for j in range(NCORES - 1):
            ul = Uld[j % 2]; ut = 'Uld%d' % (j % 2)
            P.dma('sync', ul[:], hgU_all[j, h], r=[], w=[ut])
            P.add('vector', lambda e, ul=ul, j=j: e.tensor_scalar(ul[:], ul[:], cmk[:, j:j + 1], None, ALU.mult), r=[ut, 'cmk'], w=[ut])
            P.add('vector', lambda e, ul=ul, j=j, h=h: e.scalar_tensor_tensor(S[:], S[:], Dp[:, j, h:h + 1], ul[:], ALU.mult, ALU.add),
                  r=['S', 'Dp', ut], w=['S'])
        i = wcnt[0] % 2
        wcnt[0] += 1
        wb = wblk[i]; wt = 'wblk%d' % i
        P.dma('gpsimd', wb[:, :, 0:128], w_in[:, h * 128:(h + 1) * 128].rearrange('(kt p) n -> p kt n', p=128), r=[], w=[wt])
        P.dma('gpsimd', wb[:, :, 128:256], w_in[:, 1024 + h * 128:1024 + (h + 1) * 128].rearrange('(kt p) n -> p kt n', p=128), r=[], w=[wt])
        for which in range(2):
            for half in range(2):
                bank = nextbank()
                for kt in range(KT):
                    P.mm(psum[bank][:, :], wb[:, kt, which * 128:(which + 1) * 128], xT[:, kt, half * 512:(half + 1) * 512],
                         kt == 0, kt == KT - 1, r=['xT', wt], w=['ps%d' % bank])
                if which == 0:
                    P.add('vector', lambda e, bank=bank, half=half: e.tensor_copy(qT[:, half * 512:(half + 1) * 512], psum[bank][:, :]),
                          r=['ps%d' % bank], w=['qT'])
                else:
                    P.add('scalar', lambda e, bank=bank, half=half: e.activation(fT[:, half * 512:(half + 1) * 512], psum[bank][:, :], AF.Sigmoid),
                          r=['ps%d' % bank], w=['fT'])
        P.add('vector', lambda e, h=h: e.tensor_scalar(fT[:], fT[:], oml[:, h:h + 1], lb[:, h:h + 1], ALU.mult, ALU.add), r=['fT', 'oml', 'lb'], w=['fT'])
        P.add('scalar', lambda e: e.activation(lgf[:], fT[:], AF.Ln), r=['fT'], w=['lgf'])
        P.add('gpsimd', lambda e: e.tensor_scalar(kT_[:], fT[:], -1.0, 1.0, ALU.mult, ALU.add), r=['fT'], w=['kT_'])
        P.add('vector', lambda e: e.tensor_tensor_scan(bT[:], mask01[:], lgf[:], 0.0, ALU.mult, ALU.add), r=['mask01', 'lgf'], w=['bT'])
        b3 = bT[:].rearrange('p (c s) -> p c s', s=64)
        P.add('vector', lambda e: e.tensor_scalar(rr[:], b3[:, :, 63], 0.5, None, ALU.mult), r=['bT'], w=['rr'])
        P.add('vector', lambda e: e.tensor_reduce(bsum[:], b3[:, :, 63], AX.X, ALU.add), r=['bT'], w=['bsum'])
        P.add('scalar', lambda e, h=h: e.activation(Dtot[:, h:h + 1], bsum[:], AF.Exp), r=['bsum'], w=['Dtot'])
        P.add('scalar', lambda e: e.activation(er[:], rr[:], AF.Exp), r=['rr'], w=['er'])
        P.add('vector', lambda e: e.tensor_tensor(e2r[:], er[:], er[:], ALU.mult), r=['er'], w=['e2r'])
        P.add('vector', lambda e: e.tensor_tensor(b3, b3, bc(rr[:].unsqueeze(2), [128, 16, 64]), ALU.subtract), r=['bT', 'rr'], w=['bT'])
        P.add('scalar', lambda e: e.activation(eq[:], bT[:], AF.Exp), r=['bT'], w=['eq'])
        P.add('scalar', lambda e: e.activation(ek[:], bT[:], AF.Exp, scale=-1.0), r=['bT'], w=['ek'])
        P.add('vector', lambda e: e.tensor_tensor(qt_b[:], qT[:], eq[:], ALU.mult), r=['qT', 'eq'], w=['qt_b'])
        P.add('gpsimd', lambda e: e.tensor_tensor(kt_b[:], kT_[:], ek[:], ALU.mult), r=['kT_', 'ek'], w=['kt_b'])
        bank = nextbank()
        for n in range(NT):
            P.tr(psb(bank)[:, n * 128:(n + 1) * 128], kt_b[:, n * 128:(n + 1) * 128], ident_b[:], r=['kt_b', 'ident_b'], w=['ps%d' % bank])
        P.add('vector', lambda e, bank=bank: e.tensor_copy(ktok[:], psb(bank).rearrange('p (n d) -> p n d', n=NT)), r=['ps%d' % bank], w=['ktok'])
        for n in range(NT):
            asb = att_sb[n % 2]; at = 'att_sb%d' % (n % 2)
            b_att = nextbank(); b_U = nextbank(); b_o = nextbank()
            for cc in range(2):
                c = 2 * n + cc
                tok = slice(c * 64, (c + 1) * 64)
                pr = slice(cc * 64, (cc + 1) * 64)
                P.mm(psum[b_att][pr, cc * 64:(cc + 1) * 64], kt_b[:, tok], qt_b[:, tok], True, True, r=['kt_b', 'qt_b'], w=['ps%d' % b_att])
                P.add('vector', lambda e, pr=pr, cc=cc, asb=asb, b_att=b_att: e.tensor_tensor(
                    asb[pr, cc * 64:(cc + 1) * 64], psum[b_att][pr, cc * 64:(cc + 1) * 64], tri01[pr, :], ALU.mult),
                    r=['ps%d' % b_att, 'tri01'], w=[at])
                P.mm(psum[b_U][:, cc * 128:(cc + 1) * 128], ktok[pr, n, :], v_tok[pr, n, h * 128:(h + 1) * 128], True, True,
                     r=['ktok', 'v_tok%d' % n], w=['ps%d' % b_U])
            P.mm(psum[b_o][:, 0:128], asb[:, :], v_tok[:, n, h * 128:(h + 1) * 128], True, False, r=[at, 'v_tok%d' % n], w=['ps%d' % b_o])
            for cc in range(2):
                c = 2 * n + cc
                tok = slice(c * 64, (c + 1) * 64)
                pr = slice(cc * 64, (cc + 1) * 64)
                sp = Spb[c % 4]; spt = 'Spb%d' % (c % 4)
                P.add('vector', lambda e, sp=sp, c=c: e.tensor_scalar(sp[:], S[:], er[:, c:c + 1], None, ALU.mult), r=['S', 'er'], w=[spt])
                P.mm(psum[b_o][pr, 0:128], qt_b[:, tok], sp[:, :], False, cc == 1, r=['qt_b', spt], w=['ps%d' % b_o])
                P.add('vector', lambda e, c=c, cc=cc, b_U=b_U: e.tensor_scalar(Utmp[:], psum[b_U][:, cc * 128:(cc + 1) * 128], er[:, c:c + 1], None, ALU.mult),
                      r=['ps%d' % b_U, 'er'], w=['Utmp'])
                P.add('vector', lambda e, c=c: e.scalar_tensor_tensor(S[:], S[:], e2r[:, c:c + 1], Utmp[:], ALU.mult, ALU.add),
                      r=['S', 'e2r', 'Utmp'], w=['S'])
            P.add('gpsimd', lambda e: e.memset(ss[:], 0.0), w=['ss'])
            P.add('scalar', lambda e, b_o=b_o: e.activation(junk[:], psum[b_o][:, 0:128], AF.Square, accum_out=ss[:, 0:1]), r=['ps%d' % b_o, 'ss'], w=['junk', 'ss'])
            P.add('scalar', lambda e: e.activation(ss[:, 1:2], ss[:, 0:1], AF.Ln, bias=RMS_EPS, scale=1.0 / 128.0), r=['ss'], w=['ss'])
            P.add('scalar', lambda e: e.activation(ss[:, 1:2], ss[:, 1:2], AF.Exp, scale=-0.5), r=['ss'], w=['ss'])
            P.add('vector', lambda e, b_o=b_o, h=h: e.scalar_tensor_tensor(otmp[:], psum[b_o][:, 0:128], ss[:, 1:2], ngb[:, h * 128:(h + 1) * 128], ALU.mult, ALU.mult),
                  r=['ps%d' % b_o, 'ss', 'ngb'], w=['otmp'])
            P.add('gpsimd', lambda e, n=n, h=h: e.tensor_tensor(oa_tok[:, n, h * 128:(h + 1) * 128], otmp[:], sg_tok[:, n, h * 128:(h + 1) * 128], ALU.mult),
                  r=['otmp', 'sg_tok%d' % n], w=['oa_tok%d' % n])
        P.dma('sync', hgU_out[h], S[:], r=['S'], w=['hgU_out'])
    P.dma('sync', hgD_out, Dtot[:], r=['Dtot'], w=['hgD_out'])
    if dbg:
        for n in range(NT):
            P.add('vector', lambda e, n=n: e.tensor_copy(xload[n % 2][:, 0:1024], oa_tok[:, n, :]), r=['oa_tok%d' % n], w=['xload%d' % (n % 2)])
            P.dma('sync', dbgo['oa'][n * 128:(n + 1) * 128, :], xload[n % 2][:, 0:1024], r=['xload%d' % (n % 2)], w=['dbg_oa'])
    if phases <= 2:
        P.emit()
        return nc

    P.barrier()
    sb_off[0] = mark_persist
    wblk = [sb('wblk%d' % i, [128, KT, 512], BF16) for i in range(2)]
    uT = sb('uT', [128, 8, T], F32)
    yT = sb('yT', [128, 8, T], BF16)
    Xr = [sb('Xr%d' % i, [128, T], F32) for i in range(2)]
    Xi = [sb('Xi%d' % i, [128, T], F32) for i in range(2)]
    ytmp = sb('ytmp', [128, 512], F32)
    sgm = sb('sgm', [128, 512], F32)
    pl = sb('pl', [32, 3, 128], F32)
    lst2 = sb('lst2', [32, 2], F32)
    sp = {}
    for nm in ['lamr', 'lami', 'lst', 'lr', 'dt', 'lrdt', 'lidt', 'mag', 't1', 't2', 'sn', 'cs', 'ar', 'ai', 'den', 'nr',
               'zr', 'zi', 'xinr', 'xini', 'axr', 'axi', 'Ar', 'Ai', 'tA', 'tB', 'tC', 'tD']:
        sp[nm] = sb('s5_' + nm, [128, 32], F32)
    pwr = sb('pwr', [128, 11, 32], F32)
    pwi = sb('pwi', [128, 11, 32], F32)
    npwi = sb('npwi', [128, 11, 32], F32)
    bre = sb('bre', [128, 32, 16], F32)
    bim = sb('bim', [128, 32, 16], F32)
    bbr = sb('bbr', [128, 32, 16], F32)
    bbi = sb('bbi', [128, 32, 16], F32)
    btmp = sb('btmp', [128, 32, 16], F32)
    Cld = [sb('Cld%d' % i, [128, 8, 64], F32) for i in range(2)]
    Cdup = sb('Cdup', [128, 2, 64], F32)
    dld = sb('dld', [8, 128], F32)
    dvec = sb('dvec', [128, 8], F32)
    Xe_ld = [sb('Xe_ld%d' % i, [128, 64], F32) for i in range(2)]
    Xend = sb('Xend', [128, 64], F32)
    srcB = [sb('srcB%d' % j, [128, 128], F32) for j in range(4)]
    LBr = [sb('LBr%d' % j, [128, 128], F32) for j in range(4)]
    LBi = [sb('LBi%d' % j, [128, 128], F32) for j in range(4)]
    LCr = [sb('LCr%d' % j, [128, 128], F32) for j in range(4)]
    LCi = [sb('LCi%d' % j, [128, 128], F32) for j in range(4)]
    for j in range(4):
        for tl, nm in ((srcB, 'srcB'), (LCr, 'LCr'), (LCi, 'LCi')):
            P.add('gpsimd', lambda e, t=tl[j]: e.memset(t[:], 0.0), w=['%s%d' % (nm, j)])

    V = lambda eng, fn, r, w: P.add(eng, fn, r=r, w=w)

    def tt(out, a, b, op, r, w, eng='vector'):
        P.add(eng, lambda e: e.tensor_tensor(out, a, b, op), r=r, w=w)

    def ts(out, a, s1, s2, op0, op1, r, w, eng='vector'):
        if op1 is None:
            P.add(eng, lambda e: e.tensor_scalar(out, a, s1, None, op0), r=r, w=w)
        else:
            P.add(eng, lambda e: e.tensor_scalar(out, a, s1, s2, op0, op1), r=r, w=w)

    def stt(out, a, sc, b, op0, op1, r, w, eng='vector'):
        P.add(eng, lambda e: e.scalar_tensor_tensor(out, a, sc, b, op0, op1), r=r, w=w)

    def act(out, a, func, r, w, **kw):
        P.add('scalar', lambda e: e.activation(out, a, func, **kw), r=r, w=w)

    P.dma('sync', pl[:, 0, :], lam_re, r=[], w=['pl'])
    P.dma('sync', pl[:, 1, :], lam_im, r=[], w=['pl'])
    P.dma('sync', lst2[:], log_step, r=[], w=['lst2'])
    P.add('vector', lambda e: e.tensor_copy(pl[:, 2, :].rearrange('p (a b) -> p a b', a=2), bc(lst2[:].unsqueeze(2), [32, 2, 64])), r=['lst2', 'pl'], w=['pl'])
    for i, nm in enumerate(['lamr', 'lami', 'lst']):
        P.tr(psum[4][:, i * 32:(i + 1) * 32], pl[:, i, :], ident_f[0:32, 0:32], r=['pl', 'ident_f'], w=['ps4'])
        P.add('vector', lambda e, i=i, nm=nm: e.tensor_copy(sp[nm][:], psum[4][:, i * 32:(i + 1) * 32]), r=['ps4'], w=[nm])
    P.dma('sync', dld[:], s5_d, r=[], w=['dld'])
    P.tr(psum[4][:, 128:136], dld[:], ident_f[0:8, 0:8], r=['dld', 'ident_f'], w=['ps4'])
    P.add('vector', lambda e: e.tensor_copy(dvec[:], psum[4][:, 128:136]), r=['ps4'], w=['dvec'])
    P.dma('sync', bre[:], b_re.rearrange('(gp g2) p n -> (g2 p) gp n', g2=2), r=[], w=['bre'])
    P.dma('sync', bim[:], b_im.rearrange('(gp g2) p n -> (g2 p) gp n', g2=2), r=[], w=['bim'])
    P.dma('sync', Cld[0][:], c_re.rearrange('(ct q) p -> q ct p', q=128), r=[], w=['Cld0'])
    P.dma('sync', Cld[1][:], c_im.rearrange('(ct q) p -> q ct p', q=128), r=[], w=['Cld1'])

    A = lambda nm: sp[nm][:]
    ts(A('lr'), A('lamr'), -1e-4, None, ALU.min, None, ['lamr'], ['lr'])
    act(A('dt'), A('lst'), AF.Exp, ['lst'], ['dt'])
    tt(A('lrdt'), A('lr'), A('dt'), ALU.mult, ['lr', 'dt'], ['lrdt'])
    tt(A('lidt'), A('lami'), A('dt'), ALU.mult, ['lami', 'dt'], ['lidt'])
    act(A('mag'), A('lrdt'), AF.Exp, ['lrdt'], ['mag'])
    ts(A('t1'), A('lidt'), 1.0 / TWO_PI, MAGIC, ALU.mult, ALU.add, ['lidt'], ['t1'])
    ts(A('t1'), A('t1'), -MAGIC, None, ALU.add, None, ['t1'], ['t1'])
    stt(A('t1'), A('t1'), -TWO_PI, A('lidt'), ALU.mult, ALU.add, ['t1', 'lidt'], ['t1'])
    act(A('sn'), A('t1'), AF.Sin, ['t1'], ['sn'])
    ts(A('t2'), A('lidt'), math.pi / 2, None, ALU.add, None, ['lidt'], ['t2'])
    ts(A('tA'), A('t2'), 1.0 / TWO_PI, MAGIC, ALU.mult, ALU.add, ['t2'], ['tA'])
    ts(A('tA'), A('tA'), -MAGIC, None, ALU.add, None, ['tA'], ['tA'])
    stt(A('tA'), A('tA'), -TWO_PI, A('t2'), ALU.mult, ALU.add, ['tA', 't2'], ['tA'])
    act(A('cs'), A('tA'), AF.Sin, ['tA'], ['cs'])
    tt(A('ar'), A('mag'), A('cs'), ALU.mult, ['mag', 'cs'], ['ar'])
    tt(A('ai'), A('mag'), A('sn'), ALU.mult, ['mag', 'sn'], ['ai'])
    tt(A('den'), A('lr'), A('lr'), ALU.mult, ['lr'], ['den'])
    tt(A('tB'), A('lami'), A('lami'), ALU.mult, ['lami'], ['tB'])
    tt(A('den'), A('den'), A('tB'), ALU.add, ['den', 'tB'], ['den'])
    P.add('vector', lambda e: e.reciprocal(A('den'), A('den')), r=['den'], w=['den'])
    ts(A('nr'), A('ar'), -1.0, None, ALU.add, None, ['ar'], ['nr'])
    tt(A('tB'), A('nr'), A('lr'), ALU.mult, ['nr', 'lr'], ['tB'])
    tt(A('tC'), A('ai'), A('lami'), ALU.mult, ['ai', 'lami'], ['tC'])
    tt(A('tB'), A('tB'), A('tC'), ALU.add, ['tB', 'tC'], ['tB'])
    tt(A('zr'), A('tB'), A('den'), ALU.mult, ['tB', 'den'], ['zr'])
    tt(A('tB'), A('ai'), A('lr'), ALU.mult, ['ai', 'lr'], ['tB'])
    tt(A('tC'), A('nr'), A('lami'), ALU.mult, ['nr', 'lami'], ['tC'])
    tt(A('tB'), A('tB'), A('tC'), ALU.subtract, ['tB', 'tC'], ['tB'])
    tt(A('zi'), A('tB'), A('den'), ALU.mult, ['tB', 'den'], ['zi'])
    P.add('vector', lambda e: e.tensor_copy(pwr[:, 0, :], A('ar')), r=['ar'], w=['pwr'])
    P.add('vector', lambda e: e.tensor_copy(pwi[:, 0, :], A('ai')), r=['ai'], w=['pwi'])
    for k in range(10):
        tt(A('tB'), pwr[:, k, :], pwr[:, k, :], ALU.mult, ['pwr'], ['tB'])
        tt(A('tC'), pwi[:, k, :], pwi[:, k, :], ALU.mult, ['pwi'], ['tC'])
        stt(pwi[:, k + 1, :], pwr[:, k, :], 2.0, pwi[:, k, :], ALU.mult, ALU.mult, ['pwr', 'pwi'], ['pwi'])
        tt(pwr[:, k + 1, :], A('tB'), A('tC'), ALU.subtract, ['tB', 'tC'], ['pwr'])
    ts(npwi[:], pwi[:], -1.0, None, ALU.mult, None, ['pwi'], ['npwi'])
    P.add('gpsimd', lambda e: e.memset(A('xinr'), 0.0), w=['xinr'])
    P.add('gpsimd', lambda e: e.memset(A('xini'), 0.0), w=['xini'])
    for j in range(NCORES - 1):
        xl = Xe_ld[j % 2]; xt_ = 'Xe_ld%d' % (j % 2)
        P.dma('sync', xl[:], s5X_all[j], r=[], w=[xt_])
        m = cmk[:, j:j + 1]
        ts(A('Ar'), pwr[:, 10, :], -1.0, None, ALU.add, None, ['pwr'], ['Ar'])
        ts(A('Ar'), A('Ar'), m, 1.0, ALU.mult, ALU.add, ['Ar', 'cmk'], ['Ar'])
        ts(A('Ai'), pwi[:, 10, :], m, None, ALU.mult, None, ['pwi', 'cmk'], ['Ai'])
        ts(xl[:], xl[:], m, None, ALU.mult, None, [xt_, 'cmk'], [xt_])
        tt(A('tA'), A('Ar'), A('xinr'), ALU.mult, ['Ar', 'xinr'], ['tA'])
        tt(A('tB'), A('Ai'), A('xini'), ALU.mult, ['Ai', 'xini'], ['tB'])
        tt(A('tC'), A('Ar'), A('xini'), ALU.mult, ['Ar', 'xini'], ['tC'])
        tt(A('tD'), A('Ai'), A('xinr'), ALU.mult, ['Ai', 'xinr'], ['tD'])
        tt(A('tA'), A('tA'), A('tB'), ALU.subtract, ['tA', 'tB'], ['tA'])
        tt(A('tC'), A('tC'), A('tD'), ALU.add, ['tC', 'tD'], ['tC'])
        tt(A('xinr'), A('tA'), xl[:, 0:32], ALU.add, ['tA', xt_], ['xinr'])
        tt(A('xini'), A('tC'), xl[:, 32:64], ALU.add, ['tC', xt_], ['xini'])
    tt(A('tA'), A('ar'), A('xinr'), ALU.mult, ['ar', 'xinr'], ['tA'])
    tt(A('tB'), A('ai'), A('xini'), ALU.mult, ['ai', 'xini'], ['tB'])
    tt(A('axr'), A('tA'), A('tB'), ALU.subtract, ['tA', 'tB'], ['axr'])
    tt(A('tA'), A('ar'), A('xini'), ALU.mult, ['ar', 'xini'], ['tA'])
    tt(A('tB'), A('ai'), A('xinr'), ALU.mult, ['ai', 'xinr'], ['tB'])
    tt(A('axi'), A('tA'), A('tB'), ALU.add, ['tA', 'tB'], ['axi'])
    zrb = bc(sp['zr'][:].unsqueeze(2), [128, 32, 16])
    zib = bc(sp['zi'][:].unsqueeze(2), [128, 32, 16])
    tt(bbr[:], bre[:], zrb, ALU.mult, ['bre', 'zr'], ['bbr'])
    tt(btmp[:], bim[:], zib, ALU.mult, ['bim', 'zi'], ['btmp'])
    tt(bbr[:], bbr[:], btmp[:], ALU.subtract, ['bbr', 'btmp'], ['bbr'])
    tt(bbi[:], bim[:], zrb, ALU.mult, ['bim', 'zr'], ['bbi'])
    tt(btmp[:], bre[:], zib, ALU.mult, ['bre', 'zi'], ['btmp'])
    tt(bbi[:], bbi[:], btmp[:], ALU.add, ['bbi', 'btmp'], ['bbi'])

    for blk in range(2):
        wb, wt = load_wblk(w_in, 4096 + blk * 512)
        for ntl in range(4):
            ct = blk * 4 + ntl
            for half in range(2):
                bank = nextbank(0, 4)
                for kt in range(KT):
                    P.mm(psum[bank][:, :], wb[:, kt, ntl * 128:(ntl + 1) * 128], xT[:, kt, half * 512:(half + 1) * 512],
                         kt == 0, kt == KT - 1, r=['xT', wt], w=['ps%d' % bank])
                P.add('scalar', lambda e, bank=bank, ct=ct, half=half: e.copy(uT[:, ct, half * 512:(half + 1) * 512], psum[bank][:, :]),
                      r=['ps%d' % bank], w=['uT%d' % ct])

    P.barrier()
    wb1f = wblk[1][:].rearrange('p a b -> p (a b)').bitcast(F32)
    Ms = [(wb1f[:, 0:T], wb1f[:, T:2 * T]), (wb1f[:, 2 * T:3 * T], wb1f[:, 3 * T:4 * T])]
    wb0f = wblk[0][:].rearrange('p a b -> p (a b)').bitcast(F32)
    Xr2 = [wb0f[:, 0:T], wb0f[:, T:2 * T]]
    Xi2 = [wb0f[:, 2 * T:3 * T], wb0f[:, 3 * T:4 * T]]
    XRs = [Xr, Xr2]
    XIs = [Xi, Xi2]
    for ct in range(8):
        for j in range(4):
            gp = 4 * ct + j
            for (bb, LB, nm) in ((bbr, LBr, 'LBr'), (bbi, LBi, 'LBi')):
                for g2 in range(2):
                    pr = slice(g2 * 64, (g2 + 1) * 64)
                    c0 = (2 * j + g2) * 16
                    P.add('vector', lambda e, bb=bb, pr=pr, c0=c0, j=j, gp=gp: e.tensor_copy(srcB[j][pr, c0:c0 + 16], bb[pr, gp, :]),
                          r=['bbr', 'bbi'], w=['srcB%d' % j])
                P.tr(psum[4][:, 0:128], srcB[j][:], ident_f[:], r=['srcB%d' % j, 'ident_f'], w=['ps4'])
                P.add('vector', lambda e, LB=LB, j=j: e.tensor_copy(LB[j][:], psum[4][:, 0:128]), r=['ps4'], w=['%s%d' % (nm, j)])
        for ri, (LC, nm) in enumerate(((LCr, 'LCr'), (LCi, 'LCi'))):
            P.add('vector', lambda e, ri=ri, ct=ct: e.tensor_copy(Cdup[:], bc(Cld[ri][:, ct, :].unsqueeze(1), [128, 2, 64])), r=['Cld%d' % ri], w=['Cdup'])
            P.tr(psum[5][:, 0:128], Cdup[:].rearrange('p a b -> p (a b)'), ident_f[:], r=['Cdup', 'ident_f'], w=['ps5'])
            for j in range(4):
                for g2 in range(2):
                    pr = slice(g2 * 64, (g2 + 1) * 64)
                    c0 = (2 * j + g2) * 16
                    if ri == 0:
                        P.add('vector', lambda e, LC=LC, j=j, pr=pr, c0=c0: e.tensor_copy(LC[j][pr, c0:c0 + 16], psum[5][pr, c0:c0 + 16]),
                              r=['ps5'], w=['%s%d' % (nm, j)])
                    else:
                        P.add('vector', lambda e, LC=LC, j=j, pr=pr, c0=c0: e.tensor_scalar(LC[j][pr, c0:c0 + 16], psum[5][pr, c0:c0 + 16], -1.0, None, ALU.mult),
                              r=['ps5'], w=['%s%d' % (nm, j)])
        for jp in range(2):
            js = (2 * jp, 2 * jp + 1)
            for si, j in enumerate(js):
                gp = 4 * ct + j
                XR, XI = XRs[si], XIs[si]
                t0r, t0i = 'Xr%d_0' % si, 'Xi%d_0' % si
                for half in range(2):
                    hs = slice(half * 512, (half + 1) * 512)
                    b1 = nextbank(0, 4)
                    P.mm(psum[b1][:, :], LBr[j][:], uT[:, ct, hs], True, True, r=['LBr%d' % j, 'uT%d' % ct], w=['ps%d' % b1])
                    P.add('scalar', lambda e, b1=b1, hs=hs, XR=XR: e.copy(XR[0][:, hs], psum[b1][:, :]), r=['ps%d' % b1], w=[t0r])
                    b2 = nextbank(0, 4)
                    P.mm(psum[b2][:, :], LBi[j][:], uT[:, ct, hs], True, True, r=['LBi%d' % j, 'uT%d' % ct], w=['ps%d' % b2])
                    P.add('scalar', lambda e, b2=b2, hs=hs, XI=XI: e.copy(XI[0][:, hs], psum[b2][:, :]), r=['ps%d' % b2], w=[t0i])
                tt(XR[0][:, 0:1], XR[0][:, 0:1], sp['axr'][:, gp:gp + 1], ALU.add, [t0r, 'axr'], [t0r])
                tt(XI[0][:, 0:1], XI[0][:, 0:1], sp['axi'][:, gp:gp + 1], ALU.add, [t0i, 'axi'], [t0i], eng='gpsimd')
            for k in (range(10) if summary else []):
                nn = T >> (k + 1)
                s_, d_ = k % 2, 1 - k % 2
                for si, j in enumerate(js):
                    gp = 4 * ct + j
                    XR, XI = XRs[si], XIs[si]
                    mA, mB = Ms[si]
                    tmA, tmB = 'mA%d' % si, 'mB%d' % si
                    pr_ = pwr[:, k, gp:gp + 1]; pi_ = pwi[:, k, gp:gp + 1]; npi_ = npwi[:, k, gp:gp + 1]
                    tr_s, ti_s, tr_d, ti_d = 'Xr%d_%d' % (si, s_), 'Xi%d_%d' % (si, s_), 'Xr%d_%d' % (si, d_), 'Xi%d_%d' % (si, d_)
                    xr2 = XR[s_][:, 0:2 * nn].rearrange('p (j two) -> p j two', two=2)
                    xi2 = XI[s_][:, 0:2 * nn].rearrange('p (j two) -> p j two', two=2)
                    stt(XR[d_][:, 0:nn], xr2[:, :, 0], pr_, xr2[:, :, 1], ALU.mult, ALU.add, [tr_s, 'pwr'], [tr_d])
                    stt(XR[d_][:, 0:nn], xi2[:, :, 0], npi_, XR[d_][:, 0:nn], ALU.mult, ALU.add, [ti_s, 'npwi', tr_d], [tr_d])
                    stt(XI[d_][:, 0:nn], xi2[:, :, 0], pr_, xi2[:, :, 1], ALU.mult, ALU.add, [ti_s, 'pwr'], [ti_d])
                    stt(XI[d_][:, 0:nn], xr2[:, :, 0], pi_, XI[d_][:, 0:nn], ALU.mult, ALU.add, [tr_s, 'pwi', ti_d], [ti_d])
            for si, j in (list(enumerate(js)) if summary else []):
                gp = 4 * ct + j
                XR, XI = XRs[si], XIs[si]
                P.add('vector', lambda e, gp=gp, XR=XR: e.tensor_copy(Xend[:, gp:gp + 1], XR[0][:, 0:1]), r=['Xr%d_0' % si], w=['Xend'])
                P.add('vector', lambda e, gp=gp, XI=XI: e.tensor_copy(Xend[:, 32 + gp:33 + gp], XI[0][:, 0:1]), r=['Xi%d_0' % si], w=['Xend'])
            if summary:
                continue
            for k in range(10):
                d = 1 << k
                s_, d_ = k % 2, 1 - k % 2
                for si, j in enumerate(js):
                    gp = 4 * ct + j
                    XR, XI = XRs[si], XIs[si]
                    mA, mB = Ms[si]
                    tmA, tmB = 'mA%d' % si, 'mB%d' % si
                    pr_ = pwr[:, k, gp:gp + 1]; pi_ = pwi[:, k, gp:gp + 1]; npi_ = npwi[:, k, gp:gp + 1]
                    xr_s, xi_s, xr_d, xi_d = XR[s_], XI[s_], XR[d_], XI[d_]
                    tr_s, ti_s, tr_d, ti_d = 'Xr%d_%d' % (si, s_), 'Xi%d_%d' % (si, s_), 'Xr%d_%d' % (si, d_), 'Xi%d_%d' % (si, d_)
                    P.add('vector', lambda e, xr_d=xr_d, xr_s=xr_s, d=d: e.tensor_copy(xr_d[:, 0:d], xr_s[:, 0:d]), r=[tr_s], w=[tr_d])
                    stt(xr_d[:, d:T], xr_s[:, 0:T - d], pr_, xr_s[:, d:T], ALU.mult, ALU.add, [tr_s, 'pwr'], [tr_d])
                    stt(xr_d[:, d:T], xi_s[:, 0:T - d], npi_, xr_d[:, d:T], ALU.mult, ALU.add, [ti_s, 'npwi', tr_d], [tr_d])
                    P.add('gpsimd', lambda e, xi_d=xi_d, xi_s=xi_s, d=d: e.tensor_copy(xi_d[:, 0:d], xi_s[:, 0:d]), r=[ti_s], w=[ti_d])
                    act(mA[:, 0:T - d], xi_s[:, 0:T - d], AF.Copy, [ti_s, 'pwr'], [tmA], scale=pr_)
                    act(mB[:, 0:T - d], xr_s[:, 0:T - d], AF.Copy, [tr_s, 'pwi'], [tmB], scale=pi_)
                    tt(xi_d[:, d:T], mA[:, 0:T - d], xi_s[:, d:T], ALU.add, [tmA, ti_s], [ti_d], eng='gpsimd')
                    tt(xi_d[:, d:T], mB[:, 0:T - d], xi_d[:, d:T], ALU.add, [tmB, ti_d], [ti_d], eng='gpsimd')
            for si, j in enumerate(js):
                gp = 4 * ct + j
                XR, XI = XRs[si], XIs[si]
                t0r, t0i = 'Xr%d_0' % si, 'Xi%d_0' % si
                P.add('vector', lambda e, gp=gp, XR=XR: e.tensor_copy(Xend[:, gp:gp + 1], XR[0][:, T - 1:T]), r=[t0r], w=['Xend'])
                P.add('vector', lambda e, gp=gp, XI=XI: e.tensor_copy(Xend[:, 32 + gp:33 + gp], XI[0][:, T - 1:T]), r=[t0i], w=['Xend'])
                for half in range(2):
                    hs = slice(half * 512, (half + 1) * 512)
                    P.mm(psum[6 + half][:, :], LCr[j][:], XR[0][:, hs], j == 0, False, r=['LCr%d' % j, t0r], w=['ps%d' % (6 + half)])
                    P.mm(psum[6 + half][:, :], LCi[j][:], XI[0][:, hs], False, j == 3, r=['LCi%d' % j, t0i], w=['ps%d' % (6 + half)])
        for half in ([] if summary else range(2)):
            hs = slice(half * 512, (half + 1) * 512)
            stt(ytmp[:], uT[:, ct, hs], dvec[:, ct:ct + 1], psum[6 + half][:, :], ALU.mult, ALU.add, ['uT%d' % ct, 'dvec', 'ps%d' % (6 + half)], ['ytmp'])
            act(yT[:, ct, hs], ytmp[:], AF.Gelu_apprx_tanh, ['ytmp'], ['yT'])
    P.dma('sync', s5X_out, Xend[:], r=['Xend'], w=['s5X_out'])
    if summary:
        P.emit()
        return nc
    P.barrier()
    for b in range(2):
        WA, wta = wblk[0], 'wblk0'
        WG, wtg = wblk[1], 'wblk1'
        P.dma('gpsimd', WA[:, 0:8, :], w_glu[:, b * 512:(b + 1) * 512].rearrange('(kt p) n -> p kt n', p=128), r=[], w=[wta])
        P.dma('gpsimd', WG[:, 0:8, :], w_glu[:, 1024 + b * 512:1024 + (b + 1) * 512].rearrange('(kt p) n -> p kt n', p=128), r=[], w=[wtg])
        for ntl in range(4):
            nt = 4 * b + ntl
            for half in range(2):
                hs = slice(half * 512, (half + 1) * 512)
                ba = nextbank(0, 6); bg = nextbank(0, 6)
                for kt in range(8):
                    P.mm(psum[ba][:, :], WA[:, kt, ntl * 128:(ntl + 1) * 128], yT[:, kt, hs], kt == 0, kt == 7, r=['yT', wta], w=['ps%d' % ba])
                for kt in range(8):
                    P.mm(psum[bg][:, :], WG[:, kt, ntl * 128:(ntl + 1) * 128], yT[:, kt, hs], kt == 0, kt == 7, r=['yT', wtg], w=['ps%d' % bg])
                act(sgm[:], psum[bg][:, :], AF.Sigmoid, ['ps%d' % bg], ['sgm'])
                tt(obT[:, nt, hs], psum[ba][:, :], sgm[:], ALU.mult, ['ps%d' % ba, 'sgm'], ['obT'])
    if dbg:
        for nt in range(8):
            P.add('vector', lambda e, nt=nt: e.tensor_copy(Xr[0][:], obT[:, nt, :]), r=['obT'], w=['Xr0_0'])
            P.dma('sync', dbgo['obT'][nt * 128:(nt + 1) * 128, :], Xr[0][:], r=['Xr0_0'], w=['dbg_obT'])
    if phases <= 3:
        P.emit()
        return nc

    P.barrier()
    sb_off[0] = mark_persist
    mergedT = sb('mergedT', [128, KT, T], BF16)
    mark4 = sb_off[0]
    oaT = sb('oaT', [128, 8, T], BF16)
    wA = [sb('wA%d' % i, [128, 8, 512], BF16) for i in range(2)]
    wB = [sb('wB%d' % i, [128, 8, 512], BF16) for i in range(2)]
    wGA = sb('wGA', [128, KT, 512], BF16)
    wGB = sb('wGB', [128, KT, 512], BF16)
    sga = sb('sga', [128, 512], F32)
    sgb = sb('sgb', [128, 512], F32)
    mm1 = sb('mm1', [128, 512], F32)
    mm2 = sb('mm2', [128, 512], F32)
    for n in range(NT):
        bank = nextbank()
        for kt in range(8):
            P.tr(psb(bank)[:, kt * 128:(kt + 1) * 128], oa_tok[:, n, kt * 128:(kt + 1) * 128], ident_b[:], r=['oa_tok', 'ident_b'], w=['ps%d' % bank])
        P.add('vector', lambda e, bank=bank, n=n: e.tensor_copy(oaT[:, :, n * 128:(n + 1) * 128], psb(bank).rearrange('p (j t) -> p j t', j=8)),
              r=['ps%d' % bank], w=['oaT'])
    for cb in range(4):
        i = cb % 2
        P.dma('gpsimd', wA[i][:], w_up_a[:, cb * 512:(cb + 1) * 512].rearrange('(kt p) n -> p kt n', p=128), r=[], w=['wA%d' % i])
        P.dma('gpsimd', wB[i][:], w_up_b[:, cb * 512:(cb + 1) * 512].rearrange('(kt p) n -> p kt n', p=128), r=[], w=['wB%d' % i])
        P.dma('gpsimd', wGA[:], w_in[:, 5120 + cb * 512:5120 + (cb + 1) * 512].rearrange('(kt p) n -> p kt n', p=128), r=[], w=['wGA'])
        P.dma('gpsimd', wGB[:], w_in[:, 7168 + cb * 512:7168 + (cb + 1) * 512].rearrange('(kt p) n -> p kt n', p=128), r=[], w=['wGB'])
        for ntl in range(4):
            nt = 4 * cb + ntl
            ns = slice(ntl * 128, (ntl + 1) * 128)
            for half in range(2):
                hs = slice(half * 512, (half + 1) * 512)
                ba, bb_, bga, bgb = nextbank(), nextbank(), nextbank(), nextbank()
                for kt in range(8):
                    P.mm(psum[ba][:, :], wA[i][:, kt, ns], oaT[:, kt, hs], kt == 0, kt == 7, r=['wA%d' % i, 'oaT'], w=['ps%d' % ba])
                for kt in range(8):
                    P.mm(psum[bb_][:, :], wB[i][:, kt, ns], obT[:, kt, hs], kt == 0, kt == 7, r=['wB%d' % i, 'obT'], w=['ps%d' % bb_])
                for kt in range(KT):
                    P.mm(psum[bga][:, :], wGA[:, kt, ns], xT[:, kt, hs], kt == 0, kt == KT - 1, r=['wGA', 'xT'], w=['ps%d' % bga])
                for kt in range(KT):
                    P.mm(psum[bgb][:, :], wGB[:, kt, ns], xT[:, kt, hs], kt == 0, kt == KT - 1, r=['wGB', 'xT'], w=['ps%d' % bgb])
                act(sga[:], psum[bga][:, :], AF.Sigmoid, ['ps%d' % bga], ['sga'])
                act(sgb[:], psum[bgb][:, :], AF.Sigmoid, ['ps%d' % bgb], ['sgb'])
                tt(mm1[:], psum[ba][:, :], sga[:], ALU.mult, ['ps%d' % ba, 'sga'], ['mm1'])
                tt(mm2[:], psum[bb_][:, :], sgb[:], ALU.mult, ['ps%d' % bb_, 'sgb'], ['mm2'])
                tt(mergedT[:, nt, hs], mm1[:], mm2[:], ALU.add, ['mm1', 'mm2'], ['mergedT'], eng='gpsimd')
    P.barrier()
    sb_off[0] = mark4
    wo_sb = sb('wo_sb', [128, KT, D], BF16)
    x1b = sb('x1b', [128, D], BF16)
    st4 = sb('st4', [128, 4], F32)
    sb_save = sb_off[0]
    sb_off[0] = reg_oa
    lng = sb('lng', [128, D], F32)
    lnb = sb('lnb', [128, D], F32)
    xres = sb('xres', [128, D], F32)
    h1 = sb('h1', [128, D], F32)
    sb_off[0] = sb_save
    x1T = xT
    for cb in range(4):
        P.dma('gpsimd', wo_sb[:, :, cb * 512:(cb + 1) * 512], w_o[:, cb * 512:(cb + 1) * 512].rearrange('(kt p) n -> p kt n', p=128), r=[], w=['wo_sb%d' % cb])

    def layer_norm(src, gamma_d, beta_d, hbuf, dst_dram, tagp):
        pass

    P.dma('sync', lng[:], ln1_g[0].partition_broadcast(128), r=[], w=['lng'])
    P.dma('sync', lnb[:], ln1_b[0].partition_broadcast(128), r=[], w=['lnb'])

    def ln_rows(hb, ht, gam, bet):
        P.add('vector', lambda e: e.tensor_reduce(st4[:, 0:1], hb[:], AX.X, ALU.add), r=[ht], w=['st4a'])
        ts(st4[:, 1:2], st4[:, 0:1], -1.0 / D, None, ALU.mult, None, ['st4a'], ['st4b'])
        ts(hb[:], hb[:], st4[:, 1:2], None, ALU.add, None, [ht, 'st4b'], [ht])
        P.add('gpsimd', lambda e: e.memset(st4[:, 2:3], 0.0), w=['st4c'])
        P.add('scalar', lambda e: e.activation(junkD[:], hb[:], AF.Square, accum_out=st4[:, 2:3]), r=[ht, 'st4c'], w=['junkD', 'st4c'])
        act(st4[:, 3:4], st4[:, 2:3], AF.Ln, ['st4c'], ['st4d'], bias=LN_EPS, scale=1.0 / D)
        act(st4[:, 3:4], st4[:, 3:4], AF.Exp, ['st4d'], ['st4d'], scale=-0.5)
        stt(hb[:], hb[:], st4[:, 3:4], gam[:], ALU.mult, ALU.mult, [ht, 'st4d', 'lng'], [ht])
        tt(hb[:], hb[:], bet[:], ALU.add, [ht, 'lnb'], [ht], eng='gpsimd')

    junkD = sb('junkD', [128, D], BF16)
    for n in range(NT):
        P.dma('sync', xres[:], x_in[n * 128:(n + 1) * 128, :], r=[], w=['xres'])
        for cb in range(4):
            bank = nextbank()
            for kt in range(KT):
                P.mm(psum[bank][:, :], mergedT[:, kt, n * 128:(n + 1) * 128], wo_sb[:, kt, cb * 512:(cb + 1) * 512], kt == 0, kt == KT - 1,
                     r=['mergedT', 'wo_sb%d' % cb], w=['ps%d' % bank])
            stt(h1[:, cb * 512:(cb + 1) * 512], xres[:, cb * 512:(cb + 1) * 512], ALPHA, psum[bank][:, :], ALU.mult, ALU.add,
                ['xres', 'ps%d' % bank], ['h1'])
        ln_rows(h1, 'h1', lng, lnb)
        P.dma('sync', x1d[n * 128:(n + 1) * 128, :], h1[:], r=['h1'], w=['x1d'])
        if dbg:
            P.dma('sync', dbgo['x1'][n * 128:(n + 1) * 128, :], h1[:], r=['h1'], w=['dbg_x1'])
        P.add('scalar', lambda e: e.copy(x1b[:], h1[:]), r=['h1'], w=['x1b'])
        for half in range(2):
            bank = nextbank()
            for j in range(8):
                kt = half * 8 + j
                P.tr(psb(bank)[:, j * 128:(j + 1) * 128], x1b[:, kt * 128:(kt + 1) * 128], ident_b[:], r=['x1b', 'ident_b'], w=['ps%d' % bank])
            P.add('vector', lambda e, bank=bank, half=half, n=n: e.tensor_copy(
                x1T[:, half * 8:(half + 1) * 8, n * 128:(n + 1) * 128], psb(bank).rearrange('p (j t) -> p j t', j=8)),
                r=['ps%d' % bank], w=['x1T'])
    if phases <= 4:
        P.emit()
        return nc

    P.barrier()
    sb_off[0] = mark_persist
    s_d = nc.dram_tensor('s_d', [T, 16, 128], F32).ap()
    wblk = [sb('wblk%d' % i, [128, KT, 512], BF16) for i in range(2)]
    qTb = sb('qTb', [128, 4, T], F32)
    kld = sb('kld', [128, 16, 128], F32)
    KTt = sb('KTt', [128, 16, 128], F32)
    s_st = [sb('s_st%d' % i, [128, 4, 128], F32) for i in range(2)]
    s_sb = sb('s_sb', [128, 16, 128], F32)
    mr = sb('mr', [128, 128], F32)
    top = sb('top', [128, 16, 16], F32)
    cand = sb('cand', [128, 8, 256], F32)
    cand2 = sb('cand2', [128, 8, 256], F32)
    vals = sb('vals', [128, 8, 16], F32)
    ev = sb('ev', [128, 8, 16], F32)
    Zs = sb('Zs', [128, 8], F32)
    nb = sb('nb', [128, 8], F32)
    Tbuf2 = [sb('Tbuf%d' % i, [128, 16, 128], F32) for i in range(2)]
    Eb2 = [sb('Eb%d' % i, [128, 2048], BF16) for i in range(2)]
    Pm2 = [sb('Pm%d' % i, [128, 2048], BF16) for i in range(2)]
    sb_save = sb_off[0]
    sb_off[0] = reg_oa
    G = sb('G', [128, 16384], BF16)
    sb_off[0] = sb_save

    P.dma('sync', kld[:], keys.rearrange('h i d -> i h d'), r=[], w=['kld'])
    for q4 in range(4):
        bank = nextbank()
        for hl in range(4):
            hh = 4 * q4 + hl
            P.tr(psum[bank][:, hl * 128:(hl + 1) * 128], kld[:, hh, :], ident_f[:], r=['kld', 'ident_f'], w=['ps%d' % bank])
        P.add('vector', lambda e, bank=bank, q4=q4: e.tensor_copy(KTt[:, 4 * q4:4 * q4 + 4, :], psum[bank][:, :].rearrange('p (a b) -> p a b', a=4)),
              r=['ps%d' % bank], w=['KTt'])
    for blk in range(4):
        wb, wt = load_wblk(w_q, blk * 512)
        for hl in range(4):
            for half in range(2):
                hs = slice(half * 512, (half + 1) * 512)
                bank = nextbank()
                for kt in range(KT):
                    P.mm(psum[bank][:, :], wb[:, kt, hl * 128:(hl + 1) * 128], x1T[:, kt, hs], kt == 0, kt == KT - 1, r=[wt, 'x1T'], w=['ps%d' % bank])
                P.add('scalar', lambda e, bank=bank, hl=hl, hs=hs: e.copy(qTb[:, hl, hs], psum[bank][:, :]), r=['ps%d' % bank], w=['qTb%d' % hl])
        for n in range(NT):
            bank = nextbank()
            i2 = n % 2
            for hl in range(4):
                P.mm(psum[bank][:, hl * 128:(hl + 1) * 128], qTb[:, hl, n * 128:(n + 1) * 128], KTt[:, 4 * blk + hl, :], True, True,
                     r=['qTb%d' % hl, 'KTt'], w=['ps%d' % bank])
            P.add('vector', lambda e, bank=bank, i2=i2: e.tensor_copy(s_st[i2][:], psum[bank][:, :].rearrange('p (a b) -> p a b', a=4)),
                  r=['ps%d' % bank], w=['s_st%d' % i2])
            P.dma('sync', s_d[n * 128:(n + 1) * 128, 4 * blk:4 * blk + 4, :], s_st[i2][:], r=['s_st%d' % i2], w=['s_d'])
    P.barrier()
    top4 = top[:].rearrange('p (h two) k -> p h two k', two=2)
    gcnt = 0
    for n in range(NT):
        P.dma('sync', s_sb[:], s_d[n * 128:(n + 1) * 128], r=[], w=['s_sb'])
        for hh in range(16):
            P.add('vector', lambda e, hh=hh: e.max(out=top[:, hh, 0:8], in_=s_sb[:, hh, :]), r=['s_sb'], w=['top'])
            P.add('vector', lambda e, hh=hh: e.match_replace(out=mr[:], in_to_replace=top[:, hh, 0:8], in_values=s_sb[:, hh, :], imm_value=-1e30),
                  r=['s_sb', 'top'], w=['mr'])
            P.add('vector', lambda e, hh=hh: e.max(out=top[:, hh, 8:16], in_=mr[:]), r=['mr'], w=['top'])
        tt(cand[:].rearrange('p h (a b) -> p h a b', a=16), bc(top4[:, :, 0, :].unsqueeze(3), [128, 8, 16, 16]),
           bc(top4[:, :, 1, :].unsqueeze(2), [128, 8, 16, 16]), ALU.add, ['top'], ['cand'])
        for h in range(8):
            P.add('vector', lambda e, h=h: e.max(out=vals[:, h, 0:8], in_=cand[:, h, :]), r=['cand'], w=['vals'])
            P.add('vector', lambda e, h=h: e.match_replace(out=cand2[:, h, :], in_to_replace=vals[:, h, 0:8], in_values=cand[:, h, :], imm_value=-1e30),
                  r=['cand', 'vals'], w=['cand2'])
            P.add('vector', lambda e, h=h: e.max(out=vals[:, h, 8:16], in_=cand2[:, h, :]), r=['cand2'], w=['vals'])
        tt(ev[:], vals[:], bc(vals[:, :, 0:1], [128, 8, 16]), ALU.subtract, ['vals'], ['ev'])
        act(ev[:], ev[:], AF.Exp, ['ev'], ['ev'])
        P.add('vector', lambda e: e.tensor_reduce(Zs[:], ev[:], AX.X, ALU.add), r=['ev'], w=['Zs'])
        act(Zs[:], Zs[:], AF.Ln, ['Zs'], ['Zs'])
        tt(nb[:], Zs[:], vals[:, :, 0], ALU.add, ['Zs', 'vals'], ['nb'])
        ts(nb[:], nb[:], -1.0, None, ALU.mult, None, ['nb'], ['nb'])
        its = [(h, ic) for ic in range(8) for h in range(8)]

        def emitT(idx):
            h, ic = its[idx]
            pi_ = idx % 2
            isl = slice(ic * 16, (ic + 1) * 16)
            tt(Tbuf2[pi_][:], bc(s_sb[:, 2 * h, isl].unsqueeze(2), [128, 16, 128]), bc(s_sb[:, 2 * h + 1, :].unsqueeze(1), [128, 16, 128]),
               ALU.add, ['s_sb'], ['Tbuf%d' % pi_])

        emitT(0)
        for idx, (h, ic) in enumerate(its):
            if idx + 1 < len(its):
                emitT(idx + 1)
            pi_ = idx % 2
            Tb, Ebb, Pmb = Tbuf2[pi_], Eb2[pi_], Pm2[pi_]
            tT, tE, tP = 'Tbuf%d' % pi_, 'Eb%d' % pi_, 'Pm%d' % pi_
            gsl = slice(ic * 2048, (ic + 1) * 2048)
            bset = (ic % 2) * 4
            Tf = Tb[:].rearrange('p a b -> p (a b)')
            act(Ebb[:], Tf, AF.Exp, [tT, 'nb'], [tE], bias=nb[:, h:h + 1])
            stt(Pmb[:], Tf, vals[:, h, 15:16], Ebb[:], ALU.is_ge, ALU.mult, [tT, 'vals', tE], [tP])
            for q4 in range(4):
                P.mm(psum[bset + q4][:, :], ident_b[:], Pmb[:, q4 * 512:(q4 + 1) * 512], h == 0, h == 7, r=['ident_b', tP], w=['ps%d' % (bset + q4)])
            if h == 7:
                for q4 in range(4):
                    act(G[:, ic * 2048 + q4 * 512:ic * 2048 + (q4 + 1) * 512], psum[bset + q4][:, :], AF.Copy, ['ps%d' % (bset + q4)], ['G%d' % ic])
        P.dma('sync', Gd[n * 128:(n + 1) * 128, :], G[:], r=['G%d' % i for i in range(8)], w=['Gd'])

    P.barrier()
    sb_off[0] = mark_persist
    yacc = sb('yacc', [128, NT, D], F32)
    ublk = [sb('ublk%d' % i, [128, 4, D], BF16) for i in range(2)]
    sb_save = sb_off[0]
    sb_off[0] = reg_oa
    vblk = [sb('vblk%d' % i, [128, 4, D], BF16) for i in range(2)]
    sb_off[0] = sb_save
    uTb = sb('uTb', [128, KT, 512], BF16)
    Gs = [sb('Gs%d' % i, [128, 512], BF16) for i in range(2)]
    gl2 = [sb('gl%d' % i, [128, 512], F32) for i in range(2)]
    Wb2 = [sb('Wb%d' % i, [128, 512], BF16) for i in range(2)]
    WT2 = [sb('WT%d' % i, [128, 4, 128], BF16) for i in range(2)]
    cnt = 0
    for eb in range(32):
        i = eb % 2
        P.dma('gpsimd', ublk[i][:], peer_u[eb * 512:(eb + 1) * 512, :].rearrange('(et p) d -> p et d', p=128), r=[], w=['ublk%d' % i])
        P.dma('gpsimd', vblk[i][:], peer_v[eb * 512:(eb + 1) * 512, :].rearrange('(et p) d -> p et d', p=128), r=[], w=['vblk%d' % i])
        for k2 in range(8):
            bank = nextbank()
            for kk in range(2):
                kt = 2 * k2 + kk
                for et in range(4):
                    P.tr(psb(bank)[:, (kk * 4 + et) * 128:(kk * 4 + et + 1) * 128], ublk[i][:, et, kt * 128:(kt + 1) * 128], ident_b[:],
                         r=['ublk%d' % i, 'ident_b'], w=['ps%d' % bank])
            if k2 % 2 == 0:
                P.add('vector', lambda e, bank=bank, k2=k2: e.tensor_copy(uTb[:, 2 * k2:2 * k2 + 2, :], psb(bank).rearrange('p (k e) -> p k e', k=2)),
                      r=['ps%d' % bank], w=['uTb'])
            else:
                P.add('scalar', lambda e, bank=bank, k2=k2: e.copy(uTb[:, 2 * k2:2 * k2 + 2, :], psb(bank).rearrange('p (k e) -> p k e', k=2)),
                      r=['ps%d' % bank], w=['uTb'])
        for n in range(NT):
            gi = cnt % 2
            cnt += 1
            P.dma('sync', Gs[gi][:], Gd[n * 128:(n + 1) * 128, eb * 512:(eb + 1) * 512], r=[], w=['Gs%d' % gi])
            bA = nextbank()
            for kt in range(KT):
                P.mm(psum[bA][:, :], x1T[:, kt, n * 128:(n + 1) * 128], uTb[:, kt, :], kt == 0, kt == KT - 1, r=['x1T', 'uTb'], w=['ps%d' % bA])
            gl, Wb, WT = gl2[gi], Wb2[gi], WT2[gi]
            tg, tw, twt = 'gl%d' % gi, 'Wb%d' % gi, 'WT%d' % gi
            act(gl[:], psum[bA][:, :], AF.Gelu_apprx_tanh, ['ps%d' % bA], [tg])
            tt(Wb[:], gl[:], Gs[gi][:], ALU.mult, [tg, 'Gs%d' % gi], [tw], eng='gpsimd')
            bT_ = nextbank()
            for et in range(4):
                P.tr(psb(bT_)[:, et * 128:(et + 1) * 128], Wb[:, et * 128:(et + 1) * 128], ident_b[:], r=[tw, 'ident_b'], w=['ps%d' % bT_])
            P.add('scalar', lambda e, bT_=bT_, WT=WT: e.copy(WT[:], psb(bT_)[:, 0:512].rearrange('p (a b) -> p a b', a=4)), r=['ps%d' % bT_], w=[twt])
            for db in range(4):
                ds_ = slice(db * 512, (db + 1) * 512)
                bY = nextbank()
                for et in range(4):
                    P.mm(psum[bY][:, :], WT[:, et, :], vblk[i][:, et, ds_], et == 0, et == 3, r=[twt, 'vblk%d' % i], w=['ps%d' % bY])
                if eb == 0:
                    P.add('scalar', lambda e, bY=bY, n=n, ds_=ds_: e.copy(yacc[:, n, ds_], psum[bY][:, :]), r=['ps%d' % bY], w=['yacc%d' % n])
                else:
                    tt(yacc[:, n, ds_], yacc[:, n, ds_], psum[bY][:, :], ALU.add, ['yacc%d' % n, 'ps%d' % bY], ['yacc%d' % n])

    P.barrier()
    st4_2 = sb('st4b', [128, 4], F32)
    junkD_2 = sb('junkD2', [128, D], BF16)
    sb_save = sb_off[0]
    sb_off[0] = reg_oa
    lng_2 = sb('lng2', [128, D], F32)
    lnb_2 = sb('lnb2', [128, D], F32)
    xres_2 = sb('xres2', [128, D], F32)
    h1_2 = sb('h12', [128, D], F32)
    sb_off[0] = sb_save
    P.dma('sync', lng_2[:], ln2_g[0].partition_broadcast(128), r=[], w=['lng'])
    P.dma('sync', lnb_2[:], ln2_b[0].partition_broadcast(128), r=[], w=['lnb'])

    def ln_rows2(hb, ht, gam, bet):
        P.add('vector', lambda e: e.tensor_reduce(st4_2[:, 0:1], hb[:], AX.X, ALU.add), r=[ht], w=['st4a'])
        ts(st4_2[:, 1:2], st4_2[:, 0:1], -1.0 / D, None, ALU.mult, None, ['st4a'], ['st4b'])
        ts(hb[:], hb[:], st4_2[:, 1:2], None, ALU.add, None, [ht, 'st4b'], [ht])
        P.add('gpsimd', lambda e: e.memset(st4_2[:, 2:3], 0.0), w=['st4c'])
        P.add('scalar', lambda e: e.activation(junkD_2[:], hb[:], AF.Square, accum_out=st4_2[:, 2:3]), r=[ht, 'st4c'], w=['junkD', 'st4c'])
        act(st4_2[:, 3:4], st4_2[:, 2:3], AF.Ln, ['st4c'], ['st4d'], bias=LN_EPS, scale=1.0 / D)
        act(st4_2[:, 3:4], st4_2[:, 3:4], AF.Exp, ['st4d'], ['st4d'], scale=-0.5)
        stt(hb[:], hb[:], st4_2[:, 3:4], gam[:], ALU.mult, ALU.mult, [ht, 'st4d', 'lng'], [ht])
        tt(hb[:], hb[:], bet[:], ALU.add, [ht, 'lnb'], [ht], eng='gpsimd')

    for n in range(NT):
        P.dma('sync', xres_2[:], x1d[n * 128:(n + 1) * 128, :], r=[], w=['xres'])
        stt(h1_2[:], xres_2[:], ALPHA, yacc[:, n, :], ALU.mult, ALU.add, ['xres', 'yacc%d' % n], ['h1'])
        ln_rows2(h1_2, 'h1', lng_2, lnb_2)
        P.dma('sync', x_out[n * 128:(n + 1) * 128, :], h1_2[:], r=['h1'], w=['x_out'])
    P.emit()
    return nc


def layer_inputs(z, l, phases=99):
    f = lambda a: np.ascontiguousarray(a, dtype=np.float32)
    m = {
        'w_in': f(z['w_in'][l]),
        'lb_logits': f(z['hgrn_lb_logits']),
        'lmask': np.array([[0.0, float(l >= 1), float(l >= 2), float(l >= 3)]], np.float32),
        'norm_g': f(z['hgrn_norm_g'][l]).reshape(1, 1024),
    }
    if phases > 2:
        m.update({
            'lam_re': f(z['s5_lambda_re'][l]).reshape(32, 128),
            'lam_im': f(z['s5_lambda_im'][l]).reshape(32, 128),
            'log_step': f(z['s5_log_step'][l]).reshape(32, 2),
            'b_re': f(z['s5_b_re'][l]),
            'b_im': f(z['s5_b_im'][l]),
            'c_re': f(z['s5_c_re'][l]).reshape(1024, 64),
            'c_im': f(z['s5_c_im'][l]).reshape(1024, 64),
            's5_d': f(z['s5_d'][l]).reshape(8, 128),
            'w_glu': f(z['s5_w_glu'][l]),
        })
    if phases > 3:
        m.update({
            'w_up_a': f(z['w_up_a'][l]), 'w_up_b': f(z['w_up_b'][l]), 'w_o': f(z['w_o'][l]),
            'ln1_g': f(z['ln1_g'][l]).reshape(1, D), 'ln1_b': f(z['ln1_b'][l]).reshape(1, D),
        })
    if phases > 4:
        m.update({
            'w_q': f(z['peer_w_q'][l]), 'keys': f(z['peer_keys'][l]).reshape(16, 128, 128),
            'peer_u': f(z['peer_u'][l]), 'peer_v': f(z['peer_v'][l]),
            'ln2_g': f(z['ln2_g'][l]).reshape(1, D), 'ln2_b': f(z['ln2_b'][l]).reshape(1, D),
        })
    return m


def kernel(**inputs):
    z = inputs
    f32 = np.float32
    x = np.ascontiguousarray(np.asarray(z['x'])[0], dtype=f32)
    ncA = build(dbg=False, phases=3, summary=True)
    ncB = build(dbg=False, phases=99)
    xs = [np.ascontiguousarray(x[c * T:(c + 1) * T]) for c in range(NCORES)]
    zU = np.zeros((NCORES, 8, 128, 128), f32)
    zD = np.zeros((NCORES, 128, 8), f32)
    zX = np.zeros((NCORES, 128, 64), f32)
    zc = np.zeros((1, NCORES), f32)
    cores = list(range(NCORES))
    for l in range(4):
        mA = layer_inputs(z, l, 3)
        in_maps = [dict(mA, x=xs[c], cmask=zc, hgU_all=zU, hgD_all=zD, s5X_all=zX) for c in cores]
        rA = run_bass_kernel_spmd(ncA, in_maps, core_ids=cores).results
        U = np.ascontiguousarray(np.stack([rA[c]['hgU_out'] for c in cores]), dtype=f32)
        Dd = np.ascontiguousarray(np.stack([rA[c]['hgD_out'] for c in cores]), dtype=f32)
        Xs = np.ascontiguousarray(np.stack([rA[c]['s5X_out'] for c in cores]), dtype=f32)
        mB = layer_inputs(z, l, 99)
        in_maps = [dict(mB, x=xs[c], cmask=(np.arange(NCORES) < c).astype(f32)[None, :],
                        hgU_all=U, hgD_all=Dd, s5X_all=Xs) for c in cores]
        rB = run_bass_kernel_spmd(ncB, in_maps, core_ids=cores).results
        xs = [np.ascontiguousarray(rB[c]['x_out'], dtype=f32) for c in cores]
    return np.concatenate(xs, axis=0)[None].astype(f32)
```

```python
import math
from contextlib import ExitStack
import numpy as np
import concourse.bass as bass
import concourse.mybir as mybir
from concourse.bass_utils import run_bass_kernel_spmd

F32 = mybir.dt.float32
BF16 = mybir.dt.bfloat16
ALU = mybir.AluOpType
AF = mybir.ActivationFunctionType
AX = mybir.AxisListType

ENGS = ['tensor', 'vector', 'scalar', 'gpsimd', 'sync']
NDSEM = 6
NCORES = 8
T = 1024
NT = 8
D = 2048
KT = 16
ALPHA = 8.0 ** 0.25
LN_EPS = 1e-5
RMS_EPS = 1e-6
TWO_PI = 2.0 * math.pi
MAGIC = 12582912.0


class Prog:
    def __init__(self, nc):
        self.nc = nc
        self.ops = {e: [] for e in ENGS}
        self.lastw = {}
        self.readers = {}
        self.ndma = {e: 0 for e in ENGS}
        self.dma_ops = {e: [] for e in ENGS}
        self.pending = {e: [] for e in ENGS}

    def add(self, eng, fn, r=(), w=(), dma=False):
        idx = len(self.ops[eng])
        deps = list(self.pending[eng])
        self.pending[eng] = []
        for t in r:
            if t in self.lastw:
                deps.append(self.lastw[t])
        for t in w:
            if t in self.lastw:
                deps.append(self.lastw[t])
            last = {}
            for ref in self.readers.get(t, ()):
                if self.ops[ref[0]][ref[1]]['dma']:
                    deps.append(ref)
                else:
                    last[ref[0]] = ref
            deps.extend(last.values())
        op = dict(fn=fn, deps=deps, dma=dma, signal=False, eng=eng, idx=idx)
        if dma:
            j = self.ndma[eng]
            self.ndma[eng] += 1
            op['dj'] = j
            if j >= NDSEM:
                deps.append(self.dma_ops[eng][j - NDSEM])
            self.dma_ops[eng].append((eng, idx))
        self.ops[eng].append(op)
        ref = (eng, idx)
        for t in r:
            self.readers.setdefault(t, []).append(ref)
        for t in w:
            self.lastw[t] = ref
            self.readers[t] = []
        return ref

    def barrier(self):
        deps = []
        for e in ENGS:
            nonD = [i for i, o in enumerate(self.ops[e]) if not o['dma']]
            if nonD:
                deps.append((e, nonD[-1]))
            deps.extend(self.dma_ops[e][-NDSEM:])
        for e in ENGS:
            self.pending[e] = list(self.pending[e]) + deps
        self.lastw = {}
        self.readers = {}

    def mm(self, out, lhsT, rhs, start, stop, r, w):
        return self.add('tensor', lambda e: e.matmul(out, lhsT, rhs, start=start, stop=stop), r, w)

    def tr(self, out, in_, ident, r, w):
        return self.add('tensor', lambda e: e.transpose(out, in_, ident), r, w)

    def dma(self, eng, out, in_, r, w, **kw):
        return self.add(eng, lambda e: e.dma_start(out=out, in_=in_, **kw), r, w, dma=True)

    def emit(self):
        nc = self.nc
        ops = self.ops
        for e in ENGS:
            for op in ops[e]:
                nd = []
                seen = set()
                for d in op['deps']:
                    if d in seen:
                        continue
                    seen.add(d)
                    dop = ops[d[0]][d[1]]
                    if not dop['dma']:
                        if d[0] == e and e == 'tensor':
                            continue
                        dop['signal'] = True
                    nd.append(d)
                op['deps'] = nd
        for e in ENGS:
            c = 0
            for op in ops[e]:
                if op['dma']:
                    continue
                if op['signal']:
                    c += 1
                    op['semval'] = c
        with ExitStack() as st:
            esem = {e: st.enter_context(nc.semaphore('s_' + e)) for e in ENGS}
            dsem = {e: [st.enter_context(nc.semaphore('d_%s%d' % (e, i))) for i in range(NDSEM)]
                    for e in ENGS if self.ndma[e] > 0}
            block = st.enter_context(nc.Block())

            def run(e, eng):
                waited = {e2: 0 for e2 in ENGS}
                dwaited = {e2: {} for e2 in ENGS}
                for op in ops[e]:
                    for d in op['deps']:
                        dop = ops[d[0]][d[1]]
                        if dop['dma']:
                            j = dop['dj']
                            slot = j % NDSEM
                            if dwaited[d[0]].get(slot, -1) >= j:
                                continue
                            dwaited[d[0]][slot] = j
                            eng.wait_ge(dsem[d[0]][slot], 16 * (j // NDSEM + 1))
                        else:
                            v = dop['semval']
                            if waited[d[0]] >= v:
                                continue
                            waited[d[0]] = v
                            eng.wait_ge(esem[d[0]], v)
                    ins = op['fn'](eng)
                    if op['dma']:
                        ins.then_inc(dsem[e][op['dj'] % NDSEM], 16)
                    elif op['signal']:
                        ins.then_inc(esem[e], 1)
                if self.ndma[e] > 0:
                    n = self.ndma[e]
                    for i in range(NDSEM):
                        cnt = (n - i + NDSEM - 1) // NDSEM if n > i else 0
                        if cnt > 0:
                            eng.wait_ge(dsem[e][i], 16 * cnt)

            for e in ENGS:
                if ops[e]:
                    getattr(block, e)(lambda eng, e=e: run(e, eng))


def bc(ap, shape):
    return ap.to_broadcast(list(shape))


def build(dbg=False, phases=99, summary=False):
    nc = bass.Bass('TRN2', target_bir_lowering=False)
    P = Prog(nc)

    def din(name, shape):
        return nc.dram_tensor(name, list(shape), F32, kind='ExternalInput').ap()

    def dout(name, shape):
        return nc.dram_tensor(name, list(shape), F32, kind='ExternalOutput').ap()

    x_in = din('x', [T, D])
    w_in = din('w_in', [D, 9216])
    lb_logits = din('lb_logits', [4, 1024])
    lmask = din('lmask', [1, 4])
    cmask = din('cmask', [1, 8])
    norm_g = din('norm_g', [1, 1024])
    if phases > 2:
        lam_re = din('lam_re', [32, 128])
        lam_im = din('lam_im', [32, 128])
        log_step = din('log_step', [32, 2])
        b_re = din('b_re', [64, 64, 16])
        b_im = din('b_im', [64, 64, 16])
        c_re = din('c_re', [1024, 64])
        c_im = din('c_im', [1024, 64])
        s5_d = din('s5_d', [8, 128])
        w_glu = din('w_glu', [1024, 2048])
        s5X_all = din('s5X_all', [NCORES, 128, 64])
        s5X_out = dout('s5X_out', [128, 64])
    if phases > 3:
        w_up_a = din('w_up_a', [1024, 2048])
        w_up_b = din('w_up_b', [1024, 2048])
        w_o = din('w_o', [2048, 2048])
        ln1_g = din('ln1_g', [1, D])
        ln1_b = din('ln1_b', [1, D])
    if phases > 4:
        w_q = din('w_q', [2048, 2048])
        keys = din('keys', [16, 128, 128])
        peer_u = din('peer_u', [16384, 2048])
        peer_v = din('peer_v', [16384, 2048])
        ln2_g = din('ln2_g', [1, D])
        ln2_b = din('ln2_b', [1, D])
        x_out = dout('x_out', [T, D])
    hgU_all = din('hgU_all', [NCORES, 8, 128, 128])
    hgD_all = din('hgD_all', [NCORES, 128, 8])

    hgU_out = dout('hgU_out', [8, 128, 128])
    hgD_out = dout('hgD_out', [128, 8])
    dbgo = {}
    if dbg:
        dbgo['oa'] = dout('dbg_oa', [T, 1024])
        dbgo['obT'] = dout('dbg_obT', [1024, T])
        dbgo['x1'] = dout('dbg_x1', [T, D])
        dbgo['yT'] = dout('dbg_yT', [1024, T])

    x1d = nc.dram_tensor('x1d', [T, D], F32).ap()
    Gd = nc.dram_tensor('Gd', [T, 16384], BF16).ap()

    sb_off = [16512]
    sb_cnt = [0]

    def sb(name, shape, dt):
        n = 1
        for v in shape[1:]:
            n *= v
        nbytes = n * (4 if dt == F32 else 2)
        off = sb_off[0]
        sb_off[0] = (off + nbytes + 63) // 64 * 64
        assert sb_off[0] <= 229000, (name, sb_off[0])
        sb_cnt[0] += 1
        return nc.alloc_sbuf_tensor_at('%s_%d' % (name, sb_cnt[0]), list(shape), dt, offset=off)

    ident_b = sb('ident_b', [128, 128], BF16)
    ident_f = sb('ident_f', [128, 128], F32)
    tri01 = sb('tri01', [128, 64], F32)
    mask01 = sb('mask01', [128, T], F32)
    psum = [nc.alloc_psum_tensor('ps%d' % i, [128, 512], F32) for i in range(8)]

    def psb(i):
        return psum[i][:].bitcast(BF16)

    P.add('gpsimd', lambda e: e.memset(ident_f[:], 1.0), w=['ident_f'])
    P.add('gpsimd', lambda e: e.affine_select(out=ident_f[:], in_=ident_f[:], pattern=[[-1, 128]], compare_op=ALU.is_equal,
                                              fill=0.0, base=0, channel_multiplier=1), r=['ident_f'], w=['ident_f'])
    P.add('vector', lambda e: e.tensor_copy(ident_b[:], ident_f[:]), r=['ident_f'], w=['ident_b'])
    triA = sb('triA', [128, 64], F32)
    triB = sb('triB', [128, 64], F32)
    P.add('gpsimd', lambda e: e.memset(triA[:], 1.0), w=['triA'])
    P.add('gpsimd', lambda e: e.memset(triB[:], 1.0), w=['triB'])
    P.add('gpsimd', lambda e: e.affine_select(out=triA[:], in_=triA[:], pattern=[[1, 64]], compare_op=ALU.is_ge,
                                              fill=0.0, base=0, channel_multiplier=-1), r=['triA'], w=['triA'])
    P.add('gpsimd', lambda e: e.affine_select(out=triB[:], in_=triB[:], pattern=[[1, 64]], compare_op=ALU.is_ge,
                                              fill=0.0, base=64, channel_multiplier=-1), r=['triB'], w=['triB'])
    P.add('vector', lambda e: e.tensor_copy(tri01[0:64, :], triA[0:64, :]), r=['triA'], w=['tri01'])
    P.add('vector', lambda e: e.tensor_copy(tri01[64:128, :], triB[64:128, :]), r=['triB', 'tri01'], w=['tri01'])
    P.add('gpsimd', lambda e: e.memset(mask01[:], 1.0), w=['mask01'])
    P.add('gpsimd', lambda e: e.memset(mask01[:].rearrange('p (c s) -> p c s', s=64)[:, :, 0:1], 0.0), r=['mask01'], w=['mask01'])

    xT = sb('xT', [128, KT, T], BF16)
    reg_oa = sb_off[0]
    oa_tok = sb('oa_tok', [128, NT, 1024], BF16)
    obT = sb('obT', [128, 8, T], BF16)
    lbl = sb('lbl', [128, 4, 8], F32)
    lbe = sb('lbe', [128, 4, 8], F32)
    lbs = sb('lbs', [128, 8], F32)
    lb = sb('lb', [128, 8], F32)
    oml = sb('oml', [128, 8], F32)
    lmk = sb('lmk', [128, 4], F32)
    cmk = sb('cmk', [128, 8], F32)
    mark_persist = sb_off[0]
    xload = [sb('xload%d' % i, [128, D], F32) for i in range(2)]
    xbf = [sb('xbf%d' % i, [128, D], BF16) for i in range(2)]

    def to_featmajor(src_dram, dstT, tokpref):
        for n in range(NT):
            xl = xload[n % 2]; xb = xbf[n % 2]
            tl = 'xload%d' % (n % 2); tb = 'xbf%d' % (n % 2)
            P.dma('sync', xl[:], src_dram[n * 128:(n + 1) * 128, :], r=[], w=[tl])
            P.add('scalar', lambda e, xl=xl, xb=xb: e.copy(xb[:], xl[:]), r=[tl], w=[tb])
            for half in range(2):
                bank = half
                for j in range(8):
                    kt = half * 8 + j
                    P.tr(psb(bank)[:, j * 128:(j + 1) * 128], xb[:, kt * 128:(kt + 1) * 128], ident_b[:],
                         r=[tb, 'ident_b'], w=['ps%d' % bank])
                P.add('vector', lambda e, bank=bank, half=half, n=n: e.tensor_copy(
                    dstT[:, half * 8:(half + 1) * 8, n * 128:(n + 1) * 128],
                    psb(bank).rearrange('p (j t) -> p j t', j=8)), r=['ps%d' % bank], w=[tokpref])

    to_featmajor(x_in, xT, 'xT')

    P.dma('sync', lbl[:], lb_logits.rearrange('l (h d) -> d l h', d=128), r=[], w=['lbl'], allow_slow_non_contiguous=True)
    P.dma('sync', lmk[:], lmask[0].partition_broadcast(128), r=[], w=['lmk'])
    P.dma('sync', cmk[:], cmask[0].partition_broadcast(128), r=[], w=['cmk'])
    P.add('scalar', lambda e: e.activation(lbe[:], lbl[:], AF.Exp), r=['lbl'], w=['lbe'])
    P.add('vector', lambda e: e.tensor_reduce(lbs[:], lbe[:].rearrange('p l h -> p h l'), AX.X, ALU.add), r=['lbe'], w=['lbs'])
    P.add('vector', lambda e: e.reciprocal(lbs[:], lbs[:]), r=['lbs'], w=['lbs'])
    P.add('vector', lambda e: e.tensor_tensor(lbe[:], lbe[:], bc(lmk[:].unsqueeze(2), [128, 4, 8]), ALU.mult), r=['lbe', 'lmk'], w=['lbe'])
    P.add('vector', lambda e: e.tensor_reduce(lb[:], lbe[:].rearrange('p l h -> p h l'), AX.X, ALU.add), r=['lbe'], w=['lb'])
    P.add('vector', lambda e: e.tensor_tensor(lb[:], lb[:], lbs[:], ALU.mult), r=['lb', 'lbs'], w=['lb'])
    P.add('vector', lambda e: e.tensor_scalar(oml[:], lb[:], -1.0, 1.0, ALU.mult, ALU.add), r=['lb'], w=['oml'])

    v_tok = sb('v_tok', [128, NT, 1024], BF16)
    sg_tok = sb('sg_tok', [128, NT, 1024], BF16)
    wblk = [sb('wblk%d' % i, [128, KT, 512], BF16) for i in range(2)]
    wcnt = [0]

    def load_wblk(src, c0, ncols=512, kt=KT):
        i = wcnt[0] % 2
        wcnt[0] += 1
        P.dma('gpsimd', wblk[i][:, 0:kt, 0:ncols], src[:, c0:c0 + ncols].rearrange('(kt p) n -> p kt n', p=128),
              r=[], w=['wblk%d' % i])
        return wblk[i], 'wblk%d' % i

    pscnt = [0]

    def nextbank(lo=0, hi=8):
        b = lo + pscnt[0] % (hi - lo)
        pscnt[0] += 1
        return b

    for blk in range(4):
        wb, wt = load_wblk(w_in, 2048 + blk * 512)
        for n in range(NT):
            bank = nextbank()
            for kt in range(KT):
                P.mm(psum[bank][:, :], xT[:, kt, n * 128:(n + 1) * 128], wb[:, kt, :], kt == 0, kt == KT - 1,
                     r=['xT', wt], w=['ps%d' % bank])
            if blk < 2:
                P.add('vector', lambda e, bank=bank, n=n, blk=blk: e.tensor_copy(v_tok[:, n, blk * 512:(blk + 1) * 512], psum[bank][:, :]),
                      r=['ps%d' % bank], w=['v_tok%d' % n])
            else:
                P.add('scalar', lambda e, bank=bank, n=n, blk=blk: e.activation(sg_tok[:, n, (blk - 2) * 512:(blk - 1) * 512], psum[bank][:, :], AF.Sigmoid),
                      r=['ps%d' % bank], w=['sg_tok%d' % n])

    ngb = sb('ngb', [128, 1024], F32)
    P.dma('sync', ngb[:], norm_g[0].partition_broadcast(128), r=[], w=['ngb'])
    Dp = sb('Dp', [128, NCORES, 8], F32)
    P.dma('sync', Dp[:], hgD_all.rearrange('c d h -> d c h'), r=[], w=['Dp'])
    P.add('vector', lambda e: e.tensor_scalar(Dp[:], Dp[:], -1.0, None, ALU.add), r=['Dp'], w=['Dp'])
    P.add('vector', lambda e: e.tensor_tensor(Dp[:], Dp[:], bc(cmk[:].unsqueeze(2), [128, NCORES, 8]), ALU.mult), r=['Dp', 'cmk'], w=['Dp'])
    P.add('vector', lambda e: e.tensor_scalar(Dp[:], Dp[:], 1.0, None, ALU.add), r=['Dp'], w=['Dp'])

    qT = sb('qT', [128, T], F32)
    fT = sb('fT', [128, T], F32)
    lgf = sb('lgf', [128, T], F32)
    kT_ = sb('kT_', [128, T], F32)
    bT = sb('bT', [128, T], F32)
    eq = sb('eq', [128, T], F32)
    ek = sb('ek', [128, T], F32)
    qt_b = sb('qt_b', [128, T], BF16)
    kt_b = sb('kt_b', [128, T], BF16)
    ktok = sb('ktok', [128, NT, 128], BF16)
    rr = sb('rr', [128, 16], F32)
    er = sb('er', [128, 16], F32)
    e2r = sb('e2r', [128, 16], F32)
    bsum = sb('bsum', [128, 1], F32)
    Dtot = sb('Dtot', [128, 8], F32)
    S = sb('S', [128, 128], F32)
    Uld = [sb('Uld%d' % i, [128, 128], F32) for i in range(2)]
    Spb = [sb('Spb%d' % i, [128, 128], BF16) for i in range(4)]
    Utmp = sb('Utmp', [128, 128], F32)
    att_sb = [sb('att_sb%d' % i, [128, 128], BF16) for i in range(2)]
    ss = sb('ss', [128, 2], F32)
    junk = sb('junk', [128, 128], F32)
    otmp = sb('otmp', [128, 128], F32)
    for i in range(2):
        P.add('gpsimd', lambda e, i=i: e.memset(att_sb[i][:], 0.0), w=['att_sb%d' % i])

    for h in range(8):
        P.add('gpsimd', lambda e: e.memset(S[:], 0.0), w=['S'])
        for j in range(NCORES - 1):
            ul = Uld[j % 2]; ut = 'Uld%d' % (j % 2)
            P.dma('sync', ul[:], hgU_all[j, h], r=[], w=[ut])
            P.add('vector', lambda e, ul=ul, j=j: e.tensor_scalar(ul[:], ul[:], cmk[:, j:j + 1], None, ALU.mult), r=[ut, 'cmk'], w=[ut])
            P.add('vector', lambda e, ul=ul, j=j, h=h: e.scalar_tensor_tensor(S[:], S[:], Dp[:, j, h:h + 1], ul[:], ALU.mult, ALU.add),
                  r=['S', 'Dp', ut], w=['S'])
        i = wcnt[0] % 2
        wcnt[0] += 1
        wb = wblk[i]; wt = 'wblk%d' % i
        P.dma('gpsimd', wb[:, :, 0:128], w_in[:, h * 128:(h + 1) * 128].rearrange('(kt p) n -> p kt n', p=128), r=[], w=[wt])
        P.dma('gpsimd', wb[:, :, 128:256], w_in[:, 1024 + h * 128:1024 + (h + 1) * 128].rearrange('(kt p) n -> p kt n', p=128), r=[], w=[wt])
        for which in range(2):
            for half in range(2):
                bank = nextbank()
                for kt in range(KT):
                    P.mm(psum[bank][:, :], wb[:, kt, which * 128:(which + 1) * 128], xT[:, kt, half * 512:(half + 1) * 512],
                         kt == 0, kt == KT - 1, r=['xT', wt], w=['ps%d' % bank])
                if which == 0:
                    P.add('vector', lambda e, bank=bank, half=half: e.tensor_copy(qT[:, half * 512:(half + 1) * 512], psum[bank][:, :]),
                          r=['ps%d' % bank], w=['qT'])
                else:
                    P.add('scalar', lambda e, bank=bank, half=half: e.activation(fT[:, half * 512:(half + 1) * 512], psum[bank][:, :], AF.Sigmoid),
                          r=['ps%d' % bank], w=['fT'])
        P.add('vector', lambda e, h=h: e.tensor_scalar(fT[:], fT[:], oml[:, h:h + 1], lb[:, h:h + 1], ALU.mult, ALU.add), r=['fT', 'oml', 'lb'], w=['fT'])
        P.add('scalar', lambda e: e.activation(lgf[:], fT[:], AF.Ln), r=['fT'], w=['lgf'])
        P.add('gpsimd', lambda e: e.tensor_scalar(kT_[:], fT[:], -1.0, 1.0, ALU.mult, ALU.add), r=['fT'], w=['kT_'])
        P.add('vector', lambda e: e.tensor_tensor_scan(bT[:], mask01[:], lgf[:], 0.0, ALU.mult, ALU.add), r=['mask01', 'lgf'], w=['bT'])
        b3 = bT[:].rearrange('p (c s) -> p c s', s=64)
        P.add('vector', lambda e: e.tensor_scalar(rr[:], b3[:, :, 63], 0.5, None, ALU.mult), r=['bT'], w=['rr'])
        P.add('vector', lambda e: e.tensor_reduce(bsum[:], b3[:, :, 63], AX.X, ALU.add), r=['bT'], w=['bsum'])
        P.add('scalar', lambda e, h=h: e.activation(Dtot[:, h:h + 1], bsum[:], AF.Exp), r=['bsum'], w=['Dtot'])
        P.add('scalar', lambda e: e.activation(er[:], rr[:], AF.Exp), r=['rr'], w=['er'])
        P.add('vector', lambda e: e.tensor_tensor(e2r[:], er[:], er[:], ALU.mult), r=['er'], w=['e2r'])
        P.add('vector', lambda e: e.tensor_tensor(b3, b3, bc(rr[:].unsqueeze(2), [128, 16, 64]), ALU.subtract), r=['bT', 'rr'], w=['bT'])
        P.add('scalar', lambda e: e.activation(eq[:], bT[:], AF.Exp), r=['bT'], w=['eq'])
        P.add('scalar', lambda e: e.activation(ek[:], bT[:], AF.Exp, scale=-1.0), r=['bT'], w=['ek'])
        P.add('vector', lambda e: e.tensor_tensor(qt_b[:], qT[:], eq[:], ALU.mult), r=['qT', 'eq'], w=['qt_b'])
        P.add('gpsimd', lambda e: e.tensor_tensor(kt_b[:], kT_[:], ek[:], ALU.mult), r=['kT_', 'ek'], w=['kt_b'])
        bank = nextbank()
        for n in range(NT):
            P.tr(psb(bank)[:, n * 128:(n + 1) * 128], kt_b[:, n * 128:(n + 1) * 128], ident_b[:], r=['kt_b', 'ident_b'], w=['ps%d' % bank])
        P.add('vector', lambda e, bank=bank: e.tensor_copy(ktok[:], psb(bank).rearrange('p (n d) -> p n d', n=NT)), r=['ps%d' % bank], w=['ktok'])
        for n in range(NT):
            asb = att_sb[n % 2]; at = 'att_sb%d' % (n % 2)
            b_att = nextbank(); b_U = nextbank(); b_o = nextbank()
            for cc in range(2):
                c = 2 * n + cc
                tok = slice(c * 64, (c + 1) * 64)
                pr = slice(cc * 64, (cc + 1) * 64)
                P.mm(psum[b_att][pr, cc * 64:(cc + 1) * 64], kt_b[:, tok], qt_b[:, tok], True, True, r=['kt_b', 'qt_b'], w=['ps%d' % b_att])
                P.add('vector', lambda e, pr=pr, cc=cc, asb=asb, b_att=b_att: e.tensor_tensor(
                    asb[pr, cc * 64:(cc + 1) * 64], psum[b_att][pr, cc * 64:(cc + 1) * 64], tri01[pr, :], ALU.mult),
                    r=['ps%d' % b_att, 'tri01'], w=[at])
                P.mm(psum[b_U][:, cc * 128:(cc + 1) * 128], ktok[pr, n, :], v_tok[pr, n, h * 128:(h + 1) * 128], True, True,
                     r=['ktok', 'v_tok%d' % n], w=['ps%d' % b_U])
            P.mm(psum[b_o][:, 0:128], asb[:, :], v_tok[:, n, h * 128:(h + 1) * 128], True, False, r=[at, 'v_tok%d' % n], w=['ps%d' % b_o])
            for cc in range(2):
                c = 2 * n + cc
                tok = slice(c * 64, (c + 1) * 64)
                pr = slice(cc * 64, (cc + 1) * 64)
                sp = Spb[c % 4]; spt = 'Spb%d' % (c % 4)
                P.add('vector', lambda e, sp=sp, c=c: e.tensor_scalar(sp[:], S[:], er[:, c:c + 1], None, ALU.mult), r=['S', 'er'], w=[spt])
                P.mm(psum[b_o][pr, 0:128], qt_b[:, tok], sp[:, :], False, cc == 1, r=['qt_b', spt], w=['ps%d' % b_o])
                P.add('vector', lambda e, c=c, cc=cc, b_U=b_U: e.tensor_scalar(Utmp[:], psum[b_U][:, cc * 128:(cc + 1) * 128], er[:, c:c + 1], None, ALU.mult),
                      r=['ps%d' % b_U, 'er'], w=['Utmp'])
                P.add('vector', lambda e, c=c: e.scalar_tensor_tensor(S[:], S[:], e2r[:, c:c + 1], Utmp[:], ALU.mult, ALU.add),
                      r=['S', 'e2r', 'Utmp'], w=['S'])
            P.add('gpsimd', lambda e: e.memset(ss[:], 0.0), w=['ss'])
            P.add('scalar', lambda e, b_o=b_o: e.activation(junk[:], psum[b_o][:, 0:128], AF.Square, accum_out=ss[:, 0:1]), r=['ps%d' % b_o, 'ss'], w=['junk', 'ss'])
            P.add('scalar', lambda e: e.activation(ss[:, 1:2], ss[:, 0:1], AF.Ln, bias=RMS_EPS, scale=1.0 / 128.0), r=['ss'], w=['ss'])
            P.add('scalar', lambda e: e.activation(ss[:, 1:2], ss[:, 1:2], AF.Exp, scale=-0.5), r=['ss'], w=['ss'])
            P.add('vector', lambda e, b_o=b_o, h=h: e.scalar_tensor_tensor(otmp[:], psum[b_o][:, 0:128], ss[:, 1:2], ngb[:, h * 128:(h + 1) * 128], ALU.mult, ALU.mult),
                  r=['ps%d' % b_o, 'ss', 'ngb'], w=['otmp'])
            P.add('gpsimd', lambda e, n=n, h=h: e.tensor_tensor(oa_tok[:, n, h * 128:(h + 1) * 128], otmp[:], sg_tok[:, n, h * 128:(h + 1) * 128], ALU.mult),
                  r=['otmp', 'sg_tok%d' % n], w=['oa_tok%d' % n])
        P.dma('sync', hgU_out[h], S[:], r=['S'], w=['hgU_out'])
    P.dma('sync', hgD_out, Dtot[:], r=['Dtot'], w=['hgD_out'])
    if dbg:
        for n in range(NT):
            P.add('vector', lambda e, n=n: e.tensor_copy(xload[n % 2][:, 0:1024], oa_tok[:, n, :]), r=['oa_tok%d' % n], w=['xload%d' % (n % 2)])
            P.dma('sync', dbgo['oa'][n * 128:(n + 1) * 128, :], xload[n % 2][:, 0:1024], r=['xload%d' % (n % 2)], w=['dbg_oa'])
    if phases <= 2:
        P.emit()
        return nc

    P.barrier()
    sb_off[0] = mark_persist
    wblk = [sb('wblk%d' % i, [128, KT, 512], BF16) for i in range(2)]
    uT = sb('uT', [128, 8, T], F32)
    yT = sb('yT', [128, 8, T], BF16)
    Xr = [sb('Xr%d' % i, [128, T], F32) for i in range(2)]
    Xi = [sb('Xi%d' % i, [128, T], F32) for i in range(2)]
    ytmp = sb('ytmp', [128, 512], F32)
    sgm = sb('sgm', [128, 512], F32)
    pl = sb('pl', [32, 3, 128], F32)
    lst2 = sb('lst2', [32, 2], F32)
    sp = {}
    for nm in ['lamr', 'lami', 'lst', 'lr', 'dt', 'lrdt', 'lidt', 'mag', 't1', 't2', 'sn', 'cs', 'ar', 'ai', 'den', 'nr',
               'zr', 'zi', 'xinr', 'xini', 'axr', 'axi', 'Ar', 'Ai', 'tA', 'tB', 'tC', 'tD']:
        sp[nm] = sb('s5_' + nm, [128, 32], F32)
    pwr = sb('pwr', [128, 11, 32], F32)
    pwi = sb('pwi', [128, 11, 32], F32)
    npwi = sb('npwi', [128, 11, 32], F32)
    bre = sb('bre', [128, 32, 16], F32)
    bim = sb('bim', [128, 32, 16], F32)
    bbr = sb('bbr', [128, 32, 16], F32)
    bbi = sb('bbi', [128, 32, 16], F32)
    btmp = sb('btmp', [128, 32, 16], F32)
    Cld = [sb('Cld%d' % i, [128, 8, 64], F32) for i in range(2)]
    Cdup = sb('Cdup', [128, 2, 64], F32)
    dld = sb('dld', [8, 128], F32)
    dvec = sb('dvec', [128, 8], F32)
    Xe_ld = [sb('Xe_ld%d' % i, [128, 64], F32) for i in range(2)]
    Xend = sb('Xend', [128, 64], F32)
    srcB = [sb('srcB%d' % j, [128, 128], F32) for j in range(4)]
    LBr = [sb('LBr%d' % j, [128, 128], F32) for j in range(4)]
    LBi = [sb('LBi%d' % j, [128, 128], F32) for j in range(4)]
    LCr = [sb('LCr%d' % j, [128, 128], F32) for j in range(4)]
    LCi = [sb('LCi%d' % j, [128, 128], F32) for j in range(4)]
    for j in range(4):
        for tl, nm in ((srcB, 'srcB'), (LCr, 'LCr'), (LCi, 'LCi')):
            P.add('gpsimd', lambda e, t=tl[j]: e.memset(t[:], 0.0), w=['%s%d' % (nm, j)])

    V = lambda eng, fn, r, w: P.add(eng, fn, r=r, w=w)

    def tt(out, a, b, op, r, w, eng='vector'):
        P.add(eng, lambda e: e.tensor_tensor(out, a, b, op), r=r, w=w)

    def ts(out, a, s1, s2, op0, op1, r, w, eng='vector'):
        if op1 is None:
            P.add(eng, lambda e: e.tensor_scalar(out, a, s1, None, op0), r=r, w=w)
        else:
            P.add(eng, lambda e: e.tensor_scalar(out, a, s1, s2, op0, op1), r=r, w=w)

    def stt(out, a, sc, b, op0, op1, r, w, eng='vector'):
        P.add(eng, lambda e: e.scalar_tensor_tensor(out, a, sc, b, op0, op1), r=r, w=w)

    def act(out, a, func, r, w, **kw):
        P.add('scalar', lambda e: e.activation(out, a, func, **kw), r=r, w=w)

    P.dma('sync', pl[:, 0, :], lam_re, r=[], w=['pl'])
    P.dma('sync', pl[:, 1, :], lam_im, r=[], w=['pl'])
    P.dma('sync', lst2[:], log_step, r=[], w=['lst2'])
    P.add('vector', lambda e: e.tensor_copy(pl[:, 2, :].rearrange('p (a b) -> p a b', a=2), bc(lst2[:].unsqueeze(2), [32, 2, 64])), r=['lst2', 'pl'], w=['pl'])
    for i, nm in enumerate(['lamr', 'lami', 'lst']):
        P.tr(psum[4][:, i * 32:(i + 1) * 32], pl[:, i, :], ident_f[0:32, 0:32], r=['pl', 'ident_f'], w=['ps4'])
        P.add('vector', lambda e, i=i, nm=nm: e.tensor_copy(sp[nm][:], psum[4][:, i * 32:(i + 1) * 32]), r=['ps4'], w=[nm])
    P.dma('sync', dld[:], s5_d, r=[], w=['dld'])
    P.tr(psum[4][:, 128:136], dld[:], ident_f[0:8, 0:8], r=['dld', 'ident_f'], w=['ps4'])
    P.add('vector', lambda e: e.tensor_copy(dvec[:], psum[4][:, 128:136]), r=['ps4'], w=['dvec'])
    P.dma('sync', bre[:], b_re.rearrange('(gp g2) p n -> (g2 p) gp n', g2=2), r=[], w=['bre'])
    P.dma('sync', bim[:], b_im.rearrange('(gp g2) p n -> (g2 p) gp n', g2=2), r=[], w=['bim'])
    P.dma('sync', Cld[0][:], c_re.rearrange('(ct q) p -> q ct p', q=128), r=[], w=['Cld0'])
    P.dma('sync', Cld[1][:], c_im.rearrange('(ct q) p -> q ct p', q=128), r=[], w=['Cld1'])

    A = lambda nm: sp[nm][:]
    ts(A('lr'), A('lamr'), -1e-4, None, ALU.min, None, ['lamr'], ['lr'])
    act(A('dt'), A('lst'), AF.Exp, ['lst'], ['dt'])
    tt(A('lrdt'), A('lr'), A('dt'), ALU.mult, ['lr', 'dt'], ['lrdt'])
    tt(A('lidt'), A('lami'), A('dt'), ALU.mult, ['lami', 'dt'], ['lidt'])
    act(A('mag'), A('lrdt'), AF.Exp, ['lrdt'], ['mag'])
    ts(A('t1'), A('lidt'), 1.0 / TWO_PI, MAGIC, ALU.mult, ALU.add, ['lidt'], ['t1'])
    ts(A('t1'), A('t1'), -MAGIC, None, ALU.add, None, ['t1'], ['t1'])
    stt(A('t1'), A('t1'), -TWO_PI, A('lidt'), ALU.mult, ALU.add, ['t1', 'lidt'], ['t1'])
    act(A('sn'), A('t1'), AF.Sin, ['t1'], ['sn'])
    ts(A('t2'), A('lidt'), math.pi / 2, None, ALU.add, None, ['lidt'], ['t2'])
    ts(A('tA'), A('t2'), 1.0 / TWO_PI, MAGIC, ALU.mult, ALU.add, ['t2'], ['tA'])
    ts(A('tA'), A('tA'), -MAGIC, None, ALU.add, None, ['tA'], ['tA'])
    stt(A('tA'), A('tA'), -TWO_PI, A('t2'), ALU.mult, ALU.add, ['tA', 't2'], ['tA'])
    act(A('cs'), A('tA'), AF.Sin, ['tA'], ['cs'])
    tt(A('ar'), A('mag'), A('cs'), ALU.mult, ['mag', 'cs'], ['ar'])
    tt(A('ai'), A('mag'), A('sn'), ALU.mult, ['mag', 'sn'], ['ai'])
    tt(A('den'), A('lr'), A('lr'), ALU.mult, ['lr'], ['den'])
    tt(A('tB'), A('lami'), A('lami'), ALU.mult, ['lami'], ['tB'])
    tt(A('den'), A('den'), A('tB'), ALU.add, ['den', 'tB'], ['den'])
    P.add('vector', lambda e: e.reciprocal(A('den'), A('den')), r=['den'], w=['den'])
    ts(A('nr'), A('ar'), -1.0, None, ALU.add, None, ['ar'], ['nr'])
    tt(A('tB'), A('nr'), A('lr'), ALU.mult, ['nr', 'lr'], ['tB'])
    tt(A('tC'), A('ai'), A('lami'), ALU.mult, ['ai', 'lami'], ['tC'])
    tt(A('tB'), A('tB'), A('tC'), ALU.add, ['tB', 'tC'], ['tB'])
    tt(A('zr'), A('tB'), A('den'), ALU.mult, ['tB', 'den'], ['zr'])
    tt(A('tB'), A('ai'), A('lr'), ALU.mult, ['ai', 'lr'], ['tB'])
    tt(A('tC'), A('nr'), A('lami'), ALU.mult, ['nr', 'lami'], ['tC'])
    tt(A('tB'), A('tB'), A('tC'), ALU.subtract, ['tB', 'tC'], ['tB'])
    tt(A('zi'), A('tB'), A('den'), ALU.mult, ['tB', 'den'], ['zi'])
    P.add('vector', lambda e: e.tensor_copy(pwr[:, 0, :], A('ar')), r=['ar'], w=['pwr'])
    P.add('vector', lambda e: e.tensor_copy(pwi[:, 0, :], A('ai')), r=['ai'], w=['pwi'])
    for k in range(10):
        tt(A('tB'), pwr[:, k, :], pwr[:, k, :], ALU.mult, ['pwr'], ['tB'])
        tt(A('tC'), pwi[:, k, :], pwi[:, k, :], ALU.mult, ['pwi'], ['tC'])
        stt(pwi[:, k + 1, :], pwr[:, k, :], 2.0, pwi[:, k, :], ALU.mult, ALU.mult, ['pwr', 'pwi'], ['pwi'])
        tt(pwr[:, k + 1, :], A('tB'), A('tC'), ALU.subtract, ['tB', 'tC'], ['pwr'])
    ts(npwi[:], pwi[:], -1.0, None, ALU.mult, None, ['pwi'], ['npwi'])
    P.add('gpsimd', lambda e: e.memset(A('xinr'), 0.0), w=['xinr'])
    P.add('gpsimd', lambda e: e.memset(A('xini'), 0.0), w=['xini'])
    for j in range(NCORES - 1):
        xl = Xe_ld[j % 2]; xt_ = 'Xe_ld%d' % (j % 2)
        P.dma('sync', xl[:], s5X_all[j], r=[], w=[xt_])
        m = cmk[:, j:j + 1]
        ts(A('Ar'), pwr[:, 10, :], -1.0, None, ALU.add, None, ['pwr'], ['Ar'])
        ts(A('Ar'), A('Ar'), m, 1.0, ALU.mult, ALU.add, ['Ar', 'cmk'], ['Ar'])
        ts(A('Ai'), pwi[:, 10, :], m, None, ALU.mult, None, ['pwi', 'cmk'], ['Ai'])
        ts(xl[:], xl[:], m, None, ALU.mult, None, [xt_, 'cmk'], [xt_])
        tt(A('tA'), A('Ar'), A('xinr'), ALU.mult, ['Ar', 'xinr'], ['tA'])
        tt(A('tB'), A('Ai'), A('xini'), ALU.mult, ['Ai', 'xini'], ['tB'])
        tt(A('tC'), A('Ar'), A('xini'), ALU.mult, ['Ar', 'xini'], ['tC'])
        tt(A('tD'), A('Ai'), A('xinr'), ALU.mult, ['Ai', 'xinr'], ['tD'])
        tt(A('tA'), A('tA'), A('tB'), ALU.subtract, ['tA', 'tB'], ['tA'])
        tt(A('tC'), A('tC'), A('tD'), ALU.add, ['tC', 'tD'], ['tC'])
        tt(A('xinr'), A('tA'), xl[:, 0:32], ALU.add, ['tA', xt_], ['xinr'])
        tt(A('xini'), A('tC'), xl[:, 32:64], ALU.add, ['tC', xt_], ['xini'])
    tt(A('tA'), A('ar'), A('xinr'), ALU.mult, ['ar', 'xinr'], ['tA'])
    tt(A('tB'), A('ai'), A('xini'), ALU.mult, ['ai', 'xini'], ['tB'])
    tt(A('axr'), A('tA'), A('tB'), ALU.subtract, ['tA', 'tB'], ['axr'])
    tt(A('tA'), A('ar'), A('xini'), ALU.mult, ['ar', 'xini'], ['tA'])
    tt(A('tB'), A('ai'), A('xinr'), ALU.mult, ['ai', 'xinr'], ['tB'])
    tt(A('axi'), A('tA'), A('tB'), ALU.add, ['tA', 'tB'], ['axi'])
    zrb = bc(sp['zr'][:].unsqueeze(2), [128, 32, 16])
    zib = bc(sp['zi'][:].unsqueeze(2), [128, 32, 16])
    tt(bbr[:], bre[:], zrb, ALU.mult, ['bre', 'zr'], ['bbr'])
    tt(btmp[:], bim[:], zib, ALU.mult, ['bim', 'zi'], ['btmp'])
    tt(bbr[:], bbr[:], btmp[:], ALU.subtract, ['bbr', 'btmp'], ['bbr'])
    tt(bbi[:], bim[:], zrb, ALU.mult, ['bim', 'zr'], ['bbi'])
    tt(btmp[:], bre[:], zib, ALU.mult, ['bre', 'zi'], ['btmp'])
    tt(bbi[:], bbi[:], btmp[:], ALU.add, ['bbi', 'btmp'], ['bbi'])

    for blk in range(2):
        wb, wt = load_wblk(w_in, 4096 + blk * 512)
        for ntl in range(4):
            ct = blk * 4 + ntl
            for half in range(2):
                bank = nextbank(0, 4)
                for kt in range(KT):
                    P.mm(psum[bank][:, :], wb[:, kt, ntl * 128:(ntl + 1) * 128], xT[:, kt, half * 512:(half + 1) * 512],
                         kt == 0, kt == KT - 1, r=['xT', wt], w=['ps%d' % bank])
                P.add('scalar', lambda e, bank=bank, ct=ct, half=half: e.copy(uT[:, ct, half * 512:(half + 1) * 512], psum[bank][:, :]),
                      r=['ps%d' % bank], w=['uT%d' % ct])

    P.barrier()
    wb1f = wblk[1][:].rearrange('p a b -> p (a b)').bitcast(F32)
    Ms = [(wb1f[:, 0:T], wb1f[:, T:2 * T]), (wb1f[:, 2 * T:3 * T], wb1f[:, 3 * T:4 * T])]
    wb0f = wblk[0][:].rearrange('p a b -> p (a b)').bitcast(F32)
    Xr2 = [wb0f[:, 0:T], wb0f[:, T:2 * T]]
    Xi2 = [wb0f[:, 2 * T:3 * T], wb0f[:, 3 * T:4 * T]]
    XRs = [Xr, Xr2]
    XIs = [Xi, Xi2]
    for ct in range(8):
        for j in range(4):
            gp = 4 * ct + j
            for (bb, LB, nm) in ((bbr, LBr, 'LBr'), (bbi, LBi, 'LBi')):
                for g2 in range(2):
                    pr = slice(g2 * 64, (g2 + 1) * 64)
                    c0 = (2 * j + g2) * 16
                    P.add('vector', lambda e, bb=bb, pr=pr, c0=c0, j=j, gp=gp: e.tensor_copy(srcB[j][pr, c0:c0 + 16], bb[pr, gp, :]),
                          r=['bbr', 'bbi'], w=['srcB%d' % j])
                P.tr(psum[4][:, 0:128], srcB[j][:], ident_f[:], r=['srcB%d' % j, 'ident_f'], w=['ps4'])
                P.add('vector', lambda e, LB=LB, j=j: e.tensor_copy(LB[j][:], psum[4][:, 0:128]), r=['ps4'], w=['%s%d' % (nm, j)])
        for ri, (LC, nm) in enumerate(((LCr, 'LCr'), (LCi, 'LCi'))):
            P.add('vector', lambda e, ri=ri, ct=ct: e.tensor_copy(Cdup[:], bc(Cld[ri][:, ct, :].unsqueeze(1), [128, 2, 64])), r=['Cld%d' % ri], w=['Cdup'])
            P.tr(psum[5][:, 0:128], Cdup[:].rearrange('p a b -> p (a b)'), ident_f[:], r=['Cdup', 'ident_f'], w=['ps5'])
            for j in range(4):
                for g2 in range(2):
                    pr = slice(g2 * 64, (g2 + 1) * 64)
                    c0 = (2 * j + g2) * 16
                    if ri == 0:
                        P.add('vector', lambda e, LC=LC, j=j, pr=pr, c0=c0: e.tensor_copy(LC[j][pr, c0:c0 + 16], psum[5][pr, c0:c0 + 16]),
                              r=['ps5'], w=['%s%d' % (nm, j)])
                    else:
                        P.add('vector', lambda e, LC=LC, j=j, pr=pr, c0=c0: e.tensor_scalar(LC[j][pr, c0:c0 + 16], psum[5][pr, c0:c0 + 16], -1.0, None, ALU.mult),
                              r=['ps5'], w=['%s%d' % (nm, j)])
        for jp in range(2):
            js = (2 * jp, 2 * jp + 1)
            for si, j in enumerate(js):
                gp = 4 * ct + j
                XR, XI = XRs[si], XIs[si]
                t0r, t0i = 'Xr%d_0' % si, 'Xi%d_0' % si
                for half in range(2):
                    hs = slice(half * 512, (half + 1) * 512)
                    b1 = nextbank(0, 4)
                    P.mm(psum[b1][:, :], LBr[j][:], uT[:, ct, hs], True, True, r=['LBr%d' % j, 'uT%d' % ct], w=['ps%d' % b1])
                    P.add('scalar', lambda e, b1=b1, hs=hs, XR=XR: e.copy(XR[0][:, hs], psum[b1][:, :]), r=['ps%d' % b1], w=[t0r])
                    b2 = nextbank(0, 4)
                    P.mm(psum[b2][:, :], LBi[j][:], uT[:, ct, hs], True, True, r=['LBi%d' % j, 'uT%d' % ct], w=['ps%d' % b2])
                    P.add('scalar', lambda e, b2=b2, hs=hs, XI=XI: e.copy(XI[0][:, hs], psum[b2][:, :]), r=['ps%d' % b2], w=[t0i])
                tt(XR[0][:, 0:1], XR[0][:, 0:1], sp['axr'][:, gp:gp + 1], ALU.add, [t0r, 'axr'], [t0r])
                tt(XI[0][:, 0:1], XI[0][:, 0:1], sp['axi'][:, gp:gp + 1], ALU.add, [t0i, 'axi'], [t0i], eng='gpsimd')
            for k in (range(10) if summary else []):
                nn = T >> (k + 1)
                s_, d_ = k % 2, 1 - k % 2
                for si, j in enumerate(js):
                    gp = 4 * ct + j
                    XR, XI = XRs[si], XIs[si]
                    mA, mB = Ms[si]
                    tmA, tmB = 'mA%d' % si, 'mB%d' % si
                    pr_ = pwr[:, k, gp:gp + 1]; pi_ = pwi[:, k, gp:gp + 1]; npi_ = npwi[:, k, gp:gp + 1]
                    tr_s, ti_s, tr_d, ti_d = 'Xr%d_%d' % (si, s_), 'Xi%d_%d' % (si, s_), 'Xr%d_%d' % (si, d_), 'Xi%d_%d' % (si, d_)
                    xr2 = XR[s_][:, 0:2 * nn].rearrange('p (j two) -> p j two', two=2)
                    xi2 = XI[s_][:, 0:2 * nn].rearrange('p (j two) -> p j two', two=2)
                    stt(XR[d_][:, 0:nn], xr2[:, :, 0], pr_, xr2[:, :, 1], ALU.mult, ALU.add, [tr_s, 'pwr'], [tr_d])
                    stt(XR[d_][:, 0:nn], xi2[:, :, 0], npi_, XR[d_][:, 0:nn], ALU.mult, ALU.add, [ti_s, 'npwi', tr_d], [tr_d])
                    stt(XI[d_][:, 0:nn], xi2[:, :, 0], pr_, xi2[:, :, 1], ALU.mult, ALU.add, [ti_s, 'pwr'], [ti_d])
                    stt(XI[d_][:, 0:nn], xr2[:, :, 0], pi_, XI[d_][:, 0:nn], ALU.mult, ALU.add, [tr_s, 'pwi', ti_d], [ti_d])
            for si, j in (list(enumerate(js)) if summary else []):
                gp = 4 * ct + j
                XR, XI = XRs[si], XIs[si]
                P.add('vector', lambda e, gp=gp, XR=XR: e.tensor_copy(Xend[:, gp:gp + 1], XR[0][:, 0:1]), r=['Xr%d_0' % si], w=['Xend'])
                P.add('vector', lambda e, gp=gp, XI=XI: e.tensor_copy(Xend[:, 32 + gp:33 + gp], XI[0][:, 0:1]), r=['Xi%d_0' % si], w=['Xend'])
            if summary:
                continue
            for k in range(10):
                d = 1 << k
                s_, d_ = k % 2, 1 - k % 2
                for si, j in enumerate(js):
                    gp = 4 * ct + j
                    XR, XI = XRs[si], XIs[si]
                    mA, mB = Ms[si]
                    tmA, tmB = 'mA%d' % si, 'mB%d' % si
                    pr_ = pwr[:, k, gp:gp + 1]; pi_ = pwi[:, k, gp:gp + 1]; npi_ = npwi[:, k, gp:gp + 1]
                    xr_s, xi_s, xr_d, xi_d = XR[s_], XI[s_], XR[d_], XI[d_]
                    tr_s, ti_s, tr_d, ti_d = 'Xr%d_%d' % (si, s_), 'Xi%d_%d' % (si, s_), 'Xr%d_%d' % (si, d_), 'Xi%d_%d' % (si, d_)
                    P.add('vector', lambda e, xr_d=xr_d, xr_s=xr_s, d=d: e.tensor_copy(xr_d[:, 0:d], xr_s[:, 0:d]), r=[tr_s], w=[tr_d])
                    stt(xr_d[:, d:T], xr_s[:, 0:T - d], pr_, xr_s[:, d:T], ALU.mult, ALU.add, [tr_s, 'pwr'], [tr_d])
                    stt(xr_d[:, d:T], xi_s[:, 0:T - d], npi_, xr_d[:, d:T], ALU.mult, ALU.add, [ti_s, 'npwi', tr_d], [tr_d])
                    P.add('gpsimd', lambda e, xi_d=xi_d, xi_s=xi_s, d=d: e.tensor_copy(xi_d[:, 0:d], xi_s[:, 0:d]), r=[ti_s], w=[ti_d])
                    act(mA[:, 0:T - d], xi_s[:, 0:T - d], AF.Copy, [ti_s, 'pwr'], [tmA], scale=pr_)
                    act(mB[:, 0:T - d], xr_s[:, 0:T - d], AF.Copy, [tr_s, 'pwi'], [tmB], scale=pi_)
                    tt(xi_d[:, d:T], mA[:, 0:T - d], xi_s[:, d:T], ALU.add, [tmA, ti_s], [ti_d], eng='gpsimd')
                    tt(xi_d[:, d:T], mB[:, 0:T - d], xi_d[:, d:T], ALU.add, [tmB, ti_d], [ti_d], eng='gpsimd')
            for si, j in enumerate(js):
                gp = 4 * ct + j
                XR, XI = XRs[si], XIs[si]
                t0r, t0i = 'Xr%d_0' % si, 'Xi%d_0' % si
                P.add('vector', lambda e, gp=gp, XR=XR: e.tensor_copy(Xend[:, gp:gp + 1], XR[0][:, T - 1:T]), r=[t0r], w=['Xend'])
                P.add('vector', lambda e, gp=gp, XI=XI: e.tensor_copy(Xend[:, 32 + gp:33 + gp], XI[0][:, T - 1:T]), r=[t0i], w=['Xend'])
                for half in range(2):
                    hs = slice(half * 512, (half + 1) * 512)
                    P.mm(psum[6 + half][:, :], LCr[j][:], XR[0][:, hs], j == 0, False, r=['LCr%d' % j, t0r], w=['ps%d' % (6 + half)])
                    P.mm(psum[6 + half][:, :], LCi[j][:], XI[0][:, hs], False, j == 3, r=['LCi%d' % j, t0i], w=['ps%d' % (6 + half)])
        for half in ([] if summary else range(2)):
            hs = slice(half * 512, (half + 1) * 512)
            stt(ytmp[:], uT[:, ct, hs], dvec[:, ct:ct + 1], psum[6 + half][:, :], ALU.mult, ALU.add, ['uT%d' % ct, 'dvec', 'ps%d' % (6 + half)], ['ytmp'])
            act(yT[:, ct, hs], ytmp[:], AF.Gelu_apprx_tanh, ['ytmp'], ['yT'])
    P.dma('sync', s5X_out, Xend[:], r=['Xend'], w=['s5X_out'])
    if summary:
        P.emit()
        return nc
    P.barrier()
    for b in range(2):
        WA, wta = wblk[0], 'wblk0'
        WG, wtg = wblk[1], 'wblk1'
        P.dma('gpsimd', WA[:, 0:8, :], w_glu[:, b * 512:(b + 1) * 512].rearrange('(kt p) n -> p kt n', p=128), r=[], w=[wta])
        P.dma('gpsimd', WG[:, 0:8, :], w_glu[:, 1024 + b * 512:1024 + (b + 1) * 512].rearrange('(kt p) n -> p kt n', p=128), r=[], w=[wtg])
        for ntl in range(4):
            nt = 4 * b + ntl
            for half in range(2):
                hs = slice(half * 512, (half + 1) * 512)
                ba = nextbank(0, 6); bg = nextbank(0, 6)
                for kt in range(8):
                    P.mm(psum[ba][:, :], WA[:, kt, ntl * 128:(ntl + 1) * 128], yT[:, kt, hs], kt == 0, kt == 7, r=['yT', wta], w=['ps%d' % ba])
                for kt in range(8):
                    P.mm(psum[bg][:, :], WG[:, kt, ntl * 128:(ntl + 1) * 128], yT[:, kt, hs], kt == 0, kt == 7, r=['yT', wtg], w=['ps%d' % bg])
                act(sgm[:], psum[bg][:, :], AF.Sigmoid, ['ps%d' % bg], ['sgm'])
                tt(obT[:, nt, hs], psum[ba][:, :], sgm[:], ALU.mult, ['ps%d' % ba, 'sgm'], ['obT'])
    if dbg:
        for nt in range(8):
            P.add('vector', lambda e, nt=nt: e.tensor_copy(Xr[0][:], obT[:, nt, :]), r=['obT'], w=['Xr0_0'])
            P.dma('sync', dbgo['obT'][nt * 128:(nt + 1) * 128, :], Xr[0][:], r=['Xr0_0'], w=['dbg_obT'])
    if phases <= 3:
        P.emit()
        return nc

    P.barrier()
    sb_off[0] = mark_persist
    mergedT = sb('mergedT', [128, KT, T], BF16)
    mark4 = sb_off[0]
    oaT = sb('oaT', [128, 8, T], BF16)
    wA = [sb('wA%d' % i, [128, 8, 512], BF16) for i in range(2)]
    wB = [sb('wB%d' % i, [128, 8, 512], BF16) for i in range(2)]
    wGA = sb('wGA', [128, KT, 512], BF16)
    wGB = sb('wGB', [128, KT, 512], BF16)
    sga = sb('sga', [128, 512], F32)
    sgb = sb('sgb', [128, 512], F32)
    mm1 = sb('mm1', [128, 512], F32)
    mm2 = sb('mm2', [128, 512], F32)
    for n in range(NT):
        bank = nextbank()
        for kt in range(8):
            P.tr(psb(bank)[:, kt * 128:(kt + 1) * 128], oa_tok[:, n, kt * 128:(kt + 1) * 128], ident_b[:], r=['oa_tok', 'ident_b'], w=['ps%d' % bank])
        P.add('vector', lambda e, bank=bank, n=n: e.tensor_copy(oaT[:, :, n * 128:(n + 1) * 128], psb(bank).rearrange('p (j t) -> p j t', j=8)),
              r=['ps%d' % bank], w=['oaT'])
    for cb in range(4):
        i = cb % 2
        P.dma('gpsimd', wA[i][:], w_up_a[:, cb * 512:(cb + 1) * 512].rearrange('(kt p) n -> p kt n', p=128), r=[], w=['wA%d' % i])
        P.dma('gpsimd', wB[i][:], w_up_b[:, cb * 512:(cb + 1) * 512].rearrange('(kt p) n -> p kt n', p=128), r=[], w=['wB%d' % i])
        P.dma('gpsimd', wGA[:], w_in[:, 5120 + cb * 512:5120 + (cb + 1) * 512].rearrange('(kt p) n -> p kt n', p=128), r=[], w=['wGA'])
        P.dma('gpsimd', wGB[:], w_in[:, 7168 + cb * 512:7168 + (cb + 1) * 512].rearrange('(kt p) n -> p kt n', p=128), r=[], w=['wGB'])
        for ntl in range(4):
            nt = 4 * cb + ntl
            ns = slice(ntl * 128, (ntl + 1) * 128)
            for half in range(2):
                hs = slice(half * 512, (half + 1) * 512)
                ba, bb_, bga, bgb = nextbank(), nextbank(), nextbank(), nextbank()
                for kt in range(8):
                    P.mm(psum[ba][:, :], wA[i][:, kt, ns], oaT[:, kt, hs], kt == 0, kt == 7, r=['wA%d' % i, 'oaT'], w=['ps%d' % ba])
                for kt in range(8):
                    P.mm(psum[bb_][:, :], wB[i][:, kt, ns], obT[:, kt, hs], kt == 0, kt == 7, r=['wB%d' % i, 'obT'], w=['ps%d' % bb_])
                for kt in range(KT):
                    P.mm(psum[bga][:, :], wGA[:, kt, ns], xT[:, kt, hs], kt == 0, kt == KT - 1, r=['wGA', 'xT'], w=['ps%d' % bga])
                for kt in range(KT):
                    P.mm(psum[bgb][:, :], wGB[:, kt, ns], xT[:, kt, hs], kt == 0, kt == KT - 1, r=['wGB', 'xT'], w=['ps%d' % bgb])
                act(sga[:], psum[bga][:, :], AF.Sigmoid, ['ps%d' % bga], ['sga'])
                act(sgb[:], psum[bgb][:, :], AF.Sigmoid, ['ps%d' % bgb], ['sgb'])
                tt(mm1[:], psum[ba][:, :], sga[:], ALU.mult, ['ps%d' % ba, 'sga'], ['mm1'])
                tt(mm2[:], psum[bb_][:, :], sgb[:], ALU.mult, ['ps%d' % bb_, 'sgb'], ['mm2'])
                tt(mergedT[:, nt, hs], mm1[:], mm2[:], ALU.add, ['mm1', 'mm2'], ['mergedT'], eng='gpsimd')
    P.barrier()
    sb_off[0] = mark4
    wo_sb = sb('wo_sb', [128, KT, D], BF16)
    x1b = sb('x1b', [128, D], BF16)
    st4 = sb('st4', [128, 4], F32)
    sb_save = sb_off[0]
    sb_off[0] = reg_oa
    lng = sb('lng', [128, D], F32)
    lnb = sb('lnb', [128, D], F32)
    xres = sb('xres', [128, D], F32)
    h1 = sb('h1', [128, D], F32)
    sb_off[0] = sb_save
    x1T = xT
    for cb in range(4):
        P.dma('gpsimd', wo_sb[:, :, cb * 512:(cb + 1) * 512], w_o[:, cb * 512:(cb + 1) * 512].rearrange('(kt p) n -> p kt n', p=128), r=[], w=['wo_sb%d' % cb])

    def layer_norm(src, gamma_d, beta_d, hbuf, dst_dram, tagp):
        pass

    P.dma('sync', lng[:], ln1_g[0].partition_broadcast(128), r=[], w=['lng'])
    P.dma('sync', lnb[:], ln1_b[0].partition_broadcast(128), r=[], w=['lnb'])

    def ln_rows(hb, ht, gam, bet):
        P.add('vector', lambda e: e.tensor_reduce(st4[:, 0:1], hb[:], AX.X, ALU.add), r=[ht], w=['st4a'])
        ts(st4[:, 1:2], st4[:, 0:1], -1.0 / D, None, ALU.mult, None, ['st4a'], ['st4b'])
        ts(hb[:], hb[:], st4[:, 1:2], None, ALU.add, None, [ht, 'st4b'], [ht])
        P.add('gpsimd', lambda e: e.memset(st4[:, 2:3], 0.0), w=['st4c'])
        P.add('scalar', lambda e: e.activation(junkD[:], hb[:], AF.Square, accum_out=st4[:, 2:3]), r=[ht, 'st4c'], w=['junkD', 'st4c'])
        act(st4[:, 3:4], st4[:, 2:3], AF.Ln, ['st4c'], ['st4d'], bias=LN_EPS, scale=1.0 / D)
        act(st4[:, 3:4], st4[:, 3:4], AF.Exp, ['st4d'], ['st4d'], scale=-0.5)
        stt(hb[:], hb[:], st4[:, 3:4], gam[:], ALU.mult, ALU.mult, [ht, 'st4d', 'lng'], [ht])
        tt(hb[:], hb[:], bet[:], ALU.add, [ht, 'lnb'], [ht], eng='gpsimd')

    junkD = sb('junkD', [128, D], BF16)
    for n in range(NT):
        P.dma('sync', xres[:], x_in[n * 128:(n + 1) * 128, :], r=[], w=['xres'])
        for cb in range(4):
            bank = nextbank()
            for kt in range(KT):
                P.mm(psum[bank][:, :], mergedT[:, kt, n * 128:(n + 1) * 128], wo_sb[:, kt, cb * 512:(cb + 1) * 512], kt == 0, kt == KT - 1,
                     r=['mergedT', 'wo_sb%d' % cb], w=['ps%d' % bank])
            stt(h1[:, cb * 512:(cb + 1) * 512], xres[:, cb * 512:(cb + 1) * 512], ALPHA, psum[bank][:, :], ALU.mult, ALU.add,
                ['xres', 'ps%d' % bank], ['h1'])
        ln_rows(h1, 'h1', lng, lnb)
        P.dma('sync', x1d[n * 128:(n + 1) * 128, :], h1[:], r=['h1'], w=['x1d'])
        if dbg:
            P.dma('sync', dbgo['x1'][n * 128:(n + 1) * 128, :], h1[:], r=['h1'], w=['dbg_x1'])
        P.add('scalar', lambda e: e.copy(x1b[:], h1[:]), r=['h1'], w=['x1b'])
        for half in range(2):
            bank = nextbank()
            for j in range(8):
                kt = half * 8 + j
                P.tr(psb(bank)[:, j * 128:(j + 1) * 128], x1b[:, kt * 128:(kt + 1) * 128], ident_b[:], r=['x1b', 'ident_b'], w=['ps%d' % bank])
            P.add('vector', lambda e, bank=bank, half=half, n=n: e.tensor_copy(
                x1T[:, half * 8:(half + 1) * 8, n * 128:(n + 1) * 128], psb(bank).rearrange('p (j t) -> p j t', j=8)),
                r=['ps%d' % bank], w=['x1T'])
    if phases <= 4:
        P.emit()
        return nc

    P.barrier()
    sb_off[0] = mark_persist
    s_d = nc.dram_tensor('s_d', [T, 16, 128], F32).ap()
    wblk = [sb('wblk%d' % i, [128, KT, 512], BF16) for i in range(2)]
    qTb = sb('qTb', [128, 4, T], F32)
    kld = sb('kld', [128, 16, 128], F32)
    KTt = sb('KTt', [128, 16, 128], F32)
    s_st = [sb('s_st%d' % i, [128, 4, 128], F32) for i in range(2)]
    s_sb = sb('s_sb', [128, 16, 128], F32)
    mr = sb('mr', [128, 128], F32)
    top = sb('top', [128, 16, 16], F32)
    cand = sb('cand', [128, 8, 256], F32)
    cand2 = sb('cand2', [128, 8, 256], F32)
    vals = sb('vals', [128, 8, 16], F32)
    ev = sb('ev', [128, 8, 16], F32)
    Zs = sb('Zs', [128, 8], F32)
    nb = sb('nb', [128, 8], F32)
    Tbuf2 = [sb('Tbuf%d' % i, [128, 16, 128], F32) for i in range(2)]
    Eb2 = [sb('Eb%d' % i, [128, 2048], BF16) for i in range(2)]
    Pm2 = [sb('Pm%d' % i, [128, 2048], BF16) for i in range(2)]
    sb_save = sb_off[0]
    sb_off[0] = reg_oa
    G = sb('G', [128, 16384], BF16)
    sb_off[0] = sb_save

    P.dma('sync', kld[:], keys.rearrange('h i d -> i h d'), r=[], w=['kld'])
    for q4 in range(4):
        bank = nextbank()
        for hl in range(4):
            hh = 4 * q4 + hl
            P.tr(psum[bank][:, hl * 128:(hl + 1) * 128], kld[:, hh, :], ident_f[:], r=['kld', 'ident_f'], w=['ps%d' % bank])
        P.add('vector', lambda e, bank=bank, q4=q4: e.tensor_copy(KTt[:, 4 * q4:4 * q4 + 4, :], psum[bank][:, :].rearrange('p (a b) -> p a b', a=4)),
              r=['ps%d' % bank], w=['KTt'])
    for blk in range(4):
        wb, wt = load_wblk(w_q, blk * 512)
        for hl in range(4):
            for half in range(2):
                hs = slice(half * 512, (half + 1) * 512)
                bank = nextbank()
                for kt in range(KT):
                    P.mm(psum[bank][:, :], wb[:, kt, hl * 128:(hl + 1) * 128], x1T[:, kt, hs], kt == 0, kt == KT - 1, r=[wt, 'x1T'], w=['ps%d' % bank])
                P.add('scalar', lambda e, bank=bank, hl=hl, hs=hs: e.copy(qTb[:, hl, hs], psum[bank][:, :]), r=['ps%d' % bank], w=['qTb%d' % hl])
        for n in range(NT):
            bank = nextbank()
            i2 = n % 2
            for hl in range(4):
                P.mm(psum[bank][:, hl * 128:(hl + 1) * 128], qTb[:, hl, n * 128:(n + 1) * 128], KTt[:, 4 * blk + hl, :], True, True,
                     r=['qTb%d' % hl, 'KTt'], w=['ps%d' % bank])
            P.add('vector', lambda e, bank=bank, i2=i2: e.tensor_copy(s_st[i2][:], psum[bank][:, :].rearrange('p (a b) -> p a b', a=4)),
                  r=['ps%d' % bank], w=['s_st%d' % i2])
            P.dma('sync', s_d[n * 128:(n + 1) * 128, 4 * blk:4 * blk + 4, :], s_st[i2][:], r=['s_st%d' % i2], w=['s_d'])
    P.barrier()
    top4 = top[:].rearrange('p (h two) k -> p h two k', two=2)
    gcnt = 0
    for n in range(NT):
        P.dma('sync', s_sb[:], s_d[n * 128:(n + 1) * 128], r=[], w=['s_sb'])
        for hh in range(16):
            P.add('vector', lambda e, hh=hh: e.max(out=top[:, hh, 0:8], in_=s_sb[:, hh, :]), r=['s_sb'], w=['top'])
            P.add('vector', lambda e, hh=hh: e.match_replace(out=mr[:], in_to_replace=top[:, hh, 0:8], in_values=s_sb[:, hh, :], imm_value=-1e30),
                  r=['s_sb', 'top'], w=['mr'])
            P.add('vector', lambda e, hh=hh: e.max(out=top[:, hh, 8:16], in_=mr[:]), r=['mr'], w=['top'])
        tt(cand[:].rearrange('p h (a b) -> p h a b', a=16), bc(top4[:, :, 0, :].unsqueeze(3), [128, 8, 16, 16]),
           bc(top4[:, :, 1, :].unsqueeze(2), [128, 8, 16, 16]), ALU.add, ['top'], ['cand'])
        for h in range(8):
            P.add('vector', lambda e, h=h: e.max(out=vals[:, h, 0:8], in_=cand[:, h, :]), r=['cand'], w=['vals'])
            P.add('vector', lambda e, h=h: e.match_replace(out=cand2[:, h, :], in_to_replace=vals[:, h, 0:8], in_values=cand[:, h, :], imm_value=-1e30),
                  r=['cand', 'vals'], w=['cand2'])
            P.add('vector', lambda e, h=h: e.max(out=vals[:, h, 8:16], in_=cand2[:, h, :]), r=['cand2'], w=['vals'])
        tt(ev[:], vals[:], bc(vals[:, :, 0:1], [128, 8, 16]), ALU.subtract, ['vals'], ['ev'])
        act(ev[:], ev[:], AF.Exp, ['ev'], ['ev'])
        P.add('vector', lambda e: e.tensor_reduce(Zs[:], ev[:], AX.X, ALU.add), r=['ev'], w=['Zs'])
        act(Zs[:], Zs[:], AF.Ln, ['Zs'], ['Zs'])
        tt(nb[:], Zs[:], vals[:, :, 0], ALU.add, ['Zs', 'vals'], ['nb'])
        ts(nb[:], nb[:], -1.0, None, ALU.mult, None, ['nb'], ['nb'])
        its = [(h, ic) for ic in range(8) for h in range(8)]

        def emitT(idx):
            h, ic = its[idx]
            pi_ = idx % 2
            isl = slice(ic * 16, (ic + 1) * 16)
            tt(Tbuf2[pi_][:], bc(s_sb[:, 2 * h, isl].unsqueeze(2), [128, 16, 128]), bc(s_sb[:, 2 * h + 1, :].unsqueeze(1), [128, 16, 128]),
               ALU.add, ['s_sb'], ['Tbuf%d' % pi_])

        emitT(0)
        for idx, (h, ic) in enumerate(its):
            if idx + 1 < len(its):
                emitT(idx + 1)
            pi_ = idx % 2
            Tb, Ebb, Pmb = Tbuf2[pi_], Eb2[pi_], Pm2[pi_]
            tT, tE, tP = 'Tbuf%d' % pi_, 'Eb%d' % pi_, 'Pm%d' % pi_
            gsl = slice(ic * 2048, (ic + 1) * 2048)
            bset = (ic % 2) * 4
            Tf = Tb[:].rearrange('p a b -> p (a b)')
            act(Ebb[:], Tf, AF.Exp, [tT, 'nb'], [tE], bias=nb[:, h:h + 1])
            stt(Pmb[:], Tf, vals[:, h, 15:16], Ebb[:], ALU.is_ge, ALU.mult, [tT, 'vals', tE], [tP])
            for q4 in range(4):
                P.mm(psum[bset + q4][:, :], ident_b[:], Pmb[:, q4 * 512:(q4 + 1) * 512], h == 0, h == 7, r=['ident_b', tP], w=['ps%d' % (bset + q4)])
            if h == 7:
                for q4 in range(4):
                    act(G[:, ic * 2048 + q4 * 512:ic * 2048 + (q4 + 1) * 512], psum[bset + q4][:, :], AF.Copy, ['ps%d' % (bset + q4)], ['G%d' % ic])
        P.dma('sync', Gd[n * 128:(n + 1) * 128, :], G[:], r=['G%d' % i for i in range(8)], w=['Gd'])

    P.barrier()
    sb_off[0] = mark_persist
    yacc = sb('yacc', [128, NT, D], F32)
    ublk = [sb('ublk%d' % i, [128, 4, D], BF16) for i in range(2)]
    sb_save = sb_off[0]
    sb_off[0] = reg_oa
    vblk = [sb('vblk%d' % i, [128, 4, D], BF16) for i in range(2)]
    sb_off[0] = sb_save
    uTb = sb('uTb', [128, KT, 512], BF16)
    Gs = [sb('Gs%d' % i, [128, 512], BF16) for i in range(2)]
    gl2 = [sb('gl%d' % i, [128, 512], F32) for i in range(2)]
    Wb2 = [sb('Wb%d' % i, [128, 512], BF16) for i in range(2)]
    WT2 = [sb('WT%d' % i, [128, 4, 128], BF16) for i in range(2)]
    def load_eb(eb):
        i = eb % 2
        P.dma('gpsimd', ublk[i][:], peer_u[eb * 512:(eb + 1) * 512, :].rearrange('(et p) d -> p et d', p=128), r=[], w=['ublk%d' % i])
        P.dma('gpsimd', vblk[i][:], peer_v[eb * 512:(eb + 1) * 512, :].rearrange('(et p) d -> p et d', p=128), r=[], w=['vblk%d' % i])
        for k2 in range(8):
            bank = nextbank()
            for kk in range(2):
                kt = 2 * k2 + kk
                for et in range(4):
                    P.tr(psb(bank)[:, (kk * 4 + et) * 128:(kk * 4 + et + 1) * 128], ublk[i][:, et, kt * 128:(kt + 1) * 128], ident_b[:],
                         r=['ublk%d' % i, 'ident_b'], w=['ps%d' % bank])
            if k2 % 2 == 0:
                P.add('vector', lambda e, bank=bank, k2=k2: e.tensor_copy(uTb[:, 2 * k2:2 * k2 + 2, :], psb(bank).rearrange('p (k e) -> p k e', k=2)),
                      r=['ps%d' % bank], w=['uTb'])
            else:
                P.add('scalar', lambda e, bank=bank, k2=k2: e.copy(uTb[:, 2 * k2:2 * k2 + 2, :], psb(bank).rearrange('p (k e) -> p k e', k=2)),
                      r=['ps%d' % bank], w=['uTb'])

    items = [(eb, n) for eb in range(32) for n in range(NT)]

    def stage_A(idx):
        eb, n = items[idx]
        gi = idx % 2
        if n == 0:
            load_eb(eb)
        P.dma('sync', Gs[gi][:], Gd[n * 128:(n + 1) * 128, eb * 512:(eb + 1) * 512], r=[], w=['Gs%d' % gi])
        bA = nextbank()
        for kt in range(KT):
            P.mm(psum[bA][:, :], x1T[:, kt, n * 128:(n + 1) * 128], uTb[:, kt, :], kt == 0, kt == KT - 1, r=['x1T', 'uTb'], w=['ps%d' % bA])
        act(gl2[gi][:], psum[bA][:, :], AF.Gelu_apprx_tanh, ['ps%d' % bA], ['gl%d' % gi])
        tt(Wb2[gi][:], gl2[gi][:], Gs[gi][:], ALU.mult, ['gl%d' % gi, 'Gs%d' % gi], ['Wb%d' % gi], eng='gpsimd')

    def stage_TY(idx):
        eb, n = items[idx]
        gi = idx % 2
        i = eb % 2
        Wb, WT = Wb2[gi], WT2[gi]
        tw, twt = 'Wb%d' % gi, 'WT%d' % gi
        bT_ = nextbank()
        for et in range(4):
            P.tr(psb(bT_)[:, et * 128:(et + 1) * 128], Wb[:, et * 128:(et + 1) * 128], ident_b[:], r=[tw, 'ident_b'], w=['ps%d' % bT_])
        P.add('scalar', lambda e, bT_=bT_, WT=WT: e.copy(WT[:], psb(bT_)[:, 0:512].rearrange('p (a b) -> p a b', a=4)), r=['ps%d' % bT_], w=[twt])
        for db in range(4):
            ds_ = slice(db * 512, (db + 1) * 512)
            bY = nextbank()
            for et in range(4):
                P.mm(psum[bY][:, :], WT[:, et, :], vblk[i][:, et, ds_], et == 0, et == 3, r=[twt, 'vblk%d' % i], w=['ps%d' % bY])
            if eb == 0:
                P.add('scalar', lambda e, bY=bY, n=n, ds_=ds_: e.copy(yacc[:, n, ds_], psum[bY][:, :]), r=['ps%d' % bY], w=['yacc%d' % n])
            else:
                tt(yacc[:, n, ds_], yacc[:, n, ds_], psum[bY][:, :], ALU.add, ['yacc%d' % n, 'ps%d' % bY], ['yacc%d' % n])

    stage_A(0)
    for idx in range(len(items)):
        if idx + 1 < len(items):
            stage_A(idx + 1)
        stage_TY(idx)

    P.barrier()
    st4_2 = sb('st4b', [128, 4], F32)
    junkD_2 = sb('junkD2', [128, D], BF16)
    sb_save = sb_off[0]
    sb_off[0] = reg_oa
    lng_2 = sb('lng2', [128, D], F32)
    lnb_2 = sb('lnb2', [128, D], F32)
    xres_2 = sb('xres2', [128, D], F32)
    h1_2 = sb('h12', [128, D], F32)
    sb_off[0] = sb_save
    P.dma('sync', lng_2[:], ln2_g[0].partition_broadcast(128), r=[], w=['lng'])
    P.dma('sync', lnb_2[:], ln2_b[0].partition_broadcast(128), r=[], w=['lnb'])

    def ln_rows2(hb, ht, gam, bet):
        P.add('vector', lambda e: e.tensor_reduce(st4_2[:, 0:1], hb[:], AX.X, ALU.add), r=[ht], w=['st4a'])
        ts(st4_2[:, 1:2], st4_2[:, 0:1], -1.0 / D, None, ALU.mult, None, ['st4a'], ['st4b'])
        ts(hb[:], hb[:], st4_2[:, 1:2], None, ALU.add, None, [ht, 'st4b'], [ht])
        P.add('gpsimd', lambda e: e.memset(st4_2[:, 2:3], 0.0), w=['st4c'])
        P.add('scalar', lambda e: e.activation(junkD_2[:], hb[:], AF.Square, accum_out=st4_2[:, 2:3]), r=[ht, 'st4c'], w=['junkD', 'st4c'])
        act(st4_2[:, 3:4], st4_2[:, 2:3], AF.Ln, ['st4c'], ['st4d'], bias=LN_EPS, scale=1.0 / D)
        act(st4_2[:, 3:4], st4_2[:, 3:4], AF.Exp, ['st4d'], ['st4d'], scale=-0.5)
        stt(hb[:], hb[:], st4_2[:, 3:4], gam[:], ALU.mult, ALU.mult, [ht, 'st4d', 'lng'], [ht])
        tt(hb[:], hb[:], bet[:], ALU.add, [ht, 'lnb'], [ht], eng='gpsimd')

    for n in range(NT):
        P.dma('sync', xres_2[:], x1d[n * 128:(n + 1) * 128, :], r=[], w=['xres'])
        stt(h1_2[:], xres_2[:], ALPHA, yacc[:, n, :], ALU.mult, ALU.add, ['xres', 'yacc%d' % n], ['h1'])
        ln_rows2(h1_2, 'h1', lng_2, lnb_2)
        P.dma('sync', x_out[n * 128:(n + 1) * 128, :], h1_2[:], r=['h1'], w=['x_out'])
    P.emit()
    return nc


def layer_inputs(z, l, phases=99):
    f = lambda a: np.ascontiguousarray(a, dtype=np.float32)
    m = {
        'w_in': f(z['w_in'][l]),
        'lb_logits': f(z['hgrn_lb_logits']),
        'lmask': np.array([[0.0, float(l >= 1), float(l >= 2), float(l >= 3)]], np.float32),
        'norm_g': f(z['hgrn_norm_g'][l]).reshape(1, 1024),
    }
    if phases > 2:
        m.update({
            'lam_re': f(z['s5_lambda_re'][l]).reshape(32, 128),
            'lam_im': f(z['s5_lambda_im'][l]).reshape(32, 128),
            'log_step': f(z['s5_log_step'][l]).reshape(32, 2),
            'b_re': f(z['s5_b_re'][l]),
            'b_im': f(z['s5_b_im'][l]),
            'c_re': f(z['s5_c_re'][l]).reshape(1024, 64),
            'c_im': f(z['s5_c_im'][l]).reshape(1024, 64),
            's5_d': f(z['s5_d'][l]).reshape(8, 128),
            'w_glu': f(z['s5_w_glu'][l]),
        })
    if phases > 3:
        m.update({
            'w_up_a': f(z['w_up_a'][l]), 'w_up_b': f(z['w_up_b'][l]), 'w_o': f(z['w_o'][l]),
            'ln1_g': f(z['ln1_g'][l]).reshape(1, D), 'ln1_b': f(z['ln1_b'][l]).reshape(1, D),
        })
    if phases > 4:
        m.update({
            'w_q': f(z['peer_w_q'][l]), 'keys': f(z['peer_keys'][l]).reshape(16, 128, 128),
            'peer_u': f(z['peer_u'][l]), 'peer_v': f(z['peer_v'][l]),
            'ln2_g': f(z['ln2_g'][l]).reshape(1, D), 'ln2_b': f(z['ln2_b'][l]).reshape(1, D),
        })
    return m


def kernel(**inputs):
    z = inputs
    f32 = np.float32
    x = np.ascontiguousarray(np.asarray(z['x'])[0], dtype=f32)
    ncA = build(dbg=False, phases=3, summary=True)
    ncB = build(dbg=False, phases=99)
    xs = [np.ascontiguousarray(x[c * T:(c + 1) * T]) for c in range(NCORES)]
    zU = np.zeros((NCORES, 8, 128, 128), f32)
    zD = np.zeros((NCORES, 128, 8), f32)
    zX = np.zeros((NCORES, 128, 64), f32)
    zc = np.zeros((1, NCORES), f32)
    cores = list(range(NCORES))
    for l in range(4):
        mA = layer_inputs(z, l, 3)
        in_maps = [dict(mA, x=xs[c], cmask=zc, hgU_all=zU, hgD_all=zD, s5X_all=zX) for c in cores]
        rA = run_bass_kernel_spmd(ncA, in_maps, core_ids=cores).results
        U = np.ascontiguousarray(np.stack([rA[c]['hgU_out'] for c in cores]), dtype=f32)
        Dd = np.ascontiguousarray(np.stack([rA[c]['hgD_out'] for c in cores]), dtype=f32)
        Xs = np.ascontiguousarray(np.stack([rA[c]['s5X_out'] for c in cores]), dtype=f32)
        mB = layer_inputs(z, l, 99)
        in_maps = [dict(mB, x=xs[c], cmask=(np.arange(NCORES) < c).astype(f32)[None, :],
                        hgU_all=U, hgD_all=Dd, s5X_all=Xs) for c in cores]
        rB = run_bass_kernel_spmd(ncB, in_maps, core_ids=cores).results
        xs = [np.ascontiguousarray(rB[c]['x_out'], dtype=f32) for c in cores]
    return np.concatenate(xs, axis=0)[None].astype(f32)
```

```python
import math
from contextlib import ExitStack
import numpy as np
import concourse.bass as bass
import concourse.mybir as mybir
from concourse.bass_utils import run_bass_kernel_spmd

F32 = mybir.dt.float32
BF16 = mybir.dt.bfloat16
ALU = mybir.AluOpType
AF = mybir.ActivationFunctionType
AX = mybir.AxisListType

ENGS = ['tensor', 'vector', 'scalar', 'gpsimd', 'sync']
NDSEM = 6
NCORES = 8
T = 1024
NT = 8
D = 2048
KT = 16
ALPHA = 8.0 ** 0.25
LN_EPS = 1e-5
RMS_EPS = 1e-6
TWO_PI = 2.0 * math.pi
MAGIC = 12582912.0


class Prog:
    def __init__(self, nc):
        self.nc = nc
        self.ops = {e: [] for e in ENGS}
        self.lastw = {}
        self.readers = {}
        self.ndma = {e: 0 for e in ENGS}
        self.dma_ops = {e: [] for e in ENGS}
        self.pending = {e: [] for e in ENGS}

    def add(self, eng, fn, r=(), w=(), dma=False):
        idx = len(self.ops[eng])
        deps = list(self.pending[eng])
        self.pending[eng] = []
        for t in r:
            if t in self.lastw:
                deps.append(self.lastw[t])
        for t in w:
            if t in self.lastw:
                deps.append(self.lastw[t])
            last = {}
            for ref in self.readers.get(t, ()):
                if self.ops[ref[0]][ref[1]]['dma']:
                    deps.append(ref)
                else:
                    last[ref[0]] = ref
            deps.extend(last.values())
        op = dict(fn=fn, deps=deps, dma=dma, signal=False, eng=eng, idx=idx)
        if dma:
            j = self.ndma[eng]
            self.ndma[eng] += 1
            op['dj'] = j
            if j >= NDSEM:
                deps.append(self.dma_ops[eng][j - NDSEM])
            self.dma_ops[eng].append((eng, idx))
        self.ops[eng].append(op)
        ref = (eng, idx)
        for t in r:
            self.readers.setdefault(t, []).append(ref)
        for t in w:
            self.lastw[t] = ref
            self.readers[t] = []
        return ref

    def barrier(self):
        deps = []
        for e in ENGS:
            nonD = [i for i, o in enumerate(self.ops[e]) if not o['dma']]
            if nonD:
                deps.append((e, nonD[-1]))
            deps.extend(self.dma_ops[e][-NDSEM:])
        for e in ENGS:
            self.pending[e] = list(self.pending[e]) + deps
        self.lastw = {}
        self.readers = {}

    def mm(self, out, lhsT, rhs, start, stop, r, w):
        return self.add('tensor', lambda e: e.matmul(out, lhsT, rhs, start=start, stop=stop), r, w)

    def tr(self, out, in_, ident, r, w):
        return self.add('tensor', lambda e: e.transpose(out, in_, ident), r, w)

    def dma(self, eng, out, in_, r, w, **kw):
        return self.add(eng, lambda e: e.dma_start(out=out, in_=in_, **kw), r, w, dma=True)

    def emit(self):
        nc = self.nc
        ops = self.ops
        for e in ENGS:
            for op in ops[e]:
                nd = []
                seen = set()
                for d in op['deps']:
                    if d in seen:
                        continue
                    seen.add(d)
                    dop = ops[d[0]][d[1]]
                    if not dop['dma']:
                        if d[0] == e and e == 'tensor':
                            continue
                        dop['signal'] = True
                    nd.append(d)
                op['deps'] = nd
        for e in ENGS:
            c = 0
            for op in ops[e]:
                if op['dma']:
                    continue
                if op['signal']:
                    c += 1
                    op['semval'] = c
        with ExitStack() as st:
            esem = {e: st.enter_context(nc.semaphore('s_' + e)) for e in ENGS}
            dsem = {e: [st.enter_context(nc.semaphore('d_%s%d' % (e, i))) for i in range(NDSEM)]
                    for e in ENGS if self.ndma[e] > 0}
            block = st.enter_context(nc.Block())

            def run(e, eng):
                waited = {e2: 0 for e2 in ENGS}
                dwaited = {e2: {} for e2 in ENGS}
                for op in ops[e]:
                    for d in op['deps']:
                        dop = ops[d[0]][d[1]]
                        if dop['dma']:
                            j = dop['dj']
                            slot = j % NDSEM
                            if dwaited[d[0]].get(slot, -1) >= j:
                                continue
                            dwaited[d[0]][slot] = j
                            eng.wait_ge(dsem[d[0]][slot], 16 * (j // NDSEM + 1))
                        else:
                            v = dop['semval']
                            if waited[d[0]] >= v:
                                continue
                            waited[d[0]] = v
                            eng.wait_ge(esem[d[0]], v)
                    ins = op['fn'](eng)
                    if op['dma']:
                        ins.then_inc(dsem[e][op['dj'] % NDSEM], 16)
                    elif op['signal']:
                        ins.then_inc(esem[e], 1)
                if self.ndma[e] > 0:
                    n = self.ndma[e]
                    for i in range(NDSEM):
                        cnt = (n - i + NDSEM - 1) // NDSEM if n > i else 0
                        if cnt > 0:
                            eng.wait_ge(dsem[e][i], 16 * cnt)

            for e in ENGS:
                if ops[e]:
                    getattr(block, e)(lambda eng, e=e: run(e, eng))


def bc(ap, shape):
    return ap.to_broadcast(list(shape))


def build(dbg=False, phases=99, summary=False):
    nc = bass.Bass('TRN2', target_bir_lowering=False)
    P = Prog(nc)

    def din(name, shape):
        return nc.dram_tensor(name, list(shape), F32, kind='ExternalInput').ap()

    def dout(name, shape):
        return nc.dram_tensor(name, list(shape), F32, kind='ExternalOutput').ap()

    x_in = din('x', [T, D])
    w_in = din('w_in', [D, 9216])
    lb_logits = din('lb_logits', [4, 1024])
    lmask = din('lmask', [1, 4])
    cmask = din('cmask', [1, 8])
    norm_g = din('norm_g', [1, 1024])
    if phases > 2:
        lam_re = din('lam_re', [32, 128])
        lam_im = din('lam_im', [32, 128])
        log_step = din('log_step', [32, 2])
        b_re = din('b_re', [64, 64, 16])
        b_im = din('b_im', [64, 64, 16])
        c_re = din('c_re', [1024, 64])
        c_im = din('c_im', [1024, 64])
        s5_d = din('s5_d', [8, 128])
        w_glu = din('w_glu', [1024, 2048])
        s5X_all = din('s5X_all', [NCORES, 128, 64])
        s5X_out = dout('s5X_out', [128, 64])
    if phases > 3:
        w_up_a = din('w_up_a', [1024, 2048])
        w_up_b = din('w_up_b', [1024, 2048])
        w_o = din('w_o', [2048, 2048])
        ln1_g = din('ln1_g', [1, D])
        ln1_b = din('ln1_b', [1, D])
    if phases > 4:
        w_q = din('w_q', [2048, 2048])
        keys = din('keys', [16, 128, 128])
        peer_u = din('peer_u', [16384, 2048])
        peer_v = din('peer_v', [16384, 2048])
        ln2_g = din('ln2_g', [1, D])
        ln2_b = din('ln2_b', [1, D])
        x_out = dout('x_out', [T, D])
    hgU_all = din('hgU_all', [NCORES, 8, 128, 128])
    hgD_all = din('hgD_all', [NCORES, 128, 8])

    hgU_out = dout('hgU_out', [8, 128, 128])
    hgD_out = dout('hgD_out', [128, 8])
    dbgo = {}
    if dbg:
        dbgo['oa'] = dout('dbg_oa', [T, 1024])
        dbgo['obT'] = dout('dbg_obT', [1024, T])
        dbgo['x1'] = dout('dbg_x1', [T, D])
        dbgo['yT'] = dout('dbg_yT', [1024, T])

    x1d = nc.dram_tensor('x1d', [T, D], F32).ap()
    Gd = nc.dram_tensor('Gd', [T, 16384], BF16).ap()

    sb_off = [16512]
    sb_cnt = [0]

    def sb(name, shape, dt):
        n = 1
        for v in shape[1:]:
            n *= v
        nbytes = n * (4 if dt == F32 else 2)
        off = sb_off[0]
        sb_off[0] = (off + nbytes + 63) // 64 * 64
        assert sb_off[0] <= 229000, (name, sb_off[0])
        sb_cnt[0] += 1
        return nc.alloc_sbuf_tensor_at('%s_%d' % (name, sb_cnt[0]), list(shape), dt, offset=off)

    ident_b = sb('ident_b', [128, 128], BF16)
    ident_f = sb('ident_f', [128, 128], F32)
    tri01 = sb('tri01', [128, 64], F32)
    mask01 = sb('mask01', [128, T], F32)
    psum = [nc.alloc_psum_tensor('ps%d' % i, [128, 512], F32) for i in range(8)]

    def psb(i):
        return psum[i][:].bitcast(BF16)

    P.add('gpsimd', lambda e: e.memset(ident_f[:], 1.0), w=['ident_f'])
    P.add('gpsimd', lambda e: e.affine_select(out=ident_f[:], in_=ident_f[:], pattern=[[-1, 128]], compare_op=ALU.is_equal,
                                              fill=0.0, base=0, channel_multiplier=1), r=['ident_f'], w=['ident_f'])
    P.add('vector', lambda e: e.tensor_copy(ident_b[:], ident_f[:]), r=['ident_f'], w=['ident_b'])
    triA = sb('triA', [128, 64], F32)
    triB = sb('triB', [128, 64], F32)
    P.add('gpsimd', lambda e: e.memset(triA[:], 1.0), w=['triA'])
    P.add('gpsimd', lambda e: e.memset(triB[:], 1.0), w=['triB'])
    P.add('gpsimd', lambda e: e.affine_select(out=triA[:], in_=triA[:], pattern=[[1, 64]], compare_op=ALU.is_ge,
                                              fill=0.0, base=0, channel_multiplier=-1), r=['triA'], w=['triA'])
    P.add('gpsimd', lambda e: e.affine_select(out=triB[:], in_=triB[:], pattern=[[1, 64]], compare_op=ALU.is_ge,
                                              fill=0.0, base=64, channel_multiplier=-1), r=['triB'], w=['triB'])
    P.add('vector', lambda e: e.tensor_copy(tri01[0:64, :], triA[0:64, :]), r=['triA'], w=['tri01'])
    P.add('vector', lambda e: e.tensor_copy(tri01[64:128, :], triB[64:128, :]), r=['triB', 'tri01'], w=['tri01'])
    P.add('gpsimd', lambda e: e.memset(mask01[:], 1.0), w=['mask01'])
    P.add('gpsimd', lambda e: e.memset(mask01[:].rearrange('p (c s) -> p c s', s=64)[:, :, 0:1], 0.0), r=['mask01'], w=['mask01'])

    xT = sb('xT', [128, KT, T], BF16)
    reg_oa = sb_off[0]
    oa_tok = sb('oa_tok', [128, NT, 1024], BF16)
    obT = sb('obT', [128, 8, T], BF16)
    lbl = sb('lbl', [128, 4, 8], F32)
    lbe = sb('lbe', [128, 4, 8], F32)
    lbs = sb('lbs', [128, 8], F32)
    lb = sb('lb', [128, 8], F32)
    oml = sb('oml', [128, 8], F32)
    lmk = sb('lmk', [128, 4], F32)
    cmk = sb('cmk', [128, 8], F32)
    mark_persist = sb_off[0]
    xload = [sb('xload%d' % i, [128, D], F32) for i in range(2)]
    xbf = [sb('xbf%d' % i, [128, D], BF16) for i in range(2)]

    def to_featmajor(src_dram, dstT, tokpref):
        for n in range(NT):
            xl = xload[n % 2]; xb = xbf[n % 2]
            tl = 'xload%d' % (n % 2); tb = 'xbf%d' % (n % 2)
            P.dma('sync', xl[:], src_dram[n * 128:(n + 1) * 128, :], r=[], w=[tl])
            P.add('scalar', lambda e, xl=xl, xb=xb: e.copy(xb[:], xl[:]), r=[tl], w=[tb])
            for half in range(2):
                bank = half
                for j in range(8):
                    kt = half * 8 + j
                    P.tr(psb(bank)[:, j * 128:(j + 1) * 128], xb[:, kt * 128:(kt + 1) * 128], ident_b[:],
                         r=[tb, 'ident_b'], w=['ps%d' % bank])
                P.add('vector', lambda e, bank=bank, half=half, n=n: e.tensor_copy(
                    dstT[:, half * 8:(half + 1) * 8, n * 128:(n + 1) * 128],
                    psb(bank).rearrange('p (j t) -> p j t', j=8)), r=['ps%d' % bank], w=[tokpref])

    to_featmajor(x_in, xT, 'xT')

    P.dma('sync', lbl[:], lb_logits.rearrange('l (h d) -> d l h', d=128), r=[], w=['lbl'], allow_slow_non_contiguous=True)
    P.dma('sync', lmk[:], lmask[0].partition_broadcast(128), r=[], w=['lmk'])
    P.dma('sync', cmk[:], cmask[0].partition_broadcast(128), r=[], w=['cmk'])
    P.add('scalar', lambda e: e.activation(lbe[:], lbl[:], AF.Exp), r=['lbl'], w=['lbe'])
    P.add('vector', lambda e: e.tensor_reduce(lbs[:], lbe[:].rearrange('p l h -> p h l'), AX.X, ALU.add), r=['lbe'], w=['lbs'])
    P.add('vector', lambda e: e.reciprocal(lbs[:], lbs[:]), r=['lbs'], w=['lbs'])
    P.add('vector', lambda e: e.tensor_tensor(lbe[:], lbe[:], bc(lmk[:].unsqueeze(2), [128, 4, 8]), ALU.mult), r=['lbe', 'lmk'], w=['lbe'])
    P.add('vector', lambda e: e.tensor_reduce(lb[:], lbe[:].rearrange('p l h -> p h l'), AX.X, ALU.add), r=['lbe'], w=['lb'])
    P.add('vector', lambda e: e.tensor_tensor(lb[:], lb[:], lbs[:], ALU.mult), r=['lb', 'lbs'], w=['lb'])
    P.add('vector', lambda e: e.tensor_scalar(oml[:], lb[:], -1.0, 1.0, ALU.mult, ALU.add), r=['lb'], w=['oml'])

    v_tok = sb('v_tok', [128, NT, 1024], BF16)
    sg_tok = sb('sg_tok', [128, NT, 1024], BF16)
    wblk = [sb('wblk%d' % i, [128, KT, 512], BF16) for i in range(2)]
    wcnt = [0]

    def load_wblk(src, c0, ncols=512, kt=KT):
        i = wcnt[0] % 2
        wcnt[0] += 1
        P.dma('gpsimd', wblk[i][:, 0:kt, 0:ncols], src[:, c0:c0 + ncols].rearrange('(kt p) n -> p kt n', p=128),
              r=[], w=['wblk%d' % i])
        return wblk[i], 'wblk%d' % i

    pscnt = [0]

    def nextbank(lo=0, hi=8):
        b = lo + pscnt[0] % (hi - lo)
        pscnt[0] += 1
        return b

    for blk in range(4):
        wb, wt = load_wblk(w_in, 2048 + blk * 512)
        for n in range(NT):
            bank = nextbank()
            for kt in range(KT):
                P.mm(psum[bank][:, :], xT[:, kt, n * 128:(n + 1) * 128], wb[:, kt, :], kt == 0, kt == KT - 1,
                     r=['xT', wt], w=['ps%d' % bank])
            if blk < 2:
                P.add('vector', lambda e, bank=bank, n=n, blk=blk: e.tensor_copy(v_tok[:, n, blk * 512:(blk + 1) * 512], psum[bank][:, :]),
                      r=['ps%d' % bank], w=['v_tok%d' % n])
            else:
                P.add('scalar', lambda e, bank=bank, n=n, blk=blk: e.activation(sg_tok[:, n, (blk - 2) * 512:(blk - 1) * 512], psum[bank][:, :], AF.Sigmoid),
                      r=['ps%d' % bank], w=['sg_tok%d' % n])

    ngb = sb('ngb', [128, 1024], F32)
    P.dma('sync', ngb[:], norm_g[0].partition_broadcast(128), r=[], w=['ngb'])
    Dp = sb('Dp', [128, NCORES, 8], F32)
    P.dma('sync', Dp[:], hgD_all.rearrange('c d h -> d c h'), r=[], w=['Dp'])
    P.add('vector', lambda e: e.tensor_scalar(Dp[:], Dp[:], -1.0, None, ALU.add), r=['Dp'], w=['Dp'])
    P.add('vector', lambda e: e.tensor_tensor(Dp[:], Dp[:], bc(cmk[:].unsqueeze(2), [128, NCORES, 8]), ALU.mult), r=['Dp', 'cmk'], w=['Dp'])
    P.add('vector', lambda e: e.tensor_scalar(Dp[:], Dp[:], 1.0, None, ALU.add), r=['Dp'], w=['Dp'])

    qT = sb('qT', [128, T], F32)
    fT = sb('fT', [128, T], F32)
    lgf = sb('lgf', [128, T], F32)
    kT_ = sb('kT_', [128, T], F32)
    bT = sb('bT', [128, T], F32)
    eq = sb('eq', [128, T], F32)
    ek = sb('ek', [128, T], F32)
    qt_b = sb('qt_b', [128, T], BF16)
    kt_b = sb('kt_b', [128, T], BF16)
    ktok = sb('ktok', [128, NT, 128], BF16)
    rr = sb('rr', [128, 16], F32)
    er = sb('er', [128, 16], F32)
    e2r = sb('e2r', [128, 16], F32)
    bsum = sb('bsum', [128, 1], F32)
    Dtot = sb('Dtot', [128, 8], F32)
    S = sb('S', [128, 128], F32)
    Uld = [sb('Uld%d' % i, [128, 128], F32) for i in range(2)]
    Spb = [sb('Spb%d' % i, [128, 128], BF16) for i in range(4)]
    Utmp = sb('Utmp', [128, 128], F32)
    att_sb = [sb('att_sb%d' % i, [128, 128], BF16) for i in range(2)]
    ss = sb('ss', [128, 2], F32)
    junk = sb('junk', [128, 128], F32)
    otmp = sb('otmp', [128, 128], F32)
    for i in range(2):
        P.add('gpsimd', lambda e, i=i: e.memset(att_sb[i][:], 0.0), w=['att_sb%d' % i])

    for h in range(8):
        P.add('gpsimd', lambda e: e.memset(S[:], 0.0), w=['S'])
        for j in range(NCORES - 1):
            ul = Uld[j % 2]; ut = 'Uld%d' % (j % 2)
            P.dma('sync', ul[:], hgU_all[j, h], r=[], w=[ut])
            P.add('vector', lambda e, ul=ul, j=j: e.tensor_scalar(ul[:], ul[:], cmk[:, j:j + 1], None, ALU.mult), r=[ut, 'cmk'], w=[ut])
            P.add('vector', lambda e, ul=ul, j=j, h=h: e.scalar_tensor_tensor(S[:], S[:], Dp[:, j, h:h + 1], ul[:], ALU.mult, ALU.add),
                  r=['S', 'Dp', ut], w=['S'])
        i = wcnt[0] % 2
        wcnt[0] += 1
        wb = wblk[i]; wt = 'wblk%d' % i
        P.dma('gpsimd', wb[:, :, 0:128], w_in[:, h * 128:(h + 1) * 128].rearrange('(kt p) n -> p kt n', p=128), r=[], w=[wt])
        P.dma('gpsimd', wb[:, :, 128:256], w_in[:, 1024 + h * 128:1024 + (h + 1) * 128].rearrange('(kt p) n -> p kt n', p=128), r=[], w=[wt])
        for which in range(2):
            for half in range(2):
                bank = nextbank()
                for kt in range(KT):
                    P.mm(psum[bank][:, :], wb[:, kt, which * 128:(which + 1) * 128], xT[:, kt, half * 512:(half + 1) * 512],
                         kt == 0, kt == KT - 1, r=['xT', wt], w=['ps%d' % bank])
                if which == 0:
                    P.add('vector', lambda e, bank=bank, half=half: e.tensor_copy(qT[:, half * 512:(half + 1) * 512], psum[bank][:, :]),
                          r=['ps%d' % bank], w=['qT'])
                else:
                    P.add('scalar', lambda e, bank=bank, half=half: e.activation(fT[:, half * 512:(half + 1) * 512], psum[bank][:, :], AF.Sigmoid),
                          r=['ps%d' % bank], w=['fT'])
        P.add('vector', lambda e, h=h: e.tensor_scalar(fT[:], fT[:], oml[:, h:h + 1], lb[:, h:h + 1], ALU.mult, ALU.add), r=['fT', 'oml', 'lb'], w=['fT'])
        P.add('scalar', lambda e: e.activation(lgf[:], fT[:], AF.Ln), r=['fT'], w=['lgf'])
        P.add('gpsimd', lambda e: e.tensor_scalar(kT_[:], fT[:], -1.0, 1.0, ALU.mult, ALU.add), r=['fT'], w=['kT_'])
        P.add('vector', lambda e: e.tensor_tensor_scan(bT[:], mask01[:], lgf[:], 0.0, ALU.mult, ALU.add), r=['mask01', 'lgf'], w=['bT'])
        b3 = bT[:].rearrange('p (c s) -> p c s', s=64)
        P.add('vector', lambda e: e.tensor_scalar(rr[:], b3[:, :, 63], 0.5, None, ALU.mult), r=['bT'], w=['rr'])
        P.add('vector', lambda e: e.tensor_reduce(bsum[:], b3[:, :, 63], AX.X, ALU.add), r=['bT'], w=['bsum'])
        P.add('scalar', lambda e, h=h: e.activation(Dtot[:, h:h + 1], bsum[:], AF.Exp), r=['bsum'], w=['Dtot'])
        P.add('scalar', lambda e: e.activation(er[:], rr[:], AF.Exp), r=['rr'], w=['er'])
        P.add('vector', lambda e: e.tensor_tensor(e2r[:], er[:], er[:], ALU.mult), r=['er'], w=['e2r'])
        P.add('vector', lambda e: e.tensor_tensor(b3, b3, bc(rr[:].unsqueeze(2), [128, 16, 64]), ALU.subtract), r=['bT', 'rr'], w=['bT'])
        P.add('scalar', lambda e: e.activation(eq[:], bT[:], AF.Exp), r=['bT'], w=['eq'])
        P.add('scalar', lambda e: e.activation(ek[:], bT[:], AF.Exp, scale=-1.0), r=['bT'], w=['ek'])
        P.add('vector', lambda e: e.tensor_tensor(qt_b[:], qT[:], eq[:], ALU.mult), r=['qT', 'eq'], w=['qt_b'])
        P.add('gpsimd', lambda e: e.tensor_tensor(kt_b[:], kT_[:], ek[:], ALU.mult), r=['kT_', 'ek'], w=['kt_b'])
        bank = nextbank()
        for n in range(NT):
            P.tr(psb(bank)[:, n * 128:(n + 1) * 128], kt_b[:, n * 128:(n + 1) * 128], ident_b[:], r=['kt_b', 'ident_b'], w=['ps%d' % bank])
        P.add('vector', lambda e, bank=bank: e.tensor_copy(ktok[:], psb(bank).rearrange('p (n d) -> p n d', n=NT)), r=['ps%d' % bank], w=['ktok'])
        for n in range(NT):
            asb = att_sb[n % 2]; at = 'att_sb%d' % (n % 2)
            b_att = nextbank(); b_U = nextbank(); b_o = nextbank()
            for cc in range(2):
                c = 2 * n + cc
                tok = slice(c * 64, (c + 1) * 64)
                pr = slice(cc * 64, (cc + 1) * 64)
                P.mm(psum[b_att][pr, cc * 64:(cc + 1) * 64], kt_b[:, tok], qt_b[:, tok], True, True, r=['kt_b', 'qt_b'], w=['ps%d' % b_att])
                P.add('vector', lambda e, pr=pr, cc=cc, asb=asb, b_att=b_att: e.tensor_tensor(
                    asb[pr, cc * 64:(cc + 1) * 64], psum[b_att][pr, cc * 64:(cc + 1) * 64], tri01[pr, :], ALU.mult),
                    r=['ps%d' % b_att, 'tri01'], w=[at])
                P.mm(psum[b_U][:, cc * 128:(cc + 1) * 128], ktok[pr, n, :], v_tok[pr, n, h * 128:(h + 1) * 128], True, True,
                     r=['ktok', 'v_tok%d' % n], w=['ps%d' % b_U])
            P.mm(psum[b_o][:, 0:128], asb[:, :], v_tok[:, n, h * 128:(h + 1) * 128], True, False, r=[at, 'v_tok%d' % n], w=['ps%d' % b_o])
            for cc in range(2):
                c = 2 * n + cc
                tok = slice(c * 64, (c + 1) * 64)
                pr = slice(cc * 64, (cc + 1) * 64)
                sp = Spb[c % 4]; spt = 'Spb%d' % (c % 4)
                P.add('vector', lambda e, sp=sp, c=c: e.tensor_scalar(sp[:], S[:], er[:, c:c + 1], None, ALU.mult), r=['S', 'er'], w=[spt])
                P.mm(psum[b_o][pr, 0:128], qt_b[:, tok], sp[:, :], False, cc == 1, r=['qt_b', spt], w=['ps%d' % b_o])
                P.add('vector', lambda e, c=c, cc=cc, b_U=b_U: e.tensor_scalar(Utmp[:], psum[b_U][:, cc * 128:(cc + 1) * 128], er[:, c:c + 1], None, ALU.mult),
                      r=['ps%d' % b_U, 'er'], w=['Utmp'])
                P.add('vector', lambda e, c=c: e.scalar_tensor_tensor(S[:], S[:], e2r[:, c:c + 1], Utmp[:], ALU.mult, ALU.add),
                      r=['S', 'e2r', 'Utmp'], w=['S'])
            P.add('gpsimd', lambda e: e.memset(ss[:], 0.0), w=['ss'])
            P.add('scalar', lambda e, b_o=b_o: e.activation(junk[:], psum[b_o][:, 0:128], AF.Square, accum_out=ss[:, 0:1]), r=['ps%d' % b_o, 'ss'], w=['junk', 'ss'])
            P.add('scalar', lambda e: e.activation(ss[:, 1:2], ss[:, 0:1], AF.Ln, bias=RMS_EPS, scale=1.0 / 128.0), r=['ss'], w=['ss'])
            P.add('scalar', lambda e: e.activation(ss[:, 1:2], ss[:, 1:2], AF.Exp, scale=-0.5), r=['ss'], w=['ss'])
            P.add('vector', lambda e, b_o=b_o, h=h: e.scalar_tensor_tensor(otmp[:], psum[b_o][:, 0:128], ss[:, 1:2], ngb[:, h * 128:(h + 1) * 128], ALU.mult, ALU.mult),
                  r=['ps%d' % b_o, 'ss', 'ngb'], w=['otmp'])
            P.add('gpsimd', lambda e, n=n, h=h: e.tensor_tensor(oa_tok[:, n, h * 128:(h + 1) * 128], otmp[:], sg_tok[:, n, h * 128:(h + 1) * 128], ALU.mult),
                  r=['otmp', 'sg_tok%d' % n], w=['oa_tok%d' % n])
        P.dma('sync', hgU_out[h], S[:], r=['S'], w=['hgU_out'])
    P.dma('sync', hgD_out, Dtot[:], r=['Dtot'], w=['hgD_out'])
    if dbg:
        for n in range(NT):
            P.add('vector', lambda e, n=n: e.tensor_copy(xload[n % 2][:, 0:1024], oa_tok[:, n, :]), r=['oa_tok%d' % n], w=['xload%d' % (n % 2)])
            P.dma('sync', dbgo['oa'][n * 128:(n + 1) * 128, :], xload[n % 2][:, 0:1024], r=['xload%d' % (n % 2)], w=['dbg_oa'])
    if phases <= 2:
        P.emit()
        return nc

    P.barrier()
    sb_off[0] = mark_persist
    wblk = [sb('wblk%d' % i, [128, KT, 512], BF16) for i in range(2)]
    uT = sb('uT', [128, 8, T], F32)
    yT = sb('yT', [128, 8, T], BF16)
    Xr = [sb('Xr%d' % i, [128, T], F32) for i in range(2)]
    Xi = [sb('Xi%d' % i, [128, T], F32) for i in range(2)]
    ytmp = sb('ytmp', [128, 512], F32)
    sgm = sb('sgm', [128, 512], F32)
    pl = sb('pl', [32, 3, 128], F32)
    lst2 = sb('lst2', [32, 2], F32)
    sp = {}
    for nm in ['lamr', 'lami', 'lst', 'lr', 'dt', 'lrdt', 'lidt', 'mag', 't1', 't2', 'sn', 'cs', 'ar', 'ai', 'den', 'nr',
               'zr', 'zi', 'xinr', 'xini', 'axr', 'axi', 'Ar', 'Ai', 'tA', 'tB', 'tC', 'tD']:
        sp[nm] = sb('s5_' + nm, [128, 32], F32)
    pwr = sb('pwr', [128, 11, 32], F32)
    pwi = sb('pwi', [128, 11, 32], F32)
    npwi = sb('npwi', [128, 11, 32], F32)
    bre = sb('bre', [128, 32, 16], F32)
    bim = sb('bim', [128, 32, 16], F32)
    bbr = sb('bbr', [128, 32, 16], F32)
    bbi = sb('bbi', [128, 32, 16], F32)
    btmp = sb('btmp', [128, 32, 16], F32)
    Cld = [sb('Cld%d' % i, [128, 8, 64], F32) for i in range(2)]
    Cdup = sb('Cdup', [128, 2, 64], F32)
    dld = sb('dld', [8, 128], F32)
    dvec = sb('dvec', [128, 8], F32)
    Xe_ld = [sb('Xe_ld%d' % i, [128, 64], F32) for i in range(2)]
    Xend = sb('Xend', [128, 64], F32)
    srcB = [sb('srcB%d' % j, [128, 128], F32) for j in range(4)]
    LBr = [sb('LBr%d' % j, [128, 128], F32) for j in range(4)]
    LBi = [sb('LBi%d' % j, [128, 128], F32) for j in range(4)]
    LCr = [sb('LCr%d' % j, [128, 128], F32) for j in range(4)]
    LCi = [sb('LCi%d' % j, [128, 128], F32) for j in range(4)]
    for j in range(4):
        for tl, nm in ((srcB, 'srcB'), (LCr, 'LCr'), (LCi, 'LCi')):
            P.add('gpsimd', lambda e, t=tl[j]: e.memset(t[:], 0.0), w=['%s%d' % (nm, j)])

    V = lambda eng, fn, r, w: P.add(eng, fn, r=r, w=w)

    def tt(out, a, b, op, r, w, eng='vector'):
        P.add(eng, lambda e: e.tensor_tensor(out, a, b, op), r=r, w=w)

    def ts(out, a, s1, s2, op0, op1, r, w, eng='vector'):
        if op1 is None:
            P.add(eng, lambda e: e.tensor_scalar(out, a, s1, None, op0), r=r, w=w)
        else:
            P.add(eng, lambda e: e.tensor_scalar(out, a, s1, s2, op0, op1), r=r, w=w)

    def stt(out, a, sc, b, op0, op1, r, w, eng='vector'):
        P.add(eng, lambda e: e.scalar_tensor_tensor(out, a, sc, b, op0, op1), r=r, w=w)

    def act(out, a, func, r, w, **kw):
        P.add('scalar', lambda e: e.activation(out, a, func, **kw), r=r, w=w)

    P.dma('sync', pl[:, 0, :], lam_re, r=[], w=['pl'])
    P.dma('sync', pl[:, 1, :], lam_im, r=[], w=['pl'])
    P.dma('sync', lst2[:], log_step, r=[], w=['lst2'])
    P.add('vector', lambda e: e.tensor_copy(pl[:, 2, :].rearrange('p (a b) -> p a b', a=2), bc(lst2[:].unsqueeze(2), [32, 2, 64])), r=['lst2', 'pl'], w=['pl'])
    for i, nm in enumerate(['lamr', 'lami', 'lst']):
        P.tr(psum[4][:, i * 32:(i + 1) * 32], pl[:, i, :], ident_f[0:32, 0:32], r=['pl', 'ident_f'], w=['ps4'])
        P.add('vector', lambda e, i=i, nm=nm: e.tensor_copy(sp[nm][:], psum[4][:, i * 32:(i + 1) * 32]), r=['ps4'], w=[nm])
    P.dma('sync', dld[:], s5_d, r=[], w=['dld'])
    P.tr(psum[4][:, 128:136], dld[:], ident_f[0:8, 0:8], r=['dld', 'ident_f'], w=['ps4'])
    P.add('vector', lambda e: e.tensor_copy(dvec[:], psum[4][:, 128:136]), r=['ps4'], w=['dvec'])
    P.dma('sync', bre[:], b_re.rearrange('(gp g2) p n -> (g2 p) gp n', g2=2), r=[], w=['bre'])
    P.dma('sync', bim[:], b_im.rearrange('(gp g2) p n -> (g2 p) gp n', g2=2), r=[], w=['bim'])
    P.dma('sync', Cld[0][:], c_re.rearrange('(ct q) p -> q ct p', q=128), r=[], w=['Cld0'])
    P.dma('sync', Cld[1][:], c_im.rearrange('(ct q) p -> q ct p', q=128), r=[], w=['Cld1'])

    A = lambda nm: sp[nm][:]
    ts(A('lr'), A('lamr'), -1e-4, None, ALU.min, None, ['lamr'], ['lr'])
    act(A('dt'), A('lst'), AF.Exp, ['lst'], ['dt'])
    tt(A('lrdt'), A('lr'), A('dt'), ALU.mult, ['lr', 'dt'], ['lrdt'])
    tt(A('lidt'), A('lami'), A('dt'), ALU.mult, ['lami', 'dt'], ['lidt'])
    act(A('mag'), A('lrdt'), AF.Exp, ['lrdt'], ['mag'])
    ts(A('t1'), A('lidt'), 1.0 / TWO_PI, MAGIC, ALU.mult, ALU.add, ['lidt'], ['t1'])
    ts(A('t1'), A('t1'), -MAGIC, None, ALU.add, None, ['t1'], ['t1'])
    stt(A('t1'), A('t1'), -TWO_PI, A('lidt'), ALU.mult, ALU.add, ['t1', 'lidt'], ['t1'])
    act(A('sn'), A('t1'), AF.Sin, ['t1'], ['sn'])
    ts(A('t2'), A('lidt'), math.pi / 2, None, ALU.add, None, ['lidt'], ['t2'])
    ts(A('tA'), A('t2'), 1.0 / TWO_PI, MAGIC, ALU.mult, ALU.add, ['t2'], ['tA'])
    ts(A('tA'), A('tA'), -MAGIC, None, ALU.add, None, ['tA'], ['tA'])
    stt(A('tA'), A('tA'), -TWO_PI, A('t2'), ALU.mult, ALU.add, ['tA', 't2'], ['tA'])
    act(A('cs'), A('tA'), AF.Sin, ['tA'], ['cs'])
    tt(A('ar'), A('mag'), A('cs'), ALU.mult, ['mag', 'cs'], ['ar'])
    tt(A('ai'), A('mag'), A('sn'), ALU.mult, ['mag', 'sn'], ['ai'])
    tt(A('den'), A('lr'), A('lr'), ALU.mult, ['lr'], ['den'])
    tt(A('tB'), A('lami'), A('lami'), ALU.mult, ['lami'], ['tB'])
    tt(A('den'), A('den'), A('tB'), ALU.add, ['den', 'tB'], ['den'])
    P.add('vector', lambda e: e.reciprocal(A('den'), A('den')), r=['den'], w=['den'])
    ts(A('nr'), A('ar'), -1.0, None, ALU.add, None, ['ar'], ['nr'])
    tt(A('tB'), A('nr'), A('lr'), ALU.mult, ['nr', 'lr'], ['tB'])
    tt(A('tC'), A('ai'), A('lami'), ALU.mult, ['ai', 'lami'], ['tC'])
    tt(A('tB'), A('tB'), A('tC'), ALU.add, ['tB', 'tC'], ['tB'])
    tt(A('zr'), A('tB'), A('den'), ALU.mult, ['tB', 'den'], ['zr'])
    tt(A('tB'), A('ai'), A('lr'), ALU.mult, ['ai', 'lr'], ['tB'])
    tt(A('tC'), A('nr'), A('lami'), ALU.mult, ['nr', 'lami'], ['tC'])
    tt(A('tB'), A('tB'), A('tC'), ALU.subtract, ['tB', 'tC'], ['tB'])
    tt(A('zi'), A('tB'), A('den'), ALU.mult, ['tB', 'den'], ['zi'])
    P.add('vector', lambda e: e.tensor_copy(pwr[:, 0, :], A('ar')), r=['ar'], w=['pwr'])
    P.add('vector', lambda e: e.tensor_copy(pwi[:, 0, :], A('ai')), r=['ai'], w=['pwi'])
    for k in range(10):
        tt(A('tB'), pwr[:, k, :], pwr[:, k, :], ALU.mult, ['pwr'], ['tB'])
        tt(A('tC'), pwi[:, k, :], pwi[:, k, :], ALU.mult, ['pwi'], ['tC'])
        stt(pwi[:, k + 1, :], pwr[:, k, :], 2.0, pwi[:, k, :], ALU.mult, ALU.mult, ['pwr', 'pwi'], ['pwi'])
        tt(pwr[:, k + 1, :], A('tB'), A('tC'), ALU.subtract, ['tB', 'tC'], ['pwr'])
    ts(npwi[:], pwi[:], -1.0, None, ALU.mult, None, ['pwi'], ['npwi'])
    P.add('gpsimd', lambda e: e.memset(A('xinr'), 0.0), w=['xinr'])
    P.add('gpsimd', lambda e: e.memset(A('xini'), 0.0), w=['xini'])
    for j in range(NCORES - 1):
        xl = Xe_ld[j % 2]; xt_ = 'Xe_ld%d' % (j % 2)
        P.dma('sync', xl[:], s5X_all[j], r=[], w=[xt_])
        m = cmk[:, j:j + 1]
        ts(A('Ar'), pwr[:, 10, :], -1.0, None, ALU.add, None, ['pwr'], ['Ar'])
        ts(A('Ar'), A('Ar'), m, 1.0, ALU.mult, ALU.add, ['Ar', 'cmk'], ['Ar'])
        ts(A('Ai'), pwi[:, 10, :], m, None, ALU.mult, None, ['pwi', 'cmk'], ['Ai'])
        ts(xl[:], xl[:], m, None, ALU.mult, None, [xt_, 'cmk'], [xt_])
        tt(A('tA'), A('Ar'), A('xinr'), ALU.mult, ['Ar', 'xinr'], ['tA'])
        tt(A('tB'), A('Ai'), A('xini'), ALU.mult, ['Ai', 'xini'], ['tB'])
        tt(A('tC'), A('Ar'), A('xini'), ALU.mult, ['Ar', 'xini'], ['tC'])
        tt(A('tD'), A('Ai'), A('xinr'), ALU.mult, ['Ai', 'xinr'], ['tD'])
        tt(A('tA'), A('tA'), A('tB'), ALU.subtract, ['tA', 'tB'], ['tA'])
        tt(A('tC'), A('tC'), A('tD'), ALU.add, ['tC', 'tD'], ['tC'])
        tt(A('xinr'), A('tA'), xl[:, 0:32], ALU.add, ['tA', xt_], ['xinr'])
        tt(A('xini'), A('tC'), xl[:, 32:64], ALU.add, ['tC', xt_], ['xini'])
    tt(A('tA'), A('ar'), A('xinr'), ALU.mult, ['ar', 'xinr'], ['tA'])
    tt(A('tB'), A('ai'), A('xini'), ALU.mult, ['ai', 'xini'], ['tB'])
    tt(A('axr'), A('tA'), A('tB'), ALU.subtract, ['tA', 'tB'], ['axr'])
    tt(A('tA'), A('ar'), A('xini'), ALU.mult, ['ar', 'xini'], ['tA'])
    tt(A('tB'), A('ai'), A('xinr'), ALU.mult, ['ai', 'xinr'], ['tB'])
    tt(A('axi'), A('tA'), A('tB'), ALU.add, ['tA', 'tB'], ['axi'])
    zrb = bc(sp['zr'][:].unsqueeze(2), [128, 32, 16])
    zib = bc(sp['zi'][:].unsqueeze(2), [128, 32, 16])
    tt(bbr[:], bre[:], zrb, ALU.mult, ['bre', 'zr'], ['bbr'])
    tt(btmp[:], bim[:], zib, ALU.mult, ['bim', 'zi'], ['btmp'])
    tt(bbr[:], bbr[:], btmp[:], ALU.subtract, ['bbr', 'btmp'], ['bbr'])
    tt(bbi[:], bim[:], zrb, ALU.mult, ['bim', 'zr'], ['bbi'])
    tt(btmp[:], bre[:], zib, ALU.mult, ['bre', 'zi'], ['btmp'])
    tt(bbi[:], bbi[:], btmp[:], ALU.add, ['bbi', 'btmp'], ['bbi'])

    for blk in range(2):
        wb, wt = load_wblk(w_in, 4096 + blk * 512)
        for ntl in range(4):
            ct = blk * 4 + ntl
            for half in range(2):
                bank = nextbank(0, 4)
                for kt in range(KT):
                    P.mm(psum[bank][:, :], wb[:, kt, ntl * 128:(ntl + 1) * 128], xT[:, kt, half * 512:(half + 1) * 512],
                         kt == 0, kt == KT - 1, r=['xT', wt], w=['ps%d' % bank])
                P.add('scalar', lambda e, bank=bank, ct=ct, half=half: e.copy(uT[:, ct, half * 512:(half + 1) * 512], psum[bank][:, :]),
                      r=['ps%d' % bank], w=['uT%d' % ct])

    P.barrier()
    wb1f = wblk[1][:].rearrange('p a b -> p (a b)').bitcast(F32)
    Ms = [(wb1f[:, 0:T], wb1f[:, T:2 * T]), (wb1f[:, 2 * T:3 * T], wb1f[:, 3 * T:4 * T])]
    wb0f = wblk[0][:].rearrange('p a b -> p (a b)').bitcast(F32)
    Xr2 = [wb0f[:, 0:T], wb0f[:, T:2 * T]]
    Xi2 = [wb0f[:, 2 * T:3 * T], wb0f[:, 3 * T:4 * T]]
    XRs = [Xr, Xr2]
    XIs = [Xi, Xi2]
    for ct in range(8):
        for j in range(4):
            gp = 4 * ct + j
            for (bb, LB, nm) in ((bbr, LBr, 'LBr'), (bbi, LBi, 'LBi')):
                for g2 in range(2):
                    pr = slice(g2 * 64, (g2 + 1) * 64)
                    c0 = (2 * j + g2) * 16
                    P.add('vector', lambda e, bb=bb, pr=pr, c0=c0, j=j, gp=gp: e.tensor_copy(srcB[j][pr, c0:c0 + 16], bb[pr, gp, :]),
                          r=['bbr', 'bbi'], w=['srcB%d' % j])
                P.tr(psum[4][:, 0:128], srcB[j][:], ident_f[:], r=['srcB%d' % j, 'ident_f'], w=['ps4'])
                P.add('vector', lambda e, LB=LB, j=j: e.tensor_copy(LB[j][:], psum[4][:, 0:128]), r=['ps4'], w=['%s%d' % (nm, j)])
        for ri, (LC, nm) in enumerate(((LCr, 'LCr'), (LCi, 'LCi'))):
            P.add('vector', lambda e, ri=ri, ct=ct: e.tensor_copy(Cdup[:], bc(Cld[ri][:, ct, :].unsqueeze(1), [128, 2, 64])), r=['Cld%d' % ri], w=['Cdup'])
            P.tr(psum[5][:, 0:128], Cdup[:].rearrange('p a b -> p (a b)'), ident_f[:], r=['Cdup', 'ident_f'], w=['ps5'])
            for j in range(4):
                for g2 in range(2):
                    pr = slice(g2 * 64, (g2 + 1) * 64)
                    c0 = (2 * j + g2) * 16
                    if ri == 0:
                        P.add('vector', lambda e, LC=LC, j=j, pr=pr, c0=c0: e.tensor_copy(LC[j][pr, c0:c0 + 16], psum[5][pr, c0:c0 + 16]),
                              r=['ps5'], w=['%s%d' % (nm, j)])
                    else:
                        P.add('vector', lambda e, LC=LC, j=j, pr=pr, c0=c0: e.tensor_scalar(LC[j][pr, c0:c0 + 16], psum[5][pr, c0:c0 + 16], -1.0, None, ALU.mult),
                              r=['ps5'], w=['%s%d' % (nm, j)])
        for jp in range(2):
            js = (2 * jp, 2 * jp + 1)
            for si, j in enumerate(js):
                gp = 4 * ct + j
                XR, XI = XRs[si], XIs[si]
                t0r, t0i = 'Xr%d_0' % si, 'Xi%d_0' % si
                for half in range(2):
                    hs = slice(half * 512, (half + 1) * 512)
                    b1 = nextbank(0, 4)
                    P.mm(psum[b1][:, :], LBr[j][:], uT[:, ct, hs], True, True, r=['LBr%d' % j, 'uT%d' % ct], w=['ps%d' % b1])
                    P.add('scalar', lambda e, b1=b1, hs=hs, XR=XR: e.copy(XR[0][:, hs], psum[b1][:, :]), r=['ps%d' % b1], w=[t0r])
                    b2 = nextbank(0, 4)
                    P.mm(psum[b2][:, :], LBi[j][:], uT[:, ct, hs], True, True, r=['LBi%d' % j, 'uT%d' % ct], w=['ps%d' % b2])
                    P.add('scalar', lambda e, b2=b2, hs=hs, XI=XI: e.copy(XI[0][:, hs], psum[b2][:, :]), r=['ps%d' % b2], w=[t0i])
                tt(XR[0][:, 0:1], XR[0][:, 0:1], sp['axr'][:, gp:gp + 1], ALU.add, [t0r, 'axr'], [t0r])
                tt(XI[0][:, 0:1], XI[0][:, 0:1], sp['axi'][:, gp:gp + 1], ALU.add, [t0i, 'axi'], [t0i], eng='gpsimd')
            for k in (range(10) if summary else []):
                nn = T >> (k + 1)
                s_, d_ = k % 2, 1 - k % 2
                for si, j in enumerate(js):
                    gp = 4 * ct + j
                    XR, XI = XRs[si], XIs[si]
                    mA, mB = Ms[si]
                    tmA, tmB = 'mA%d' % si, 'mB%d' % si
                    pr_ = pwr[:, k, gp:gp + 1]; pi_ = pwi[:, k, gp:gp + 1]; npi_ = npwi[:, k, gp:gp + 1]
                    tr_s, ti_s, tr_d, ti_d = 'Xr%d_%d' % (si, s_), 'Xi%d_%d' % (si, s_), 'Xr%d_%d' % (si, d_), 'Xi%d_%d' % (si, d_)
                    xr2 = XR[s_][:, 0:2 * nn].rearrange('p (j two) -> p j two', two=2)
                    xi2 = XI[s_][:, 0:2 * nn].rearrange('p (j two) -> p j two', two=2)
                    stt(XR[d_][:, 0:nn], xr2[:, :, 0], pr_, xr2[:, :, 1], ALU.mult, ALU.add, [tr_s, 'pwr'], [tr_d])
                    stt(XR[d_][:, 0:nn], xi2[:, :, 0], npi_, XR[d_][:, 0:nn], ALU.mult, ALU.add, [ti_s, 'npwi', tr_d], [tr_d])
                    stt(XI[d_][:, 0:nn], xi2[:, :, 0], pr_, xi2[:, :, 1], ALU.mult, ALU.add, [ti_s, 'pwr'], [ti_d])
                    stt(XI[d_][:, 0:nn], xr2[:, :, 0], pi_, XI[d_][:, 0:nn], ALU.mult, ALU.add, [tr_s, 'pwi', ti_d], [ti_d])
            for si, j in (list(enumerate(js)) if summary else []):
                gp = 4 * ct + j
                XR, XI = XRs[si], XIs[si]
                P.add('vector', lambda e, gp=gp, XR=XR: e.tensor_copy(Xend[:, gp:gp + 1], XR[0][:, 0:1]), r=['Xr%d_0' % si], w=['Xend'])
                P.add('vector', lambda e, gp=gp, XI=XI: e.tensor_copy(Xend[:, 32 + gp:33 + gp], XI[0][:, 0:1]), r=['Xi%d_0' % si], w=['Xend'])
            if summary:
                continue
            for k in range(10):
                d = 1 << k
                s_, d_ = k % 2, 1 - k % 2
                for si, j in enumerate(js):
                    gp = 4 * ct + j
                    XR, XI = XRs[si], XIs[si]
                    mA, mB = Ms[si]
                    tmA, tmB = 'mA%d' % si, 'mB%d' % si
                    pr_ = pwr[:, k, gp:gp + 1]; pi_ = pwi[:, k, gp:gp + 1]; npi_ = npwi[:, k, gp:gp + 1]
                    xr_s, xi_s, xr_d, xi_d = XR[s_], XI[s_], XR[d_], XI[d_]
                    tr_s, ti_s, tr_d, ti_d = 'Xr%d_%d' % (si, s_), 'Xi%d_%d' % (si, s_), 'Xr%d_%d' % (si, d_), 'Xi%d_%d' % (si, d_)
                    P.add('scalar', lambda e, xr_d=xr_d, xr_s=xr_s, d=d: e.copy(xr_d[:, 0:d], xr_s[:, 0:d]), r=[tr_s], w=[tr_d])
                    stt(xr_d[:, d:T], xr_s[:, 0:T - d], pr_, xr_s[:, d:T], ALU.mult, ALU.add, [tr_s, 'pwr'], [tr_d])
                    stt(xr_d[:, d:T], xi_s[:, 0:T - d], npi_, xr_d[:, d:T], ALU.mult, ALU.add, [ti_s, 'npwi', tr_d], [tr_d])
                    P.add('scalar', lambda e, xi_d=xi_d, xi_s=xi_s, d=d: e.copy(xi_d[:, 0:d], xi_s[:, 0:d]), r=[ti_s], w=[ti_d])
                    act(mA[:, 0:T - d], xi_s[:, 0:T - d], AF.Copy, [ti_s, 'pwr'], [tmA], scale=pr_)
                    act(mB[:, 0:T - d], xr_s[:, 0:T - d], AF.Copy, [tr_s, 'pwi'], [tmB], scale=pi_)
                    tt(xi_d[:, d:T], mA[:, 0:T - d], xi_s[:, d:T], ALU.add, [tmA, ti_s], [ti_d], eng='gpsimd')
                    tt(xi_d[:, d:T], mB[:, 0:T - d], xi_d[:, d:T], ALU.add, [tmB, ti_d], [ti_d], eng='gpsimd')
            for si, j in enumerate(js):
                gp = 4 * ct + j
                XR, XI = XRs[si], XIs[si]
                t0r, t0i = 'Xr%d_0' % si, 'Xi%d_0' % si
                P.add('vector', lambda e, gp=gp, XR=XR: e.tensor_copy(Xend[:, gp:gp + 1], XR[0][:, T - 1:T]), r=[t0r], w=['Xend'])
                P.add('vector', lambda e, gp=gp, XI=XI: e.tensor_copy(Xend[:, 32 + gp:33 + gp], XI[0][:, T - 1:T]), r=[t0i], w=['Xend'])
                for half in range(2):
                    hs = slice(half * 512, (half + 1) * 512)
                    P.mm(psum[6 + half][:, :], LCr[j][:], XR[0][:, hs], j == 0, False, r=['LCr%d' % j, t0r], w=['ps%d' % (6 + half)])
                    P.mm(psum[6 + half][:, :], LCi[j][:], XI[0][:, hs], False, j == 3, r=['LCi%d' % j, t0i], w=['ps%d' % (6 + half)])
        for half in ([] if summary else range(2)):
            hs = slice(half * 512, (half + 1) * 512)
            stt(ytmp[:], uT[:, ct, hs], dvec[:, ct:ct + 1], psum[6 + half][:, :], ALU.mult, ALU.add, ['uT%d' % ct, 'dvec', 'ps%d' % (6 + half)], ['ytmp'])
            act(yT[:, ct, hs], ytmp[:], AF.Gelu_apprx_tanh, ['ytmp'], ['yT'])
    P.dma('sync', s5X_out, Xend[:], r=['Xend'], w=['s5X_out'])
    if summary:
        P.emit()
        return nc
    P.barrier()
    for b in range(2):
        WA, wta = wblk[0], 'wblk0'
        WG, wtg = wblk[1], 'wblk1'
        P.dma('gpsimd', WA[:, 0:8, :], w_glu[:, b * 512:(b + 1) * 512].rearrange('(kt p) n -> p kt n', p=128), r=[], w=[wta])
        P.dma('gpsimd', WG[:, 0:8, :], w_glu[:, 1024 + b * 512:1024 + (b + 1) * 512].rearrange('(kt p) n -> p kt n', p=128), r=[], w=[wtg])
        for ntl in range(4):
            nt = 4 * b + ntl
            for half in range(2):
                hs = slice(half * 512, (half + 1) * 512)
                ba = nextbank(0, 6); bg = nextbank(0, 6)
                for kt in range(8):
                    P.mm(psum[ba][:, :], WA[:, kt, ntl * 128:(ntl + 1) * 128], yT[:, kt, hs], kt == 0, kt == 7, r=['yT', wta], w=['ps%d' % ba])
                for kt in range(8):
                    P.mm(psum[bg][:, :], WG[:, kt, ntl * 128:(ntl + 1) * 128], yT[:, kt, hs], kt == 0, kt == 7, r=['yT', wtg], w=['ps%d' % bg])
                act(sgm[:], psum[bg][:, :], AF.Sigmoid, ['ps%d' % bg], ['sgm'])
                tt(obT[:, nt, hs], psum[ba][:, :], sgm[:], ALU.mult, ['ps%d' % ba, 'sgm'], ['obT'])
    if dbg:
        for nt in range(8):
            P.add('vector', lambda e, nt=nt: e.tensor_copy(Xr[0][:], obT[:, nt, :]), r=['obT'], w=['Xr0_0'])
            P.dma('sync', dbgo['obT'][nt * 128:(nt + 1) * 128, :], Xr[0][:], r=['Xr0_0'], w=['dbg_obT'])
    if phases <= 3:
        P.emit()
        return nc

    P.barrier()
    sb_off[0] = mark_persist
    mergedT = sb('mergedT', [128, KT, T], BF16)
    mark4 = sb_off[0]
    oaT = sb('oaT', [128, 8, T], BF16)
    wA = [sb('wA%d' % i, [128, 8, 512], BF16) for i in range(2)]
    wB = [sb('wB%d' % i, [128, 8, 512], BF16) for i in range(2)]
    wGA = sb('wGA', [128, KT, 512], BF16)
    wGB = sb('wGB', [128, KT, 512], BF16)
    sga = sb('sga', [128, 512], F32)
    sgb = sb('sgb', [128, 512], F32)
    mm1 = sb('mm1', [128, 512], F32)
    mm2 = sb('mm2', [128, 512], F32)
    for n in range(NT):
        bank = nextbank()
        for kt in range(8):
            P.tr(psb(bank)[:, kt * 128:(kt + 1) * 128], oa_tok[:, n, kt * 128:(kt + 1) * 128], ident_b[:], r=['oa_tok', 'ident_b'], w=['ps%d' % bank])
        P.add('vector', lambda e, bank=bank, n=n: e.tensor_copy(oaT[:, :, n * 128:(n + 1) * 128], psb(bank).rearrange('p (j t) -> p j t', j=8)),
              r=['ps%d' % bank], w=['oaT'])
    for cb in range(4):
        i = cb % 2
        P.dma('gpsimd', wA[i][:], w_up_a[:, cb * 512:(cb + 1) * 512].rearrange('(kt p) n -> p kt n', p=128), r=[], w=['wA%d' % i])
        P.dma('gpsimd', wB[i][:], w_up_b[:, cb * 512:(cb + 1) * 512].rearrange('(kt p) n -> p kt n', p=128), r=[], w=['wB%d' % i])
        P.dma('gpsimd', wGA[:], w_in[:, 5120 + cb * 512:5120 + (cb + 1) * 512].rearrange('(kt p) n -> p kt n', p=128), r=[], w=['wGA'])
        P.dma('gpsimd', wGB[:], w_in[:, 7168 + cb * 512:7168 + (cb + 1) * 512].rearrange('(kt p) n -> p kt n', p=128), r=[], w=['wGB'])
        for ntl in range(4):
            nt = 4 * cb + ntl
            ns = slice(ntl * 128, (ntl + 1) * 128)
            for half in range(2):
                hs = slice(half * 512, (half + 1) * 512)
                ba, bb_, bga, bgb = nextbank(), nextbank(), nextbank(), nextbank()
                for kt in range(8):
                    P.mm(psum[ba][:, :], wA[i][:, kt, ns], oaT[:, kt, hs], kt == 0, kt == 7, r=['wA%d' % i, 'oaT'], w=['ps%d' % ba])
                for kt in range(8):
                    P.mm(psum[bb_][:, :], wB[i][:, kt, ns], obT[:, kt, hs], kt == 0, kt == 7, r=['wB%d' % i, 'obT'], w=['ps%d' % bb_])
                for kt in range(KT):
                    P.mm(psum[bga][:, :], wGA[:, kt, ns], xT[:, kt, hs], kt == 0, kt == KT - 1, r=['wGA', 'xT'], w=['ps%d' % bga])
                for kt in range(KT):
                    P.mm(psum[bgb][:, :], wGB[:, kt, ns], xT[:, kt, hs], kt == 0, kt == KT - 1, r=['wGB', 'xT'], w=['ps%d' % bgb])
                act(sga[:], psum[bga][:, :], AF.Sigmoid, ['ps%d' % bga], ['sga'])
                act(sgb[:], psum[bgb][:, :], AF.Sigmoid, ['ps%d' % bgb], ['sgb'])
                tt(mm1[:], psum[ba][:, :], sga[:], ALU.mult, ['ps%d' % ba, 'sga'], ['mm1'])
                tt(mm2[:], psum[bb_][:, :], sgb[:], ALU.mult, ['ps%d' % bb_, 'sgb'], ['mm2'])
                tt(mergedT[:, nt, hs], mm1[:], mm2[:], ALU.add, ['mm1', 'mm2'], ['mergedT'], eng='gpsimd')
    P.barrier()
    sb_off[0] = mark4
    wo_sb = sb('wo_sb', [128, KT, D], BF16)
    x1b = sb('x1b', [128, D], BF16)
    st4 = sb('st4', [128, 4], F32)
    sb_save = sb_off[0]
    sb_off[0] = reg_oa
    lng = sb('lng', [128, D], F32)
    lnb = sb('lnb', [128, D], F32)
    xres = sb('xres', [128, D], F32)
    h1 = sb('h1', [128, D], F32)
    sb_off[0] = sb_save
    x1T = xT
    for cb in range(4):
        P.dma('gpsimd', wo_sb[:, :, cb * 512:(cb + 1) * 512], w_o[:, cb * 512:(cb + 1) * 512].rearrange('(kt p) n -> p kt n', p=128), r=[], w=['wo_sb%d' % cb])

    def layer_norm(src, gamma_d, beta_d, hbuf, dst_dram, tagp):
        pass

    P.dma('sync', lng[:], ln1_g[0].partition_broadcast(128), r=[], w=['lng'])
    P.dma('sync', lnb[:], ln1_b[0].partition_broadcast(128), r=[], w=['lnb'])

    def ln_rows(hb, ht, gam, bet):
        P.add('vector', lambda e: e.tensor_reduce(st4[:, 0:1], hb[:], AX.X, ALU.add), r=[ht], w=['st4a'])
        ts(st4[:, 1:2], st4[:, 0:1], -1.0 / D, None, ALU.mult, None, ['st4a'], ['st4b'])
        ts(hb[:], hb[:], st4[:, 1:2], None, ALU.add, None, [ht, 'st4b'], [ht])
        P.add('gpsimd', lambda e: e.memset(st4[:, 2:3], 0.0), w=['st4c'])
        P.add('scalar', lambda e: e.activation(junkD[:], hb[:], AF.Square, accum_out=st4[:, 2:3]), r=[ht, 'st4c'], w=['junkD', 'st4c'])
        act(st4[:, 3:4], st4[:, 2:3], AF.Ln, ['st4c'], ['st4d'], bias=LN_EPS, scale=1.0 / D)
        act(st4[:, 3:4], st4[:, 3:4], AF.Exp, ['st4d'], ['st4d'], scale=-0.5)
        stt(hb[:], hb[:], st4[:, 3:4], gam[:], ALU.mult, ALU.mult, [ht, 'st4d', 'lng'], [ht])
        tt(hb[:], hb[:], bet[:], ALU.add, [ht, 'lnb'], [ht], eng='gpsimd')

    junkD = sb('junkD', [128, D], BF16)
    for n in range(NT):
        P.dma('sync', xres[:], x_in[n * 128:(n + 1) * 128, :], r=[], w=['xres'])
        for cb in range(4):
            bank = nextbank()
            for kt in range(KT):
                P.mm(psum[bank][:, :], mergedT[:, kt, n * 128:(n + 1) * 128], wo_sb[:, kt, cb * 512:(cb + 1) * 512], kt == 0, kt == KT - 1,
                     r=['mergedT', 'wo_sb%d' % cb], w=['ps%d' % bank])
            stt(h1[:, cb * 512:(cb + 1) * 512], xres[:, cb * 512:(cb + 1) * 512], ALPHA, psum[bank][:, :], ALU.mult, ALU.add,
                ['xres', 'ps%d' % bank], ['h1'])
        ln_rows(h1, 'h1', lng, lnb)
        P.dma('sync', x1d[n * 128:(n + 1) * 128, :], h1[:], r=['h1'], w=['x1d'])
        if dbg:
            P.dma('sync', dbgo['x1'][n * 128:(n + 1) * 128, :], h1[:], r=['h1'], w=['dbg_x1'])
        P.add('scalar', lambda e: e.copy(x1b[:], h1[:]), r=['h1'], w=['x1b'])
        for half in range(2):
            bank = nextbank()
            for j in range(8):
                kt = half * 8 + j
                P.tr(psb(bank)[:, j * 128:(j + 1) * 128], x1b[:, kt * 128:(kt + 1) * 128], ident_b[:], r=['x1b', 'ident_b'], w=['ps%d' % bank])
            P.add('vector', lambda e, bank=bank, half=half, n=n: e.tensor_copy(
                x1T[:, half * 8:(half + 1) * 8, n * 128:(n + 1) * 128], psb(bank).rearrange('p (j t) -> p j t', j=8)),
                r=['ps%d' % bank], w=['x1T'])
    if phases <= 4:
        P.emit()
        return nc

    P.barrier()
    sb_off[0] = mark_persist
    s_d = nc.dram_tensor('s_d', [T, 16, 128], F32).ap()
    wblk = [sb('wblk%d' % i, [128, KT, 512], BF16) for i in range(2)]
    qTb = sb('qTb', [128, 4, T], F32)
    kld = sb('kld', [128, 16, 128], F32)
    KTt = sb('KTt', [128, 16, 128], F32)
    s_st = [sb('s_st%d' % i, [128, 4, 128], F32) for i in range(2)]
    s_sb = sb('s_sb', [128, 16, 128], F32)
    mr = sb('mr', [128, 128], F32)
    top = sb('top', [128, 16, 16], F32)
    cand = sb('cand', [128, 8, 256], F32)
    cand2 = sb('cand2', [128, 8, 256], F32)
    vals = sb('vals', [128, 8, 16], F32)
    ev = sb('ev', [128, 8, 16], F32)
    Zs = sb('Zs', [128, 8], F32)
    nb = sb('nb', [128, 8], F32)
    Tbuf2 = [sb('Tbuf%d' % i, [128, 16, 128], F32) for i in range(2)]
    Eb2 = [sb('Eb%d' % i, [128, 2048], BF16) for i in range(2)]
    Pm2 = [sb('Pm%d' % i, [128, 2048], BF16) for i in range(2)]
    sb_save = sb_off[0]
    sb_off[0] = reg_oa
    G = sb('G', [128, 16384], BF16)
    sb_off[0] = sb_save

    P.dma('sync', kld[:], keys.rearrange('h i d -> i h d'), r=[], w=['kld'])
    for q4 in range(4):
        bank = nextbank()
        for hl in range(4):
            hh = 4 * q4 + hl
            P.tr(psum[bank][:, hl * 128:(hl + 1) * 128], kld[:, hh, :], ident_f[:], r=['kld', 'ident_f'], w=['ps%d' % bank])
        P.add('vector', lambda e, bank=bank, q4=q4: e.tensor_copy(KTt[:, 4 * q4:4 * q4 + 4, :], psum[bank][:, :].rearrange('p (a b) -> p a b', a=4)),
              r=['ps%d' % bank], w=['KTt'])
    for blk in range(4):
        wb, wt = load_wblk(w_q, blk * 512)
        for hl in range(4):
            for half in range(2):
                hs = slice(half * 512, (half + 1) * 512)
                bank = nextbank()
                for kt in range(KT):
                    P.mm(psum[bank][:, :], wb[:, kt, hl * 128:(hl + 1) * 128], x1T[:, kt, hs], kt == 0, kt == KT - 1, r=[wt, 'x1T'], w=['ps%d' % bank])
                P.add('scalar', lambda e, bank=bank, hl=hl, hs=hs: e.copy(qTb[:, hl, hs], psum[bank][:, :]), r=['ps%d' % bank], w=['qTb%d' % hl])
        for n in range(NT):
            bank = nextbank()
            i2 = n % 2
            for hl in range(4):
                P.mm(psum[bank][:, hl * 128:(hl + 1) * 128], qTb[:, hl, n * 128:(n + 1) * 128], KTt[:, 4 * blk + hl, :], True, True,
                     r=['qTb%d' % hl, 'KTt'], w=['ps%d' % bank])
            P.add('vector', lambda e, bank=bank, i2=i2: e.tensor_copy(s_st[i2][:], psum[bank][:, :].rearrange('p (a b) -> p a b', a=4)),
                  r=['ps%d' % bank], w=['s_st%d' % i2])
            P.dma('sync', s_d[n * 128:(n + 1) * 128, 4 * blk:4 * blk + 4, :], s_st[i2][:], r=['s_st%d' % i2], w=['s_d'])
    P.barrier()
    top4 = top[:].rearrange('p (h two) k -> p h two k', two=2)
    gcnt = 0
    for n in range(NT):
        P.dma('sync', s_sb[:], s_d[n * 128:(n + 1) * 128], r=[], w=['s_sb'])
        for hh in range(16):
            P.add('vector', lambda e, hh=hh: e.max(out=top[:, hh, 0:8], in_=s_sb[:, hh, :]), r=['s_sb'], w=['top'])
            P.add('vector', lambda e, hh=hh: e.match_replace(out=mr[:], in_to_replace=top[:, hh, 0:8], in_values=s_sb[:, hh, :], imm_value=-1e30),
                  r=['s_sb', 'top'], w=['mr'])
            P.add('vector', lambda e, hh=hh: e.max(out=top[:, hh, 8:16], in_=mr[:]), r=['mr'], w=['top'])
        tt(cand[:].rearrange('p h (a b) -> p h a b', a=16), bc(top4[:, :, 0, :].unsqueeze(3), [128, 8, 16, 16]),
           bc(top4[:, :, 1, :].unsqueeze(2), [128, 8, 16, 16]), ALU.add, ['top'], ['cand'])
        for h in range(8):
            P.add('vector', lambda e, h=h: e.max(out=vals[:, h, 0:8], in_=cand[:, h, :]), r=['cand'], w=['vals'])
            P.add('vector', lambda e, h=h: e.match_replace(out=cand2[:, h, :], in_to_replace=vals[:, h, 0:8], in_values=cand[:, h, :], imm_value=-1e30),
                  r=['cand', 'vals'], w=['cand2'])
            P.add('vector', lambda e, h=h: e.max(out=vals[:, h, 8:16], in_=cand2[:, h, :]), r=['cand2'], w=['vals'])
        tt(ev[:], vals[:], bc(vals[:, :, 0:1], [128, 8, 16]), ALU.subtract, ['vals'], ['ev'])
        act(ev[:], ev[:], AF.Exp, ['ev'], ['ev'])
        P.add('vector', lambda e: e.tensor_reduce(Zs[:], ev[:], AX.X, ALU.add), r=['ev'], w=['Zs'])
        act(Zs[:], Zs[:], AF.Ln, ['Zs'], ['Zs'])
        tt(nb[:], Zs[:], vals[:, :, 0], ALU.add, ['Zs', 'vals'], ['nb'])
        ts(nb[:], nb[:], -1.0, None, ALU.mult, None, ['nb'], ['nb'])
        its = [(h, ic) for ic in range(8) for h in range(8)]

        def emitT(idx):
            h, ic = its[idx]
            pi_ = idx % 2
            isl = slice(ic * 16, (ic + 1) * 16)
            tt(Tbuf2[pi_][:], bc(s_sb[:, 2 * h, isl].unsqueeze(2), [128, 16, 128]), bc(s_sb[:, 2 * h + 1, :].unsqueeze(1), [128, 16, 128]),
               ALU.add, ['s_sb'], ['Tbuf%d' % pi_])

        emitT(0)
        for idx, (h, ic) in enumerate(its):
            if idx + 1 < len(its):
                emitT(idx + 1)
            pi_ = idx % 2
            Tb, Ebb, Pmb = Tbuf2[pi_], Eb2[pi_], Pm2[pi_]
            tT, tE, tP = 'Tbuf%d' % pi_, 'Eb%d' % pi_, 'Pm%d' % pi_
            gsl = slice(ic * 2048, (ic + 1) * 2048)
            bset = (ic % 2) * 4
            Tf = Tb[:].rearrange('p a b -> p (a b)')
            act(Ebb[:], Tf, AF.Exp, [tT, 'nb'], [tE], bias=nb[:, h:h + 1])
            stt(Pmb[:], Tf, vals[:, h, 15:16], Ebb[:], ALU.is_ge, ALU.mult, [tT, 'vals', tE], [tP])
            for q4 in range(4):
                P.mm(psum[bset + q4][:, :], ident_b[:], Pmb[:, q4 * 512:(q4 + 1) * 512], h == 0, h == 7, r=['ident_b', tP], w=['ps%d' % (bset + q4)])
            if h == 7:
                for q4 in range(4):
                    act(G[:, ic * 2048 + q4 * 512:ic * 2048 + (q4 + 1) * 512], psum[bset + q4][:, :], AF.Copy, ['ps%d' % (bset + q4)], ['G%d' % ic])
        P.dma('sync', Gd[n * 128:(n + 1) * 128, :], G[:], r=['G%d' % i for i in range(8)], w=['Gd'])

    P.barrier()
    sb_off[0] = mark_persist
    yacc = sb('yacc', [128, NT, D], F32)
    ublk = [sb('ublk%d' % i, [128, 4, D], BF16) for i in range(2)]
    sb_save = sb_off[0]
    sb_off[0] = reg_oa
    vblk = [sb('vblk%d' % i, [128, 4, D], BF16) for i in range(2)]
    sb_off[0] = sb_save
    uTb = sb('uTb', [128, KT, 512], BF16)
    Gs = [sb('Gs%d' % i, [128, 512], BF16) for i in range(2)]
    gl2 = [sb('gl%d' % i, [128, 512], F32) for i in range(2)]
    Wb2 = [sb('Wb%d' % i, [128, 512], BF16) for i in range(2)]
    WT2 = [sb('WT%d' % i, [128, 4, 128], BF16) for i in range(2)]
    def load_eb(eb):
        i = eb % 2
        P.dma('gpsimd', ublk[i][:], peer_u[eb * 512:(eb + 1) * 512, :].rearrange('(et p) d -> p et d', p=128), r=[], w=['ublk%d' % i])
        P.dma('gpsimd', vblk[i][:], peer_v[eb * 512:(eb + 1) * 512, :].rearrange('(et p) d -> p et d', p=128), r=[], w=['vblk%d' % i])
        for k2 in range(8):
            bank = nextbank()
            for kk in range(2):
                kt = 2 * k2 + kk
                for et in range(4):
                    P.tr(psb(bank)[:, (kk * 4 + et) * 128:(kk * 4 + et + 1) * 128], ublk[i][:, et, kt * 128:(kt + 1) * 128], ident_b[:],
                         r=['ublk%d' % i, 'ident_b'], w=['ps%d' % bank])
            if k2 % 2 == 0:
                P.add('vector', lambda e, bank=bank, k2=k2: e.tensor_copy(uTb[:, 2 * k2:2 * k2 + 2, :], psb(bank).rearrange('p (k e) -> p k e', k=2)),
                      r=['ps%d' % bank], w=['uTb'])
            else:
                P.add('scalar', lambda e, bank=bank, k2=k2: e.copy(uTb[:, 2 * k2:2 * k2 + 2, :], psb(bank).rearrange('p (k e) -> p k e', k=2)),
                      r=['ps%d' % bank], w=['uTb'])

    items = [(eb, n) for eb in range(32) for n in range(NT)]

    def stage_A(idx):
        eb, n = items[idx]
        gi = idx % 2
        if n == 0:
            load_eb(eb)
        P.dma('sync', Gs[gi][:], Gd[n * 128:(n + 1) * 128, eb * 512:(eb + 1) * 512], r=[], w=['Gs%d' % gi])
        bA = nextbank()
        for kt in range(KT):
            P.mm(psum[bA][:, :], x1T[:, kt, n * 128:(n + 1) * 128], uTb[:, kt, :], kt == 0, kt == KT - 1, r=['x1T', 'uTb'], w=['ps%d' % bA])
        act(gl2[gi][:], psum[bA][:, :], AF.Gelu_apprx_tanh, ['ps%d' % bA], ['gl%d' % gi])
        tt(Wb2[gi][:], gl2[gi][:], Gs[gi][:], ALU.mult, ['gl%d' % gi, 'Gs%d' % gi], ['Wb%d' % gi], eng='gpsimd')

    def stage_TY(idx):
        eb, n = items[idx]
        gi = idx % 2
        i = eb % 2
        Wb, WT = Wb2[gi], WT2[gi]
        tw, twt = 'Wb%d' % gi, 'WT%d' % gi
        bT_ = nextbank()
        for et in range(4):
            P.tr(psb(bT_)[:, et * 128:(et + 1) * 128], Wb[:, et * 128:(et + 1) * 128], ident_b[:], r=[tw, 'ident_b'], w=['ps%d' % bT_])
        P.add('scalar', lambda e, bT_=bT_, WT=WT: e.copy(WT[:], psb(bT_)[:, 0:512].rearrange('p (a b) -> p a b', a=4)), r=['ps%d' % bT_], w=[twt])
        for db in range(4):
            ds_ = slice(db * 512, (db + 1) * 512)
            bY = nextbank()
            for et in range(4):
                P.mm(psum[bY][:, :], WT[:, et, :], vblk[i][:, et, ds_], et == 0, et == 3, r=[twt, 'vblk%d' % i], w=['ps%d' % bY])
            if eb == 0:
                P.add('scalar', lambda e, bY=bY, n=n, ds_=ds_: e.copy(yacc[:, n, ds_], psum[bY][:, :]), r=['ps%d' % bY], w=['yacc%d' % n])
            else:
                tt(yacc[:, n, ds_], yacc[:, n, ds_], psum[bY][:, :], ALU.add, ['yacc%d' % n, 'ps%d' % bY], ['yacc%d' % n])

    stage_A(0)
    for idx in range(len(items)):
        if idx + 1 < len(items):
            stage_A(idx + 1)
        stage_TY(idx)

    P.barrier()
    st4_2 = sb('st4b', [128, 4], F32)
    junkD_2 = sb('junkD2', [128, D], BF16)
    sb_save = sb_off[0]
    sb_off[0] = reg_oa
    lng_2 = sb('lng2', [128, D], F32)
    lnb_2 = sb('lnb2', [128, D], F32)
    xres_2 = sb('xres2', [128, D], F32)
    h1_2 = sb('h12', [128, D], F32)
    sb_off[0] = sb_save
    P.dma('sync', lng_2[:], ln2_g[0].partition_broadcast(128), r=[], w=['lng'])
    P.dma('sync', lnb_2[:], ln2_b[0].partition_broadcast(128), r=[], w=['lnb'])

    def ln_rows2(hb, ht, gam, bet):
        P.add('vector', lambda e: e.tensor_reduce(st4_2[:, 0:1], hb[:], AX.X, ALU.add), r=[ht], w=['st4a'])
        ts(st4_2[:, 1:2], st4_2[:, 0:1], -1.0 / D, None, ALU.mult, None, ['st4a'], ['st4b'])
        ts(hb[:], hb[:], st4_2[:, 1:2], None, ALU.add, None, [ht, 'st4b'], [ht])
        P.add('gpsimd', lambda e: e.memset(st4_2[:, 2:3], 0.0), w=['st4c'])
        P.add('scalar', lambda e: e.activation(junkD_2[:], hb[:], AF.Square, accum_out=st4_2[:, 2:3]), r=[ht, 'st4c'], w=['junkD', 'st4c'])
        act(st4_2[:, 3:4], st4_2[:, 2:3], AF.Ln, ['st4c'], ['st4d'], bias=LN_EPS, scale=1.0 / D)
        act(st4_2[:, 3:4], st4_2[:, 3:4], AF.Exp, ['st4d'], ['st4d'], scale=-0.5)
        stt(hb[:], hb[:], st4_2[:, 3:4], gam[:], ALU.mult, ALU.mult, [ht, 'st4d', 'lng'], [ht])
        tt(hb[:], hb[:], bet[:], ALU.add, [ht, 'lnb'], [ht], eng='gpsimd')

    for n in range(NT):
        P.dma('sync', xres_2[:], x1d[n * 128:(n + 1) * 128, :], r=[], w=['xres'])
        stt(h1_2[:], xres_2[:], ALPHA, yacc[:, n, :], ALU.mult, ALU.add, ['xres', 'yacc%d' % n], ['h1'])
        ln_rows2(h1_2, 'h1', lng_2, lnb_2)
        P.dma('sync', x_out[n * 128:(n + 1) * 128, :], h1_2[:], r=['h1'], w=['x_out'])
    P.emit()
    return nc


def layer_inputs(z, l, phases=99):
    f = lambda a: np.ascontiguousarray(a, dtype=np.float32)
    m = {
        'w_in': f(z['w_in'][l]),
        'lb_logits': f(z['hgrn_lb_logits']),
        'lmask': np.array([[0.0, float(l >= 1), float(l >= 2), float(l >= 3)]], np.float32),
        'norm_g': f(z['hgrn_norm_g'][l]).reshape(1, 1024),
    }
    if phases > 2:
        m.update({
            'lam_re': f(z['s5_lambda_re'][l]).reshape(32, 128),
            'lam_im': f(z['s5_lambda_im'][l]).reshape(32, 128),
            'log_step': f(z['s5_log_step'][l]).reshape(32, 2),
            'b_re': f(z['s5_b_re'][l]),
            'b_im': f(z['s5_b_im'][l]),
            'c_re': f(z['s5_c_re'][l]).reshape(1024, 64),
            'c_im': f(z['s5_c_im'][l]).reshape(1024, 64),
            's5_d': f(z['s5_d'][l]).reshape(8, 128),
            'w_glu': f(z['s5_w_glu'][l]),
        })
    if phases > 3:
        m.update({
            'w_up_a': f(z['w_up_a'][l]), 'w_up_b': f(z['w_up_b'][l]), 'w_o': f(z['w_o'][l]),
            'ln1_g': f(z['ln1_g'][l]).reshape(1, D), 'ln1_b': f(z['ln1_b'][l]).reshape(1, D),
        })
    if phases > 4:
        m.update({
            'w_q': f(z['peer_w_q'][l]), 'keys': f(z['peer_keys'][l]).reshape(16, 128, 128),
            'peer_u': f(z['peer_u'][l]), 'peer_v': f(z['peer_v'][l]),
            'ln2_g': f(z['ln2_g'][l]).reshape(1, D), 'ln2_b': f(z['ln2_b'][l]).reshape(1, D),
        })
    return m


def kernel(**inputs):
    z = inputs
    f32 = np.float32
    x = np.ascontiguousarray(np.asarray(z['x'])[0], dtype=f32)
    ncA = build(dbg=False, phases=3, summary=True)
    ncB = build(dbg=False, phases=99)
    xs = [np.ascontiguousarray(x[c * T:(c + 1) * T]) for c in range(NCORES)]
    zU = np.zeros((NCORES, 8, 128, 128), f32)
    zD = np.zeros((NCORES, 128, 8), f32)
    zX = np.zeros((NCORES, 128, 64), f32)
    zc = np.zeros((1, NCORES), f32)
    cores = list(range(NCORES))
    for l in range(4):
        mA = layer_inputs(z, l, 3)
        in_maps = [dict(mA, x=xs[c], cmask=zc, hgU_all=zU, hgD_all=zD, s5X_all=zX) for c in cores]
        rA = run_bass_kernel_spmd(ncA, in_maps, core_ids=cores).results
        U = np.ascontiguousarray(np.stack([rA[c]['hgU_out'] for c in cores]), dtype=f32)
        Dd = np.ascontiguousarray(np.stack([rA[c]['hgD_out'] for c in cores]), dtype=f32)
        Xs = np.ascontiguousarray(np.stack([rA[c]['s5X_out'] for c in cores]), dtype=f32)
        mB = layer_inputs(z, l, 99)
        in_maps = [dict(mB, x=xs[c], cmask=(np.arange(NCORES) < c).astype(f32)[None, :],
                        hgU_all=U, hgD_all=Dd, s5X_all=Xs) for c in cores]
        rB = run_bass_kernel_spmd(ncB, in_maps, core_ids=cores).results
        xs = [np.ascontiguousarray(rB[c]['x_out'], dtype=f32) for c in cores]
    return np.concatenate(xs, axis=0)[None].astype(f32)
```

```python
import math
from contextlib import ExitStack
import numpy as np
import concourse.bass as bass
import concourse.mybir as mybir
from concourse.bass_utils import run_bass_kernel_spmd

F32 = mybir.dt.float32
BF16 = mybir.dt.bfloat16
ALU = mybir.AluOpType
AF = mybir.ActivationFunctionType
AX = mybir.AxisListType

ENGS = ['tensor', 'vector', 'scalar', 'gpsimd', 'sync']
NDSEM = 6
NCORES = 8
T = 1024
NT = 8
D = 2048
KT = 16
ALPHA = 8.0 ** 0.25
LN_EPS = 1e-5
RMS_EPS = 1e-6
TWO_PI = 2.0 * math.pi
MAGIC = 12582912.0


class Prog:
    def __init__(self, nc):
        self.nc = nc
        self.ops = {e: [] for e in ENGS}
        self.lastw = {}
        self.readers = {}
        self.ndma = {e: 0 for e in ENGS}
        self.dma_ops = {e: [] for e in ENGS}
        self.pending = {e: [] for e in ENGS}

    def add(self, eng, fn, r=(), w=(), dma=False):
        idx = len(self.ops[eng])
        deps = list(self.pending[eng])
        self.pending[eng] = []
        for t in r:
            if t in self.lastw:
                deps.append(self.lastw[t])
        for t in w:
            if t in self.lastw:
                deps.append(self.lastw[t])
            last = {}
            for ref in self.readers.get(t, ()):
                if self.ops[ref[0]][ref[1]]['dma']:
                    deps.append(ref)
                else:
                    last[ref[0]] = ref
            deps.extend(last.values())
        op = dict(fn=fn, deps=deps, dma=dma, signal=False, eng=eng, idx=idx)
        if dma:
            j = self.ndma[eng]
            self.ndma[eng] += 1
            op['dj'] = j
            if j >= NDSEM:
                deps.append(self.dma_ops[eng][j - NDSEM])
            self.dma_ops[eng].append((eng, idx))
        self.ops[eng].append(op)
        ref = (eng, idx)
        for t in r:
            self.readers.setdefault(t, []).append(ref)
        for t in w:
            self.lastw[t] = ref
            self.readers[t] = []
        return ref

    def barrier(self):
        deps = []
        for e in ENGS:
            nonD = [i for i, o in enumerate(self.ops[e]) if not o['dma']]
            if nonD:
                deps.append((e, nonD[-1]))
            deps.extend(self.dma_ops[e][-NDSEM:])
        for e in ENGS:
            self.pending[e] = list(self.pending[e]) + deps
        self.lastw = {}
        self.readers = {}

    def mm(self, out, lhsT, rhs, start, stop, r, w):
        return self.add('tensor', lambda e: e.matmul(out, lhsT, rhs, start=start, stop=stop), r, w)

    def tr(self, out, in_, ident, r, w):
        return self.add('tensor', lambda e: e.transpose(out, in_, ident), r, w)

    def dma(self, eng, out, in_, r, w, **kw):
        return self.add(eng, lambda e: e.dma_start(out=out, in_=in_, **kw), r, w, dma=True)

    def emit(self):
        nc = self.nc
        ops = self.ops
        for e in ENGS:
            for op in ops[e]:
                nd = []
                seen = set()
                for d in op['deps']:
                    if d in seen:
                        continue
                    seen.add(d)
                    dop = ops[d[0]][d[1]]
                    if not dop['dma']:
                        if d[0] == e and e == 'tensor':
                            continue
                        dop['signal'] = True
                    nd.append(d)
                op['deps'] = nd
        for e in ENGS:
            c = 0
            for op in ops[e]:
                if op['dma']:
                    continue
                if op['signal']:
                    c += 1
                    op['semval'] = c
        with ExitStack() as st:
            esem = {e: st.enter_context(nc.semaphore('s_' + e)) for e in ENGS}
            dsem = {e: [st.enter_context(nc.semaphore('d_%s%d' % (e, i))) for i in range(NDSEM)]
                    for e in ENGS if self.ndma[e] > 0}
            block = st.enter_context(nc.Block())

            def run(e, eng):
                waited = {e2: 0 for e2 in ENGS}
                dwaited = {e2: {} for e2 in ENGS}
                for op in ops[e]:
                    for d in op['deps']:
                        dop = ops[d[0]][d[1]]
                        if dop['dma']:
                            j = dop['dj']
                            slot = j % NDSEM
                            if dwaited[d[0]].get(slot, -1) >= j:
                                continue
                            dwaited[d[0]][slot] = j
                            eng.wait_ge(dsem[d[0]][slot], 16 * (j // NDSEM + 1))
                        else:
                            v = dop['semval']
                            if waited[d[0]] >= v:
                                continue
                            waited[d[0]] = v
                            eng.wait_ge(esem[d[0]], v)
                    ins = op['fn'](eng)
                    if op['dma']:
                        ins.then_inc(dsem[e][op['dj'] % NDSEM], 16)
                    elif op['signal']:
                        ins.then_inc(esem[e], 1)
                if self.ndma[e] > 0:
                    n = self.ndma[e]
                    for i in range(NDSEM):
                        cnt = (n - i + NDSEM - 1) // NDSEM if n > i else 0
                        if cnt > 0:
                            eng.wait_ge(dsem[e][i], 16 * cnt)

            for e in ENGS:
                if ops[e]:
                    getattr(block, e)(lambda eng, e=e: run(e, eng))


def bc(ap, shape):
    return ap.to_broadcast(list(shape))


def build(dbg=False, phases=99, summary=False):
    nc = bass.Bass('TRN2', target_bir_lowering=False)
    P = Prog(nc)

    def din(name, shape):
        return nc.dram_tensor(name, list(shape), F32, kind='ExternalInput').ap()

    def dout(name, shape):
        return nc.dram_tensor(name, list(shape), F32, kind='ExternalOutput').ap()

    x_in = din('x', [T, D])
    w_in = din('w_in', [D, 9216])
    lb_logits = din('lb_logits', [4, 1024])
    lmask = din('lmask', [1, 4])
    cmask = din('cmask', [1, 8])
    norm_g = din('norm_g', [1, 1024])
    if phases > 2:
        lam_re = din('lam_re', [32, 128])
        lam_im = din('lam_im', [32, 128])
        log_step = din('log_step', [32, 2])
        b_re = din('b_re', [64, 64, 16])
        b_im = din('b_im', [64, 64, 16])
        c_re = din('c_re', [1024, 64])
        c_im = din('c_im', [1024, 64])
        s5_d = din('s5_d', [8, 128])
        w_glu = din('w_glu', [1024, 2048])
        s5X_all = din('s5X_all', [NCORES, 128, 64])
        s5X_out = dout('s5X_out', [128, 64])
    if phases > 3:
        w_up_a = din('w_up_a', [1024, 2048])
        w_up_b = din('w_up_b', [1024, 2048])
        w_o = din('w_o', [2048, 2048])
        ln1_g = din('ln1_g', [1, D])
        ln1_b = din('ln1_b', [1, D])
    if phases > 4:
        w_q = din('w_q', [2048, 2048])
        keys = din('keys', [16, 128, 128])
        peer_u = din('peer_u', [16384, 2048])
        peer_v = din('peer_v', [16384, 2048])
        ln2_g = din('ln2_g', [1, D])
        ln2_b = din('ln2_b', [1, D])
        x_out = dout('x_out', [T, D])
    hgU_all = din('hgU_all', [NCORES, 8, 128, 128])
    hgD_all = din('hgD_all', [NCORES, 128, 8])

    hgU_out = dout('hgU_out', [8, 128, 128])
    hgD_out = dout('hgD_out', [128, 8])
    dbgo = {}
    if dbg:
        dbgo['oa'] = dout('dbg_oa', [T, 1024])
        dbgo['obT'] = dout('dbg_obT', [1024, T])
        dbgo['x1'] = dout('dbg_x1', [T, D])
        dbgo['yT'] = dout('dbg_yT', [1024, T])

    x1d = nc.dram_tensor('x1d', [T, D], F32).ap()
    Gd = nc.dram_tensor('Gd', [T, 16384], BF16).ap()

    sb_off = [16512]
    sb_cnt = [0]

    def sb(name, shape, dt):
        n = 1
        for v in shape[1:]:
            n *= v
        nbytes = n * (4 if dt == F32 else 2)
        off = sb_off[0]
        sb_off[0] = (off + nbytes + 63) // 64 * 64
        assert sb_off[0] <= 229000, (name, sb_off[0])
        sb_cnt[0] += 1
        return nc.alloc_sbuf_tensor_at('%s_%d' % (name, sb_cnt[0]), list(shape), dt, offset=off)

    ident_b = sb('ident_b', [128, 128], BF16)
    ident_f = sb('ident_f', [128, 128], F32)
    tri01 = sb('tri01', [128, 64], F32)
    mask01 = sb('mask01', [128, T], F32)
    psum = [nc.alloc_psum_tensor('ps%d' % i, [128, 512], F32) for i in range(8)]

    def psb(i):
        return psum[i][:].bitcast(BF16)

    P.add('gpsimd', lambda e: e.memset(ident_f[:], 1.0), w=['ident_f'])
    P.add('gpsimd', lambda e: e.affine_select(out=ident_f[:], in_=ident_f[:], pattern=[[-1, 128]], compare_op=ALU.is_equal,
                                              fill=0.0, base=0, channel_multiplier=1), r=['ident_f'], w=['ident_f'])
    P.add('vector', lambda e: e.tensor_copy(ident_b[:], ident_f[:]), r=['ident_f'], w=['ident_b'])
    triA = sb('triA', [128, 64], F32)
    triB = sb('triB', [128, 64], F32)
    P.add('gpsimd', lambda e: e.memset(triA[:], 1.0), w=['triA'])
    P.add('gpsimd', lambda e: e.memset(triB[:], 1.0), w=['triB'])
    P.add('gpsimd', lambda e: e.affine_select(out=triA[:], in_=triA[:], pattern=[[1, 64]], compare_op=ALU.is_ge,
                                              fill=0.0, base=0, channel_multiplier=-1), r=['triA'], w=['triA'])
    P.add('gpsimd', lambda e: e.affine_select(out=triB[:], in_=triB[:], pattern=[[1, 64]], compare_op=ALU.is_ge,
                                              fill=0.0, base=64, channel_multiplier=-1), r=['triB'], w=['triB'])
    P.add('vector', lambda e: e.tensor_copy(tri01[0:64, :], triA[0:64, :]), r=['triA'], w=['tri01'])
    P.add('vector', lambda e: e.tensor_copy(tri01[64:128, :], triB[64:128, :]), r=['triB', 'tri01'], w=['tri01'])
    P.add('gpsimd', lambda e: e.memset(mask01[:], 1.0), w=['mask01'])
    P.add('gpsimd', lambda e: e.memset(mask01[:].rearrange('p (c s) -> p c s', s=64)[:, :, 0:1], 0.0), r=['mask01'], w=['mask01'])

    xT = sb('xT', [128, KT, T], BF16)
    reg_oa = sb_off[0]
    oa_tok = sb('oa_tok', [128, NT, 1024], BF16)
    obT = sb('obT', [128, 8, T], BF16)
    lbl = sb('lbl', [128, 4, 8], F32)
    lbe = sb('lbe', [128, 4, 8], F32)
    lbs = sb('lbs', [128, 8], F32)
    lb = sb('lb', [128, 8], F32)
    oml = sb('oml', [128, 8], F32)
    lmk = sb('lmk', [128, 4], F32)
    cmk = sb('cmk', [128, 8], F32)
    mark_persist = sb_off[0]
    xload = [sb('xload%d' % i, [128, D], F32) for i in range(2)]
    xbf = [sb('xbf%d' % i, [128, D], BF16) for i in range(2)]

    def to_featmajor(src_dram, dstT, tokpref):
        for n in range(NT):
            xl = xload[n % 2]; xb = xbf[n % 2]
            tl = 'xload%d' % (n % 2); tb = 'xbf%d' % (n % 2)
            P.dma('sync', xl[:], src_dram[n * 128:(n + 1) * 128, :], r=[], w=[tl])
            P.add('scalar', lambda e, xl=xl, xb=xb: e.copy(xb[:], xl[:]), r=[tl], w=[tb])
            for half in range(2):
                bank = half
                for j in range(8):
                    kt = half * 8 + j
                    P.tr(psb(bank)[:, j * 128:(j + 1) * 128], xb[:, kt * 128:(kt + 1) * 128], ident_b[:],
                         r=[tb, 'ident_b'], w=['ps%d' % bank])
                P.add('vector', lambda e, bank=bank, half=half, n=n: e.tensor_copy(
                    dstT[:, half * 8:(half + 1) * 8, n * 128:(n + 1) * 128],
                    psb(bank).rearrange('p (j t) -> p j t', j=8)), r=['ps%d' % bank], w=[tokpref])

    to_featmajor(x_in, xT, 'xT')

    P.dma('sync', lbl[:], lb_logits.rearrange('l (h d) -> d l h', d=128), r=[], w=['lbl'], allow_slow_non_contiguous=True)
    P.dma('sync', lmk[:], lmask[0].partition_broadcast(128), r=[], w=['lmk'])
    P.dma('sync', cmk[:], cmask[0].partition_broadcast(128), r=[], w=['cmk'])
    P.add('scalar', lambda e: e.activation(lbe[:], lbl[:], AF.Exp), r=['lbl'], w=['lbe'])
    P.add('vector', lambda e: e.tensor_reduce(lbs[:], lbe[:].rearrange('p l h -> p h l'), AX.X, ALU.add), r=['lbe'], w=['lbs'])
    P.add('vector', lambda e: e.reciprocal(lbs[:], lbs[:]), r=['lbs'], w=['lbs'])
    P.add('vector', lambda e: e.tensor_tensor(lbe[:], lbe[:], bc(lmk[:].unsqueeze(2), [128, 4, 8]), ALU.mult), r=['lbe', 'lmk'], w=['lbe'])
    P.add('vector', lambda e: e.tensor_reduce(lb[:], lbe[:].rearrange('p l h -> p h l'), AX.X, ALU.add), r=['lbe'], w=['lb'])
    P.add('vector', lambda e: e.tensor_tensor(lb[:], lb[:], lbs[:], ALU.mult), r=['lb', 'lbs'], w=['lb'])
    P.add('vector', lambda e: e.tensor_scalar(oml[:], lb[:], -1.0, 1.0, ALU.mult, ALU.add), r=['lb'], w=['oml'])

    v_tok = sb('v_tok', [128, NT, 1024], BF16)
    sg_tok = sb('sg_tok', [128, NT, 1024], BF16)
    wblk = [sb('wblk%d' % i, [128, KT, 512], BF16) for i in range(2)]
    wcnt = [0]

    def load_wblk(src, c0, ncols=512, kt=KT):
        i = wcnt[0] % 2
        wcnt[0] += 1
        P.dma('gpsimd', wblk[i][:, 0:kt, 0:ncols], src[:, c0:c0 + ncols].rearrange('(kt p) n -> p kt n', p=128),
              r=[], w=['wblk%d' % i])
        return wblk[i], 'wblk%d' % i

    pscnt = [0]

    def nextbank(lo=0, hi=8):
        b = lo + pscnt[0] % (hi - lo)
        pscnt[0] += 1
        return b

    for blk in (range(2) if summary else range(4)):
        wb, wt = load_wblk(w_in, 2048 + blk * 512)
        for n in range(NT):
            bank = nextbank()
            for kt in range(KT):
                P.mm(psum[bank][:, :], xT[:, kt, n * 128:(n + 1) * 128], wb[:, kt, :], kt == 0, kt == KT - 1,
                     r=['xT', wt], w=['ps%d' % bank])
            if blk < 2:
                P.add('vector', lambda e, bank=bank, n=n, blk=blk: e.tensor_copy(v_tok[:, n, blk * 512:(blk + 1) * 512], psum[bank][:, :]),
                      r=['ps%d' % bank], w=['v_tok%d' % n])
            else:
                P.add('scalar', lambda e, bank=bank, n=n, blk=blk: e.activation(sg_tok[:, n, (blk - 2) * 512:(blk - 1) * 512], psum[bank][:, :], AF.Sigmoid),
                      r=['ps%d' % bank], w=['sg_tok%d' % n])

    ngb = sb('ngb', [128, 1024], F32)
    P.dma('sync', ngb[:], norm_g[0].partition_broadcast(128), r=[], w=['ngb'])
    Dp = sb('Dp', [128, NCORES, 8], F32)
    P.dma('sync', Dp[:], hgD_all.rearrange('c d h -> d c h'), r=[], w=['Dp'])
    P.add('vector', lambda e: e.tensor_scalar(Dp[:], Dp[:], -1.0, None, ALU.add), r=['Dp'], w=['Dp'])
    P.add('vector', lambda e: e.tensor_tensor(Dp[:], Dp[:], bc(cmk[:].unsqueeze(2), [128, NCORES, 8]), ALU.mult), r=['Dp', 'cmk'], w=['Dp'])
    P.add('vector', lambda e: e.tensor_scalar(Dp[:], Dp[:], 1.0, None, ALU.add), r=['Dp'], w=['Dp'])

    qT = sb('qT', [128, T], F32)
    fT = sb('fT', [128, T], F32)
    lgf = sb('lgf', [128, T], F32)
    kT_ = sb('kT_', [128, T], F32)
    bT = sb('bT', [128, T], F32)
    eq = sb('eq', [128, T], F32)
    ek = sb('ek', [128, T], F32)
    qt_b = sb('qt_b', [128, T], BF16)
    kt_b = sb('kt_b', [128, T], BF16)
    ktok = sb('ktok', [128, NT, 128], BF16)
    rr = sb('rr', [128, 16], F32)
    er = sb('er', [128, 16], F32)
    e2r = sb('e2r', [128, 16], F32)
    bsum = sb('bsum', [128, 1], F32)
    Dtot = sb('Dtot', [128, 8], F32)
    S = sb('S', [128, 128], F32)
    Uld = [sb('Uld%d' % i, [128, 128], F32) for i in range(2)]
    Spb = [sb('Spb%d' % i, [128, 128], BF16) for i in range(4)]
    Utmp = sb('Utmp', [128, 128], F32)
    att_sb = [sb('att_sb%d' % i, [128, 128], BF16) for i in range(2)]
    ss = sb('ss', [128, 2], F32)
    junk = sb('junk', [128, 128], F32)
    otmp = sb('otmp', [128, 128], F32)
    for i in range(2):
        P.add('gpsimd', lambda e, i=i: e.memset(att_sb[i][:], 0.0), w=['att_sb%d' % i])

    for h in range(8):
        P.add('gpsimd', lambda e: e.memset(S[:], 0.0), w=['S'])
        for j in ([] if summary else range(NCORES - 1)):
            ul = Uld[j % 2]; ut = 'Uld%d' % (j % 2)
            P.dma('sync', ul[:], hgU_all[j, h], r=[], w=[ut])
            P.add('vector', lambda e, ul=ul, j=j: e.tensor_scalar(ul[:], ul[:], cmk[:, j:j + 1], None, ALU.mult), r=[ut, 'cmk'], w=[ut])
            P.add('vector', lambda e, ul=ul, j=j, h=h: e.scalar_tensor_tensor(S[:], S[:], Dp[:, j, h:h + 1], ul[:], ALU.mult, ALU.add),
                  r=['S', 'Dp', ut], w=['S'])
        i = wcnt[0] % 2
        wcnt[0] += 1
        wb = wblk[i]; wt = 'wblk%d' % i
        P.dma('gpsimd', wb[:, :, 0:128], w_in[:, h * 128:(h + 1) * 128].rearrange('(kt p) n -> p kt n', p=128), r=[], w=[wt])
        P.dma('gpsimd', wb[:, :, 128:256], w_in[:, 1024 + h * 128:1024 + (h + 1) * 128].rearrange('(kt p) n -> p kt n', p=128), r=[], w=[wt])
        for which in range(2):
            for half in range(2):
                bank = nextbank()
                for kt in range(KT):
                    P.mm(psum[bank][:, :], wb[:, kt, which * 128:(which + 1) * 128], xT[:, kt, half * 512:(half + 1) * 512],
                         kt == 0, kt == KT - 1, r=['xT', wt], w=['ps%d' % bank])
                if which == 0:
                    P.add('vector', lambda e, bank=bank, half=half: e.tensor_copy(qT[:, half * 512:(half + 1) * 512], psum[bank][:, :]),
                          r=['ps%d' % bank], w=['qT'])
                else:
                    P.add('scalar', lambda e, bank=bank, half=half: e.activation(fT[:, half * 512:(half + 1) * 512], psum[bank][:, :], AF.Sigmoid),
                          r=['ps%d' % bank], w=['fT'])
        P.add('vector', lambda e, h=h: e.tensor_scalar(fT[:], fT[:], oml[:, h:h + 1], lb[:, h:h + 1], ALU.mult, ALU.add), r=['fT', 'oml', 'lb'], w=['fT'])
        P.add('scalar', lambda e: e.activation(lgf[:], fT[:], AF.Ln), r=['fT'], w=['lgf'])
        P.add('gpsimd', lambda e: e.tensor_scalar(kT_[:], fT[:], -1.0, 1.0, ALU.mult, ALU.add), r=['fT'], w=['kT_'])
        P.add('vector', lambda e: e.tensor_tensor_scan(bT[:], mask01[:], lgf[:], 0.0, ALU.mult, ALU.add), r=['mask01', 'lgf'], w=['bT'])
        b3 = bT[:].rearrange('p (c s) -> p c s', s=64)
        P.add('vector', lambda e: e.tensor_scalar(rr[:], b3[:, :, 63], 0.5, None, ALU.mult), r=['bT'], w=['rr'])
        P.add('vector', lambda e: e.tensor_reduce(bsum[:], b3[:, :, 63], AX.X, ALU.add), r=['bT'], w=['bsum'])
        P.add('scalar', lambda e, h=h: e.activation(Dtot[:, h:h + 1], bsum[:], AF.Exp), r=['bsum'], w=['Dtot'])
        P.add('scalar', lambda e: e.activation(er[:], rr[:], AF.Exp), r=['rr'], w=['er'])
        P.add('vector', lambda e: e.tensor_tensor(e2r[:], er[:], er[:], ALU.mult), r=['er'], w=['e2r'])
        P.add('vector', lambda e: e.tensor_tensor(b3, b3, bc(rr[:].unsqueeze(2), [128, 16, 64]), ALU.subtract), r=['bT', 'rr'], w=['bT'])
        P.add('scalar', lambda e: e.activation(eq[:], bT[:], AF.Exp), r=['bT'], w=['eq'])
        P.add('scalar', lambda e: e.activation(ek[:], bT[:], AF.Exp, scale=-1.0), r=['bT'], w=['ek'])
        P.add('vector', lambda e: e.tensor_tensor(qt_b[:], qT[:], eq[:], ALU.mult), r=['qT', 'eq'], w=['qt_b'])
        P.add('gpsimd', lambda e: e.tensor_tensor(kt_b[:], kT_[:], ek[:], ALU.mult), r=['kT_', 'ek'], w=['kt_b'])
        bank = nextbank()
        for n in range(NT):
            P.tr(psb(bank)[:, n * 128:(n + 1) * 128], kt_b[:, n * 128:(n + 1) * 128], ident_b[:], r=['kt_b', 'ident_b'], w=['ps%d' % bank])
        P.add('vector', lambda e, bank=bank: e.tensor_copy(ktok[:], psb(bank).rearrange('p (n d) -> p n d', n=NT)), r=['ps%d' % bank], w=['ktok'])
        for n in range(NT):
            asb = att_sb[n % 2]; at = 'att_sb%d' % (n % 2)
            b_att = nextbank(); b_U = nextbank(); b_o = nextbank()
            for cc in range(2):
                c = 2 * n + cc
                tok = slice(c * 64, (c + 1) * 64)
                pr = slice(cc * 64, (cc + 1) * 64)
                P.mm(psum[b_att][pr, cc * 64:(cc + 1) * 64], kt_b[:, tok], qt_b[:, tok], True, True, r=['kt_b', 'qt_b'], w=['ps%d' % b_att])
                P.add('vector', lambda e, pr=pr, cc=cc, asb=asb, b_att=b_att: e.tensor_tensor(
                    asb[pr, cc * 64:(cc + 1) * 64], psum[b_att][pr, cc * 64:(cc + 1) * 64], tri01[pr, :], ALU.mult),
                    r=['ps%d' % b_att, 'tri01'], w=[at])
                P.mm(psum[b_U][:, cc * 128:(cc + 1) * 128], ktok[pr, n, :], v_tok[pr, n, h * 128:(h + 1) * 128], True, True,
                     r=['ktok', 'v_tok%d' % n], w=['ps%d' % b_U])
            P.mm(psum[b_o][:, 0:128], asb[:, :], v_tok[:, n, h * 128:(h + 1) * 128], True, False, r=[at, 'v_tok%d' % n], w=['ps%d' % b_o])
            for cc in range(2):
                c = 2 * n + cc
                tok = slice(c * 64, (c + 1) * 64)
                pr = slice(cc * 64, (cc + 1) * 64)
                sp = Spb[c % 4]; spt = 'Spb%d' % (c % 4)
                P.add('vector', lambda e, sp=sp, c=c: e.tensor_scalar(sp[:], S[:], er[:, c:c + 1], None, ALU.mult), r=['S', 'er'], w=[spt])
                P.mm(psum[b_o][pr, 0:128], qt_b[:, tok], sp[:, :], False, cc == 1, r=['qt_b', spt], w=['ps%d' % b_o])
                P.add('vector', lambda e, c=c, cc=cc, b_U=b_U: e.tensor_scalar(Utmp[:], psum[b_U][:, cc * 128:(cc + 1) * 128], er[:, c:c + 1], None, ALU.mult),
                      r=['ps%d' % b_U, 'er'], w=['Utmp'])
                P.add('vector', lambda e, c=c: e.scalar_tensor_tensor(S[:], S[:], e2r[:, c:c + 1], Utmp[:], ALU.mult, ALU.add),
                      r=['S', 'e2r', 'Utmp'], w=['S'])
            P.add('gpsimd', lambda e: e.memset(ss[:], 0.0), w=['ss'])
            P.add('scalar', lambda e, b_o=b_o: e.activation(junk[:], psum[b_o][:, 0:128], AF.Square, accum_out=ss[:, 0:1]), r=['ps%d' % b_o, 'ss'], w=['junk', 'ss'])
            P.add('scalar', lambda e: e.activation(ss[:, 1:2], ss[:, 0:1], AF.Ln, bias=RMS_EPS, scale=1.0 / 128.0), r=['ss'], w=['ss'])
            P.add('scalar', lambda e: e.activation(ss[:, 1:2], ss[:, 1:2], AF.Exp, scale=-0.5), r=['ss'], w=['ss'])
            P.add('vector', lambda e, b_o=b_o, h=h: e.scalar_tensor_tensor(otmp[:], psum[b_o][:, 0:128], ss[:, 1:2], ngb[:, h * 128:(h + 1) * 128], ALU.mult, ALU.mult),
                  r=['ps%d' % b_o, 'ss', 'ngb'], w=['otmp'])
            P.add('gpsimd', lambda e, n=n, h=h: e.tensor_tensor(oa_tok[:, n, h * 128:(h + 1) * 128], otmp[:], sg_tok[:, n, h * 128:(h + 1) * 128], ALU.mult),
                  r=['otmp', 'sg_tok%d' % n], w=['oa_tok%d' % n])
        P.dma('sync', hgU_out[h], S[:], r=['S'], w=['hgU_out'])
    P.dma('sync', hgD_out, Dtot[:], r=['Dtot'], w=['hgD_out'])
    if dbg:
        for n in range(NT):
            P.add('vector', lambda e, n=n: e.tensor_copy(xload[n % 2][:, 0:1024], oa_tok[:, n, :]), r=['oa_tok%d' % n], w=['xload%d' % (n % 2)])
            P.dma('sync', dbgo['oa'][n * 128:(n + 1) * 128, :], xload[n % 2][:, 0:1024], r=['xload%d' % (n % 2)], w=['dbg_oa'])
    if phases <= 2:
        P.emit()
        return nc

    P.barrier()
    sb_off[0] = mark_persist
    wblk = [sb('wblk%d' % i, [128, KT, 512], BF16) for i in range(2)]
    uT = sb('uT', [128, 8, T], F32)
    yT = sb('yT', [128, 8, T], BF16)
    Xr = [sb('Xr%d' % i, [128, T], F32) for i in range(2)]
    Xi = [sb('Xi%d' % i, [128, T], F32) for i in range(2)]
    ytmp = sb('ytmp', [128, 512], F32)
    sgm = sb('sgm', [128, 512], F32)
    pl = sb('pl', [32, 3, 128], F32)
    lst2 = sb('lst2', [32, 2], F32)
    sp = {}
    for nm in ['lamr', 'lami', 'lst', 'lr', 'dt', 'lrdt', 'lidt', 'mag', 't1', 't2', 'sn', 'cs', 'ar', 'ai', 'den', 'nr',
               'zr', 'zi', 'xinr', 'xini', 'axr', 'axi', 'Ar', 'Ai', 'tA', 'tB', 'tC', 'tD']:
        sp[nm] = sb('s5_' + nm, [128, 32], F32)
    pwr = sb('pwr', [128, 11, 32], F32)
    pwi = sb('pwi', [128, 11, 32], F32)
    npwi = sb('npwi', [128, 11, 32], F32)
    bre = sb('bre', [128, 32, 16], F32)
    bim = sb('bim', [128, 32, 16], F32)
    bbr = sb('bbr', [128, 32, 16], F32)
    bbi = sb('bbi', [128, 32, 16], F32)
    btmp = sb('btmp', [128, 32, 16], F32)
    Cld = [sb('Cld%d' % i, [128, 8, 64], F32) for i in range(2)]
    Cdup = sb('Cdup', [128, 2, 64], F32)
    dld = sb('dld', [8, 128], F32)
    dvec = sb('dvec', [128, 8], F32)
    Xe_ld = [sb('Xe_ld%d' % i, [128, 64], F32) for i in range(2)]
    Xend = sb('Xend', [128, 64], F32)
    srcB = [sb('srcB%d' % j, [128, 128], F32) for j in range(4)]
    LBr = [sb('LBr%d' % j, [128, 128], F32) for j in range(4)]
    LBi = [sb('LBi%d' % j, [128, 128], F32) for j in range(4)]
    LCr = [sb('LCr%d' % j, [128, 128], F32) for j in range(4)]
    LCi = [sb('LCi%d' % j, [128, 128], F32) for j in range(4)]
    for j in range(4):
        for tl, nm in ((srcB, 'srcB'), (LCr, 'LCr'), (LCi, 'LCi')):
            P.add('gpsimd', lambda e, t=tl[j]: e.memset(t[:], 0.0), w=['%s%d' % (nm, j)])

    V = lambda eng, fn, r, w: P.add(eng, fn, r=r, w=w)

    def tt(out, a, b, op, r, w, eng='vector'):
        P.add(eng, lambda e: e.tensor_tensor(out, a, b, op), r=r, w=w)

    def ts(out, a, s1, s2, op0, op1, r, w, eng='vector'):
        if op1 is None:
            P.add(eng, lambda e: e.tensor_scalar(out, a, s1, None, op0), r=r, w=w)
        else:
            P.add(eng, lambda e: e.tensor_scalar(out, a, s1, s2, op0, op1), r=r, w=w)

    def stt(out, a, sc, b, op0, op1, r, w, eng='vector'):
        P.add(eng, lambda e: e.scalar_tensor_tensor(out, a, sc, b, op0, op1), r=r, w=w)

    def act(out, a, func, r, w, **kw):
        P.add('scalar', lambda e: e.activation(out, a, func, **kw), r=r, w=w)

    P.dma('sync', pl[:, 0, :], lam_re, r=[], w=['pl'])
    P.dma('sync', pl[:, 1, :], lam_im, r=[], w=['pl'])
    P.dma('sync', lst2[:], log_step, r=[], w=['lst2'])
    P.add('vector', lambda e: e.tensor_copy(pl[:, 2, :].rearrange('p (a b) -> p a b', a=2), bc(lst2[:].unsqueeze(2), [32, 2, 64])), r=['lst2', 'pl'], w=['pl'])
    for i, nm in enumerate(['lamr', 'lami', 'lst']):
        P.tr(psum[4][:, i * 32:(i + 1) * 32], pl[:, i, :], ident_f[0:32, 0:32], r=['pl', 'ident_f'], w=['ps4'])
        P.add('vector', lambda e, i=i, nm=nm: e.tensor_copy(sp[nm][:], psum[4][:, i * 32:(i + 1) * 32]), r=['ps4'], w=[nm])
    P.dma('sync', dld[:], s5_d, r=[], w=['dld'])
    P.tr(psum[4][:, 128:136], dld[:], ident_f[0:8, 0:8], r=['dld', 'ident_f'], w=['ps4'])
    P.add('vector', lambda e: e.tensor_copy(dvec[:], psum[4][:, 128:136]), r=['ps4'], w=['dvec'])
    P.dma('sync', bre[:], b_re.rearrange('(gp g2) p n -> (g2 p) gp n', g2=2), r=[], w=['bre'])
    P.dma('sync', bim[:], b_im.rearrange('(gp g2) p n -> (g2 p) gp n', g2=2), r=[], w=['bim'])
    P.dma('sync', Cld[0][:], c_re.rearrange('(ct q) p -> q ct p', q=128), r=[], w=['Cld0'])
    P.dma('sync', Cld[1][:], c_im.rearrange('(ct q) p -> q ct p', q=128), r=[], w=['Cld1'])

    A = lambda nm: sp[nm][:]
    ts(A('lr'), A('lamr'), -1e-4, None, ALU.min, None, ['lamr'], ['lr'])
    act(A('dt'), A('lst'), AF.Exp, ['lst'], ['dt'])
    tt(A('lrdt'), A('lr'), A('dt'), ALU.mult, ['lr', 'dt'], ['lrdt'])
    tt(A('lidt'), A('lami'), A('dt'), ALU.mult, ['lami', 'dt'], ['lidt'])
    act(A('mag'), A('lrdt'), AF.Exp, ['lrdt'], ['mag'])
    ts(A('t1'), A('lidt'), 1.0 / TWO_PI, MAGIC, ALU.mult, ALU.add, ['lidt'], ['t1'])
    ts(A('t1'), A('t1'), -MAGIC, None, ALU.add, None, ['t1'], ['t1'])
    stt(A('t1'), A('t1'), -TWO_PI, A('lidt'), ALU.mult, ALU.add, ['t1', 'lidt'], ['t1'])
    act(A('sn'), A('t1'), AF.Sin, ['t1'], ['sn'])
    ts(A('t2'), A('lidt'), math.pi / 2, None, ALU.add, None, ['lidt'], ['t2'])
    ts(A('tA'), A('t2'), 1.0 / TWO_PI, MAGIC, ALU.mult, ALU.add, ['t2'], ['tA'])
    ts(A('tA'), A('tA'), -MAGIC, None, ALU.add, None, ['tA'], ['tA'])
    stt(A('tA'), A('tA'), -TWO_PI, A('t2'), ALU.mult, ALU.add, ['tA', 't2'], ['tA'])
    act(A('cs'), A('tA'), AF.Sin, ['tA'], ['cs'])
    tt(A('ar'), A('mag'), A('cs'), ALU.mult, ['mag', 'cs'], ['ar'])
    tt(A('ai'), A('mag'), A('sn'), ALU.mult, ['mag', 'sn'], ['ai'])
    tt(A('den'), A('lr'), A('lr'), ALU.mult, ['lr'], ['den'])
    tt(A('tB'), A('lami'), A('lami'), ALU.mult, ['lami'], ['tB'])
    tt(A('den'), A('den'), A('tB'), ALU.add, ['den', 'tB'], ['den'])
    P.add('vector', lambda e: e.reciprocal(A('den'), A('den')), r=['den'], w=['den'])
    ts(A('nr'), A('ar'), -1.0, None, ALU.add, None, ['ar'], ['nr'])
    tt(A('tB'), A('nr'), A('lr'), ALU.mult, ['nr', 'lr'], ['tB'])
    tt(A('tC'), A('ai'), A('lami'), ALU.mult, ['ai', 'lami'], ['tC'])
    tt(A('tB'), A('tB'), A('tC'), ALU.add, ['tB', 'tC'], ['tB'])
    tt(A('zr'), A('tB'), A('den'), ALU.mult, ['tB', 'den'], ['zr'])
    tt(A('tB'), A('ai'), A('lr'), ALU.mult, ['ai', 'lr'], ['tB'])
    tt(A('tC'), A('nr'), A('lami'), ALU.mult, ['nr', 'lami'], ['tC'])
    tt(A('tB'), A('tB'), A('tC'), ALU.subtract, ['tB', 'tC'], ['tB'])
    tt(A('zi'), A('tB'), A('den'), ALU.mult, ['tB', 'den'], ['zi'])
    P.add('vector', lambda e: e.tensor_copy(pwr[:, 0, :], A('ar')), r=['ar'], w=['pwr'])
    P.add('vector', lambda e: e.tensor_copy(pwi[:, 0, :], A('ai')), r=['ai'], w=['pwi'])
    for k in range(10):
        tt(A('tB'), pwr[:, k, :], pwr[:, k, :], ALU.mult, ['pwr'], ['tB'])
        tt(A('tC'), pwi[:, k, :], pwi[:, k, :], ALU.mult, ['pwi'], ['tC'])
        stt(pwi[:, k + 1, :], pwr[:, k, :], 2.0, pwi[:, k, :], ALU.mult, ALU.mult, ['pwr', 'pwi'], ['pwi'])
        tt(pwr[:, k + 1, :], A('tB'), A('tC'), ALU.subtract, ['tB', 'tC'], ['pwr'])
    ts(npwi[:], pwi[:], -1.0, None, ALU.mult, None, ['pwi'], ['npwi'])
    P.add('gpsimd', lambda e: e.memset(A('xinr'), 0.0), w=['xinr'])
    P.add('gpsimd', lambda e: e.memset(A('xini'), 0.0), w=['xini'])
    for j in range(NCORES - 1):
        xl = Xe_ld[j % 2]; xt_ = 'Xe_ld%d' % (j % 2)
        P.dma('sync', xl[:], s5X_all[j], r=[], w=[xt_])
        m = cmk[:, j:j + 1]
        ts(A('Ar'), pwr[:, 10, :], -1.0, None, ALU.add, None, ['pwr'], ['Ar'])
        ts(A('Ar'), A('Ar'), m, 1.0, ALU.mult, ALU.add, ['Ar', 'cmk'], ['Ar'])
        ts(A('Ai'), pwi[:, 10, :], m, None, ALU.mult, None, ['pwi', 'cmk'], ['Ai'])
        ts(xl[:], xl[:], m, None, ALU.mult, None, [xt_, 'cmk'], [xt_])
        tt(A('tA'), A('Ar'), A('xinr'), ALU.mult, ['Ar', 'xinr'], ['tA'])
        tt(A('tB'), A('Ai'), A('xini'), ALU.mult, ['Ai', 'xini'], ['tB'])
        tt(A('tC'), A('Ar'), A('xini'), ALU.mult, ['Ar', 'xini'], ['tC'])
        tt(A('tD'), A('Ai'), A('xinr'), ALU.mult, ['Ai', 'xinr'], ['tD'])
        tt(A('tA'), A('tA'), A('tB'), ALU.subtract, ['tA', 'tB'], ['tA'])
        tt(A('tC'), A('tC'), A('tD'), ALU.add, ['tC', 'tD'], ['tC'])
        tt(A('xinr'), A('tA'), xl[:, 0:32], ALU.add, ['tA', xt_], ['xinr'])
        tt(A('xini'), A('tC'), xl[:, 32:64], ALU.add, ['tC', xt_], ['xini'])
    tt(A('tA'), A('ar'), A('xinr'), ALU.mult, ['ar', 'xinr'], ['tA'])
    tt(A('tB'), A('ai'), A('xini'), ALU.mult, ['ai', 'xini'], ['tB'])
    tt(A('axr'), A('tA'), A('tB'), ALU.subtract, ['tA', 'tB'], ['axr'])
    tt(A('tA'), A('ar'), A('xini'), ALU.mult, ['ar', 'xini'], ['tA'])
    tt(A('tB'), A('ai'), A('xinr'), ALU.mult, ['ai', 'xinr'], ['tB'])
    tt(A('axi'), A('tA'), A('tB'), ALU.add, ['tA', 'tB'], ['axi'])
    zrb = bc(sp['zr'][:].unsqueeze(2), [128, 32, 16])
    zib = bc(sp['zi'][:].unsqueeze(2), [128, 32, 16])
    tt(bbr[:], bre[:], zrb, ALU.mult, ['bre', 'zr'], ['bbr'])
    tt(btmp[:], bim[:], zib, ALU.mult, ['bim', 'zi'], ['btmp'])
    tt(bbr[:], bbr[:], btmp[:], ALU.subtract, ['bbr', 'btmp'], ['bbr'])
    tt(bbi[:], bim[:], zrb, ALU.mult, ['bim', 'zr'], ['bbi'])
    tt(btmp[:], bre[:], zib, ALU.mult, ['bre', 'zi'], ['btmp'])
    tt(bbi[:], bbi[:], btmp[:], ALU.add, ['bbi', 'btmp'], ['bbi'])

    for blk in range(2):
        wb, wt = load_wblk(w_in, 4096 + blk * 512)
        for ntl in range(4):
            ct = blk * 4 + ntl
            for half in range(2):
                bank = nextbank(0, 4)
                for kt in range(KT):
                    P.mm(psum[bank][:, :], wb[:, kt, ntl * 128:(ntl + 1) * 128], xT[:, kt, half * 512:(half + 1) * 512],
                         kt == 0, kt == KT - 1, r=['xT', wt], w=['ps%d' % bank])
                P.add('scalar', lambda e, bank=bank, ct=ct, half=half: e.copy(uT[:, ct, half * 512:(half + 1) * 512], psum[bank][:, :]),
                      r=['ps%d' % bank], w=['uT%d' % ct])

    P.barrier()
    wb1f = wblk[1][:].rearrange('p a b -> p (a b)').bitcast(F32)
    Ms = [(wb1f[:, 0:T], wb1f[:, T:2 * T]), (wb1f[:, 2 * T:3 * T], wb1f[:, 3 * T:4 * T])]
    wb0f = wblk[0][:].rearrange('p a b -> p (a b)').bitcast(F32)
    Xr2 = [wb0f[:, 0:T], wb0f[:, T:2 * T]]
    Xi2 = [wb0f[:, 2 * T:3 * T], wb0f[:, 3 * T:4 * T]]
    XRs = [Xr, Xr2]
    XIs = [Xi, Xi2]
    for ct in range(8):
        for j in range(4):
            gp = 4 * ct + j
            for (bb, LB, nm) in ((bbr, LBr, 'LBr'), (bbi, LBi, 'LBi')):
                for g2 in range(2):
                    pr = slice(g2 * 64, (g2 + 1) * 64)
                    c0 = (2 * j + g2) * 16
                    P.add('vector', lambda e, bb=bb, pr=pr, c0=c0, j=j, gp=gp: e.tensor_copy(srcB[j][pr, c0:c0 + 16], bb[pr, gp, :]),
                          r=['bbr', 'bbi'], w=['srcB%d' % j])
                P.tr(psum[4][:, 0:128], srcB[j][:], ident_f[:], r=['srcB%d' % j, 'ident_f'], w=['ps4'])
                P.add('vector', lambda e, LB=LB, j=j: e.tensor_copy(LB[j][:], psum[4][:, 0:128]), r=['ps4'], w=['%s%d' % (nm, j)])
        for ri, (LC, nm) in enumerate(((LCr, 'LCr'), (LCi, 'LCi'))):
            P.add('vector', lambda e, ri=ri, ct=ct: e.tensor_copy(Cdup[:], bc(Cld[ri][:, ct, :].unsqueeze(1), [128, 2, 64])), r=['Cld%d' % ri], w=['Cdup'])
            P.tr(psum[5][:, 0:128], Cdup[:].rearrange('p a b -> p (a b)'), ident_f[:], r=['Cdup', 'ident_f'], w=['ps5'])
            for j in range(4):
                for g2 in range(2):
                    pr = slice(g2 * 64, (g2 + 1) * 64)
                    c0 = (2 * j + g2) * 16
                    if ri == 0:
                        P.add('vector', lambda e, LC=LC, j=j, pr=pr, c0=c0: e.tensor_copy(LC[j][pr, c0:c0 + 16], psum[5][pr, c0:c0 + 16]),
                              r=['ps5'], w=['%s%d' % (nm, j)])
                    else:
                        P.add('vector', lambda e, LC=LC, j=j, pr=pr, c0=c0: e.tensor_scalar(LC[j][pr, c0:c0 + 16], psum[5][pr, c0:c0 + 16], -1.0, None, ALU.mult),
                              r=['ps5'], w=['%s%d' % (nm, j)])
        for jp in range(2):
            js = (2 * jp, 2 * jp + 1)
            for si, j in enumerate(js):
                gp = 4 * ct + j
                XR, XI = XRs[si], XIs[si]
                t0r, t0i = 'Xr%d_0' % si, 'Xi%d_0' % si
                for half in range(2):
                    hs = slice(half * 512, (half + 1) * 512)
                    b1 = nextbank(0, 4)
                    P.mm(psum[b1][:, :], LBr[j][:], uT[:, ct, hs], True, True, r=['LBr%d' % j, 'uT%d' % ct], w=['ps%d' % b1])
                    P.add('scalar', lambda e, b1=b1, hs=hs, XR=XR: e.copy(XR[0][:, hs], psum[b1][:, :]), r=['ps%d' % b1], w=[t0r])
                    b2 = nextbank(0, 4)
                    P.mm(psum[b2][:, :], LBi[j][:], uT[:, ct, hs], True, True, r=['LBi%d' % j, 'uT%d' % ct], w=['ps%d' % b2])
                    P.add('scalar', lambda e, b2=b2, hs=hs, XI=XI: e.copy(XI[0][:, hs], psum[b2][:, :]), r=['ps%d' % b2], w=[t0i])
                tt(XR[0][:, 0:1], XR[0][:, 0:1], sp['axr'][:, gp:gp + 1], ALU.add, [t0r, 'axr'], [t0r])
                tt(XI[0][:, 0:1], XI[0][:, 0:1], sp['axi'][:, gp:gp + 1], ALU.add, [t0i, 'axi'], [t0i], eng='gpsimd')
            for k in (range(10) if summary else []):
                nn = T >> (k + 1)
                s_, d_ = k % 2, 1 - k % 2
                for si, j in enumerate(js):
                    gp = 4 * ct + j
                    XR, XI = XRs[si], XIs[si]
                    mA, mB = Ms[si]
                    tmA, tmB = 'mA%d' % si, 'mB%d' % si
                    pr_ = pwr[:, k, gp:gp + 1]; pi_ = pwi[:, k, gp:gp + 1]; npi_ = npwi[:, k, gp:gp + 1]
                    tr_s, ti_s, tr_d, ti_d = 'Xr%d_%d' % (si, s_), 'Xi%d_%d' % (si, s_), 'Xr%d_%d' % (si, d_), 'Xi%d_%d' % (si, d_)
                    xr2 = XR[s_][:, 0:2 * nn].rearrange('p (j two) -> p j two', two=2)
                    xi2 = XI[s_][:, 0:2 * nn].rearrange('p (j two) -> p j two', two=2)
                    stt(XR[d_][:, 0:nn], xr2[:, :, 0], pr_, xr2[:, :, 1], ALU.mult, ALU.add, [tr_s, 'pwr'], [tr_d])
                    stt(XR[d_][:, 0:nn], xi2[:, :, 0], npi_, XR[d_][:, 0:nn], ALU.mult, ALU.add, [ti_s, 'npwi', tr_d], [tr_d])
                    stt(XI[d_][:, 0:nn], xi2[:, :, 0], pr_, xi2[:, :, 1], ALU.mult, ALU.add, [ti_s, 'pwr'], [ti_d])
                    stt(XI[d_][:, 0:nn], xr2[:, :, 0], pi_, XI[d_][:, 0:nn], ALU.mult, ALU.add, [tr_s, 'pwi', ti_d], [ti_d])
            for si, j in (list(enumerate(js)) if summary else []):
                gp = 4 * ct + j
                XR, XI = XRs[si], XIs[si]
                P.add('vector', lambda e, gp=gp, XR=XR: e.tensor_copy(Xend[:, gp:gp + 1], XR[0][:, 0:1]), r=['Xr%d_0' % si], w=['Xend'])
                P.add('vector', lambda e, gp=gp, XI=XI: e.tensor_copy(Xend[:, 32 + gp:33 + gp], XI[0][:, 0:1]), r=['Xi%d_0' % si], w=['Xend'])
            if summary:
                continue
            for k in range(10):
                d = 1 << k
                s_, d_ = k % 2, 1 - k % 2
                for si, j in enumerate(js):
                    gp = 4 * ct + j
                    XR, XI = XRs[si], XIs[si]
                    mA, mB = Ms[si]
                    tmA, tmB = 'mA%d' % si, 'mB%d' % si
                    pr_ = pwr[:, k, gp:gp + 1]; pi_ = pwi[:, k, gp:gp + 1]; npi_ = npwi[:, k, gp:gp + 1]
                    xr_s, xi_s, xr_d, xi_d = XR[s_], XI[s_], XR[d_], XI[d_]
                    tr_s, ti_s, tr_d, ti_d = 'Xr%d_%d' % (si, s_), 'Xi%d_%d' % (si, s_), 'Xr%d_%d' % (si, d_), 'Xi%d_%d' % (si, d_)
                    P.add('scalar', lambda e, xr_d=xr_d, xr_s=xr_s, d=d: e.copy(xr_d[:, 0:d], xr_s[:, 0:d]), r=[tr_s], w=[tr_d])
                    stt(xr_d[:, d:T], xr_s[:, 0:T - d], pr_, xr_s[:, d:T], ALU.mult, ALU.add, [tr_s, 'pwr'], [tr_d])
                    stt(xr_d[:, d:T], xi_s[:, 0:T - d], npi_, xr_d[:, d:T], ALU.mult, ALU.add, [ti_s, 'npwi', tr_d], [tr_d])
                    P.add('scalar', lambda e, xi_d=xi_d, xi_s=xi_s, d=d: e.copy(xi_d[:, 0:d], xi_s[:, 0:d]), r=[ti_s], w=[ti_d])
                    act(mA[:, 0:T - d], xi_s[:, 0:T - d], AF.Copy, [ti_s, 'pwr'], [tmA], scale=pr_)
                    act(mB[:, 0:T - d], xr_s[:, 0:T - d], AF.Copy, [tr_s, 'pwi'], [tmB], scale=pi_)
                    tt(xi_d[:, d:T], mA[:, 0:T - d], xi_s[:, d:T], ALU.add, [tmA, ti_s], [ti_d], eng='gpsimd')
                    tt(xi_d[:, d:T], mB[:, 0:T - d], xi_d[:, d:T], ALU.add, [tmB, ti_d], [ti_d], eng='gpsimd')
            for si, j in enumerate(js):
                gp = 4 * ct + j
                XR, XI = XRs[si], XIs[si]
                t0r, t0i = 'Xr%d_0' % si, 'Xi%d_0' % si
                P.add('vector', lambda e, gp=gp, XR=XR: e.tensor_copy(Xend[:, gp:gp + 1], XR[0][:, T - 1:T]), r=[t0r], w=['Xend'])
                P.add('vector', lambda e, gp=gp, XI=XI: e.tensor_copy(Xend[:, 32 + gp:33 + gp], XI[0][:, T - 1:T]), r=[t0i], w=['Xend'])
                for half in range(2):
                    hs = slice(half * 512, (half + 1) * 512)
                    P.mm(psum[6 + half][:, :], LCr[j][:], XR[0][:, hs], j == 0, False, r=['LCr%d' % j, t0r], w=['ps%d' % (6 + half)])
                    P.mm(psum[6 + half][:, :], LCi[j][:], XI[0][:, hs], False, j == 3, r=['LCi%d' % j, t0i], w=['ps%d' % (6 + half)])
        for half in ([] if summary else range(2)):
            hs = slice(half * 512, (half + 1) * 512)
            stt(ytmp[:], uT[:, ct, hs], dvec[:, ct:ct + 1], psum[6 + half][:, :], ALU.mult, ALU.add, ['uT%d' % ct, 'dvec', 'ps%d' % (6 + half)], ['ytmp'])
            act(yT[:, ct, hs], ytmp[:], AF.Gelu_apprx_tanh, ['ytmp'], ['yT'])
    P.dma('sync', s5X_out, Xend[:], r=['Xend'], w=['s5X_out'])
    if summary:
        P.emit()
        return nc
    P.barrier()
    for b in range(2):
        WA, wta = wblk[0], 'wblk0'
        WG, wtg = wblk[1], 'wblk1'
        P.dma('gpsimd', WA[:, 0:8, :], w_glu[:, b * 512:(b + 1) * 512].rearrange('(kt p) n -> p kt n', p=128), r=[], w=[wta])
        P.dma('gpsimd', WG[:, 0:8, :], w_glu[:, 1024 + b * 512:1024 + (b + 1) * 512].rearrange('(kt p) n -> p kt n', p=128), r=[], w=[wtg])
        for ntl in range(4):
            nt = 4 * b + ntl
            for half in range(2):
                hs = slice(half * 512, (half + 1) * 512)
                ba = nextbank(0, 6); bg = nextbank(0, 6)
                for kt in range(8):
                    P.mm(psum[ba][:, :], WA[:, kt, ntl * 128:(ntl + 1) * 128], yT[:, kt, hs], kt == 0, kt == 7, r=['yT', wta], w=['ps%d' % ba])
                for kt in range(8):
                    P.mm(psum[bg][:, :], WG[:, kt, ntl * 128:(ntl + 1) * 128], yT[:, kt, hs], kt == 0, kt == 7, r=['yT', wtg], w=['ps%d' % bg])
                act(sgm[:], psum[bg][:, :], AF.Sigmoid, ['ps%d' % bg], ['sgm'])
                tt(obT[:, nt, hs], psum[ba][:, :], sgm[:], ALU.mult, ['ps%d' % ba, 'sgm'], ['obT'])
    if dbg:
        for nt in range(8):
            P.add('vector', lambda e, nt=nt: e.tensor_copy(Xr[0][:], obT[:, nt, :]), r=['obT'], w=['Xr0_0'])
            P.dma('sync', dbgo['obT'][nt * 128:(nt + 1) * 128, :], Xr[0][:], r=['Xr0_0'], w=['dbg_obT'])
    if phases <= 3:
        P.emit()
        return nc

    P.barrier()
    sb_off[0] = mark_persist
    mergedT = sb('mergedT', [128, KT, T], BF16)
    mark4 = sb_off[0]
    oaT = sb('oaT', [128, 8, T], BF16)
    wA = [sb('wA%d' % i, [128, 8, 512], BF16) for i in range(2)]
    wB = [sb('wB%d' % i, [128, 8, 512], BF16) for i in range(2)]
    wGA = sb('wGA', [128, KT, 512], BF16)
    wGB = sb('wGB', [128, KT, 512], BF16)
    sga = sb('sga', [128, 512], F32)
    sgb = sb('sgb', [128, 512], F32)
    mm1 = sb('mm1', [128, 512], F32)
    mm2 = sb('mm2', [128, 512], F32)
    for n in range(NT):
        bank = nextbank()
        for kt in range(8):
            P.tr(psb(bank)[:, kt * 128:(kt + 1) * 128], oa_tok[:, n, kt * 128:(kt + 1) * 128], ident_b[:], r=['oa_tok', 'ident_b'], w=['ps%d' % bank])
        P.add('vector', lambda e, bank=bank, n=n: e.tensor_copy(oaT[:, :, n * 128:(n + 1) * 128], psb(bank).rearrange('p (j t) -> p j t', j=8)),
              r=['ps%d' % bank], w=['oaT'])
    for cb in range(4):
        i = cb % 2
        P.dma('gpsimd', wA[i][:], w_up_a[:, cb * 512:(cb + 1) * 512].rearrange('(kt p) n -> p kt n', p=128), r=[], w=['wA%d' % i])
        P.dma('gpsimd', wB[i][:], w_up_b[:, cb * 512:(cb + 1) * 512].rearrange('(kt p) n -> p kt n', p=128), r=[], w=['wB%d' % i])
        P.dma('gpsimd', wGA[:], w_in[:, 5120 + cb * 512:5120 + (cb + 1) * 512].rearrange('(kt p) n -> p kt n', p=128), r=[], w=['wGA'])
        P.dma('gpsimd', wGB[:], w_in[:, 7168 + cb * 512:7168 + (cb + 1) * 512].rearrange('(kt p) n -> p kt n', p=128), r=[], w=['wGB'])
        for ntl in range(4):
            nt = 4 * cb + ntl
            ns = slice(ntl * 128, (ntl + 1) * 128)
            for half in range(2):
                hs = slice(half * 512, (half + 1) * 512)
                ba, bb_, bga, bgb = nextbank(), nextbank(), nextbank(), nextbank()
                for kt in range(8):
                    P.mm(psum[ba][:, :], wA[i][:, kt, ns], oaT[:, kt, hs], kt == 0, kt == 7, r=['wA%d' % i, 'oaT'], w=['ps%d' % ba])
                for kt in range(8):
                    P.mm(psum[bb_][:, :], wB[i][:, kt, ns], obT[:, kt, hs], kt == 0, kt == 7, r=['wB%d' % i, 'obT'], w=['ps%d' % bb_])
                for kt in range(KT):
                    P.mm(psum[bga][:, :], wGA[:, kt, ns], xT[:, kt, hs], kt == 0, kt == KT - 1, r=['wGA', 'xT'], w=['ps%d' % bga])
                for kt in range(KT):
                    P.mm(psum[bgb][:, :], wGB[:, kt, ns], xT[:, kt, hs], kt == 0, kt == KT - 1, r=['wGB', 'xT'], w=['ps%d' % bgb])
                act(sga[:], psum[bga][:, :], AF.Sigmoid, ['ps%d' % bga], ['sga'])
                act(sgb[:], psum[bgb][:, :], AF.Sigmoid, ['ps%d' % bgb], ['sgb'])
                tt(mm1[:], psum[ba][:, :], sga[:], ALU.mult, ['ps%d' % ba, 'sga'], ['mm1'])
                tt(mm2[:], psum[bb_][:, :], sgb[:], ALU.mult, ['ps%d' % bb_, 'sgb'], ['mm2'])
                tt(mergedT[:, nt, hs], mm1[:], mm2[:], ALU.add, ['mm1', 'mm2'], ['mergedT'], eng='gpsimd')
    P.barrier()
    sb_off[0] = mark4
    wo_sb = sb('wo_sb', [128, KT, D], BF16)
    x1b = sb('x1b', [128, D], BF16)
    st4 = sb('st4', [128, 4], F32)
    sb_save = sb_off[0]
    sb_off[0] = reg_oa
    lng = sb('lng', [128, D], F32)
    lnb = sb('lnb', [128, D], F32)
    xres = sb('xres', [128, D], F32)
    h1 = sb('h1', [128, D], F32)
    sb_off[0] = sb_save
    x1T = xT
    for cb in range(4):
        P.dma('gpsimd', wo_sb[:, :, cb * 512:(cb + 1) * 512], w_o[:, cb * 512:(cb + 1) * 512].rearrange('(kt p) n -> p kt n', p=128), r=[], w=['wo_sb%d' % cb])

    def layer_norm(src, gamma_d, beta_d, hbuf, dst_dram, tagp):
        pass

    P.dma('sync', lng[:], ln1_g[0].partition_broadcast(128), r=[], w=['lng'])
    P.dma('sync', lnb[:], ln1_b[0].partition_broadcast(128), r=[], w=['lnb'])

    def ln_rows(hb, ht, gam, bet):
        P.add('vector', lambda e: e.tensor_reduce(st4[:, 0:1], hb[:], AX.X, ALU.add), r=[ht], w=['st4a'])
        ts(st4[:, 1:2], st4[:, 0:1], -1.0 / D, None, ALU.mult, None, ['st4a'], ['st4b'])
        ts(hb[:], hb[:], st4[:, 1:2], None, ALU.add, None, [ht, 'st4b'], [ht])
        P.add('gpsimd', lambda e: e.memset(st4[:, 2:3], 0.0), w=['st4c'])
        P.add('scalar', lambda e: e.activation(junkD[:], hb[:], AF.Square, accum_out=st4[:, 2:3]), r=[ht, 'st4c'], w=['junkD', 'st4c'])
        act(st4[:, 3:4], st4[:, 2:3], AF.Ln, ['st4c'], ['st4d'], bias=LN_EPS, scale=1.0 / D)
        act(st4[:, 3:4], st4[:, 3:4], AF.Exp, ['st4d'], ['st4d'], scale=-0.5)
        stt(hb[:], hb[:], st4[:, 3:4], gam[:], ALU.mult, ALU.mult, [ht, 'st4d', 'lng'], [ht])
        tt(hb[:], hb[:], bet[:], ALU.add, [ht, 'lnb'], [ht], eng='gpsimd')

    junkD = sb('junkD', [128, D], BF16)
    for n in range(NT):
        P.dma('sync', xres[:], x_in[n * 128:(n + 1) * 128, :], r=[], w=['xres'])
        for cb in range(4):
            bank = nextbank()
            for kt in range(KT):
                P.mm(psum[bank][:, :], mergedT[:, kt, n * 128:(n + 1) * 128], wo_sb[:, kt, cb * 512:(cb + 1) * 512], kt == 0, kt == KT - 1,
                     r=['mergedT', 'wo_sb%d' % cb], w=['ps%d' % bank])
            stt(h1[:, cb * 512:(cb + 1) * 512], xres[:, cb * 512:(cb + 1) * 512], ALPHA, psum[bank][:, :], ALU.mult, ALU.add,
                ['xres', 'ps%d' % bank], ['h1'])
        ln_rows(h1, 'h1', lng, lnb)
        P.dma('sync', x1d[n * 128:(n + 1) * 128, :], h1[:], r=['h1'], w=['x1d'])
        if dbg:
            P.dma('sync', dbgo['x1'][n * 128:(n + 1) * 128, :], h1[:], r=['h1'], w=['dbg_x1'])
        P.add('scalar', lambda e: e.copy(x1b[:], h1[:]), r=['h1'], w=['x1b'])
        for half in range(2):
            bank = nextbank()
            for j in range(8):
                kt = half * 8 + j
                P.tr(psb(bank)[:, j * 128:(j + 1) * 128], x1b[:, kt * 128:(kt + 1) * 128], ident_b[:], r=['x1b', 'ident_b'], w=['ps%d' % bank])
            P.add('vector', lambda e, bank=bank, half=half, n=n: e.tensor_copy(
                x1T[:, half * 8:(half + 1) * 8, n * 128:(n + 1) * 128], psb(bank).rearrange('p (j t) -> p j t', j=8)),
                r=['ps%d' % bank], w=['x1T'])
    if phases <= 4:
        P.emit()
        return nc

    P.barrier()
    sb_off[0] = mark_persist
    s_d = nc.dram_tensor('s_d', [T, 16, 128], F32).ap()
    wblk = [sb('wblk%d' % i, [128, KT, 512], BF16) for i in range(2)]
    qTb = sb('qTb', [128, 4, T], F32)
    kld = sb('kld', [128, 16, 128], F32)
    KTt = sb('KTt', [128, 16, 128], F32)
    s_st = [sb('s_st%d' % i, [128, 4, 128], F32) for i in range(2)]
    s_sb = sb('s_sb', [128, 16, 128], F32)
    mr = sb('mr', [128, 128], F32)
    top = sb('top', [128, 16, 16], F32)
    cand = sb('cand', [128, 8, 256], F32)
    cand2 = sb('cand2', [128, 8, 256], F32)
    vals = sb('vals', [128, 8, 16], F32)
    ev = sb('ev', [128, 8, 16], F32)
    Zs = sb('Zs', [128, 8], F32)
    nb = sb('nb', [128, 8], F32)
    Tbuf2 = [sb('Tbuf%d' % i, [128, 16, 128], F32) for i in range(2)]
    Eb2 = [sb('Eb%d' % i, [128, 2048], BF16) for i in range(2)]
    Pm2 = [sb('Pm%d' % i, [128, 2048], BF16) for i in range(2)]
    sb_save = sb_off[0]
    sb_off[0] = reg_oa
    G = sb('G', [128, 16384], BF16)
    sb_off[0] = sb_save

    P.dma('sync', kld[:], keys.rearrange('h i d -> i h d'), r=[], w=['kld'])
    for q4 in range(4):
        bank = nextbank()
        for hl in range(4):
            hh = 4 * q4 + hl
            P.tr(psum[bank][:, hl * 128:(hl + 1) * 128], kld[:, hh, :], ident_f[:], r=['kld', 'ident_f'], w=['ps%d' % bank])
        P.add('vector', lambda e, bank=bank, q4=q4: e.tensor_copy(KTt[:, 4 * q4:4 * q4 + 4, :], psum[bank][:, :].rearrange('p (a b) -> p a b', a=4)),
              r=['ps%d' % bank], w=['KTt'])
    for blk in range(4):
        wb, wt = load_wblk(w_q, blk * 512)
        for hl in range(4):
            for half in range(2):
                hs = slice(half * 512, (half + 1) * 512)
                bank = nextbank()
                for kt in range(KT):
                    P.mm(psum[bank][:, :], wb[:, kt, hl * 128:(hl + 1) * 128], x1T[:, kt, hs], kt == 0, kt == KT - 1, r=[wt, 'x1T'], w=['ps%d' % bank])
                P.add('scalar', lambda e, bank=bank, hl=hl, hs=hs: e.copy(qTb[:, hl, hs], psum[bank][:, :]), r=['ps%d' % bank], w=['qTb%d' % hl])
        for n in range(NT):
            bank = nextbank()
            i2 = n % 2
            for hl in range(4):
                P.mm(psum[bank][:, hl * 128:(hl + 1) * 128], qTb[:, hl, n * 128:(n + 1) * 128], KTt[:, 4 * blk + hl, :], True, True,
                     r=['qTb%d' % hl, 'KTt'], w=['ps%d' % bank])
            P.add('vector', lambda e, bank=bank, i2=i2: e.tensor_copy(s_st[i2][:], psum[bank][:, :].rearrange('p (a b) -> p a b', a=4)),
                  r=['ps%d' % bank], w=['s_st%d' % i2])
            P.dma('sync', s_d[n * 128:(n + 1) * 128, 4 * blk:4 * blk + 4, :], s_st[i2][:], r=['s_st%d' % i2], w=['s_d'])
    P.barrier()
    top4 = top[:].rearrange('p (h two) k -> p h two k', two=2)
    gcnt = 0
    for n in range(NT):
        P.dma('sync', s_sb[:], s_d[n * 128:(n + 1) * 128], r=[], w=['s_sb'])
        for hh in range(16):
            P.add('vector', lambda e, hh=hh: e.max(out=top[:, hh, 0:8], in_=s_sb[:, hh, :]), r=['s_sb'], w=['top'])
            P.add('vector', lambda e, hh=hh: e.match_replace(out=mr[:], in_to_replace=top[:, hh, 0:8], in_values=s_sb[:, hh, :], imm_value=-1e30),
                  r=['s_sb', 'top'], w=['mr'])
            P.add('vector', lambda e, hh=hh: e.max(out=top[:, hh, 8:16], in_=mr[:]), r=['mr'], w=['top'])
        tt(cand[:].rearrange('p h (a b) -> p h a b', a=16), bc(top4[:, :, 0, :].unsqueeze(3), [128, 8, 16, 16]),
           bc(top4[:, :, 1, :].unsqueeze(2), [128, 8, 16, 16]), ALU.add, ['top'], ['cand'])
        for h in range(8):
            P.add('vector', lambda e, h=h: e.max(out=vals[:, h, 0:8], in_=cand[:, h, :]), r=['cand'], w=['vals'])
            P.add('vector', lambda e, h=h: e.match_replace(out=cand2[:, h, :], in_to_replace=vals[:, h, 0:8], in_values=cand[:, h, :], imm_value=-1e30),
                  r=['cand', 'vals'], w=['cand2'])
            P.add('vector', lambda e, h=h: e.max(out=vals[:, h, 8:16], in_=cand2[:, h, :]), r=['cand2'], w=['vals'])
        tt(ev[:], vals[:], bc(vals[:, :, 0:1], [128, 8, 16]), ALU.subtract, ['vals'], ['ev'])
        act(ev[:], ev[:], AF.Exp, ['ev'], ['ev'])
        P.add('vector', lambda e: e.tensor_reduce(Zs[:], ev[:], AX.X, ALU.add), r=['ev'], w=['Zs'])
        act(Zs[:], Zs[:], AF.Ln, ['Zs'], ['Zs'])
        tt(nb[:], Zs[:], vals[:, :, 0], ALU.add, ['Zs', 'vals'], ['nb'])
        ts(nb[:], nb[:], -1.0, None, ALU.mult, None, ['nb'], ['nb'])
        its = [(h, ic) for ic in range(8) for h in range(8)]

        def emitT(idx):
            h, ic = its[idx]
            pi_ = idx % 2
            isl = slice(ic * 16, (ic + 1) * 16)
            tt(Tbuf2[pi_][:], bc(s_sb[:, 2 * h, isl].unsqueeze(2), [128, 16, 128]), bc(s_sb[:, 2 * h + 1, :].unsqueeze(1), [128, 16, 128]),
               ALU.add, ['s_sb'], ['Tbuf%d' % pi_])

        emitT(0)
        for idx, (h, ic) in enumerate(its):
            if idx + 1 < len(its):
                emitT(idx + 1)
            pi_ = idx % 2
            Tb, Ebb, Pmb = Tbuf2[pi_], Eb2[pi_], Pm2[pi_]
            tT, tE, tP = 'Tbuf%d' % pi_, 'Eb%d' % pi_, 'Pm%d' % pi_
            gsl = slice(ic * 2048, (ic + 1) * 2048)
            bset = (ic % 2) * 4
            Tf = Tb[:].rearrange('p a b -> p (a b)')
            act(Ebb[:], Tf, AF.Exp, [tT, 'nb'], [tE], bias=nb[:, h:h + 1])
            stt(Pmb[:], Tf, vals[:, h, 15:16], Ebb[:], ALU.is_ge, ALU.mult, [tT, 'vals', tE], [tP])
            for q4 in range(4):
                P.mm(psum[bset + q4][:, :], ident_b[:], Pmb[:, q4 * 512:(q4 + 1) * 512], h == 0, h == 7, r=['ident_b', tP], w=['ps%d' % (bset + q4)])
            if h == 7:
                for q4 in range(4):
                    act(G[:, ic * 2048 + q4 * 512:ic * 2048 + (q4 + 1) * 512], psum[bset + q4][:, :], AF.Copy, ['ps%d' % (bset + q4)], ['G%d' % ic])
        P.dma('sync', Gd[n * 128:(n + 1) * 128, :], G[:], r=['G%d' % i for i in range(8)], w=['Gd'])

    P.barrier()
    sb_off[0] = mark_persist
    yacc = sb('yacc', [128, NT, D], F32)
    ublk = [sb('ublk%d' % i, [128, 4, D], BF16) for i in range(2)]
    sb_save = sb_off[0]
    sb_off[0] = reg_oa
    vblk = [sb('vblk%d' % i, [128, 4, D], BF16) for i in range(2)]
    sb_off[0] = sb_save
    uTb = sb('uTb', [128, KT, 512], BF16)
    Gs = [sb('Gs%d' % i, [128, 512], BF16) for i in range(2)]
    gl2 = [sb('gl%d' % i, [128, 512], F32) for i in range(2)]
    Wb2 = [sb('Wb%d' % i, [128, 512], BF16) for i in range(2)]
    WT2 = [sb('WT%d' % i, [128, 4, 128], BF16) for i in range(2)]
    def load_eb(eb):
        i = eb % 2
        P.dma('gpsimd', ublk[i][:], peer_u[eb * 512:(eb + 1) * 512, :].rearrange('(et p) d -> p et d', p=128), r=[], w=['ublk%d' % i])
        P.dma('gpsimd', vblk[i][:], peer_v[eb * 512:(eb + 1) * 512, :].rearrange('(et p) d -> p et d', p=128), r=[], w=['vblk%d' % i])
        for k2 in range(8):
            bank = nextbank()
            for kk in range(2):
                kt = 2 * k2 + kk
                for et in range(4):
                    P.tr(psb(bank)[:, (kk * 4 + et) * 128:(kk * 4 + et + 1) * 128], ublk[i][:, et, kt * 128:(kt + 1) * 128], ident_b[:],
                         r=['ublk%d' % i, 'ident_b'], w=['ps%d' % bank])
            if k2 % 2 == 0:
                P.add('vector', lambda e, bank=bank, k2=k2: e.tensor_copy(uTb[:, 2 * k2:2 * k2 + 2, :], psb(bank).rearrange('p (k e) -> p k e', k=2)),
                      r=['ps%d' % bank], w=['uTb'])
            else:
                P.add('scalar', lambda e, bank=bank, k2=k2: e.copy(uTb[:, 2 * k2:2 * k2 + 2, :], psb(bank).rearrange('p (k e) -> p k e', k=2)),
                      r=['ps%d' % bank], w=['uTb'])

    items = [(eb, n) for eb in range(32) for n in range(NT)]

    def stage_A(idx):
        eb, n = items[idx]
        gi = idx % 2
        if n == 0:
            load_eb(eb)
        P.dma('sync', Gs[gi][:], Gd[n * 128:(n + 1) * 128, eb * 512:(eb + 1) * 512], r=[], w=['Gs%d' % gi])
        bA = nextbank()
        for kt in range(KT):
            P.mm(psum[bA][:, :], x1T[:, kt, n * 128:(n + 1) * 128], uTb[:, kt, :], kt == 0, kt == KT - 1, r=['x1T', 'uTb'], w=['ps%d' % bA])
        act(gl2[gi][:], psum[bA][:, :], AF.Gelu_apprx_tanh, ['ps%d' % bA], ['gl%d' % gi])
        tt(Wb2[gi][:], gl2[gi][:], Gs[gi][:], ALU.mult, ['gl%d' % gi, 'Gs%d' % gi], ['Wb%d' % gi], eng='gpsimd')

    def stage_TY(idx):
        eb, n = items[idx]
        gi = idx % 2
        i = eb % 2
        Wb, WT = Wb2[gi], WT2[gi]
        tw, twt = 'Wb%d' % gi, 'WT%d' % gi
        bT_ = nextbank()
        for et in range(4):
            P.tr(psb(bT_)[:, et * 128:(et + 1) * 128], Wb[:, et * 128:(et + 1) * 128], ident_b[:], r=[tw, 'ident_b'], w=['ps%d' % bT_])
        P.add('scalar', lambda e, bT_=bT_, WT=WT: e.copy(WT[:], psb(bT_)[:, 0:512].rearrange('p (a b) -> p a b', a=4)), r=['ps%d' % bT_], w=[twt])
        for db in range(4):
            ds_ = slice(db * 512, (db + 1) * 512)
            bY = nextbank()
            for et in range(4):
                P.mm(psum[bY][:, :], WT[:, et, :], vblk[i][:, et, ds_], et == 0, et == 3, r=[twt, 'vblk%d' % i], w=['ps%d' % bY])
            if eb == 0:
                P.add('scalar', lambda e, bY=bY, n=n, ds_=ds_: e.copy(yacc[:, n, ds_], psum[bY][:, :]), r=['ps%d' % bY], w=['yacc%d' % n])
            else:
                tt(yacc[:, n, ds_], yacc[:, n, ds_], psum[bY][:, :], ALU.add, ['yacc%d' % n, 'ps%d' % bY], ['yacc%d' % n])

    stage_A(0)
    for idx in range(len(items)):
        if idx + 1 < len(items):
            stage_A(idx + 1)
        stage_TY(idx)

    P.barrier()
    st4_2 = sb('st4b', [128, 4], F32)
    junkD_2 = sb('junkD2', [128, D], BF16)
    sb_save = sb_off[0]
    sb_off[0] = reg_oa
    lng_2 = sb('lng2', [128, D], F32)
    lnb_2 = sb('lnb2', [128, D], F32)
    xres_2 = sb('xres2', [128, D], F32)
    h1_2 = sb('h12', [128, D], F32)
    sb_off[0] = sb_save
    P.dma('sync', lng_2[:], ln2_g[0].partition_broadcast(128), r=[], w=['lng'])
    P.dma('sync', lnb_2[:], ln2_b[0].partition_broadcast(128), r=[], w=['lnb'])

    def ln_rows2(hb, ht, gam, bet):
        P.add('vector', lambda e: e.tensor_reduce(st4_2[:, 0:1], hb[:], AX.X, ALU.add), r=[ht], w=['st4a'])
        ts(st4_2[:, 1:2], st4_2[:, 0:1], -1.0 / D, None, ALU.mult, None, ['st4a'], ['st4b'])
        ts(hb[:], hb[:], st4_2[:, 1:2], None, ALU.add, None, [ht, 'st4b'], [ht])
        P.add('gpsimd', lambda e: e.memset(st4_2[:, 2:3], 0.0), w=['st4c'])
        P.add('scalar', lambda e: e.activation(junkD_2[:], hb[:], AF.Square, accum_out=st4_2[:, 2:3]), r=[ht, 'st4c'], w=['junkD', 'st4c'])
        act(st4_2[:, 3:4], st4_2[:, 2:3], AF.Ln, ['st4c'], ['st4d'], bias=LN_EPS, scale=1.0 / D)
        act(st4_2[:, 3:4], st4_2[:, 3:4], AF.Exp, ['st4d'], ['st4d'], scale=-0.5)
        stt(hb[:], hb[:], st4_2[:, 3:4], gam[:], ALU.mult, ALU.mult, [ht, 'st4d', 'lng'], [ht])
        tt(hb[:], hb[:], bet[:], ALU.add, [ht, 'lnb'], [ht], eng='gpsimd')

    for n in range(NT):
        P.dma('sync', xres_2[:], x1d[n * 128:(n + 1) * 128, :], r=[], w=['xres'])
        stt(h1_2[:], xres_2[:], ALPHA, yacc[:, n, :], ALU.mult, ALU.add, ['xres', 'yacc%d' % n], ['h1'])
        ln_rows2(h1_2, 'h1', lng_2, lnb_2)
        P.dma('sync', x_out[n * 128:(n + 1) * 128, :], h1_2[:], r=['h1'], w=['x_out'])
    P.emit()
    return nc


def layer_inputs(z, l, phases=99):
    f = lambda a: np.ascontiguousarray(a, dtype=np.float32)
    m = {
        'w_in': f(z['w_in'][l]),
        'lb_logits': f(z['hgrn_lb_logits']),
        'lmask': np.array([[0.0, float(l >= 1), float(l >= 2), float(l >= 3)]], np.float32),
        'norm_g': f(z['hgrn_norm_g'][l]).reshape(1, 1024),
    }
    if phases > 2:
        m.update({
            'lam_re': f(z['s5_lambda_re'][l]).reshape(32, 128),
            'lam_im': f(z['s5_lambda_im'][l]).reshape(32, 128),
            'log_step': f(z['s5_log_step'][l]).reshape(32, 2),
            'b_re': f(z['s5_b_re'][l]),
            'b_im': f(z['s5_b_im'][l]),
            'c_re': f(z['s5_c_re'][l]).reshape(1024, 64),
            'c_im': f(z['s5_c_im'][l]).reshape(1024, 64),
            's5_d': f(z['s5_d'][l]).reshape(8, 128),
            'w_glu': f(z['s5_w_glu'][l]),
        })
    if phases > 3:
        m.update({
            'w_up_a': f(z['w_up_a'][l]), 'w_up_b': f(z['w_up_b'][l]), 'w_o': f(z['w_o'][l]),
            'ln1_g': f(z['ln1_g'][l]).reshape(1, D), 'ln1_b': f(z['ln1_b'][l]).reshape(1, D),
        })
    if phases > 4:
        m.update({
            'w_q': f(z['peer_w_q'][l]), 'keys': f(z['peer_keys'][l]).reshape(16, 128, 128),
            'peer_u': f(z['peer_u'][l]), 'peer_v': f(z['peer_v'][l]),
            'ln2_g': f(z['ln2_g'][l]).reshape(1, D), 'ln2_b': f(z['ln2_b'][l]).reshape(1, D),
        })
    return m


def kernel(**inputs):
    z = inputs
    f32 = np.float32
    x = np.ascontiguousarray(np.asarray(z['x'])[0], dtype=f32)
    ncA = build(dbg=False, phases=3, summary=True)
    ncB = build(dbg=False, phases=99)
    xs = [np.ascontiguousarray(x[c * T:(c + 1) * T]) for c in range(NCORES)]
    zU = np.zeros((NCORES, 8, 128, 128), f32)
    zD = np.zeros((NCORES, 128, 8), f32)
    zX = np.zeros((NCORES, 128, 64), f32)
    zc = np.zeros((1, NCORES), f32)
    cores = list(range(NCORES))
    for l in range(4):
        mA = layer_inputs(z, l, 3)
        in_maps = [dict(mA, x=xs[c], cmask=zc, hgU_all=zU, hgD_all=zD, s5X_all=zX) for c in cores]
        rA = run_bass_kernel_spmd(ncA, in_maps, core_ids=cores).results
        U = np.ascontiguousarray(np.stack([rA[c]['hgU_out'] for c in cores]), dtype=f32)
        Dd = np.ascontiguousarray(np.stack([rA[c]['hgD_out'] for c in cores]), dtype=f32)
        Xs = np.ascontiguousarray(np.stack([rA[c]['s5X_out'] for c in cores]), dtype=f32)
        mB = layer_inputs(z, l, 99)
        in_maps = [dict(mB, x=xs[c], cmask=(np.arange(NCORES) < c).astype(f32)[None, :],
                        hgU_all=U, hgD_all=Dd, s5X_all=Xs) for c in cores]
        rB = run_bass_kernel_spmd(ncB, in_maps, core_ids=cores).results
        xs = [np.ascontiguousarray(rB[c]['x_out'], dtype=f32) for c in cores]
    return np.concatenate(xs, axis=0)[None].astype(f32)
```
